# Optimizing a Trainium2 kernel written in Bass

```python
import math
import jax, jax.numpy as jnp
from jax import lax
import numpy as np

D_MODEL = 1024
BATCH = 4
SEQ = 8192
DEPTH = 2

N_MIXERS = 2
N_Q_HEADS = 8
N_KV_HEADS = 2
HEAD_DIM = D_MODEL // N_Q_HEADS
Q_PER_KV = N_Q_HEADS // N_KV_HEADS
ROT_DIM = HEAD_DIM // 4
ROPE_THETA = 500000.0
WINDOW = 128
BLOCK = 128
D_RNN = 3 * D_MODEL // 2
N_RNN_BLOCKS = 16
RNN_BLOCK_W = D_RNN // N_RNN_BLOCKS
CONV_W = 4
CONV_LEFT = 2
LRU_C = 8.0
D_FF = 4 * D_MODEL
DN_ALPHA = (2.0 * DEPTH) ** 0.25
DN_BETA = (8.0 * DEPTH) ** -0.25
LN_EPS = 1e-5
ADA_INIT = 0.5

N_ATTN_LAYERS = (DEPTH + N_MIXERS - 1) // N_MIXERS
N_RNN_LAYERS = DEPTH // N_MIXERS

kernel_name = "hybrid_swa_rglru_adaln_deepnorm_encoder"


def layer_norm(x, g, b):
    xf = x.astype(jnp.float32)
    mu = jnp.mean(xf, axis=-1, keepdims=True)
    var = jnp.mean(jnp.square(xf - mu), axis=-1, keepdims=True)
    y = (xf - mu) * lax.rsqrt(var + LN_EPS)
    return (y * g.astype(jnp.float32) + b.astype(jnp.float32)).astype(x.dtype)


def ada_modulation(c, w, b):
    mod = jax.nn.silu(c) @ w + b
    shift, scale, gate = jnp.split(mod, 3, axis=-1)
    return shift[:, None, :], scale[:, None, :], gate[:, None, :]


def partial_rope(t, cos, sin):
    half = ROT_DIM // 2
    tr = t[..., :ROT_DIM].astype(jnp.float32)
    t1, t2 = tr[..., :half], tr[..., half:]
    cs, sn = cos[None, :, None, :], sin[None, :, None, :]
    rot = jnp.concatenate([t1 * cs - t2 * sn, t2 * cs + t1 * sn], axis=-1).astype(t.dtype)
    return jnp.concatenate([rot, t[..., ROT_DIM:]], axis=-1)


def windowed_gqa(h, w_in, w_out, sinks):
    B, S, _ = h.shape
    nblk = S // BLOCK
    qkv = h @ w_in
    q, k, v = jnp.split(qkv, [N_Q_HEADS * HEAD_DIM, (N_Q_HEADS + N_KV_HEADS) * HEAD_DIM], axis=-1)
    q = q.reshape(B, S, N_Q_HEADS, HEAD_DIM)
    k = k.reshape(B, S, N_KV_HEADS, HEAD_DIM)
    v = v.reshape(B, S, N_KV_HEADS, HEAD_DIM)
    pos = jnp.arange(S, dtype=jnp.float32)
    inv_freq = ROPE_THETA ** (-jnp.arange(0, ROT_DIM, 2, dtype=jnp.float32) / ROT_DIM)
    ang = pos[:, None] * inv_freq[None, :]
    cos, sin = jnp.cos(ang), jnp.sin(ang)
    q = partial_rope(q, cos, sin)
    k = partial_rope(k, cos, sin)
    qb = q.reshape(B, nblk, BLOCK, N_KV_HEADS, Q_PER_KV, HEAD_DIM)

    def band(t):
        tp = jnp.pad(t, ((0, 0), (BLOCK, BLOCK), (0, 0), (0, 0)))
        tb = tp.reshape(B, nblk + 2, BLOCK, N_KV_HEADS, HEAD_DIM)
        return jnp.concatenate([tb[:, :-2], tb[:, 1:-1], tb[:, 2:]], axis=2)

    kb, vb = band(k), band(v)
    scores = jnp.einsum("bnqhgd,bnjhd->bnhgqj", qb, kb,
                        preferred_element_type=jnp.float32) * (HEAD_DIM ** -0.5)
    blk = jnp.arange(nblk)[:, None, None] * BLOCK
    qpos = blk + jnp.arange(BLOCK)[None, :, None]
    kpos = blk - BLOCK + jnp.arange(3 * BLOCK)[None, None, :]
    valid = (jnp.abs(qpos - kpos) <= WINDOW) & (kpos >= 0) & (kpos < S)
    scores = jnp.where(valid[None, :, None, None], scores, -jnp.inf)
    sink = sinks.astype(jnp.float32).reshape(N_KV_HEADS, Q_PER_KV)[None, None, :, :, None, None]
    m = jnp.maximum(jnp.max(scores, axis=-1, keepdims=True), sink)
    p = jnp.exp(scores - m)
    denom = jnp.sum(p, axis=-1, keepdims=True) + jnp.exp(sink - m)
    probs = (p / denom).astype(v.dtype)
    o = jnp.einsum("bnhgqj,bnjhd->bnqhgd", probs, vb)
    o = o.reshape(B, S, N_Q_HEADS * HEAD_DIM)
    return o @ w_out


def centred_depthwise_conv(x, w, b):
    S = x.shape[1]
    xp = jnp.pad(x, ((0, 0), (CONV_LEFT, CONV_W - 1 - CONV_LEFT), (0, 0)))
    out = xp[:, 0:S] * w[0] + b
    for j in range(1, CONV_W):
        out = out + xp[:, j:j + S] * w[j]
    return out


def rg_lru(x, w_a, b_a, w_x, b_x, lam, reverse):
    B, S, _ = x.shape
    xb = x.reshape(B, S, N_RNN_BLOCKS, RNN_BLOCK_W)
    r = jax.nn.sigmoid((jnp.einsum("bsni,nij->bsnj", xb, w_a).reshape(B, S, D_RNN) + b_a).astype(jnp.float32))
    i = jax.nn.sigmoid((jnp.einsum("bsni,nij->bsnj", xb, w_x).reshape(B, S, D_RNN) + b_x).astype(jnp.float32))
    log_a = -LRU_C * r * jax.nn.softplus(-lam.astype(jnp.float32))
    a = jnp.exp(log_a)
    mult = jnp.sqrt(-jnp.expm1(2.0 * log_a))
    u = mult * (i * x.astype(jnp.float32))

    def combine(left, right):
        a_l, b_l = left
        a_r, b_r = right
        return a_l * a_r, a_r * b_l + b_r

    _, hs = lax.associative_scan(combine, (a, u), axis=1, reverse=reverse)
    return hs


def recurrent_block(h, w_in, conv_w, conv_b, w_a, b_a, w_x, b_x, lam, w_out):
    z = h @ w_in
    xr, gate = jnp.split(z, 2, axis=-1)
    xr = centred_depthwise_conv(xr, conv_w, conv_b)
    y = (rg_lru(xr, w_a[0], b_a[0], w_x[0], b_x[0], lam[0], reverse=False)
         + rg_lru(xr, w_a[1], b_a[1], w_x[1], b_x[1], lam[1], reverse=True))
    y = y.astype(h.dtype) * jax.nn.gelu(gate)
    return y @ w_out


def sq_relu_mlp(h, w1, w2):
    return jnp.square(jax.nn.relu(h @ w1)) @ w2


def setup_inputs(seed: int = 0) -> dict:
    key = jax.random.key(seed)
    ks = jax.random.split(key, 24)
    nrm = lambda k, shape, s: jax.random.normal(k, shape, jnp.float32) * s
    u = jax.random.uniform(ks[17], (N_RNN_LAYERS, 2, D_RNN), jnp.float32, 0.9, 0.999)
    s_lam = u ** (1.0 / LRU_C)
    lam = jnp.log(s_lam) - jnp.log1p(-s_lam)
    return {
        "x": nrm(ks[0], (BATCH, SEQ, D_MODEL), 1.0),
        "c": nrm(ks[1], (BATCH, D_MODEL), 1.0),
        "ada_w": nrm(ks[2], (DEPTH, 2, D_MODEL, 3 * D_MODEL), ADA_INIT * D_MODEL ** -0.5),
        "ada_b": nrm(ks[3], (DEPTH, 2, 3 * D_MODEL), 0.01),
        "ln_g": 1.0 + nrm(ks[4], (DEPTH, 2, D_MODEL), 0.02),
        "ln_b": nrm(ks[5], (DEPTH, 2, D_MODEL), 0.02),
        "attn_w_in": nrm(ks[6], (N_ATTN_LAYERS, D_MODEL, (N_Q_HEADS + 2 * N_KV_HEADS) * HEAD_DIM), D_MODEL ** -0.5),
        "attn_w_out": nrm(ks[7], (N_ATTN_LAYERS, N_Q_HEADS * HEAD_DIM, D_MODEL), DN_BETA * (N_Q_HEADS * HEAD_DIM) ** -0.5),
        "attn_sinks": nrm(ks[8], (N_ATTN_LAYERS, N_Q_HEADS), 1.0),
        "rnn_w_in": nrm(ks[9], (N_RNN_LAYERS, D_MODEL, 2 * D_RNN), D_MODEL ** -0.5),
        "rnn_conv_w": nrm(ks[10], (N_RNN_LAYERS, CONV_W, D_RNN), CONV_W ** -0.5),
        "rnn_conv_b": nrm(ks[11], (N_RNN_LAYERS, D_RNN), 0.01),
        "rnn_w_a": nrm(ks[12], (N_RNN_LAYERS, 2, N_RNN_BLOCKS, RNN_BLOCK_W, RNN_BLOCK_W), RNN_BLOCK_W ** -0.5),
        "rnn_b_a": nrm(ks[13], (N_RNN_LAYERS, 2, D_RNN), 0.01),
        "rnn_w_x": nrm(ks[14], (N_RNN_LAYERS, 2, N_RNN_BLOCKS, RNN_BLOCK_W, RNN_BLOCK_W), RNN_BLOCK_W ** -0.5),
        "rnn_b_x": nrm(ks[15], (N_RNN_LAYERS, 2, D_RNN), 0.01),
        "rnn_lam": lam,
        "rnn_w_out": nrm(ks[16], (N_RNN_LAYERS, D_RNN, D_MODEL), DN_BETA * D_RNN ** -0.5),
        "mlp_w1": nrm(ks[18], (DEPTH, D_MODEL, D_FF), D_MODEL ** -0.5),
        "mlp_w2": nrm(ks[19], (DEPTH, D_FF, D_MODEL), DN_BETA * D_FF ** -0.5),
    }


def reference(x, c, ada_w, ada_b, ln_g, ln_b, attn_w_in, attn_w_out, attn_sinks,
              rnn_w_in, rnn_conv_w, rnn_conv_b, rnn_w_a, rnn_b_a, rnn_w_x, rnn_b_x, rnn_lam,
              rnn_w_out, mlp_w1, mlp_w2):
    for i in range(DEPTH):
        j = i // N_MIXERS
        shift, scale, gate = ada_modulation(c, ada_w[i, 0], ada_b[i, 0])
        h = x * (1.0 + scale) + shift
        if i % N_MIXERS == 0:
            y = windowed_gqa(h, attn_w_in[j], attn_w_out[j], attn_sinks[j])
        else:
            y = recurrent_block(h, rnn_w_in[j], rnn_conv_w[j], rnn_conv_b[j], rnn_w_a[j], rnn_b_a[j],
                                rnn_w_x[j], rnn_b_x[j], rnn_lam[j], rnn_w_out[j])
        x = layer_norm(DN_ALPHA * x + (1.0 + gate) * y, ln_g[i, 0], ln_b[i, 0])
        shift, scale, gate = ada_modulation(c, ada_w[i, 1], ada_b[i, 1])
        y = sq_relu_mlp(x * (1.0 + scale) + shift, mlp_w1[i], mlp_w2[i])
        x = layer_norm(DN_ALPHA * x + (1.0 + gate) * y, ln_g[i, 1], ln_b[i, 1])
    return x
```

```python
import numpy as np
from contextlib import ExitStack
import concourse.bass as bass
import concourse.mybir as mybir
from concourse.bass_utils import run_bass_kernel_spmd

AF = mybir.ActivationFunctionType
ALU = mybir.AluOpType
F32 = mybir.dt.float32
BF16 = mybir.dt.bfloat16

D = 1024
KC = 8
T = 4096
NCORES = 8
DFF = 4096
DEPTH = 2
DN_ALPHA = (2.0 * DEPTH) ** 0.25
LN_EPS = 1e-5
DRNN = 1536
NBLK = 16
BW = 96
SEM_LIMIT = 3000


class Buf:
    __slots__ = ("name", "w", "r", "dsem", "dcnt")

    def __init__(self, name):
        self.name = name
        self.w = None
        self.r = []
        self.dsem = None
        self.dcnt = 0


class Sched:
    ENG = ("pe", "act", "dve", "pool", "sp")

    def __init__(self, nc, stack):
        self.nc = nc
        self.stack = stack
        self.stream = {e: [] for e in self.ENG}
        self.sem = {e: None for e in self.ENG}
        self.cnt = {e: 0 for e in self.ENG}
        self.seen = {e: {} for e in self.ENG}
        self.nsem = 0
        self.handles = []
        self.dma_bufs = []

    def _newsem(self, tag):
        self.nsem += 1
        h = self.nc.alloc_semaphore(name="%ss%d_%s" % (getattr(self, "prefix", ""), self.nsem, tag))
        self.handles.append(h)
        return h

    def _peek(self, e):
        if self.sem[e] is None or self.cnt[e] >= SEM_LIMIT:
            self.sem[e] = self._newsem(e)
            self.cnt[e] = 0
        return (self.sem[e], self.cnt[e] + 1)

    def _waits(self, e, reads, writes, skip_sem=None):
        need = {}

        def add(t):
            if t is None:
                return
            s, v = t
            k = id(s)
            if k not in need or need[k][1] < v:
                need[k] = (s, v)

        for b in reads:
            add(b.w)
        for b in writes:
            add(b.w)
            for t in b.r:
                add(t)
        out = []
        seen = self.seen[e]
        for k, (s, v) in need.items():
            if e == "pe" and s is self.sem["pe"]:
                continue
            if skip_sem is not None and s is skip_sem:
                continue
            if seen.get(k, 0) >= v:
                continue
            seen[k] = v
            out.append((s, v))
        return out

    def op(self, e, fn, reads=(), writes=(), inc=True):
        waits = self._waits(e, reads, writes)
        tk = self._peek(e)
        if inc:
            self.cnt[e] += 1
        self.stream[e].append((waits, fn, tk if inc else None))
        for b in reads:
            b.r.append(tk)
        for b in writes:
            b.w = tk
            b.r = []
        return tk

    def dma(self, e, out_ap, in_ap, reads=(), writes=(), anchor=None):
        a = anchor
        if a.dsem is None or a.dcnt >= SEM_LIMIT:
            a.dsem = self._newsem("d")
            a.dcnt = 0
            self.dma_bufs.append(a)
        waits = self._waits(e, reads, writes, skip_sem=a.dsem)
        a.dcnt += 16
        tk = (a.dsem, a.dcnt)

        def fn(eng, out_ap=out_ap, in_ap=in_ap):
            return eng.dma_start(out=out_ap, in_=in_ap)

        self.stream[e].append((waits, fn, ("dma", a.dsem)))
        for b in reads:
            b.r.append(tk)
        for b in writes:
            b.w = tk
            b.r = []
        return tk

    def collective(self, in_ap, out_ap, reads=(), writes=(), anchor=None):
        a = anchor
        if a.dsem is None:
            a.dsem = self._newsem("cc")
            a.dcnt = 0
            self.dma_bufs.append(a)
        waits = self._waits("pool", reads, writes, skip_sem=a.dsem)
        a.dcnt += 1
        tk = (a.dsem, a.dcnt)

        def fn(eng, in_ap=in_ap, out_ap=out_ap):
            return eng.collective_compute("AllGather", ALU.bypass, replica_groups=[list(range(NCORES))], ins=[in_ap], outs=[out_ap])

        self.stream["pool"].append((waits, fn, ("cc", a.dsem)))
        for b in reads:
            b.r.append(tk)
        for b in writes:
            b.w = tk
            b.r = []
        return tk

    def barrier(self):
        targets = []
        for e in self.ENG:
            if self.sem[e] is not None and self.cnt[e] > 0:
                targets.append((self.sem[e], self.cnt[e]))
        for a in self.dma_bufs:
            targets.append((a.dsem, a.dcnt))
        for e in self.ENG:
            w = []
            for (s, v) in targets:
                if e == "pe" and s is self.sem["pe"]:
                    continue
                if self.seen[e].get(id(s), 0) >= v:
                    continue
                self.seen[e][id(s)] = v
                w.append((s, v))
            if w:
                self.stream[e].append((w, None, None))

    def final_wait(self, e, bufs):
        w = []
        for b in bufs:
            for t in [b.w] + list(b.r):
                if t is not None:
                    w.append(t)
        self.stream[e].append((w, None, None))

    def emit(self):
        nc = self.nc
        block = self.stack.enter_context(nc.Block())

        def replay(eng, items):
            for waits, fn, tk in items:
                for (s, v) in waits:
                    eng.wait_ge(s, v)
                if fn is None:
                    continue
                ins = fn(eng)
                if tk is None:
                    continue
                if tk[0] == "dma":
                    ins.then_inc(tk[1], 16)
                elif tk[0] == "cc":
                    ins.then_inc(tk[1])
                else:
                    ins.then_inc(tk[0], 1)

        st = self.stream

        @block.sync
        def _(eng):
            replay(eng, st["sp"])

        @block.tensor
        def _(eng):
            replay(eng, st["pe"])

        @block.scalar
        def _(eng):
            replay(eng, st["act"])

        @block.vector
        def _(eng):
            replay(eng, st["dve"])

        @block.gpsimd
        def _(eng):
            replay(eng, st["pool"])


class Ctx:
    def __init__(self, fused=False):
        self.nc = bass.Bass("TRN2", target_bir_lowering=False)
        self.fused = fused
        self.gstack = ExitStack()
        self.nbuf = 0
        self.decl = {}
        self.ada_cache = {}
        self.prefix = ""
        self.over = {}
        self.stack = None
        self.s = None

    def begin_phase(self, prefix="", over=None):
        self.prefix = prefix
        self.over = over or {}
        self.stack = ExitStack()
        self.s = Sched(self.nc, self.stack)
        self.s.prefix = prefix

    def end_phase(self):
        if self.fused:
            self.s.barrier()
        self.s.emit()
        self.stack.close()
        self.nc.all_engine_barrier()
        self.nc.clear_and_free_semaphores(self.s.handles)
        self.nc.all_engine_barrier()
        return self.nc

    def dram_in(self, name, shape, dt=F32):
        if name in self.over:
            return self.over[name]
        ap = self.nc.dram_tensor(self.prefix + name, list(shape), dt, kind="ExternalInput").ap()
        self.decl[self.prefix + name] = ap
        return ap

    def dram_out(self, name, shape, dt=F32):
        if name in self.over:
            return self.over[name]
        return self.nc.dram_tensor(self.prefix + name, list(shape), dt, kind="ExternalOutput").ap()

    def dram_tmp(self, name, shape, dt=F32):
        return self.nc.dram_tensor(name, list(shape), dt)

    def gsb(self, name, shape, dt):
        return self.gstack.enter_context(self.nc.sbuf_tensor(name, list(shape), dt))

    def sb(self, name, shape, dt):
        return self.stack.enter_context(self.nc.sbuf_tensor(self.prefix + name, list(shape), dt))

    def ps(self, name, shape=(128, 512), dt=F32):
        return self.stack.enter_context(self.nc.psum_tensor(self.prefix + name, list(shape), dt))

    def buf(self, name="b"):
        self.nbuf += 1
        return Buf("%s%d" % (name, self.nbuf))


def emit_consts(cx):
    s = cx.s
    c = {}
    c["ones_t"] = cx.sb("ones_t", [128, 128], BF16)
    c["ones_b"] = cx.buf("ones")
    c["eps_t"] = cx.sb("eps_t", [128, 1], F32)
    c["eps_b"] = cx.buf("eps")
    s.op("pool", lambda e: e.memset(c["ones_t"][:], 1.0 / 1024.0), writes=[c["ones_b"]])
    s.op("pool", lambda e: e.memset(c["eps_t"][:], LN_EPS), writes=[c["eps_b"]])
    return c


def emit_adaln(cx, c_col, ada_w, ada_bT, scratch_ps, scratch_ps_b, pieces, piece_bufs, owner_bufs):
    s = cx.s
    key = cx.over.get("ada_key")
    if key is not None and key in cx.ada_cache:
        mod, mod1p = cx.ada_tiles[key]
        return dict(mod=mod, mod1p=mod1p, b_mod=cx.buf("adamod"), b_mod1p=cx.buf("adamod1p"))
    ccol = cx.sb("ada_c", [128, 8], F32)
    csil = cx.sb("ada_cs", [128, 8], F32)
    bT = cx.sb("ada_b", [128, 24], F32)
    if key is not None:
        mod, mod1p = cx.ada_tiles[key]
        cx.ada_cache[key] = True
    else:
        mod = cx.sb("ada_mod", [128, 24], F32)
        mod1p = cx.sb("ada_mod1p", [128, 24], F32)
    b_c, b_cs, b_mod, b_mod1p = cx.buf("adac"), cx.buf("adacs"), cx.buf("adamod"), cx.buf("adamod1p")
    b_bT = b_c
    s.dma("sp", ccol[:], c_col[:, :], writes=[b_c], anchor=b_c)
    s.dma("sp", bT[:], ada_bT[:, :], writes=[b_bT], anchor=b_bT)
    s.op("act", lambda e: e.activation(out=csil[:], in_=ccol[:], func=AF.Silu), reads=[b_c], writes=[b_cs])
    wv = ada_w.rearrange("(k p) n -> p k n", p=128)
    for col in range(24):
        t = pieces[col % 2]
        tb = piece_bufs[col % 2]
        s.dma("sp", t, wv[:, :, col * 128:(col + 1) * 128], writes=[tb], anchor=tb)
        for k in range(8):
            s.op("pe", lambda e, t=t, k=k, col=col: e.matmul(
                scratch_ps[:, col:col + 1], lhsT=t[:, k, :], rhs=csil[:, k:k + 1],
                start=(k == 0), stop=(k == 7)),
                reads=[tb, b_cs], writes=[scratch_ps_b], inc=(k == 7))
    s.op("dve", lambda e: e.tensor_tensor(out=mod[:], in0=scratch_ps[:, 0:24], in1=bT[:], op=ALU.add),
         reads=[scratch_ps_b, b_bT] + list(piece_bufs), writes=[b_mod] + list(owner_bufs))
    s.op("dve", lambda e: e.tensor_scalar_add(out=mod1p[:], in0=mod[:], scalar1=1.0), reads=[b_mod], writes=[b_mod1p])
    return dict(mod=mod, mod1p=mod1p, b_mod=b_mod, b_mod1p=b_mod1p)


def emit_ln_tile(cx, cst, z, zb_t, zsq_t, ps_sum, b_sum, ps_sq, b_sq, tmp, N, gT, bT, b_par, out_t, bz, bzb, bzsq, btmp, bout):
    s = cx.s
    for k in range(8):
        s.op("act", lambda e, k=k: e.activation(out=zb_t[:, k, :], in_=z[:, k, :], func=AF.Identity), reads=[bz], writes=[bzb])
        s.op("act", lambda e, k=k: e.activation(out=zsq_t[:, k, :], in_=z[:, k, :], func=AF.Square), reads=[bz], writes=[bzsq])
    for k in range(8):
        s.op("pe", lambda e, k=k: e.matmul(ps_sum, lhsT=cst["ones_t"][:], rhs=zb_t[:, k, :], start=(k == 0), stop=(k == 7)),
             reads=[cst["ones_b"], bzb], writes=[b_sum], inc=(k == 7))
    for k in range(8):
        s.op("pe", lambda e, k=k: e.matmul(ps_sq, lhsT=cst["ones_t"][:], rhs=zsq_t[:, k, :], start=(k == 0), stop=(k == 7)),
             reads=[cst["ones_b"], bzsq], writes=[b_sq], inc=(k == 7))
    mean = tmp[:, 0, :]
    msq = tmp[:, 1, :]
    var = tmp[:, 2, :]
    rstd = tmp[:, 3, :]
    s.op("dve", lambda e: e.tensor_copy(out=mean, in_=ps_sum), reads=[b_sum], writes=[btmp])
    s.op("dve", lambda e: e.tensor_tensor(out=msq, in0=mean, in1=mean, op=ALU.mult), reads=[btmp], writes=[btmp])
    s.op("dve", lambda e: e.tensor_tensor(out=var, in0=ps_sq, in1=msq, op=ALU.subtract), reads=[b_sq, btmp], writes=[btmp])
    s.op("act", lambda e: e.activation(out=var, in_=var, func=AF.Sqrt, bias=cst["eps_t"][:, 0:1]), reads=[btmp, cst["eps_b"]], writes=[btmp])
    s.op("dve", lambda e: e.reciprocal(out=rstd, in_=var), reads=[btmp], writes=[btmp])
    for k in range(8):
        s.op("dve", lambda e, k=k: e.tensor_tensor(out=z[:, k, :], in0=z[:, k, :], in1=mean, op=ALU.subtract), reads=[bz, btmp], writes=[bz])
        s.op("dve", lambda e, k=k: e.tensor_tensor(out=z[:, k, :], in0=z[:, k, :], in1=rstd, op=ALU.mult), reads=[bz, btmp], writes=[bz])
        s.op("act", lambda e, k=k: e.activation(out=out_t[:, k, :], in_=z[:, k, :], func=AF.Identity,
                                                 bias=bT[:, k:k + 1], scale=gT[:, k:k + 1]),
             reads=[bz, b_par], writes=[bout])


def build_mlp(cx=None, prefix="", over=None):
    cx = cx or Ctx()
    cx.begin_phase(prefix, over)
    s = cx.s
    N = 256
    NT = T // N
    xT = cx.dram_in("xT", [D, T])
    c_col = cx.dram_in("c_col", [128, 8])
    ada_w = cx.dram_in("ada_w", [D, 3 * D])
    ada_bT = cx.dram_in("ada_bT", [128, 24])
    ln_gT = cx.dram_in("ln_gT", [128, 8])
    ln_bT = cx.dram_in("ln_bT", [128, 8])
    w1 = cx.dram_in("w1", [D, DFF])
    w2 = cx.dram_in("w2", [DFF, D])
    yT = cx.dram_out("yT", [D, T])

    cst = emit_consts(cx)
    acc = [cx.ps("acc%d" % i) for i in range(4)]
    acc_b = [cx.buf("acc") for i in range(8)]
    ph = [cx.ps("ph%d" % i) for i in range(3)]
    ph_b = [cx.buf("ph") for i in range(3)]
    stp = cx.ps("stp")
    st_b = cx.buf("st")

    z = cx.sb("z", [128, 8, N], F32)
    bz = cx.buf("z")
    pieces = [z[:, 0:4, :].rearrange("p a (b n) -> p (a b) n", n=128), z[:, 4:8, :].rearrange("p a (b n) -> p (a b) n", n=128)]
    ada = emit_adaln(cx, c_col, ada_w, ada_bT, stp, st_b, pieces, [cx.buf("pc0"), cx.buf("pc1")], [bz])
    mod, mod1p = ada["mod"], ada["mod1p"]
    b_mod, b_mod1p = ada["b_mod"], ada["b_mod1p"]

    gT = cx.sb("gT", [128, 8], F32)
    bT = cx.sb("bT", [128, 8], F32)
    b_par = cx.buf("par")
    s.dma("sp", gT[:], ln_gT[:, :], writes=[b_par], anchor=b_par)
    s.dma("sp", bT[:], ln_bT[:, :], writes=[b_par], anchor=b_par)

    w1b = cx.sb("w1b", [128, 8, DFF], BF16)
    w2b = cx.sb("w2b", [128, 32, D], BF16)
    b_w1x = cx.buf("w1")
    b_w2x = [cx.buf("w2") for k in range(4)]
    b_w1 = [b_w1x for k in range(8)]
    b_w2 = [b_w2x[k // 8] for k in range(32)]
    w1v = w1.rearrange("(k p) n -> p k n", p=128)
    w2v = w2.rearrange("(k p) n -> p k n", p=128)
    for k in range(8):
        s.dma("pool", w1b[:, k, :], w1v[:, k, :], writes=[b_w1[k]], anchor=b_w1[k])
    for k in range(32):
        s.dma("pool", w2b[:, k, :], w2v[:, k, :], writes=[b_w2[k]], anchor=b_w2[k])

    jobs = cx.over.get("jobs") or [(xT, yT, cx.over.get("tail_out"))]
    jobv = [(xj.rearrange("(k p) t -> p k t", p=128), yj.rearrange("(k p) t -> p k t", p=128), tj) for (xj, yj, tj) in jobs]
    NXB = 2
    xt = [cx.sb("xt%d" % i, [128, 8, N], F32) for i in range(NXB)]
    xt_b = [cx.buf("xt") for i in range(NXB)]
    ht = cx.sb("ht", [128, 8, N], BF16)
    ht_b = cx.buf("ht")
    rb = [cx.sb("rb%d" % i, [128, N], F32) for i in range(3)]
    rb_b = [cx.buf("rb") for i in range(3)]
    hid = [cx.sb("hid%d" % i, [128, N], BF16) for i in range(4)]
    hid_b = [cx.buf("hid") for i in range(4)]
    zb_t = cx.sb("zb", [128, 8, N], BF16)
    bzb = cx.buf("zb")
    zsq_t = cx.sb("zsq", [128, 8, N], BF16)
    bzsq = cx.buf("zsq")
    tmp = cx.sb("tmp", [128, 4, N], F32)
    btmp = cx.buf("tmp")

    def load_x(gi):
        xv_ = jobv[gi // NT][0]
        it_ = gi % NT
        s.dma("sp", xt[gi % NXB][:], xv_[:, :, it_ * N:(it_ + 1) * N], writes=[xt_b[gi % NXB]], anchor=xt_b[gi % NXB])

    load_x(0)
    for gi in range(NT * len(jobv)):
        it = gi % NT
        yv, tail_out = jobv[gi // NT][1], jobv[gi // NT][2]
        if gi + 1 < NT * len(jobv):
            load_x(gi + 1)
        x_t = xt[gi % NXB]
        xb = xt_b[gi % NXB]
        for k in range(8):
            s.op("act", lambda e, k=k, x_t=x_t: e.activation(out=ht[:, k, :], in_=x_t[:, k, :], func=AF.Identity,
                                                             bias=mod[:, k:k + 1], scale=mod1p[:, 8 + k:9 + k]),
                 reads=[xb, b_mod, b_mod1p], writes=[ht_b])
        s.op("pool", lambda e, x_t=x_t: e.tensor_scalar(out=x_t[:], in0=x_t[:], scalar1=float(DN_ALPHA), scalar2=None, op0=ALU.mult),
             reads=[xb], writes=[xb])

        def g1(hc):
            p = ph[hc % 3]
            for k in range(8):
                s.op("pe", lambda e, p=p, k=k, hc=hc: e.matmul(p[:, 0:N], lhsT=w1b[:, k, hc * 128:(hc + 1) * 128], rhs=ht[:, k, :],
                                                              start=(k == 0), stop=(k == 7)),
                     reads=[b_w1[k], ht_b], writes=[ph_b[hc % 3]], inc=(k == 7))

        def ew(hc):
            p = ph[hc % 3]
            r = rb[hc % 3]
            h_ = hid[hc % 4]
            s.op("act", lambda e, p=p, r=r: e.activation(out=r[:], in_=p[:, 0:N], func=AF.Relu), reads=[ph_b[hc % 3]], writes=[rb_b[hc % 3]])
            s.op("dve", lambda e, r=r, h_=h_: e.tensor_tensor(out=h_[:], in0=r[:], in1=r[:], op=ALU.mult), reads=[rb_b[hc % 3]], writes=[hid_b[hc % 4]])

        def g2(hc):
            h_ = hid[hc % 4]
            for oc in range(8):
                a = acc[oc // 2]
                s.op("pe", lambda e, a=a, oc=oc, hc=hc, h_=h_: e.matmul(a[:, (oc % 2) * N:(oc % 2 + 1) * N],
                                                                        lhsT=w2b[:, hc, oc * 128:(oc + 1) * 128], rhs=h_[:],
                                                                        start=(hc == 0 and oc % 2 == 0), stop=(hc == 31),
                                                                        skip_group_check=True),
                     reads=[b_w2[hc], hid_b[hc % 4]], writes=[acc_b[oc]], inc=(oc == 7 or hc == 31))

        g1(0)
        g1(1)
        ew(0)
        for hc in range(32):
            if hc + 2 < 32:
                g1(hc + 2)
            if hc + 1 < 32:
                ew(hc + 1)
            g2(hc)
        for oc in range(8):
            a = acc[oc // 2]
            s.op("dve", lambda e, a=a, oc=oc, x_t=x_t: e.scalar_tensor_tensor(
                out=z[:, oc, :], in0=a[:, (oc % 2) * N:(oc % 2 + 1) * N], scalar=mod1p[:, 16 + oc:17 + oc], in1=x_t[:, oc, :],
                op0=ALU.mult, op1=ALU.add), reads=[acc_b[oc], xb, b_mod1p], writes=[bz])
        emit_ln_tile(cx, cst, z, zb_t, zsq_t, stp[:, 0:N], st_b, stp[:, N:2 * N], st_b, tmp, N, gT, bT, b_par, z, bz, bzb, bzsq, btmp, bz)
        s.dma("sp", yv[:, :, it * N:(it + 1) * N], z[:], reads=[bz], anchor=bz)
        if it == NT - 1 and tail_out is not None:
            s.dma("sp", tail_out.rearrange("p (k t) -> p k t", t=2), z[:, :, N - 2:N], reads=[bz], anchor=bz)
    s.final_wait("sp", [bz])
    return cx.end_phase()


TH = T + 128


def build_attn(cx=None, prefix="", over=None):
    cx = cx or Ctx()
    cx.begin_phase(prefix, over)
    s = cx.s
    N = 512
    NT = T // N
    SCALE = 128.0 ** -0.5
    xT = cx.dram_in("xT", [D, TH])
    c_col = cx.dram_in("c_col", [128, 8])
    ada_w = cx.dram_in("ada_w", [D, 3 * D])
    ada_bT = cx.dram_in("ada_bT", [128, 24])
    ln_gT = cx.dram_in("ln_gT", [128, 8])
    ln_bT = cx.dram_in("ln_bT", [128, 8])
    w_in = cx.dram_in("w_in", [D, 1536])
    w_out = cx.dram_in("w_out", [D, D])
    cosT = cx.dram_in("cosT", [32, TH])
    sinT = cx.dram_in("sinT", [32, TH])
    perm = cx.dram_in("perm", [32, 32])
    mprev = cx.dram_in("mprev", [128, 512])
    mnext = cx.dram_in("mnext", [128, 512])
    sink_rep = cx.dram_in("sink_rep", [128, 8])
    yT = cx.dram_out("yT", [D, T])

    cst = emit_consts(cx)
    one_t = cx.sb("one_t", [128, 128], BF16)
    one_b = cx.buf("one")
    s.op("pool", lambda e: e.memset(one_t[:], 1.0), writes=[one_b])

    banks = [cx.ps("bank%d" % i) for i in range(8)]
    bank_b = [cx.buf("bank") for i in range(8)]
    bctr = [0]

    def bank():
        i = bctr[0] % 8
        bctr[0] += 1
        return banks[i], bank_b[i]

    z = cx.sb("z", [128, 8, N], F32)
    bz = cx.buf("z")
    pieces = [z[:, 0:2, :].rearrange("p a (b n) -> p (a b) n", n=128), z[:, 2:4, :].rearrange("p a (b n) -> p (a b) n", n=128)]
    pb, pbb = bank()
    ada = emit_adaln(cx, c_col, ada_w, ada_bT, pb, pbb, pieces, [cx.buf("pc0"), cx.buf("pc1")], [bz])
    mod, mod1p = ada["mod"], ada["mod1p"]
    b_mod, b_mod1p = ada["b_mod"], ada["b_mod1p"]

    gT = cx.sb("gT", [128, 8], F32)
    bT = cx.sb("bT", [128, 8], F32)
    b_par = cx.buf("par")
    s.dma("sp", gT[:], ln_gT[:, :], writes=[b_par], anchor=b_par)
    s.dma("sp", bT[:], ln_bT[:, :], writes=[b_par], anchor=b_par)

    perm_t = cx.sb("perm_t", [32, 32], BF16)
    mprev_t = cx.sb("mprev_t", [128, 512], BF16)
    mnext_t = cx.sb("mnext_t", [128, 512], BF16)
    b_cm = cx.buf("cm")
    s.dma("pool", perm_t[:], perm[:, :], writes=[b_cm], anchor=b_cm)
    s.dma("pool", mprev_t[:], mprev[:, :], writes=[b_cm], anchor=b_cm)
    s.dma("pool", mnext_t[:], mnext[:, :], writes=[b_cm], anchor=b_cm)
    sk = cx.sb("sk", [128, 8], F32)
    b_sk = cx.buf("sk")
    s.dma("sp", sk[:], sink_rep[:, :], writes=[b_sk], anchor=b_sk)
    s.op("act", lambda e: e.activation(out=sk[:], in_=sk[:], func=AF.Exp), reads=[b_sk], writes=[b_sk])
    es_full = cx.sb("es_full", [128, 2, 512], F32)
    b_es = cx.buf("es")
    s.op("pool", lambda e: e.memset(es_full[:], 0.0), writes=[b_es])
    for hh in range(8):
        s.op("pool", lambda e, hh=hh: e.tensor_scalar(out=es_full[:, hh // 4, (hh % 4) * 128:(hh % 4 + 1) * 128],
                                                       in0=es_full[:, hh // 4, (hh % 4) * 128:(hh % 4 + 1) * 128],
                                                       scalar1=sk[:, hh:hh + 1], scalar2=None, op0=ALU.add),
             reads=[b_sk, b_es], writes=[b_es])

    w_inb = cx.sb("w_inb", [128, 8, 1536], BF16)
    w_outb = cx.sb("w_outb", [128, 8, D], BF16)
    b_win = cx.buf("win")
    b_wout = cx.buf("wout")
    winv = w_in.rearrange("(k p) n -> p k n", p=128)
    woutv = w_out.rearrange("(k p) n -> p k n", p=128)
    for k in range(8):
        s.dma("pool", w_inb[:, k, :], winv[:, k, :], writes=[b_win], anchor=b_win)
    for k in range(8):
        s.dma("pool", w_outb[:, k, :], woutv[:, k, :], writes=[b_wout], anchor=b_wout)

    kT_all = cx.sb("kT_all", [128, 2, TH], BF16)
    v_all = cx.sb("v_all", [128, TH // 128, 256], BF16)
    b_k = [cx.buf("k") for i in range(NT + 1)]
    b_v = [cx.buf("v") for i in range(NT + 1)]

    xv = xT.rearrange("(k p) t -> p k t", p=128)
    yv = yT.rearrange("(k p) t -> p k t", p=128)
    xt = cx.sb("xt", [128, 8, N], F32)
    xb = cx.buf("xt")
    ht = cx.sb("ht", [128, 8, N], BF16)
    ht_b = cx.buf("ht")
    cs_t = cx.sb("cs_t", [32, 2, N], F32)
    b_cs = cx.buf("cs")
    rtmp = cx.sb("rtmp", [32, 2, N], F32)
    b_rtmp = cx.buf("rtmp")
    qtmp = cx.sb("qtmp", [128, N], BF16)
    b_qtmp = cx.buf("qtmp")
    qT = cx.sb("qT", [128, 4, 8, 128], BF16)
    b_q = cx.buf("q")
    pT = [cx.sb("pT%d" % i, [128, 512], BF16) for i in range(6)]
    pT_b = [cx.buf("pT") for i in range(6)]
    pctr = [0]
    oT = cx.sb("oT", [128, 8, N], BF16)
    b_o = cx.buf("o")
    zb_t = cx.sb("zb", [128, 8, N], BF16)
    bzb = cx.buf("zb")
    zsq_t = cx.sb("zsq", [128, 8, N], BF16)
    bzsq = cx.buf("zsq")
    tmp = cx.sb("tmp", [128, 4, N], F32)
    btmp = cx.buf("tmp")

    def load_tile(t0, n):
        s.dma("sp", xt[:, :, 0:n], xv[:, :, t0:t0 + n], writes=[xb], anchor=xb)
        s.dma("sp", cs_t[:, 0, 0:n], cosT[:, t0:t0 + n], writes=[b_cs], anchor=b_cs)
        s.dma("sp", cs_t[:, 1, 0:n], sinT[:, t0:t0 + n], writes=[b_cs], anchor=b_cs)
        for k in range(8):
            s.op("act", lambda e, k=k, n=n: e.activation(out=ht[:, k, 0:n], in_=xt[:, k, 0:n], func=AF.Identity,
                                                         bias=mod[:, k:k + 1], scale=mod1p[:, 8 + k:9 + k]),
                 reads=[xb, b_mod, b_mod1p], writes=[ht_b])

    def rope(view32, vb, n):
        pbk, pbb_ = bank()
        s.op("pe", lambda e: e.matmul(pbk[0:32, 0:n], lhsT=perm_t[:, :], rhs=view32, start=True, stop=True),
             reads=[b_cm, vb], writes=[pbb_])
        s.op("dve", lambda e: e.tensor_tensor(out=rtmp[:, 0, 0:n], in0=pbk[0:32, 0:n], in1=cs_t[:, 1, 0:n], op=ALU.mult),
             reads=[pbb_, b_cs], writes=[b_rtmp])
        s.op("dve", lambda e: e.tensor_tensor(out=rtmp[:, 1, 0:n], in0=view32, in1=cs_t[:, 0, 0:n], op=ALU.mult),
             reads=[vb, b_cs], writes=[b_rtmp])
        s.op("dve", lambda e: e.tensor_tensor(out=view32, in0=rtmp[:, 0, 0:n], in1=rtmp[:, 1, 0:n], op=ALU.add),
             reads=[b_rtmp], writes=[vb])

    for it in range(NT + 1):
        t0 = it * N
        n = N if it < NT else 128
        load_tile(t0, n)
        for kvh in range(2):
            pbk, pbb_ = bank()
            for k in range(8):
                s.op("pe", lambda e, pbk=pbk, k=k, kvh=kvh, n=n: e.matmul(pbk[:, 0:n], lhsT=w_inb[:, k, 1024 + kvh * 128:1152 + kvh * 128],
                                                                          rhs=ht[:, k, 0:n], start=(k == 0), stop=(k == 7)),
                     reads=[b_win, ht_b], writes=[pbb_], inc=(k == 7))
            s.op("act", lambda e, pbk=pbk, kvh=kvh, t0=t0, n=n: e.activation(out=kT_all[:, kvh, t0:t0 + n], in_=pbk[:, 0:n], func=AF.Identity),
                 reads=[pbb_], writes=[b_k[it]])
            rope(kT_all[0:32, kvh, t0:t0 + n], b_k[it], n)
        for blk in range(n // 128):
            pbk, pbb_ = bank()
            for k in range(8):
                s.op("pe", lambda e, pbk=pbk, k=k, blk=blk: e.matmul(pbk[:, 0:256], lhsT=ht[:, k, blk * 128:(blk + 1) * 128],
                                                                     rhs=w_inb[:, k, 1280:1536], start=(k == 0), stop=(k == 7)),
                     reads=[b_win, ht_b], writes=[pbb_], inc=(k == 7))
            s.op("act", lambda e, pbk=pbk, blk=blk, t0=t0: e.activation(out=v_all[:, t0 // 128 + blk, :], in_=pbk[:, 0:256], func=AF.Identity),
                 reads=[pbb_], writes=[b_v[it]])

    for it in range(NT):
        t0 = it * N
        load_tile(t0, N)
        for hd in range(8):
            pbk, pbb_ = bank()
            for k in range(8):
                s.op("pe", lambda e, pbk=pbk, k=k, hd=hd: e.matmul(pbk[:, 0:N], lhsT=w_inb[:, k, hd * 128:(hd + 1) * 128],
                                                                   rhs=ht[:, k, :], start=(k == 0), stop=(k == 7)),
                     reads=[b_win, ht_b], writes=[pbb_], inc=(k == 7))
            s.op("act", lambda e, pbk=pbk: e.activation(out=qtmp[:], in_=pbk[:, 0:N], func=AF.Identity),
                 reads=[pbb_], writes=[b_qtmp])
            rope(qtmp[0:32, :], b_qtmp, N)
            s.op("pool", lambda e, hd=hd: e.tensor_copy(out=qT[:, :, hd, :], in_=qtmp[:].rearrange("p (a n) -> p a n", n=128)),
                 reads=[b_qtmp], writes=[b_q])
        s.op("pool", lambda e: e.tensor_scalar(out=xt[:], in0=xt[:], scalar1=float(DN_ALPHA), scalar2=None, op0=ALU.mult),
             reads=[xb], writes=[xb])
        for qb in range(4):
            B = it * 4 + qb
            jbs = [jb for jb in (B - 1, B, B + 1) if jb >= 0]
            for kvh in range(2):
                qrhs = qT[:, qb, kvh * 4:(kvh + 1) * 4, :].rearrange("p h n -> p (h n)")
                pts = []
                for jb in jbs:
                    pbk, pbb_ = bank()
                    kb = b_k[min(jb // 4, NT)]
                    s.op("pe", lambda e, pbk=pbk, jb=jb, kvh=kvh, qrhs=qrhs: e.matmul(pbk[:, :], lhsT=kT_all[:, kvh, jb * 128:(jb + 1) * 128],
                                                                                    rhs=qrhs, start=True, stop=True),
                         reads=[kb, b_q], writes=[pbb_])
                    pi = pctr[0] % 6
                    pctr[0] += 1
                    p_t, p_b = pT[pi], pT_b[pi]
                    s.op("act", lambda e, pbk=pbk, p_t=p_t: e.activation(out=p_t[:], in_=pbk[:, :], func=AF.Exp, scale=SCALE),
                         reads=[pbb_], writes=[p_b])
                    if jb == B - 1:
                        s.op("pool", lambda e, p_t=p_t: e.tensor_tensor(out=p_t[:], in0=p_t[:], in1=mprev_t[:], op=ALU.mult),
                             reads=[p_b, b_cm], writes=[p_b])
                    elif jb == B + 1:
                        s.op("pool", lambda e, p_t=p_t: e.tensor_tensor(out=p_t[:], in0=p_t[:], in1=mnext_t[:], op=ALU.mult),
                             reads=[p_b, b_cm], writes=[p_b])
                    pts.append((jb, p_t, p_b))
                po, pob = bank()
                pd, pdb = bank()
                for i, (jb, p_t, p_b) in enumerate(pts):
                    vb = b_v[min(jb // 4, NT)]
                    s.op("pe", lambda e, po=po, jb=jb, kvh=kvh, p_t=p_t, i=i: e.matmul(po[:, :], lhsT=v_all[:, jb, kvh * 128:(kvh + 1) * 128],
                                                                                     rhs=p_t[:], start=(i == 0), stop=(i == len(pts) - 1)),
                         reads=[vb, p_b], writes=[pob], inc=(i == len(pts) - 1))
                for i, (jb, p_t, p_b) in enumerate(pts):
                    s.op("pe", lambda e, pd=pd, p_t=p_t, i=i: e.matmul(pd[:, :], lhsT=one_t[:], rhs=p_t[:], start=(i == 0), stop=(i == len(pts) - 1)),
                         reads=[one_b, p_b], writes=[pdb], inc=(i == len(pts) - 1))
                den = tmp[:, 0, :]
                s.op("dve", lambda e, pd=pd, kvh=kvh: e.tensor_tensor(out=den, in0=pd[:, :], in1=es_full[:, kvh, :], op=ALU.add),
                     reads=[pdb, b_es], writes=[btmp])
                s.op("dve", lambda e: e.reciprocal(out=den, in_=den), reads=[btmp], writes=[btmp])
                s.op("dve", lambda e, po=po, kvh=kvh, qb=qb: e.tensor_tensor(
                    out=oT[:, kvh * 4:(kvh + 1) * 4, qb * 128:(qb + 1) * 128], in0=po[:, :].rearrange("p (h n) -> p h n", n=128),
                    in1=den.rearrange("p (h n) -> p h n", n=128), op=ALU.mult),
                    reads=[pob, btmp], writes=[b_o])
        for oc in range(8):
            pbk, pbb_ = bank()
            for k in range(8):
                s.op("pe", lambda e, pbk=pbk, k=k, oc=oc: e.matmul(pbk[:, :], lhsT=w_outb[:, k, oc * 128:(oc + 1) * 128], rhs=oT[:, k, :],
                                                                   start=(k == 0), stop=(k == 7)),
                     reads=[b_wout, b_o], writes=[pbb_], inc=(k == 7))
            s.op("dve", lambda e, pbk=pbk, oc=oc: e.scalar_tensor_tensor(out=z[:, oc, :], in0=pbk[:, :], scalar=mod1p[:, 16 + oc:17 + oc],
                                                                         in1=xt[:, oc, :], op0=ALU.mult, op1=ALU.add),
                 reads=[pbb_, xb, b_mod1p], writes=[bz])
        ps1, ps1b = bank()
        ps2, ps2b = bank()
        emit_ln_tile(cx, cst, z, zb_t, zsq_t, ps1[:, :], ps1b, ps2[:, :], ps2b, tmp, N, gT, bT, b_par, z, bz, bzb, bzsq, btmp, bz)
        s.dma("sp", yv[:, :, t0:t0 + N], z[:], reads=[bz], anchor=bz)
    s.final_wait("sp", [bz])
    return cx.end_phase()


class Rot:
    def __init__(self, cx, name, n, shape, dt):
        self.t = [cx.sb("%s%d" % (name, i), shape, dt) for i in range(n)]
        self.b = [cx.buf(name) for i in range(n)]
        self.i = 0

    def get(self):
        j = self.i % len(self.t)
        self.i += 1
        return self.t[j], self.b[j]


def load_gate_params(cx, w_a, w_x, b_aT, b_xT, lamT):
    s = cx.s
    g = {}
    g["wa"] = cx.sb("g_wa", [96, 16, 96], BF16)
    g["wx"] = cx.sb("g_wx", [96, 16, 96], BF16)
    g["b_w"] = cx.buf("gw")
    s.dma("pool", g["wa"][:], w_a.rearrange("n i j -> i n j"), writes=[g["b_w"]], anchor=g["b_w"])
    s.dma("pool", g["wx"][:], w_x.rearrange("n i j -> i n j"), writes=[g["b_w"]], anchor=g["b_w"])
    g["ba"] = cx.sb("g_ba", [96, 16], F32)
    g["bx"] = cx.sb("g_bx", [96, 16], F32)
    g["lamc"] = cx.sb("g_lamc", [96, 16], F32)
    g["one1"] = cx.sb("g_one1", [96, 1], F32)
    g["b_p"] = cx.buf("gp")
    s.dma("sp", g["ba"][:], b_aT[:, :], writes=[g["b_p"]], anchor=g["b_p"])
    s.dma("sp", g["bx"][:], b_xT[:, :], writes=[g["b_p"]], anchor=g["b_p"])
    s.dma("sp", g["lamc"][:], lamT[:, :], writes=[g["b_p"]], anchor=g["b_p"])
    s.op("pool", lambda e: e.memset(g["one1"][:], 1.0), writes=[g["b_p"]])
    s.op("act", lambda e: e.activation(out=g["lamc"][:], in_=g["lamc"][:], func=AF.Exp, scale=-1.0), reads=[g["b_p"]], writes=[g["b_p"]])
    s.op("act", lambda e: e.activation(out=g["lamc"][:], in_=g["lamc"][:], func=AF.Ln, bias=g["one1"][:, 0:1]), reads=[g["b_p"]], writes=[g["b_p"]])
    s.op("pool", lambda e: e.tensor_scalar(out=g["lamc"][:], in0=g["lamc"][:], scalar1=-4.0, scalar2=None, op0=ALU.mult),
         reads=[g["b_p"]], writes=[g["b_p"]])
    g["lam2"] = cx.sb("g_lam2", [96, 16], F32)
    s.op("pool", lambda e: e.tensor_scalar(out=g["lam2"][:], in0=g["lamc"][:], scalar1=2.0, scalar2=None, op0=ALU.mult),
         reads=[g["b_p"]], writes=[g["b_p"]])
    s.op("pool", lambda e: e.tensor_scalar(out=g["ba"][:], in0=g["ba"][:], scalar1=0.5, scalar2=None, op0=ALU.mult),
         reads=[g["b_p"]], writes=[g["b_p"]])
    s.op("pool", lambda e: e.tensor_scalar(out=g["bx"][:], in0=g["bx"][:], scalar1=0.5, scalar2=None, op0=ALU.mult),
         reads=[g["b_p"]], writes=[g["b_p"]])
    return g


def emit_gates(cx, g, n, c_t, c_b, cb_t, cb_b, bank_fn, pools, N):
    s = cx.s
    pbk, pbb_ = bank_fn()
    s.op("pe", lambda e: e.matmul(pbk[0:96, 0:N], lhsT=g["wa"][:, n, :], rhs=cb_t[:], start=True, stop=True),
         reads=[g["b_w"], cb_b], writes=[pbb_])
    s.op("pe", lambda e: e.matmul(pbk[0:96, N:2 * N], lhsT=g["wx"][:, n, :], rhs=cb_t[:], start=True, stop=True),
         reads=[g["b_w"], cb_b], writes=[pbb_])
    r_t, r_b = pools["r"].get()
    i_t, i_b = pools["i"].get()
    a_t, a_b = pools["a"].get()
    m_t, m_b = pools["m"].get()
    u_t, u_b = pools["u"].get()
    s.op("act", lambda e: e.activation(out=r_t[:], in_=pbk[0:96, 0:N], func=AF.Sigmoid, bias=g["ba"][:, n:n + 1]),
         reads=[pbb_, g["b_p"]], writes=[r_b])
    s.op("act", lambda e: e.activation(out=i_t[:], in_=pbk[0:96, N:2 * N], func=AF.Sigmoid, bias=g["bx"][:, n:n + 1]),
         reads=[pbb_, g["b_p"]], writes=[i_b])
    s.op("act", lambda e: e.activation(out=a_t[:], in_=r_t[:], func=AF.Exp, scale=g["lamc"][:, n:n + 1]),
         reads=[r_b, g["b_p"]], writes=[a_b])
    s.op("dve", lambda e: e.tensor_tensor(out=m_t[:], in0=a_t[:], in1=a_t[:], op=ALU.mult), reads=[a_b], writes=[m_b])
    s.op("dve", lambda e: e.tensor_scalar(out=m_t[:], in0=m_t[:], scalar1=-1.0, scalar2=1.0, op0=ALU.mult, op1=ALU.add),
         reads=[m_b], writes=[m_b])
    s.op("act", lambda e: e.activation(out=m_t[:], in_=m_t[:], func=AF.Sqrt), reads=[m_b], writes=[m_b])
    s.op("dve", lambda e: e.tensor_tensor(out=u_t[:], in0=i_t[:], in1=c_t, op=ALU.mult), reads=[i_b, c_b], writes=[u_b])
    s.op("dve", lambda e: e.tensor_tensor(out=u_t[:], in0=u_t[:], in1=m_t[:], op=ALU.mult), reads=[u_b, m_b], writes=[u_b])
    return a_t, a_b, u_t, u_b


def gate_pools(cx, N):
    return {k: Rot(cx, "gp_" + k, 2, [96, N], F32) for k in ("r", "i", "a", "m", "u")}


def make_banks(cx, nb):
    banks = [cx.ps("bank%d" % i) for i in range(nb)]
    bank_b = [cx.buf("bank") for i in range(nb)]
    ctr = [0]

    def bank():
        i = ctr[0] % nb
        ctr[0] += 1
        return banks[i], bank_b[i]
    return bank


def run_pipeline(items, stages):
    S = len(stages)
    for step in range(len(items) + S - 1):
        for k in range(S - 1, -1, -1):
            i = step - k
            if 0 <= i < len(items):
                stages[k](items[i])


class RotBanks:
    def __init__(self, cx, name, n):
        self.t = [cx.ps("%s%d" % (name, i)) for i in range(n)]
        self.b = [cx.buf(name) for i in range(n)]
        self.i = 0

    def get(self):
        j = self.i % len(self.t)
        self.i += 1
        return self.t[j], self.b[j]


SQB = 4


def gate_stage_fns(cx, g, N, RI, P, items_ref):
    s = cx.s

    def st_gmm(it):
        n = it["n"]
        cb_t, cb_b = P["cb"].get()
        c_t = it["c_t"]
        s.op("pool", lambda e: e.tensor_copy(out=cb_t[:], in_=c_t[:]), reads=[it["c_b"]], writes=[cb_b])
        pbk, pbb_ = RI.get()
        it["ri"], it["ri_b"] = pbk, pbb_
        s.op("pe", lambda e: e.matmul(pbk[0:96, 0:N], lhsT=g["wa"][:, n, :], rhs=cb_t[:], start=True, stop=True),
             reads=[g["b_w"], cb_b], writes=[pbb_])
        s.op("pe", lambda e: e.matmul(pbk[0:96, N:2 * N], lhsT=g["wx"][:, n, :], rhs=cb_t[:], start=True, stop=True),
             reads=[g["b_w"], cb_b], writes=[pbb_])

    def st_sig(it):
        n = it["n"]
        pbk, pbb_ = it["ri"], it["ri_b"]
        r_t, r_b = P["r"].get()
        i_t, i_b = P["i"].get()
        a_t, a_b = P["a"].get()
        m_t, m_b = P["m"].get()
        it.update(i_t=i_t, i_b=i_b, a_t=a_t, a_b=a_b, m_t=m_t, m_b=m_b)
        s.op("act", lambda e: e.activation(out=r_t[:], in_=pbk[0:96, 0:N], func=AF.Tanh, bias=g["ba"][:, n:n + 1], scale=0.5),
             reads=[pbb_, g["b_p"]], writes=[r_b])
        s.op("act", lambda e: e.activation(out=i_t[:], in_=pbk[0:96, N:2 * N], func=AF.Tanh, bias=g["bx"][:, n:n + 1], scale=0.5),
             reads=[pbb_, g["b_p"]], writes=[i_b])
        s.op("act", lambda e: e.activation(out=a_t[:], in_=r_t[:], func=AF.Exp, bias=g["lamc"][:, n:n + 1], scale=g["lamc"][:, n:n + 1]),
             reads=[r_b, g["b_p"]], writes=[a_b])
        s.op("act", lambda e: e.activation(out=m_t[:], in_=r_t[:], func=AF.Exp, bias=g["lam2"][:, n:n + 1], scale=g["lam2"][:, n:n + 1]),
             reads=[r_b, g["b_p"]], writes=[m_b])

    def st_m(it):
        i_t, i_b = it["i_t"], it["i_b"]
        m_t, m_b = it["m_t"], it["m_b"]
        u_t, u_b = P["u"].get()
        it.update(u_t=u_t, u_b=u_b)
        c_t = it["c_t"]
        s.op("pool", lambda e: e.tensor_scalar(out=m_t[:], in0=m_t[:], scalar1=-1.0, scalar2=1.0, op0=ALU.mult, op1=ALU.add),
             reads=[m_b], writes=[m_b])
        s.op("dve", lambda e: e.scalar_tensor_tensor(out=u_t[:], in0=i_t[:], scalar=1.0, in1=c_t[:], op0=ALU.add, op1=ALU.mult),
             reads=[i_b, it["c_b"]], writes=[u_b])

    def st_sqrt(it):
        if it["idx"] % SQB != SQB - 1:
            return
        for j in range(it["idx"] - SQB + 1, it["idx"] + 1):
            m_t, m_b = items_ref[j]["m_t"], items_ref[j]["m_b"]
            s.op("act", lambda e, m_t=m_t: e.activation(out=m_t[:], in_=m_t[:], func=AF.Sqrt), reads=[m_b], writes=[m_b])

    return st_gmm, st_sig, st_m, st_sqrt


def build_rnn1(cx=None, prefix="", over=None):
    cx = cx or Ctx()
    cx.begin_phase(prefix, over)
    s = cx.s
    N = 256
    NT = T // N
    NX = N + 2
    xT = cx.dram_in("xT", [D, T + 2])
    c_col = cx.dram_in("c_col", [128, 8])
    ada_w = cx.dram_in("ada_w", [D, 3 * D])
    ada_bT = cx.dram_in("ada_bT", [128, 24])
    w_in = cx.dram_in("w_in", [D, 2 * DRNN])
    convT = cx.dram_in("convT", [96, 16 * 6])
    w_a = cx.dram_in("w_a", [16, 96, 96])
    w_x = cx.dram_in("w_x", [16, 96, 96])
    b_aT = cx.dram_in("b_aT", [96, 16])
    b_xT = cx.dram_in("b_xT", [96, 16])
    lamT = cx.dram_in("lamT", [96, 16])
    carry_only = bool(cx.over.get("carry_only"))
    if not carry_only:
        cT = cx.dram_out("cT", [DRNN, T])
        GT = cx.dram_out("GT", [DRNN, T])
        h1T = cx.dram_out("h1T", [DRNN, T])
        cv = cT.rearrange("(n p) t -> p n t", p=96)
        Gv = GT.rearrange("(n p) t -> p n t", p=96)
        hv = h1T.rearrange("(n p) t -> p n t", p=96)

    XB = RotBanks(cx, "pX", 3)
    GB = RotBanks(cx, "pG", 2)
    RI = RotBanks(cx, "pRI", 2)
    misc = cx.ps("pmisc")
    misc_b = cx.buf("pmisc")
    zt = cx.sb("zt", [128, 8, N], F32)
    bzt = cx.buf("zt")
    pieces = [zt[:, 0:4, :].rearrange("p a (b n) -> p (a b) n", n=128), zt[:, 4:8, :].rearrange("p a (b n) -> p (a b) n", n=128)]
    ada = emit_adaln(cx, c_col, ada_w, ada_bT, misc, misc_b, pieces, [cx.buf("pc0"), cx.buf("pc1")], [bzt])
    mod, mod1p, b_mod, b_mod1p = ada["mod"], ada["mod1p"], ada["b_mod"], ada["b_mod1p"]

    w_inb = cx.sb("w_inb", [128, 8, 2 * DRNN], BF16)
    b_win = cx.buf("win")
    winv = w_in.rearrange("(k p) n -> p k n", p=128)
    for k in range(8):
        s.dma("pool", w_inb[:, k, :], winv[:, k, :], writes=[b_win], anchor=b_win)
    g = load_gate_params(cx, w_a, w_x, b_aT, b_xT, lamT)
    cw = cx.sb("cw", [96, 16 * 6], F32)
    b_cw = cx.buf("cw")
    s.dma("sp", cw[:], convT[:, :], writes=[b_cw], anchor=b_cw)

    xv = xT.rearrange("(k p) t -> p k t", p=128)
    xt = Rot(cx, "xt", 2, [128, 8, NX], F32)
    htp = Rot(cx, "ht", 2, [128, 8, NX], BF16)
    tail = cx.sb("tail", [96, 16, 2], F32)
    b_tail = cx.buf("tail")
    s.op("pool", lambda e: e.memset(tail[:], 0.0), writes=[b_tail])
    carry = cx.sb("carry", [96, 16], F32)
    b_carry = cx.buf("carry")
    s.op("pool", lambda e: e.memset(carry[:], 0.0), writes=[b_carry])
    P = {"xr": Rot(cx, "xrb", 3, [96, NX + 2], F32), "G": Rot(cx, "G", 2, [96, N], F32),
         "c": Rot(cx, "c", 5, [96, N], F32), "ct": Rot(cx, "ct", 2, [96, N], F32), "cb": Rot(cx, "cb", 2, [96, N], BF16),
         "r": Rot(cx, "r", 2, [96, N], F32), "i": Rot(cx, "i", 3, [96, N], F32), "a": Rot(cx, "a", 11, [96, N], F32),
         "m": Rot(cx, "m", 10, [96, N], F32), "u": Rot(cx, "u", 10, [96, N], F32), "h": Rot(cx, "h", 2, [96, N], F32)}
    halo = cx.over.get("halo_sb")
    state = {"nxt": None}

    def load_x(it):
        t_, b_ = xt.get()
        if halo is not None and it == NT - 1:
            s.dma("sp", t_[:, :, 0:N], xv[:, :, it * N:it * N + N], writes=[b_], anchor=b_)
            s.op("pool", lambda e, t_=t_: e.tensor_copy(out=t_[:, :, N:NX], in_=halo[:]), writes=[b_])
        else:
            s.dma("sp", t_[:], xv[:, :, it * N:it * N + NX], writes=[b_], anchor=b_)
        return t_, b_

    state["nxt"] = load_x(0)

    def st_proj(itm):
        it, n = itm["it"], itm["n"]
        if n == 0:
            x_t, xb = state["nxt"]
            if it + 1 < NT:
                state["nxt"] = load_x(it + 1)
            h_t, h_b = htp.get()
            state["ht"] = (h_t, h_b)
            for k in range(8):
                s.op("act", lambda e, k=k: e.activation(out=h_t[:, k, :], in_=x_t[:, k, :], func=AF.Identity,
                                                        bias=mod[:, k:k + 1], scale=mod1p[:, 8 + k:9 + k]),
                     reads=[xb, b_mod, b_mod1p], writes=[h_b])
            if not carry_only:
                for n2 in range(16):
                    pg, pgb = GB.get()
                    for k in range(8):
                        s.op("pe", lambda e, k=k, pg=pg, n2=n2: e.matmul(pg[0:96, 0:N], lhsT=w_inb[:, k, DRNN + n2 * 96:DRNN + (n2 + 1) * 96],
                                                                       rhs=h_t[:, k, 0:N], start=(k == 0), stop=(k == 7)),
                             reads=[b_win, h_b], writes=[pgb], inc=(k == 7))
                    G_t, G_b = P["G"].get()
                    s.op("act", lambda e, pg=pg, G_t=G_t: e.activation(out=G_t[:], in_=pg[0:96, 0:N], func=AF.Gelu), reads=[pgb], writes=[G_b])
                    s.dma("sp", Gv[:, n2, it * N:it * N + N], G_t[:], reads=[G_b], anchor=G_b)
        h_t, h_b = state["ht"]
        pbk, pbb_ = XB.get()
        itm["X"], itm["X_b"] = pbk, pbb_
        for k in range(8):
            s.op("pe", lambda e, k=k: e.matmul(pbk[0:96, 0:NX], lhsT=w_inb[:, k, n * 96:(n + 1) * 96], rhs=h_t[:, k, :],
                                               start=(k == 0), stop=(k == 7)),
                 reads=[b_win, h_b], writes=[pbb_], inc=(k == 7))

    def st_evac(itm):
        it, n = itm["it"], itm["n"]
        t0 = it * N
        xr_t, xr_b = P["xr"].get()
        itm["xr_t"], itm["xr_b"] = xr_t, xr_b
        pbk = itm["X"]
        s.op("act", lambda e: e.activation(out=xr_t[:, 2:NX + 2], in_=pbk[0:96, 0:NX], func=AF.Identity),
             reads=[itm["X_b"]], writes=[xr_b])

    def st_conv(itm):
        it, n = itm["it"], itm["n"]
        t0 = it * N
        xr_t, xr_b = itm["xr_t"], itm["xr_b"]
        s.op("pool", lambda e: e.tensor_copy(out=xr_t[:, 0:2], in_=tail[:, n, :]), reads=[b_tail], writes=[xr_b])
        s.op("pool", lambda e: e.tensor_copy(out=tail[:, n, :], in_=xr_t[:, N:N + 2]), reads=[xr_b], writes=[b_tail])
        c_t, c_b = P["c"].get()
        itm["c_t"], itm["c_b"] = c_t, c_b
        s.op("pool", lambda e: e.tensor_scalar(out=c_t[:], in0=xr_t[:, 0:N], scalar1=cw[:, n * 6:n * 6 + 1],
                                               scalar2=cw[:, n * 6 + 5:n * 6 + 6], op0=ALU.mult, op1=ALU.add),
             reads=[xr_b, b_cw], writes=[c_b])
        for j in range(1, 5):
            s.op("dve", lambda e, j=j: e.scalar_tensor_tensor(out=c_t[:], in0=xr_t[:, j:j + N], scalar=cw[:, n * 6 + j:n * 6 + j + 1],
                                                              in1=c_t[:], op0=ALU.mult, op1=ALU.add),
                 reads=[xr_b, b_cw, c_b], writes=[c_b])
        if not carry_only:
            s.dma("sp", cv[:, n, t0:t0 + N], c_t[:], reads=[c_b], anchor=c_b)

    items = [{"it": it, "n": n, "idx": it * 16 + n} for it in range(NT) for n in range(16)]
    st_gmm, st_sig, st_m, st_sqrt = gate_stage_fns(cx, g, N, RI, P, items)

    def st_scan(itm):
        it, n = itm["it"], itm["n"]
        t0 = it * N
        u_t, u_b, m_t, m_b, a_t, a_b = itm["u_t"], itm["u_b"], itm["m_t"], itm["m_b"], itm["a_t"], itm["a_b"]
        s.op("dve", lambda e: e.scalar_tensor_tensor(out=u_t[:], in0=u_t[:], scalar=0.5, in1=m_t[:], op0=ALU.mult, op1=ALU.mult),
             reads=[u_b, m_b], writes=[u_b])
        h_t, h_b = P["h"].get()
        s.op("dve", lambda e: e.tensor_tensor_scan(out=h_t[:], data0=a_t[:], data1=u_t[:], initial=carry[:, n:n + 1],
                                                   op0=ALU.mult, op1=ALU.add),
             reads=[a_b, u_b, b_carry], writes=[h_b])
        s.op("pool", lambda e: e.tensor_copy(out=carry[:, n:n + 1], in_=h_t[:, N - 1:N]), reads=[h_b], writes=[b_carry])
        if not carry_only:
            s.dma("sp", hv[:, n, t0:t0 + N], h_t[:], reads=[h_b], anchor=h_b)

    nop = lambda itm: None
    run_pipeline(items, [st_proj, st_evac, st_conv, st_gmm, st_sig, st_m, nop, nop, nop, st_sqrt, nop, nop, nop, st_scan])
    if cx.over.get("carry_out") is not None:
        s.dma("sp", cx.over["carry_out"], carry[:], reads=[b_carry], anchor=b_carry)
    s.final_wait("sp", P["c"].b + P["G"].b + P["h"].b + [b_carry])
    return cx.end_phase()


def build_rnn2(cx=None, prefix="", over=None):
    cx = cx or Ctx()
    cx.begin_phase(prefix, over)
    s = cx.s
    N = 256
    NT = T // N
    xT = cx.dram_in("xT", [D, T])
    cT = cx.dram_in("cT", [DRNN, T])
    GT = cx.dram_in("GT", [DRNN, T])
    h1T = cx.dram_in("h1T", [DRNN, T])
    carry_in = None if (over and over.get("carry_sb") is not None) else cx.dram_in("carry_in", [96, 16])
    c_col = cx.dram_in("c_col", [128, 8])
    ada_w = cx.dram_in("ada_w", [D, 3 * D])
    ada_bT = cx.dram_in("ada_bT", [128, 24])
    ln_gT = cx.dram_in("ln_gT", [128, 8])
    ln_bT = cx.dram_in("ln_bT", [128, 8])
    w_a = cx.dram_in("w_a", [16, 96, 96])
    w_x = cx.dram_in("w_x", [16, 96, 96])
    b_aT = cx.dram_in("b_aT", [96, 16])
    b_xT = cx.dram_in("b_xT", [96, 16])
    lamT = cx.dram_in("lamT", [96, 16])
    w_out = cx.dram_in("w_out", [DRNN, D])
    yT = cx.dram_out("yT", [D, T])

    cst = emit_consts(cx)
    RI = RotBanks(cx, "pRI", 3)
    WB = RotBanks(cx, "pW", 4)
    stp = cx.ps("pst")
    st_b = cx.buf("pst")
    z = cx.sb("z", [128, 8, N], F32)
    bz = cx.buf("z")
    pieces = [z[:, 0:4, :].rearrange("p a (b n) -> p (a b) n", n=128), z[:, 4:8, :].rearrange("p a (b n) -> p (a b) n", n=128)]
    ada = emit_adaln(cx, c_col, ada_w, ada_bT, stp, st_b, pieces, [cx.buf("pc0"), cx.buf("pc1")], [bz])
    mod1p, b_mod1p = ada["mod1p"], ada["b_mod1p"]
    gT = cx.sb("gT", [128, 8], F32)
    bT = cx.sb("bT", [128, 8], F32)
    b_par = cx.buf("par")
    s.dma("sp", gT[:], ln_gT[:, :], writes=[b_par], anchor=b_par)
    s.dma("sp", bT[:], ln_bT[:, :], writes=[b_par], anchor=b_par)
    g = load_gate_params(cx, w_a, w_x, b_aT, b_xT, lamT)
    woutb = cx.sb("woutb", [96, 16, D], BF16)
    b_wout = cx.buf("wout")
    s_w = w_out.rearrange("(n p) d -> p n d", p=96)
    for n in range(16):
        s.dma("pool", woutb[:, n, :], s_w[:, n, :], writes=[b_wout], anchor=b_wout)
    carry = cx.sb("carry", [96, 16], F32)
    b_carry = cx.buf("carry")
    if cx.over.get("carry_sb") is not None:
        s.op("pool", lambda e: e.tensor_copy(out=carry[:], in_=cx.over["carry_sb"][:]), writes=[b_carry])
    else:
        s.dma("sp", carry[:], carry_in[:, :], writes=[b_carry], anchor=b_carry)

    xv = xT.rearrange("(k p) t -> p k t", p=128)
    yv = yT.rearrange("(k p) t -> p k t", p=128)
    cv = cT.rearrange("(n p) t -> p n t", p=96)
    Gv = GT.rearrange("(n p) t -> p n t", p=96)
    hv = h1T.rearrange("(n p) t -> p n t", p=96)
    xt = Rot(cx, "xt", 2, [128, 8, N], F32)
    P = {"c": Rot(cx, "c", 5, [96, N], F32), "G": Rot(cx, "G", 4, [96, N], F32), "h1": Rot(cx, "h1", 4, [96, N], F32),
         "cb": Rot(cx, "cb", 2, [96, N], BF16), "r": Rot(cx, "r", 2, [96, N], F32), "i": Rot(cx, "i", 3, [96, N], F32),
         "a": Rot(cx, "a", 11, [96, N], F32), "m": Rot(cx, "m", 10, [96, N], F32), "u": Rot(cx, "u", 10, [96, N], F32),
         "h2": Rot(cx, "h2", 2, [96, N], F32)}
    ytp = Rot(cx, "yt", 2, [96, 16, N], BF16)
    zb_t = cx.sb("zb", [128, 8, N], BF16)
    bzb = cx.buf("zb")
    zsq_t = cx.sb("zsq", [128, 8, N], BF16)
    bzsq = cx.buf("zsq")
    tmp = cx.sb("tmp", [128, 4, N], F32)
    btmp = cx.buf("tmp")
    state = {}

    def st_load(itm):
        it, n = itm["it"], itm["n"]
        t0 = it * N
        if n == 0:
            x_t, xb = xt.get()
            s.dma("sp", x_t[:], xv[:, :, t0:t0 + N], writes=[xb], anchor=xb)
            s.op("pool", lambda e: e.tensor_scalar(out=x_t[:], in0=x_t[:], scalar1=float(DN_ALPHA), scalar2=None, op0=ALU.mult),
                 reads=[xb], writes=[xb])
            state[("x", it)] = (x_t, xb)
            state[("y", it)] = ytp.get()
        c_t, c_b = P["c"].get()
        itm.update(c_t=c_t, c_b=c_b)
        s.dma("sp", c_t[:], cv[:, n, t0:t0 + N], writes=[c_b], anchor=c_b)

    def st_load2(itm):
        it, n = itm["it"], itm["n"]
        t0 = it * N
        G_t, G_b = P["G"].get()
        h1_t, h1_b = P["h1"].get()
        itm.update(G_t=G_t, G_b=G_b, h1_t=h1_t, h1_b=h1_b)
        s.dma("sp", G_t[:], Gv[:, n, t0:t0 + N], writes=[G_b], anchor=G_b)
        s.dma("sp", h1_t[:], hv[:, n, t0:t0 + N], writes=[h1_b], anchor=h1_b)

    items = [{"it": it, "n": n} for it in range(NT - 1, -1, -1) for n in range(16)]
    for j_, itm_ in enumerate(items):
        itm_["idx"] = j_
    st_gmm, st_sig, st_m, st_sqrt = gate_stage_fns(cx, g, N, RI, P, items)

    def st_scan(itm):
        it, n = itm["it"], itm["n"]
        t0 = it * N
        u_t, u_b, m_t, m_b, a_t, a_b = itm["u_t"], itm["u_b"], itm["m_t"], itm["m_b"], itm["a_t"], itm["a_b"]
        h1_t, h1_b, G_t, G_b = itm["h1_t"], itm["h1_b"], itm["G_t"], itm["G_b"]
        y_t, y_b = state[("y", it)]
        s.op("dve", lambda e: e.scalar_tensor_tensor(out=u_t[:], in0=u_t[:], scalar=0.5, in1=m_t[:], op0=ALU.mult, op1=ALU.mult),
             reads=[u_b, m_b], writes=[u_b])
        h2_t, h2_b = P["h2"].get()
        s.op("dve", lambda e: e.tensor_tensor_scan(out=h2_t[:, ::-1], data0=a_t[:, ::-1], data1=u_t[:, ::-1], initial=carry[:, n:n + 1],
                                                   op0=ALU.mult, op1=ALU.add),
             reads=[a_b, u_b, b_carry], writes=[h2_b])
        s.op("pool", lambda e: e.tensor_copy(out=carry[:, n:n + 1], in_=h2_t[:, 0:1]), reads=[h2_b], writes=[b_carry])
        s.op("dve", lambda e: e.tensor_tensor(out=h2_t[:], in0=h2_t[:], in1=h1_t[:], op=ALU.add), reads=[h2_b, h1_b], writes=[h2_b])
        s.op("dve", lambda e: e.tensor_tensor(out=y_t[:, n, :], in0=h2_t[:], in1=G_t[:], op=ALU.mult), reads=[h2_b, G_b], writes=[y_b])
        if n == 15:
            x_t, xb = state[("x", it)]
            for oc in range(8):
                pbk, pbb_ = WB.get()
                for nn in range(16):
                    s.op("pe", lambda e, pbk=pbk, nn=nn, oc=oc: e.matmul(pbk[:, 0:N], lhsT=woutb[:, nn, oc * 128:(oc + 1) * 128], rhs=y_t[:, nn, :],
                                                                         start=(nn == 0), stop=(nn == 15)),
                         reads=[b_wout, y_b], writes=[pbb_], inc=(nn == 15))
                s.op("dve", lambda e, pbk=pbk, oc=oc: e.scalar_tensor_tensor(out=z[:, oc, :], in0=pbk[:, 0:N], scalar=mod1p[:, 16 + oc:17 + oc],
                                                                             in1=x_t[:, oc, :], op0=ALU.mult, op1=ALU.add),
                     reads=[pbb_, xb, b_mod1p], writes=[bz])
            emit_ln_tile(cx, cst, z, zb_t, zsq_t, stp[:, 0:N], st_b, stp[:, N:2 * N], st_b, tmp, N, gT, bT, b_par, z, bz, bzb, bzsq, btmp, bz)
            s.dma("sp", yv[:, :, t0:t0 + N], z[:], reads=[bz], anchor=bz)

    nop = lambda itm: None
    run_pipeline(items, [st_load, st_gmm, st_sig, st_m, nop, nop, nop, st_sqrt, nop, st_load2, nop, st_scan])
    s.final_wait("sp", [bz])
    return cx.end_phase()


def build_xch(cx, prefix, bounce_in, bounce_out, P, F, result_sb, swap2):
    cx.begin_phase(prefix, None)
    s = cx.s
    sel_d = cx.dram_in("sel", [128, 8])
    sel = cx.sb("sel_sb", [128, 8], F32)
    b_sel = cx.buf("sel")
    s.dma("sp", sel[:], sel_d[:, :], writes=[b_sel], anchor=b_sel)
    b_in, b_out = cx.buf("bin"), cx.buf("bout")
    s.collective(bounce_in.ap(), bounce_out.ap(), reads=[b_in], writes=[b_out], anchor=b_out)
    g = cx.sb("xg", [P, 8, F], F32)
    b_g = cx.buf("xg")
    s.dma("sp", g[:], bounce_out.ap().rearrange("(r p) f -> p r f", p=P), reads=[b_out], writes=[b_g], anchor=b_g)
    acc = cx.sb("xacc", [P, F], F32)
    b_acc = cx.buf("xacc")
    s.op("dve", lambda e: e.tensor_scalar(out=acc[:], in0=g[:, 0, :], scalar1=sel[0:P, 0:1], scalar2=None, op0=ALU.mult),
         reads=[b_g, b_sel], writes=[b_acc])
    for r in range(1, 8):
        s.op("dve", lambda e, r=r: e.scalar_tensor_tensor(out=acc[:], in0=g[:, r, :], scalar=sel[0:P, r:r + 1], in1=acc[:],
                                                          op0=ALU.mult, op1=ALU.add),
             reads=[b_g, b_sel, b_acc], writes=[b_acc])
    b_res = cx.buf("res")
    if swap2:
        av = acc[:].rearrange("p (k t) -> p k t", t=2)
        s.op("dve", lambda e: e.tensor_copy(out=result_sb[:, :, 0:1], in_=av[:, :, 1:2]), reads=[b_acc], writes=[b_res])
        s.op("dve", lambda e: e.tensor_copy(out=result_sb[:, :, 1:2], in_=av[:, :, 0:1]), reads=[b_acc], writes=[b_res])
    else:
        s.op("dve", lambda e: e.tensor_copy(out=result_sb[:], in_=acc[:]), reads=[b_acc], writes=[b_res])
    return cx.end_phase()


def build_lxch(cx, prefix, bounce, P, F, result_sb, swap2):
    cx.begin_phase(prefix, None)
    s = cx.s
    acc = cx.sb("xacc", [P, F], F32)
    b_acc = cx.buf("xacc")
    s.dma("sp", acc[:], bounce.ap(), writes=[b_acc], anchor=b_acc)
    b_res = cx.buf("res")
    if swap2:
        av = acc[:].rearrange("p (k t) -> p k t", t=2)
        s.op("dve", lambda e: e.tensor_copy(out=result_sb[:, :, 0:1], in_=av[:, :, 1:2]), reads=[b_acc], writes=[b_res])
        s.op("dve", lambda e: e.tensor_copy(out=result_sb[:, :, 1:2], in_=av[:, :, 0:1]), reads=[b_acc], writes=[b_res])
    else:
        s.op("dve", lambda e: e.tensor_copy(out=result_sb[:], in_=acc[:]), reads=[b_acc], writes=[b_res])
    return cx.end_phase()


def build_fused2():
    cx = Ctx(fused=True)
    tmp = lambda n, sh: cx.dram_tmp(n, sh).ap()
    x0s, x1s, x0o, x1o = tmp("x0s", [D, T]), tmp("x1s", [D, T]), tmp("x0o", [D, T]), tmp("x1o", [D, T])
    cT, GT, h1T, x2 = tmp("cT_s", [DRNN, T]), tmp("GT_s", [DRNN, T]), tmp("h1T_s", [DRNN, T]), tmp("x2", [D, T])
    bh_s, bh_o = cx.dram_tmp("bh_s", [128, 16]), cx.dram_tmp("bh_o", [128, 16])
    bc_o = cx.dram_tmp("bc_o", [96, 16])
    bc_s = cx.dram_tmp("bc_s", [96, 16])
    halo_s = cx.gsb("halo_s", [128, 8, 2], F32)
    halo_o = cx.gsb("halo_o", [128, 8, 2], F32)
    carry_sb = cx.gsb("carry_sb", [96, 16], F32)
    cx.ada_tiles = {k: (cx.gsb("ada_mod_" + k, [128, 24], F32), cx.gsb("ada_mod1p_" + k, [128, 24], F32)) for k in ("00", "01", "10", "11")}
    out = cx.nc.dram_tensor("out", [D, T], F32, kind="ExternalOutput").ap()
    D_ = cx.decl

    def share(src, dst_names):
        return {n: D_[src + n] for n in dst_names}

    build_attn(cx, "a_", {"yT": x0s, "ada_key": "00"})
    ov = share("a_", ["c_col", "ada_w", "ada_bT", "ln_gT", "ln_bT", "w_in", "w_out", "perm", "mprev", "mnext", "sink_rep"])
    ov.update({"yT": x0o, "ada_key": "00"})
    build_attn(cx, "b_", ov)
    build_mlp(cx, "m0_", {"xT": x0s, "yT": x1s, "ada_key": "01",
                           "jobs": [(x0s, x1s, bh_s.ap()), (x0o, x1o, bh_o.ap())]})
    build_lxch(cx, "l1_", bh_o, 128, 16, halo_s, True)
    build_lxch(cx, "l2_", bh_s, 128, 16, halo_o, True)
    build_rnn1(cx, "r1_", {"xT": x1s, "halo_sb": halo_s, "cT": cT, "GT": GT, "h1T": h1T, "carry_out": bc_s.ap(), "ada_key": "10"})
    ov = share("r1_", ["c_col", "ada_w", "ada_bT", "w_in"])
    ov.update({"xT": x1o, "halo_sb": halo_o, "carry_only": True, "carry_out": bc_o.ap(), "ada_key": "10"})
    build_rnn1(cx, "q1_", ov)
    build_lxch(cx, "l3_", bc_o, 96, 16, carry_sb, False)
    ov = share("r1_", ["c_col", "ada_w", "ada_bT"])
    ov.update({"xT": x1s, "cT": cT, "GT": GT, "h1T": h1T, "carry_sb": carry_sb, "yT": x2, "ada_key": "10"})
    build_rnn2(cx, "r2_", ov)
    build_mlp(cx, "m1_", {"xT": x2, "yT": out, "ada_key": "11"})
    cx.gstack.close()
    return cx.nc


def build_fused():
    cx = Ctx(fused=True)
    x0a = cx.dram_tmp("x0a", [D, T]).ap()
    x1 = cx.dram_tmp("x1", [D, T]).ap()
    cT = cx.dram_tmp("cT_s", [DRNN, T]).ap()
    GT = cx.dram_tmp("GT_s", [DRNN, T]).ap()
    h1T = cx.dram_tmp("h1T_s", [DRNN, T]).ap()
    x2 = cx.dram_tmp("x2", [D, T]).ap()
    b1_in = cx.dram_tmp("b1_in", [128, 16])
    b1_out = cx.dram_tmp("b1_out", [8 * 128, 16])
    b2_in = cx.dram_tmp("b2_in", [96, 16])
    b2_out = cx.dram_tmp("b2_out", [8 * 96, 16])
    halo_sb = cx.gsb("halo_sb", [128, 8, 2], F32)
    carry_sb = cx.gsb("carry_sb", [96, 16], F32)
    out = cx.nc.dram_tensor("out", [D, T], F32, kind="ExternalOutput").ap()
    build_attn(cx, "a_", {"yT": x0a})
    build_mlp(cx, "m0_", {"xT": x0a, "yT": x1, "tail_out": b1_in.ap()})
    build_xch(cx, "x1_", b1_in, b1_out, 128, 16, halo_sb, True)
    build_rnn1(cx, "r1_", {"xT": x1, "halo_sb": halo_sb, "cT": cT, "GT": GT, "h1T": h1T, "carry_out": b2_in.ap()})
    build_xch(cx, "x2_", b2_in, b2_out, 96, 16, carry_sb, False)
    build_rnn2(cx, "r2_", {"xT": x1, "cT": cT, "GT": GT, "h1T": h1T, "carry_sb": carry_sb, "yT": x2})
    build_mlp(cx, "m1_", {"xT": x2, "yT": out})
    cx.gstack.close()
    return cx.nc


def colT(v, n):
    return np.ascontiguousarray(np.asarray(v, np.float32).reshape(n, 128).T)


_PROGS = {}


def get_prog(name):
    if name not in _PROGS:
        _PROGS[name] = {"mlp": build_mlp, "attn": build_attn, "rnn1": build_rnn1, "rnn2": build_rnn2, "fused": build_fused, "fused2": build_fused2}[name]()
    return _PROGS[name]


def run_mlp(xT_list, c, ada_w, ada_b, ln_g, ln_b, w1, w2):
    nc = get_prog("mlp")
    in_maps = []
    for core in range(NCORES):
        b = core // 2
        in_maps.append({
            "xT": xT_list[core], "c_col": colT(c[b], 8), "ada_w": np.ascontiguousarray(ada_w),
            "ada_bT": colT(ada_b, 24), "ln_gT": colT(ln_g, 8), "ln_bT": colT(ln_b, 8),
            "w1": np.ascontiguousarray(w1), "w2": np.ascontiguousarray(w2),
        })
    res = run_bass_kernel_spmd(nc, in_maps, core_ids=list(range(NCORES)))
    return [r["yT"] for r in res.results]


ROT = 32
ROPE_THETA = 500000.0


def rope_tables(pos):
    inv_freq = (np.float32(ROPE_THETA) ** (-np.arange(0, ROT, 2, dtype=np.float32) / np.float32(ROT))).astype(np.float32)
    ang = (pos.astype(np.float32)[None, :] * inv_freq[:, None]).astype(np.float32)
    c = np.cos(ang).astype(np.float32)
    sn = np.sin(ang).astype(np.float32)
    return np.ascontiguousarray(np.concatenate([c, c], 0)), np.ascontiguousarray(np.concatenate([-sn, sn], 0))


def local_positions(core):
    half = core % 2
    if half == 0:
        return np.arange(0, T + 128)
    return np.arange(2 * T - 1, T - 129, -1)


def attn_consts():
    perm = np.zeros((32, 32), np.float32)
    for i in range(32):
        perm[(i + 16) % 32, i] = 1.0
    j = np.arange(128)[:, None]
    q = np.arange(128)[None, :]
    mprev = np.tile((j >= q).astype(np.float32), (1, 4))
    mnext = np.tile((j <= q).astype(np.float32), (1, 4))
    return perm, np.ascontiguousarray(mprev), np.ascontiguousarray(mnext)


def run_attn(xTh_list, c, ada_w, ada_b, ln_g, ln_b, w_in, w_out, sinks):
    nc = get_prog("attn")
    perm, mprev, mnext = attn_consts()
    in_maps = []
    for core in range(NCORES):
        b = core // 2
        cosT, sinT = rope_tables(local_positions(core))
        in_maps.append({
            "xT": xTh_list[core], "c_col": colT(c[b], 8), "ada_w": np.ascontiguousarray(ada_w),
            "ada_bT": colT(ada_b, 24), "ln_gT": colT(ln_g, 8), "ln_bT": colT(ln_b, 8),
            "w_in": np.ascontiguousarray(w_in), "w_out": np.ascontiguousarray(w_out),
            "cosT": cosT, "sinT": sinT, "perm": perm, "mprev": mprev, "mnext": mnext,
            "sink_rep": np.ascontiguousarray(np.tile(np.asarray(sinks, np.float32)[None, :], (128, 1))),
        })
    res = run_bass_kernel_spmd(nc, in_maps, core_ids=list(range(NCORES)))
    return [r["yT"] for r in res.results]


def shard_x(x):
    out = []
    for core in range(NCORES):
        b = core // 2
        idx = local_positions(core)
        out.append(np.ascontiguousarray(x[b][idx, :].T))
    return out


def colT96(v):
    return np.ascontiguousarray(np.asarray(v, np.float32).reshape(16, 96).T)


def conv_table(conv_w, conv_b, half):
    taps = np.zeros((5, DRNN), np.float32)
    for j in range(4):
        if half == 0:
            taps[j] = conv_w[j]
        else:
            taps[4 - j] = conv_w[j]
    tab = np.zeros((96, 16, 6), np.float32)
    for j in range(5):
        tab[:, :, j] = taps[j].reshape(16, 96).T
    tab[:, :, 5] = np.asarray(conv_b, np.float32).reshape(16, 96).T
    return np.ascontiguousarray(tab.reshape(96, 96))


def _common(c, ada_w, ada_b, core):
    b = core // 2
    return {"c_col": colT(c[b], 8), "ada_w": np.ascontiguousarray(ada_w), "ada_bT": colT(ada_b, 24)}


def run_rnn1(x1_list, c, ada_w, ada_b, w_in, conv_w, conv_b, w_a, b_a, w_x, b_x, lam):
    nc = get_prog("rnn1")
    in_maps = []
    for core in range(NCORES):
        half = core % 2
        par = x1_list[core ^ 1]
        xh = np.ascontiguousarray(np.concatenate([x1_list[core], par[:, T - 1:T], par[:, T - 2:T - 1]], axis=1))
        d1 = half
        m = _common(c, ada_w, ada_b, core)
        m.update({"xT": xh, "w_in": np.ascontiguousarray(w_in), "convT": conv_table(conv_w, conv_b, half),
                  "w_a": np.ascontiguousarray(w_a[d1]), "w_x": np.ascontiguousarray(w_x[d1]),
                  "b_aT": colT96(b_a[d1]), "b_xT": colT96(b_x[d1]), "lamT": colT96(lam[d1])})
        in_maps.append(m)
    res = run_bass_kernel_spmd(nc, in_maps, core_ids=list(range(NCORES)))
    return [(r["cT"], r["GT"], r["h1T"]) for r in res.results]


def run_rnn2(x1_list, r1, c, ada_w, ada_b, ln_g, ln_b, w_a, b_a, w_x, b_x, lam, w_out):
    nc = get_prog("rnn2")
    in_maps = []
    for core in range(NCORES):
        half = core % 2
        d2 = 1 - half
        cT, GT, h1T = r1[core]
        carry = colT96(r1[core ^ 1][2][:, T - 1])
        m = _common(c, ada_w, ada_b, core)
        m.update({"xT": x1_list[core], "cT": cT, "GT": GT, "h1T": h1T, "carry_in": carry,
                  "ln_gT": colT(ln_g, 8), "ln_bT": colT(ln_b, 8),
                  "w_a": np.ascontiguousarray(w_a[d2]), "w_x": np.ascontiguousarray(w_x[d2]),
                  "b_aT": colT96(b_a[d2]), "b_xT": colT96(b_x[d2]), "lamT": colT96(lam[d2]),
                  "w_out": np.ascontiguousarray(w_out)})
        in_maps.append(m)
    res = run_bass_kernel_spmd(nc, in_maps, core_ids=list(range(NCORES)))
    return [r["yT"] for r in res.results]


def kernel_unfused(x, c, ada_w, ada_b, ln_g, ln_b, attn_w_in, attn_w_out, attn_sinks,
                   rnn_w_in, rnn_conv_w, rnn_conv_b, rnn_w_a, rnn_b_a, rnn_w_x, rnn_b_x, rnn_lam,
                   rnn_w_out, mlp_w1, mlp_w2):
    f = lambda a: np.asarray(a, np.float32)
    x, c, ada_w, ada_b, ln_g, ln_b = f(x), f(c), f(ada_w), f(ada_b), f(ln_g), f(ln_b)
    xs = shard_x(x)
    a0 = run_attn(xs, c, ada_w[0, 0], ada_b[0, 0], ln_g[0, 0], ln_b[0, 0], f(attn_w_in)[0], f(attn_w_out)[0], f(attn_sinks)[0])
    m0 = run_mlp(a0, c, ada_w[0, 1], ada_b[0, 1], ln_g[0, 1], ln_b[0, 1], f(mlp_w1)[0], f(mlp_w2)[0])
    r1 = run_rnn1(m0, c, ada_w[1, 0], ada_b[1, 0], f(rnn_w_in)[0], f(rnn_conv_w)[0], f(rnn_conv_b)[0],
                  f(rnn_w_a)[0], f(rnn_b_a)[0], f(rnn_w_x)[0], f(rnn_b_x)[0], f(rnn_lam)[0])
    r2 = run_rnn2(m0, r1, c, ada_w[1, 0], ada_b[1, 0], ln_g[1, 0], ln_b[1, 0],
                  f(rnn_w_a)[0], f(rnn_b_a)[0], f(rnn_w_x)[0], f(rnn_b_x)[0], f(rnn_lam)[0], f(rnn_w_out)[0])
    m1 = run_mlp(r2, c, ada_w[1, 1], ada_b[1, 1], ln_g[1, 1], ln_b[1, 1], f(mlp_w1)[1], f(mlp_w2)[1])
    out = np.empty((4, 2 * T, D), np.float32)
    for core in range(NCORES):
        idx = local_positions(core)[:T]
        out[core // 2][idx, :] = m1[core].T
    return out


def kernel(x, c, ada_w, ada_b, ln_g, ln_b, attn_w_in, attn_w_out, attn_sinks,
           rnn_w_in, rnn_conv_w, rnn_conv_b, rnn_w_a, rnn_b_a, rnn_w_x, rnn_b_x, rnn_lam,
           rnn_w_out, mlp_w1, mlp_w2):
    f = lambda a: np.ascontiguousarray(np.asarray(a, np.float32))
    x, c, ada_w, ada_b, ln_g, ln_b = f(x), f(c), f(ada_w), f(ada_b), f(ln_g), f(ln_b)
    attn_w_in, attn_w_out, attn_sinks = f(attn_w_in), f(attn_w_out), f(attn_sinks)
    rnn_w_in, rnn_conv_w, rnn_conv_b, rnn_w_out = f(rnn_w_in), f(rnn_conv_w), f(rnn_conv_b), f(rnn_w_out)
    rnn_w_a, rnn_b_a, rnn_w_x, rnn_b_x, rnn_lam = f(rnn_w_a), f(rnn_b_a), f(rnn_w_x), f(rnn_b_x), f(rnn_lam)
    mlp_w1, mlp_w2 = f(mlp_w1), f(mlp_w2)
    nc = get_prog("fused2")
    xs = shard_x(x)
    perm, mprev, mnext = attn_consts()
    in_maps = []
    for core in range(NCORES):
        b = core // 2
        half = core % 2
        d1, d2 = half, 1 - half
        cosT, sinT = rope_tables(local_positions(core))
        sel = np.zeros((128, 8), np.float32)
        sel[:, core ^ 1] = 1.0
        m = {}

        def ada(pfx, i, j, ln=True):
            m[pfx + "c_col"] = colT(c[b], 8)
            m[pfx + "ada_w"] = ada_w[i, j]
            m[pfx + "ada_bT"] = colT(ada_b[i, j], 24)
            if ln:
                m[pfx + "ln_gT"] = colT(ln_g[i, j], 8)
                m[pfx + "ln_bT"] = colT(ln_b[i, j], 8)

        ada("a_", 0, 0)
        m.update({"a_xT": xs[core], "a_w_in": attn_w_in[0], "a_w_out": attn_w_out[0], "a_cosT": cosT, "a_sinT": sinT,
                  "a_perm": perm, "a_mprev": mprev, "a_mnext": mnext,
                  "a_sink_rep": np.ascontiguousarray(np.tile(attn_sinks[0][None, :], (128, 1)))})
        ada("m0_", 0, 1)
        m.update({"m0_w1": mlp_w1[0], "m0_w2": mlp_w2[0]})
        oc_ = core ^ 1
        oh = oc_ % 2
        cosO, sinO = rope_tables(local_positions(oc_))
        m.update({"b_xT": xs[oc_], "b_cosT": cosO, "b_sinT": sinO})
        m.update({"q1_convT": conv_table(rnn_conv_w[0], rnn_conv_b[0], oh),
                  "q1_w_a": rnn_w_a[0, oh], "q1_w_x": rnn_w_x[0, oh], "q1_b_aT": colT96(rnn_b_a[0, oh]),
                  "q1_b_xT": colT96(rnn_b_x[0, oh]), "q1_lamT": colT96(rnn_lam[0, oh])})
        ada("r1_", 1, 0, ln=False)
        m.update({"r1_w_in": rnn_w_in[0], "r1_convT": conv_table(rnn_conv_w[0], rnn_conv_b[0], half),
                  "r1_w_a": rnn_w_a[0, d1], "r1_w_x": rnn_w_x[0, d1], "r1_b_aT": colT96(rnn_b_a[0, d1]),
                  "r1_b_xT": colT96(rnn_b_x[0, d1]), "r1_lamT": colT96(rnn_lam[0, d1])})
        m["r2_ln_gT"] = colT(ln_g[1, 0], 8)
        m["r2_ln_bT"] = colT(ln_b[1, 0], 8)
        m.update({"r2_w_a": rnn_w_a[0, d2], "r2_w_x": rnn_w_x[0, d2], "r2_b_aT": colT96(rnn_b_a[0, d2]),
                  "r2_b_xT": colT96(rnn_b_x[0, d2]), "r2_lamT": colT96(rnn_lam[0, d2]), "r2_w_out": rnn_w_out[0]})
        ada("m1_", 1, 1)
        m.update({"m1_w1": mlp_w1[1], "m1_w2": mlp_w2[1]})
        in_maps.append({k: np.ascontiguousarray(v) for k, v in m.items()})
    res = run_bass_kernel_spmd(nc, in_maps, core_ids=list(range(NCORES)))
    out = np.empty((4, 2 * T, D), np.float32)
    for core in range(NCORES):
        idx = local_positions(core)[:T]
        out[core // 2][idx, :] = res.results[core]["out"].T
    return out
```

```python
import numpy as np
from contextlib import ExitStack
import concourse.bass as bass
import concourse.mybir as mybir
from concourse.bass_utils import run_bass_kernel_spmd

AF = mybir.ActivationFunctionType
ALU = mybir.AluOpType
F32 = mybir.dt.float32
BF16 = mybir.dt.bfloat16

D = 1024
KC = 8
T = 4096
NCORES = 8
DFF = 4096
DEPTH = 2
DN_ALPHA = (2.0 * DEPTH) ** 0.25
LN_EPS = 1e-5
DRNN = 1536
NBLK = 16
BW = 96
SEM_LIMIT = 3000


class Buf:
    __slots__ = ("name", "w", "r", "dsem", "dcnt")

    def __init__(self, name):
        self.name = name
        self.w = None
        self.r = []
        self.dsem = None
        self.dcnt = 0


class Sched:
    ENG = ("pe", "act", "dve", "pool", "sp")

    def __init__(self, nc, stack):
        self.nc = nc
        self.stack = stack
        self.stream = {e: [] for e in self.ENG}
        self.sem = {e: None for e in self.ENG}
        self.cnt = {e: 0 for e in self.ENG}
        self.seen = {e: {} for e in self.ENG}
        self.nsem = 0
        self.handles = []
        self.dma_bufs = []

    def _newsem(self, tag):
        self.nsem += 1
        h = self.nc.alloc_semaphore(name="%ss%d_%s" % (getattr(self, "prefix", ""), self.nsem, tag))
        self.handles.append(h)
        return h

    def _peek(self, e):
        if self.sem[e] is None or self.cnt[e] >= SEM_LIMIT:
            self.sem[e] = self._newsem(e)
            self.cnt[e] = 0
        return (self.sem[e], self.cnt[e] + 1)

    def _waits(self, e, reads, writes, skip_sem=None):
        need = {}

        def add(t):
            if t is None:
                return
            s, v = t
            k = id(s)
            if k not in need or need[k][1] < v:
                need[k] = (s, v)

        for b in reads:
            add(b.w)
        for b in writes:
            add(b.w)
            for t in b.r:
                add(t)
        out = []
        seen = self.seen[e]
        for k, (s, v) in need.items():
            if e == "pe" and s is self.sem["pe"]:
                continue
            if skip_sem is not None and s is skip_sem:
                continue
            if seen.get(k, 0) >= v:
                continue
            seen[k] = v
            out.append((s, v))
        return out

    def op(self, e, fn, reads=(), writes=(), inc=True):
        waits = self._waits(e, reads, writes)
        tk = self._peek(e)
        if inc:
            self.cnt[e] += 1
        self.stream[e].append((waits, fn, tk if inc else None))
        for b in reads:
            b.r.append(tk)
        for b in writes:
            b.w = tk
            b.r = []
        return tk

    def dma(self, e, out_ap, in_ap, reads=(), writes=(), anchor=None):
        a = anchor
        if a.dsem is None or a.dcnt >= SEM_LIMIT:
            a.dsem = self._newsem("d")
            a.dcnt = 0
            self.dma_bufs.append(a)
        waits = self._waits(e, reads, writes, skip_sem=a.dsem)
        a.dcnt += 16
        tk = (a.dsem, a.dcnt)

        def fn(eng, out_ap=out_ap, in_ap=in_ap):
            return eng.dma_start(out=out_ap, in_=in_ap)

        self.stream[e].append((waits, fn, ("dma", a.dsem)))
        for b in reads:
            b.r.append(tk)
        for b in writes:
            b.w = tk
            b.r = []
        return tk

    def collective(self, in_ap, out_ap, reads=(), writes=(), anchor=None):
        a = anchor
        if a.dsem is None:
            a.dsem = self._newsem("cc")
            a.dcnt = 0
            self.dma_bufs.append(a)
        waits = self._waits("pool", reads, writes, skip_sem=a.dsem)
        a.dcnt += 1
        tk = (a.dsem, a.dcnt)

        def fn(eng, in_ap=in_ap, out_ap=out_ap):
            return eng.collective_compute("AllGather", ALU.bypass, replica_groups=[list(range(NCORES))], ins=[in_ap], outs=[out_ap])

        self.stream["pool"].append((waits, fn, ("cc", a.dsem)))
        for b in reads:
            b.r.append(tk)
        for b in writes:
            b.w = tk
            b.r = []
        return tk

    def barrier(self):
        targets = []
        for e in self.ENG:
            if self.sem[e] is not None and self.cnt[e] > 0:
                targets.append((self.sem[e], self.cnt[e]))
        for a in self.dma_bufs:
            targets.append((a.dsem, a.dcnt))
        for e in self.ENG:
            w = []
            for (s, v) in targets:
                if e == "pe" and s is self.sem["pe"]:
                    continue
                if self.seen[e].get(id(s), 0) >= v:
                    continue
                self.seen[e][id(s)] = v
                w.append((s, v))
            if w:
                self.stream[e].append((w, None, None))

    def final_wait(self, e, bufs):
        w = []
        for b in bufs:
            for t in [b.w] + list(b.r):
                if t is not None:
                    w.append(t)
        self.stream[e].append((w, None, None))

    def emit(self):
        nc = self.nc
        block = self.stack.enter_context(nc.Block())

        def replay(eng, items):
            for waits, fn, tk in items:
                for (s, v) in waits:
                    eng.wait_ge(s, v)
                if fn is None:
                    continue
                ins = fn(eng)
                if tk is None:
                    continue
                if tk[0] == "dma":
                    ins.then_inc(tk[1], 16)
                elif tk[0] == "cc":
                    ins.then_inc(tk[1])
                else:
                    ins.then_inc(tk[0], 1)

        st = self.stream

        @block.sync
        def _(eng):
            replay(eng, st["sp"])

        @block.tensor
        def _(eng):
            replay(eng, st["pe"])

        @block.scalar
        def _(eng):
            replay(eng, st["act"])

        @block.vector
        def _(eng):
            replay(eng, st["dve"])

        @block.gpsimd
        def _(eng):
            replay(eng, st["pool"])


class Ctx:
    def __init__(self, fused=False):
        self.nc = bass.Bass("TRN2", target_bir_lowering=False)
        self.fused = fused
        self.gstack = ExitStack()
        self.nbuf = 0
        self.decl = {}
        self.ada_cache = {}
        self.prefix = ""
        self.over = {}
        self.stack = None
        self.s = None

    def begin_phase(self, prefix="", over=None):
        self.prefix = prefix
        self.over = over or {}
        self.stack = ExitStack()
        self.s = Sched(self.nc, self.stack)
        self.s.prefix = prefix

    def end_phase(self):
        if self.fused:
            self.s.barrier()
        self.s.emit()
        self.stack.close()
        self.nc.all_engine_barrier()
        self.nc.clear_and_free_semaphores(self.s.handles)
        self.nc.all_engine_barrier()
        return self.nc

    def dram_in(self, name, shape, dt=F32):
        if name in self.over:
            return self.over[name]
        ap = self.nc.dram_tensor(self.prefix + name, list(shape), dt, kind="ExternalInput").ap()
        self.decl[self.prefix + name] = ap
        return ap

    def dram_out(self, name, shape, dt=F32):
        if name in self.over:
            return self.over[name]
        return self.nc.dram_tensor(self.prefix + name, list(shape), dt, kind="ExternalOutput").ap()

    def dram_tmp(self, name, shape, dt=F32):
        return self.nc.dram_tensor(name, list(shape), dt)

    def gsb(self, name, shape, dt):
        return self.gstack.enter_context(self.nc.sbuf_tensor(name, list(shape), dt))

    def sb(self, name, shape, dt):
        return self.stack.enter_context(self.nc.sbuf_tensor(self.prefix + name, list(shape), dt))

    def ps(self, name, shape=(128, 512), dt=F32):
        return self.stack.enter_context(self.nc.psum_tensor(self.prefix + name, list(shape), dt))

    def buf(self, name="b"):
        self.nbuf += 1
        return Buf("%s%d" % (name, self.nbuf))


def emit_consts(cx):
    s = cx.s
    c = {}
    c["ones_t"] = cx.sb("ones_t", [128, 128], BF16)
    c["ones_b"] = cx.buf("ones")
    c["eps_t"] = cx.sb("eps_t", [128, 1], F32)
    c["eps_b"] = cx.buf("eps")
    s.op("pool", lambda e: e.memset(c["ones_t"][:], 1.0 / 1024.0), writes=[c["ones_b"]])
    s.op("pool", lambda e: e.memset(c["eps_t"][:], LN_EPS), writes=[c["eps_b"]])
    return c


def emit_adaln(cx, c_col, ada_w, ada_bT, scratch_ps, scratch_ps_b, pieces, piece_bufs, owner_bufs):
    s = cx.s
    key = cx.over.get("ada_key")
    if key is not None and key in cx.ada_cache:
        mod, mod1p = cx.ada_tiles[key]
        return dict(mod=mod, mod1p=mod1p, b_mod=cx.buf("adamod"), b_mod1p=cx.buf("adamod1p"))
    ccol = cx.sb("ada_c", [128, 8], F32)
    csil = cx.sb("ada_cs", [128, 8], F32)
    bT = cx.sb("ada_b", [128, 24], F32)
    if key is not None:
        mod, mod1p = cx.ada_tiles[key]
        cx.ada_cache[key] = True
    else:
        mod = cx.sb("ada_mod", [128, 24], F32)
        mod1p = cx.sb("ada_mod1p", [128, 24], F32)
    b_c, b_cs, b_mod, b_mod1p = cx.buf("adac"), cx.buf("adacs"), cx.buf("adamod"), cx.buf("adamod1p")
    b_bT = b_c
    s.dma("sp", ccol[:], c_col[:, :], writes=[b_c], anchor=b_c)
    s.dma("sp", bT[:], ada_bT[:, :], writes=[b_bT], anchor=b_bT)
    s.op("act", lambda e: e.activation(out=csil[:], in_=ccol[:], func=AF.Silu), reads=[b_c], writes=[b_cs])
    wv = ada_w.rearrange("(k p) n -> p k n", p=128)
    for col in range(24):
        t = pieces[col % 2]
        tb = piece_bufs[col % 2]
        s.dma("sp", t, wv[:, :, col * 128:(col + 1) * 128], writes=[tb], anchor=tb)
        for k in range(8):
            s.op("pe", lambda e, t=t, k=k, col=col: e.matmul(
                scratch_ps[:, col:col + 1], lhsT=t[:, k, :], rhs=csil[:, k:k + 1],
                start=(k == 0), stop=(k == 7)),
                reads=[tb, b_cs], writes=[scratch_ps_b], inc=(k == 7))
    s.op("dve", lambda e: e.tensor_tensor(out=mod[:], in0=scratch_ps[:, 0:24], in1=bT[:], op=ALU.add),
         reads=[scratch_ps_b, b_bT] + list(piece_bufs), writes=[b_mod] + list(owner_bufs))
    s.op("dve", lambda e: e.tensor_scalar_add(out=mod1p[:], in0=mod[:], scalar1=1.0), reads=[b_mod], writes=[b_mod1p])
    return dict(mod=mod, mod1p=mod1p, b_mod=b_mod, b_mod1p=b_mod1p)


def emit_ln_tile(cx, cst, z, zb_t, zsq_t, ps_sum, b_sum, ps_sq, b_sq, tmp, N, gT, bT, b_par, out_t, bz, bzb, bzsq, btmp, bout):
    s = cx.s
    for k in range(8):
        s.op("act", lambda e, k=k: e.activation(out=zb_t[:, k, :], in_=z[:, k, :], func=AF.Identity), reads=[bz], writes=[bzb])
        s.op("act", lambda e, k=k: e.activation(out=zsq_t[:, k, :], in_=z[:, k, :], func=AF.Square), reads=[bz], writes=[bzsq])
    for k in range(8):
        s.op("pe", lambda e, k=k: e.matmul(ps_sum, lhsT=cst["ones_t"][:], rhs=zb_t[:, k, :], start=(k == 0), stop=(k == 7)),
             reads=[cst["ones_b"], bzb], writes=[b_sum], inc=(k == 7))
    for k in range(8):
        s.op("pe", lambda e, k=k: e.matmul(ps_sq, lhsT=cst["ones_t"][:], rhs=zsq_t[:, k, :], start=(k == 0), stop=(k == 7)),
             reads=[cst["ones_b"], bzsq], writes=[b_sq], inc=(k == 7))
    mean = tmp[:, 0, :]
    msq = tmp[:, 1, :]
    var = tmp[:, 2, :]
    rstd = tmp[:, 3, :]
    s.op("dve", lambda e: e.tensor_copy(out=mean, in_=ps_sum), reads=[b_sum], writes=[btmp])
    s.op("dve", lambda e: e.tensor_tensor(out=msq, in0=mean, in1=mean, op=ALU.mult), reads=[btmp], writes=[btmp])
    s.op("dve", lambda e: e.tensor_tensor(out=var, in0=ps_sq, in1=msq, op=ALU.subtract), reads=[b_sq, btmp], writes=[btmp])
    s.op("act", lambda e: e.activation(out=var, in_=var, func=AF.Sqrt, bias=cst["eps_t"][:, 0:1]), reads=[btmp, cst["eps_b"]], writes=[btmp])
    s.op("dve", lambda e: e.reciprocal(out=rstd, in_=var), reads=[btmp], writes=[btmp])
    for k in range(8):
        s.op("dve", lambda e, k=k: e.tensor_tensor(out=z[:, k, :], in0=z[:, k, :], in1=mean, op=ALU.subtract), reads=[bz, btmp], writes=[bz])
        s.op("dve", lambda e, k=k: e.tensor_tensor(out=z[:, k, :], in0=z[:, k, :], in1=rstd, op=ALU.mult), reads=[bz, btmp], writes=[bz])
        s.op("act", lambda e, k=k: e.activation(out=out_t[:, k, :], in_=z[:, k, :], func=AF.Identity,
                                                 bias=bT[:, k:k + 1], scale=gT[:, k:k + 1]),
             reads=[bz, b_par], writes=[bout])


def build_mlp(cx=None, prefix="", over=None):
    cx = cx or Ctx()
    cx.begin_phase(prefix, over)
    s = cx.s
    N = 256
    NT = T // N
    xT = cx.dram_in("xT", [D, T])
    c_col = cx.dram_in("c_col", [128, 8])
    ada_w = cx.dram_in("ada_w", [D, 3 * D])
    ada_bT = cx.dram_in("ada_bT", [128, 24])
    ln_gT = cx.dram_in("ln_gT", [128, 8])
    ln_bT = cx.dram_in("ln_bT", [128, 8])
    w1 = cx.dram_in("w1", [D, DFF])
    w2 = cx.dram_in("w2", [DFF, D])
    yT = cx.dram_out("yT", [D, T])

    cst = emit_consts(cx)
    acc = [cx.ps("acc%d" % i) for i in range(4)]
    acc_b = [cx.buf("acc") for i in range(8)]
    ph = [cx.ps("ph%d" % i) for i in range(3)]
    ph_b = [cx.buf("ph") for i in range(3)]
    stp = cx.ps("stp")
    st_b = cx.buf("st")

    z = cx.sb("z", [128, 8, N], F32)
    bz = cx.buf("z")
    pieces = [z[:, 0:4, :].rearrange("p a (b n) -> p (a b) n", n=128), z[:, 4:8, :].rearrange("p a (b n) -> p (a b) n", n=128)]
    ada = emit_adaln(cx, c_col, ada_w, ada_bT, stp, st_b, pieces, [cx.buf("pc0"), cx.buf("pc1")], [bz])
    mod, mod1p = ada["mod"], ada["mod1p"]
    b_mod, b_mod1p = ada["b_mod"], ada["b_mod1p"]

    gT = cx.sb("gT", [128, 8], F32)
    bT = cx.sb("bT", [128, 8], F32)
    b_par = cx.buf("par")
    s.dma("sp", gT[:], ln_gT[:, :], writes=[b_par], anchor=b_par)
    s.dma("sp", bT[:], ln_bT[:, :], writes=[b_par], anchor=b_par)

    w1b = cx.sb("w1b", [128, 8, DFF], BF16)
    w2b = cx.sb("w2b", [128, 32, D], BF16)
    b_w1x = cx.buf("w1")
    b_w2x = [cx.buf("w2") for k in range(4)]
    b_w1 = [b_w1x for k in range(8)]
    b_w2 = [b_w2x[k // 8] for k in range(32)]
    w1v = w1.rearrange("(k p) n -> p k n", p=128)
    w2v = w2.rearrange("(k p) n -> p k n", p=128)
    for k in range(8):
        s.dma("pool", w1b[:, k, :], w1v[:, k, :], writes=[b_w1[k]], anchor=b_w1[k])
    for k in range(32):
        s.dma("pool", w2b[:, k, :], w2v[:, k, :], writes=[b_w2[k]], anchor=b_w2[k])

    jobs = cx.over.get("jobs") or [(xT, yT, cx.over.get("tail_out"))]
    jobv = [(xj.rearrange("(k p) t -> p k t", p=128), yj.rearrange("(k p) t -> p k t", p=128), tj) for (xj, yj, tj) in jobs]
    NXB = 2
    xt = [cx.sb("xt%d" % i, [128, 8, N], F32) for i in range(NXB)]
    xt_b = [cx.buf("xt") for i in range(NXB)]
    ht = cx.sb("ht", [128, 8, N], BF16)
    ht_b = cx.buf("ht")
    rb = [cx.sb("rb%d" % i, [128, N], F32) for i in range(3)]
    rb_b = [cx.buf("rb") for i in range(3)]
    hid = [cx.sb("hid%d" % i, [128, N], BF16) for i in range(4)]
    hid_b = [cx.buf("hid") for i in range(4)]
    zb_t = cx.sb("zb", [128, 8, N], BF16)
    bzb = cx.buf("zb")
    zsq_t = cx.sb("zsq", [128, 8, N], BF16)
    bzsq = cx.buf("zsq")
    tmp = cx.sb("tmp", [128, 4, N], F32)
    btmp = cx.buf("tmp")

    def load_x(gi):
        xv_ = jobv[gi // NT][0]
        it_ = gi % NT
        s.dma("sp", xt[gi % NXB][:], xv_[:, :, it_ * N:(it_ + 1) * N], writes=[xt_b[gi % NXB]], anchor=xt_b[gi % NXB])

    load_x(0)
    for gi in range(NT * len(jobv)):
        it = gi % NT
        yv, tail_out = jobv[gi // NT][1], jobv[gi // NT][2]
        if gi + 1 < NT * len(jobv):
            load_x(gi + 1)
        x_t = xt[gi % NXB]
        xb = xt_b[gi % NXB]
        for k in range(8):
            s.op("act", lambda e, k=k, x_t=x_t: e.activation(out=ht[:, k, :], in_=x_t[:, k, :], func=AF.Identity,
                                                             bias=mod[:, k:k + 1], scale=mod1p[:, 8 + k:9 + k]),
                 reads=[xb, b_mod, b_mod1p], writes=[ht_b])
        s.op("pool", lambda e, x_t=x_t: e.tensor_scalar(out=x_t[:], in0=x_t[:], scalar1=float(DN_ALPHA), scalar2=None, op0=ALU.mult),
             reads=[xb], writes=[xb])

        def g1(hc):
            p = ph[hc % 3]
            for k in range(8):
                s.op("pe", lambda e, p=p, k=k, hc=hc: e.matmul(p[:, 0:N], lhsT=w1b[:, k, hc * 128:(hc + 1) * 128], rhs=ht[:, k, :],
                                                              start=(k == 0), stop=(k == 7)),
                     reads=[b_w1[k], ht_b], writes=[ph_b[hc % 3]], inc=(k == 7))

        def ew(hc):
            p = ph[hc % 3]
            r = rb[hc % 3]
            h_ = hid[hc % 4]
            s.op("act", lambda e, p=p, r=r: e.activation(out=r[:], in_=p[:, 0:N], func=AF.Relu), reads=[ph_b[hc % 3]], writes=[rb_b[hc % 3]])
            s.op("dve", lambda e, r=r, h_=h_: e.tensor_tensor(out=h_[:], in0=r[:], in1=r[:], op=ALU.mult), reads=[rb_b[hc % 3]], writes=[hid_b[hc % 4]])

        def g2(hc):
            h_ = hid[hc % 4]
            for oc in range(8):
                a = acc[oc // 2]
                s.op("pe", lambda e, a=a, oc=oc, hc=hc, h_=h_: e.matmul(a[:, (oc % 2) * N:(oc % 2 + 1) * N],
                                                                        lhsT=w2b[:, hc, oc * 128:(oc + 1) * 128], rhs=h_[:],
                                                                        start=(hc == 0 and oc % 2 == 0), stop=(hc == 31),
                                                                        skip_group_check=True),
                     reads=[b_w2[hc], hid_b[hc % 4]], writes=[acc_b[oc]], inc=(oc == 7 or hc == 31))

        g1(0)
        g1(1)
        ew(0)
        for hc in range(32):
            if hc + 2 < 32:
                g1(hc + 2)
            if hc + 1 < 32:
                ew(hc + 1)
            g2(hc)
        for oc in range(8):
            a = acc[oc // 2]
            s.op("dve", lambda e, a=a, oc=oc, x_t=x_t: e.scalar_tensor_tensor(
                out=z[:, oc, :], in0=a[:, (oc % 2) * N:(oc % 2 + 1) * N], scalar=mod1p[:, 16 + oc:17 + oc], in1=x_t[:, oc, :],
                op0=ALU.mult, op1=ALU.add), reads=[acc_b[oc], xb, b_mod1p], writes=[bz])
        emit_ln_tile(cx, cst, z, zb_t, zsq_t, stp[:, 0:N], st_b, stp[:, N:2 * N], st_b, tmp, N, gT, bT, b_par, z, bz, bzb, bzsq, btmp, bz)
        s.dma("sp", yv[:, :, it * N:(it + 1) * N], z[:], reads=[bz], anchor=bz)
        if it == NT - 1 and tail_out is not None:
            s.dma("sp", tail_out.rearrange("p (k t) -> p k t", t=2), z[:, :, N - 2:N], reads=[bz], anchor=bz)
    s.final_wait("sp", [bz])
    return cx.end_phase()


TH = T + 128


def build_attn(cx=None, prefix="", over=None):
    cx = cx or Ctx()
    cx.begin_phase(prefix, over)
    s = cx.s
    N = 512
    NT = T // N
    SCALE = 128.0 ** -0.5
    xT = cx.dram_in("xT", [D, TH])
    c_col = cx.dram_in("c_col", [128, 8])
    ada_w = cx.dram_in("ada_w", [D, 3 * D])
    ada_bT = cx.dram_in("ada_bT", [128, 24])
    ln_gT = cx.dram_in("ln_gT", [128, 8])
    ln_bT = cx.dram_in("ln_bT", [128, 8])
    w_in = cx.dram_in("w_in", [D, 1536])
    w_out = cx.dram_in("w_out", [D, D])
    cosT = cx.dram_in("cosT", [32, TH])
    sinT = cx.dram_in("sinT", [32, TH])
    perm = cx.dram_in("perm", [32, 32])
    mprev = cx.dram_in("mprev", [128, 512])
    mnext = cx.dram_in("mnext", [128, 512])
    sink_rep = cx.dram_in("sink_rep", [128, 8])
    yT = cx.dram_out("yT", [D, T])

    cst = emit_consts(cx)
    one_t = cx.sb("one_t", [128, 128], BF16)
    one_b = cx.buf("one")
    s.op("pool", lambda e: e.memset(one_t[:], 1.0), writes=[one_b])

    banks = [cx.ps("bank%d" % i) for i in range(8)]
    bank_b = [cx.buf("bank") for i in range(8)]
    bctr = [0]

    def bank():
        i = bctr[0] % 8
        bctr[0] += 1
        return banks[i], bank_b[i]

    z = cx.sb("z", [128, 8, N], F32)
    bz = cx.buf("z")
    pieces = [z[:, 0:2, :].rearrange("p a (b n) -> p (a b) n", n=128), z[:, 2:4, :].rearrange("p a (b n) -> p (a b) n", n=128)]
    pb, pbb = bank()
    ada = emit_adaln(cx, c_col, ada_w, ada_bT, pb, pbb, pieces, [cx.buf("pc0"), cx.buf("pc1")], [bz])
    mod, mod1p = ada["mod"], ada["mod1p"]
    b_mod, b_mod1p = ada["b_mod"], ada["b_mod1p"]

    gT = cx.sb("gT", [128, 8], F32)
    bT = cx.sb("bT", [128, 8], F32)
    b_par = cx.buf("par")
    s.dma("sp", gT[:], ln_gT[:, :], writes=[b_par], anchor=b_par)
    s.dma("sp", bT[:], ln_bT[:, :], writes=[b_par], anchor=b_par)

    perm_t = cx.sb("perm_t", [32, 32], BF16)
    mprev_t = cx.sb("mprev_t", [128, 512], BF16)
    mnext_t = cx.sb("mnext_t", [128, 512], BF16)
    b_cm = cx.buf("cm")
    s.dma("pool", perm_t[:], perm[:, :], writes=[b_cm], anchor=b_cm)
    s.dma("pool", mprev_t[:], mprev[:, :], writes=[b_cm], anchor=b_cm)
    s.dma("pool", mnext_t[:], mnext[:, :], writes=[b_cm], anchor=b_cm)
    sk = cx.sb("sk", [128, 8], F32)
    b_sk = cx.buf("sk")
    s.dma("sp", sk[:], sink_rep[:, :], writes=[b_sk], anchor=b_sk)
    s.op("act", lambda e: e.activation(out=sk[:], in_=sk[:], func=AF.Exp), reads=[b_sk], writes=[b_sk])
    es_full = cx.sb("es_full", [128, 2, 512], F32)
    b_es = cx.buf("es")
    s.op("pool", lambda e: e.memset(es_full[:], 0.0), writes=[b_es])
    for hh in range(8):
        s.op("pool", lambda e, hh=hh: e.tensor_scalar(out=es_full[:, hh // 4, (hh % 4) * 128:(hh % 4 + 1) * 128],
                                                       in0=es_full[:, hh // 4, (hh % 4) * 128:(hh % 4 + 1) * 128],
                                                       scalar1=sk[:, hh:hh + 1], scalar2=None, op0=ALU.add),
             reads=[b_sk, b_es], writes=[b_es])

    w_inb = cx.sb("w_inb", [128, 8, 1536], BF16)
    w_outb = cx.sb("w_outb", [128, 8, D], BF16)
    b_win = cx.buf("win")
    b_wout = cx.buf("wout")
    winv = w_in.rearrange("(k p) n -> p k n", p=128)
    woutv = w_out.rearrange("(k p) n -> p k n", p=128)
    for k in range(8):
        s.dma("pool", w_inb[:, k, :], winv[:, k, :], writes=[b_win], anchor=b_win)
    for k in range(8):
        s.dma("pool", w_outb[:, k, :], woutv[:, k, :], writes=[b_wout], anchor=b_wout)

    kT_all = cx.sb("kT_all", [128, 2, TH], BF16)
    v_all = cx.sb("v_all", [128, TH // 128, 256], BF16)
    b_k = [cx.buf("k") for i in range(NT + 1)]
    b_v = [cx.buf("v") for i in range(NT + 1)]

    xv = xT.rearrange("(k p) t -> p k t", p=128)
    yv = yT.rearrange("(k p) t -> p k t", p=128)
    xt = cx.sb("xt", [128, 8, N], F32)
    xb = cx.buf("xt")
    ht = cx.sb("ht", [128, 8, N], BF16)
    ht_b = cx.buf("ht")
    cs_t = cx.sb("cs_t", [32, 2, N], F32)
    b_cs = cx.buf("cs")
    rtmp = cx.sb("rtmp", [32, 2, N], F32)
    b_rtmp = cx.buf("rtmp")
    qtmp = cx.sb("qtmp", [128, N], BF16)
    b_qtmp = cx.buf("qtmp")
    qT = cx.sb("qT", [128, 4, 8, 128], BF16)
    b_q = cx.buf("q")
    pT = [cx.sb("pT%d" % i, [128, 512], BF16) for i in range(6)]
    pT_b = [cx.buf("pT") for i in range(6)]
    pctr = [0]
    oT = cx.sb("oT", [128, 8, N], BF16)
    b_o = cx.buf("o")
    zb_t = cx.sb("zb", [128, 8, N], BF16)
    bzb = cx.buf("zb")
    zsq_t = cx.sb("zsq", [128, 8, N], BF16)
    bzsq = cx.buf("zsq")
    tmp = cx.sb("tmp", [128, 4, N], F32)
    btmp = cx.buf("tmp")

    def load_tile(t0, n):
        s.dma("sp", xt[:, :, 0:n], xv[:, :, t0:t0 + n], writes=[xb], anchor=xb)
        s.dma("sp", cs_t[:, 0, 0:n], cosT[:, t0:t0 + n], writes=[b_cs], anchor=b_cs)
        s.dma("sp", cs_t[:, 1, 0:n], sinT[:, t0:t0 + n], writes=[b_cs], anchor=b_cs)
        for k in range(8):
            s.op("act", lambda e, k=k, n=n: e.activation(out=ht[:, k, 0:n], in_=xt[:, k, 0:n], func=AF.Identity,
                                                         bias=mod[:, k:k + 1], scale=mod1p[:, 8 + k:9 + k]),
                 reads=[xb, b_mod, b_mod1p], writes=[ht_b])

    def rope(view32, vb, n):
        pbk, pbb_ = bank()
        s.op("pe", lambda e: e.matmul(pbk[0:32, 0:n], lhsT=perm_t[:, :], rhs=view32, start=True, stop=True),
             reads=[b_cm, vb], writes=[pbb_])
        s.op("dve", lambda e: e.tensor_tensor(out=rtmp[:, 0, 0:n], in0=pbk[0:32, 0:n], in1=cs_t[:, 1, 0:n], op=ALU.mult),
             reads=[pbb_, b_cs], writes=[b_rtmp])
        s.op("dve", lambda e: e.tensor_tensor(out=rtmp[:, 1, 0:n], in0=view32, in1=cs_t[:, 0, 0:n], op=ALU.mult),
             reads=[vb, b_cs], writes=[b_rtmp])
        s.op("dve", lambda e: e.tensor_tensor(out=view32, in0=rtmp[:, 0, 0:n], in1=rtmp[:, 1, 0:n], op=ALU.add),
             reads=[b_rtmp], writes=[vb])

    for it in range(NT + 1):
        t0 = it * N
        n = N if it < NT else 128
        load_tile(t0, n)
        for kvh in range(2):
            pbk, pbb_ = bank()
            for k in range(8):
                s.op("pe", lambda e, pbk=pbk, k=k, kvh=kvh, n=n: e.matmul(pbk[:, 0:n], lhsT=w_inb[:, k, 1024 + kvh * 128:1152 + kvh * 128],
                                                                          rhs=ht[:, k, 0:n], start=(k == 0), stop=(k == 7)),
                     reads=[b_win, ht_b], writes=[pbb_], inc=(k == 7))
            s.op("act", lambda e, pbk=pbk, kvh=kvh, t0=t0, n=n: e.activation(out=kT_all[:, kvh, t0:t0 + n], in_=pbk[:, 0:n], func=AF.Identity),
                 reads=[pbb_], writes=[b_k[it]])
            rope(kT_all[0:32, kvh, t0:t0 + n], b_k[it], n)
        for blk in range(n // 128):
            pbk, pbb_ = bank()
            for k in range(8):
                s.op("pe", lambda e, pbk=pbk, k=k, blk=blk: e.matmul(pbk[:, 0:256], lhsT=ht[:, k, blk * 128:(blk + 1) * 128],
                                                                     rhs=w_inb[:, k, 1280:1536], start=(k == 0), stop=(k == 7)),
                     reads=[b_win, ht_b], writes=[pbb_], inc=(k == 7))
            s.op("act", lambda e, pbk=pbk, blk=blk, t0=t0: e.activation(out=v_all[:, t0 // 128 + blk, :], in_=pbk[:, 0:256], func=AF.Identity),
                 reads=[pbb_], writes=[b_v[it]])

    class _RV:
        def __init__(self, ts, bs):
            self.t, self.b, self.i = ts, bs, 0

        def get(self):
            j = self.i % len(self.t)
            self.i += 1
            return self.t[j], self.b[j]

    SCB = _RV(banks[0:4], bank_b[0:4])
    PVB = _RV(banks[4:8], bank_b[4:8])
    PTP = _RV(pT, pT_b)
    DENP = Rot(cx, "den", 2, [128, 512], F32)
    for it in range(NT):
        t0 = it * N
        load_tile(t0, N)
        for hd in range(8):
            pbk, pbb_ = bank()
            for k in range(8):
                s.op("pe", lambda e, pbk=pbk, k=k, hd=hd: e.matmul(pbk[:, 0:N], lhsT=w_inb[:, k, hd * 128:(hd + 1) * 128],
                                                                   rhs=ht[:, k, :], start=(k == 0), stop=(k == 7)),
                     reads=[b_win, ht_b], writes=[pbb_], inc=(k == 7))
            s.op("act", lambda e, pbk=pbk: e.activation(out=qtmp[:], in_=pbk[:, 0:N], func=AF.Identity),
                 reads=[pbb_], writes=[b_qtmp])
            rope(qtmp[0:32, :], b_qtmp, N)
            s.op("pool", lambda e, hd=hd: e.tensor_copy(out=qT[:, :, hd, :], in_=qtmp[:].rearrange("p (a n) -> p a n", n=128)),
                 reads=[b_qtmp], writes=[b_q])
        s.op("pool", lambda e: e.tensor_scalar(out=xt[:], in0=xt[:], scalar1=float(DN_ALPHA), scalar2=None, op0=ALU.mult),
             reads=[xb], writes=[xb])
        def a_s0(itm, it=it):
            qb, kvh = itm["qb"], itm["kvh"]
            B = it * 4 + qb
            qrhs = qT[:, qb, kvh * 4:(kvh + 1) * 4, :].rearrange("p h n -> p (h n)")
            itm["sc"] = []
            for jb in (B - 1, B, B + 1):
                if jb < 0:
                    continue
                pbk, pbb_ = SCB.get()
                kb = b_k[min(jb // 4, NT)]
                s.op("pe", lambda e, pbk=pbk, jb=jb: e.matmul(pbk[:, :], lhsT=kT_all[:, kvh, jb * 128:(jb + 1) * 128], rhs=qrhs,
                                                              start=True, stop=True),
                     reads=[kb, b_q], writes=[pbb_])
                itm["sc"].append((jb, B, pbk, pbb_))

        def a_s1(itm):
            itm["pts"] = []
            for (jb, B, pbk, pbb_) in itm["sc"]:
                p_t, p_b = PTP.get()
                s.op("act", lambda e, pbk=pbk, p_t=p_t: e.activation(out=p_t[:], in_=pbk[:, :], func=AF.Exp, scale=SCALE),
                     reads=[pbb_], writes=[p_b])
                if jb == B - 1:
                    s.op("pool", lambda e, p_t=p_t: e.tensor_tensor(out=p_t[:], in0=p_t[:], in1=mprev_t[:], op=ALU.mult),
                         reads=[p_b, b_cm], writes=[p_b])
                elif jb == B + 1:
                    s.op("pool", lambda e, p_t=p_t: e.tensor_tensor(out=p_t[:], in0=p_t[:], in1=mnext_t[:], op=ALU.mult),
                         reads=[p_b, b_cm], writes=[p_b])
                itm["pts"].append((jb, p_t, p_b))

        def a_s2(itm):
            kvh = itm["kvh"]
            pts = itm["pts"]
            po, pob = PVB.get()
            pd, pdb = PVB.get()
            itm.update(po=po, pob=pob, pd=pd, pdb=pdb)
            for i, (jb, p_t, p_b) in enumerate(pts):
                vb = b_v[min(jb // 4, NT)]
                s.op("pe", lambda e, jb=jb, p_t=p_t, i=i: e.matmul(po[:, :], lhsT=v_all[:, jb, kvh * 128:(kvh + 1) * 128],
                                                                   rhs=p_t[:], start=(i == 0), stop=(i == len(pts) - 1)),
                     reads=[vb, p_b], writes=[pob], inc=(i == len(pts) - 1))
            for i, (jb, p_t, p_b) in enumerate(pts):
                s.op("pe", lambda e, p_t=p_t, i=i: e.matmul(pd[:, :], lhsT=one_t[:], rhs=p_t[:], start=(i == 0), stop=(i == len(pts) - 1)),
                     reads=[one_b, p_b], writes=[pdb], inc=(i == len(pts) - 1))

        def a_s3(itm):
            qb, kvh = itm["qb"], itm["kvh"]
            po, pob, pd, pdb = itm["po"], itm["pob"], itm["pd"], itm["pdb"]
            den, den_b = DENP.get()
            s.op("dve", lambda e: e.tensor_tensor(out=den[:], in0=pd[:, :], in1=es_full[:, kvh, :], op=ALU.add),
                 reads=[pdb, b_es], writes=[den_b])
            s.op("dve", lambda e: e.reciprocal(out=den[:], in_=den[:]), reads=[den_b], writes=[den_b])
            s.op("dve", lambda e: e.tensor_tensor(
                out=oT[:, kvh * 4:(kvh + 1) * 4, qb * 128:(qb + 1) * 128], in0=po[:, :].rearrange("p (h n) -> p h n", n=128),
                in1=den[:].rearrange("p (h n) -> p h n", n=128), op=ALU.mult),
                reads=[pob, den_b], writes=[b_o])

        run_pipeline([{"qb": qb, "kvh": kvh} for qb in range(4) for kvh in range(2)], [a_s0, a_s1, a_s2, a_s3])
        for oc in range(8):
            pbk, pbb_ = bank()
            for k in range(8):
                s.op("pe", lambda e, pbk=pbk, k=k, oc=oc: e.matmul(pbk[:, :], lhsT=w_outb[:, k, oc * 128:(oc + 1) * 128], rhs=oT[:, k, :],
                                                                   start=(k == 0), stop=(k == 7)),
                     reads=[b_wout, b_o], writes=[pbb_], inc=(k == 7))
            s.op("dve", lambda e, pbk=pbk, oc=oc: e.scalar_tensor_tensor(out=z[:, oc, :], in0=pbk[:, :], scalar=mod1p[:, 16 + oc:17 + oc],
                                                                         in1=xt[:, oc, :], op0=ALU.mult, op1=ALU.add),
                 reads=[pbb_, xb, b_mod1p], writes=[bz])
        ps1, ps1b = bank()
        ps2, ps2b = bank()
        emit_ln_tile(cx, cst, z, zb_t, zsq_t, ps1[:, :], ps1b, ps2[:, :], ps2b, tmp, N, gT, bT, b_par, z, bz, bzb, bzsq, btmp, bz)
        s.dma("sp", yv[:, :, t0:t0 + N], z[:], reads=[bz], anchor=bz)
    s.final_wait("sp", [bz])
    return cx.end_phase()


class Rot:
    def __init__(self, cx, name, n, shape, dt):
        self.t = [cx.sb("%s%d" % (name, i), shape, dt) for i in range(n)]
        self.b = [cx.buf(name) for i in range(n)]
        self.i = 0

    def get(self):
        j = self.i % len(self.t)
        self.i += 1
        return self.t[j], self.b[j]


def load_gate_params(cx, w_a, w_x, b_aT, b_xT, lamT):
    s = cx.s
    g = {}
    g["wa"] = cx.sb("g_wa", [96, 16, 96], BF16)
    g["wx"] = cx.sb("g_wx", [96, 16, 96], BF16)
    g["b_w"] = cx.buf("gw")
    s.dma("pool", g["wa"][:], w_a.rearrange("n i j -> i n j"), writes=[g["b_w"]], anchor=g["b_w"])
    s.dma("pool", g["wx"][:], w_x.rearrange("n i j -> i n j"), writes=[g["b_w"]], anchor=g["b_w"])
    g["ba"] = cx.sb("g_ba", [96, 16], F32)
    g["bx"] = cx.sb("g_bx", [96, 16], F32)
    g["lamc"] = cx.sb("g_lamc", [96, 16], F32)
    g["one1"] = cx.sb("g_one1", [96, 1], F32)
    g["b_p"] = cx.buf("gp")
    s.dma("sp", g["ba"][:], b_aT[:, :], writes=[g["b_p"]], anchor=g["b_p"])
    s.dma("sp", g["bx"][:], b_xT[:, :], writes=[g["b_p"]], anchor=g["b_p"])
    s.dma("sp", g["lamc"][:], lamT[:, :], writes=[g["b_p"]], anchor=g["b_p"])
    s.op("pool", lambda e: e.memset(g["one1"][:], 1.0), writes=[g["b_p"]])
    s.op("act", lambda e: e.activation(out=g["lamc"][:], in_=g["lamc"][:], func=AF.Exp, scale=-1.0), reads=[g["b_p"]], writes=[g["b_p"]])
    s.op("act", lambda e: e.activation(out=g["lamc"][:], in_=g["lamc"][:], func=AF.Ln, bias=g["one1"][:, 0:1]), reads=[g["b_p"]], writes=[g["b_p"]])
    s.op("pool", lambda e: e.tensor_scalar(out=g["lamc"][:], in0=g["lamc"][:], scalar1=-4.0, scalar2=None, op0=ALU.mult),
         reads=[g["b_p"]], writes=[g["b_p"]])
    g["lam2"] = cx.sb("g_lam2", [96, 16], F32)
    s.op("pool", lambda e: e.tensor_scalar(out=g["lam2"][:], in0=g["lamc"][:], scalar1=2.0, scalar2=None, op0=ALU.mult),
         reads=[g["b_p"]], writes=[g["b_p"]])
    s.op("pool", lambda e: e.tensor_scalar(out=g["ba"][:], in0=g["ba"][:], scalar1=0.5, scalar2=None, op0=ALU.mult),
         reads=[g["b_p"]], writes=[g["b_p"]])
    s.op("pool", lambda e: e.tensor_scalar(out=g["bx"][:], in0=g["bx"][:], scalar1=0.5, scalar2=None, op0=ALU.mult),
         reads=[g["b_p"]], writes=[g["b_p"]])
    return g


def emit_gates(cx, g, n, c_t, c_b, cb_t, cb_b, bank_fn, pools, N):
    s = cx.s
    pbk, pbb_ = bank_fn()
    s.op("pe", lambda e: e.matmul(pbk[0:96, 0:N], lhsT=g["wa"][:, n, :], rhs=cb_t[:], start=True, stop=True),
         reads=[g["b_w"], cb_b], writes=[pbb_])
    s.op("pe", lambda e: e.matmul(pbk[0:96, N:2 * N], lhsT=g["wx"][:, n, :], rhs=cb_t[:], start=True, stop=True),
         reads=[g["b_w"], cb_b], writes=[pbb_])
    r_t, r_b = pools["r"].get()
    i_t, i_b = pools["i"].get()
    a_t, a_b = pools["a"].get()
    m_t, m_b = pools["m"].get()
    u_t, u_b = pools["u"].get()
    s.op("act", lambda e: e.activation(out=r_t[:], in_=pbk[0:96, 0:N], func=AF.Sigmoid, bias=g["ba"][:, n:n + 1]),
         reads=[pbb_, g["b_p"]], writes=[r_b])
    s.op("act", lambda e: e.activation(out=i_t[:], in_=pbk[0:96, N:2 * N], func=AF.Sigmoid, bias=g["bx"][:, n:n + 1]),
         reads=[pbb_, g["b_p"]], writes=[i_b])
    s.op("act", lambda e: e.activation(out=a_t[:], in_=r_t[:], func=AF.Exp, scale=g["lamc"][:, n:n + 1]),
         reads=[r_b, g["b_p"]], writes=[a_b])
    s.op("dve", lambda e: e.tensor_tensor(out=m_t[:], in0=a_t[:], in1=a_t[:], op=ALU.mult), reads=[a_b], writes=[m_b])
    s.op("dve", lambda e: e.tensor_scalar(out=m_t[:], in0=m_t[:], scalar1=-1.0, scalar2=1.0, op0=ALU.mult, op1=ALU.add),
         reads=[m_b], writes=[m_b])
    s.op("act", lambda e: e.activation(out=m_t[:], in_=m_t[:], func=AF.Sqrt), reads=[m_b], writes=[m_b])
    s.op("dve", lambda e: e.tensor_tensor(out=u_t[:], in0=i_t[:], in1=c_t, op=ALU.mult), reads=[i_b, c_b], writes=[u_b])
    s.op("dve", lambda e: e.tensor_tensor(out=u_t[:], in0=u_t[:], in1=m_t[:], op=ALU.mult), reads=[u_b, m_b], writes=[u_b])
    return a_t, a_b, u_t, u_b


def gate_pools(cx, N):
    return {k: Rot(cx, "gp_" + k, 2, [96, N], F32) for k in ("r", "i", "a", "m", "u")}


def make_banks(cx, nb):
    banks = [cx.ps("bank%d" % i) for i in range(nb)]
    bank_b = [cx.buf("bank") for i in range(nb)]
    ctr = [0]

    def bank():
        i = ctr[0] % nb
        ctr[0] += 1
        return banks[i], bank_b[i]
    return bank


def run_pipeline(items, stages):
    S = len(stages)
    for step in range(len(items) + S - 1):
        for k in range(S - 1, -1, -1):
            i = step - k
            if 0 <= i < len(items):
                stages[k](items[i])


class RotBanks:
    def __init__(self, cx, name, n):
        self.t = [cx.ps("%s%d" % (name, i)) for i in range(n)]
        self.b = [cx.buf(name) for i in range(n)]
        self.i = 0

    def get(self):
        j = self.i % len(self.t)
        self.i += 1
        return self.t[j], self.b[j]


SQB = 4


def gate_stage_fns(cx, g, N, RI, P, items_ref):
    s = cx.s

    def st_gmm(it):
        n = it["n"]
        cb_t, cb_b = P["cb"].get()
        c_t = it["c_t"]
        s.op("pool", lambda e: e.tensor_copy(out=cb_t[:], in_=c_t[:]), reads=[it["c_b"]], writes=[cb_b])
        pbk, pbb_ = RI.get()
        it["ri"], it["ri_b"] = pbk, pbb_
        s.op("pe", lambda e: e.matmul(pbk[0:96, 0:N], lhsT=g["wa"][:, n, :], rhs=cb_t[:], start=True, stop=True),
             reads=[g["b_w"], cb_b], writes=[pbb_])
        s.op("pe", lambda e: e.matmul(pbk[0:96, N:2 * N], lhsT=g["wx"][:, n, :], rhs=cb_t[:], start=True, stop=True),
             reads=[g["b_w"], cb_b], writes=[pbb_])

    def st_sig(it):
        n = it["n"]
        pbk, pbb_ = it["ri"], it["ri_b"]
        r_t, r_b = P["r"].get()
        i_t, i_b = P["i"].get()
        a_t, a_b = P["a"].get()
        m_t, m_b = P["m"].get()
        it.update(i_t=i_t, i_b=i_b, a_t=a_t, a_b=a_b, m_t=m_t, m_b=m_b)
        s.op("act", lambda e: e.activation(out=r_t[:], in_=pbk[0:96, 0:N], func=AF.Tanh, bias=g["ba"][:, n:n + 1], scale=0.5),
             reads=[pbb_, g["b_p"]], writes=[r_b])
        s.op("act", lambda e: e.activation(out=i_t[:], in_=pbk[0:96, N:2 * N], func=AF.Tanh, bias=g["bx"][:, n:n + 1], scale=0.5),
             reads=[pbb_, g["b_p"]], writes=[i_b])
        s.op("act", lambda e: e.activation(out=a_t[:], in_=r_t[:], func=AF.Exp, bias=g["lamc"][:, n:n + 1], scale=g["lamc"][:, n:n + 1]),
             reads=[r_b, g["b_p"]], writes=[a_b])
        s.op("act", lambda e: e.activation(out=m_t[:], in_=r_t[:], func=AF.Exp, bias=g["lam2"][:, n:n + 1], scale=g["lam2"][:, n:n + 1]),
             reads=[r_b, g["b_p"]], writes=[m_b])

    def st_m(it):
        i_t, i_b = it["i_t"], it["i_b"]
        m_t, m_b = it["m_t"], it["m_b"]
        u_t, u_b = P["u"].get()
        it.update(u_t=u_t, u_b=u_b)
        c_t = it["c_t"]
        s.op("pool", lambda e: e.tensor_scalar(out=m_t[:], in0=m_t[:], scalar1=-1.0, scalar2=1.0, op0=ALU.mult, op1=ALU.add),
             reads=[m_b], writes=[m_b])
        s.op("dve", lambda e: e.scalar_tensor_tensor(out=u_t[:], in0=i_t[:], scalar=1.0, in1=c_t[:], op0=ALU.add, op1=ALU.mult),
             reads=[i_b, it["c_b"]], writes=[u_b])

    def st_sqrt(it):
        if it["idx"] % SQB != SQB - 1:
            return
        for j in range(it["idx"] - SQB + 1, it["idx"] + 1):
            m_t, m_b = items_ref[j]["m_t"], items_ref[j]["m_b"]
            s.op("act", lambda e, m_t=m_t: e.activation(out=m_t[:], in_=m_t[:], func=AF.Sqrt), reads=[m_b], writes=[m_b])

    return st_gmm, st_sig, st_m, st_sqrt


def build_rnn1(cx=None, prefix="", over=None):
    cx = cx or Ctx()
    cx.begin_phase(prefix, over)
    s = cx.s
    N = 256
    NT = T // N
    NX = N + 2
    xT = cx.dram_in("xT", [D, T + 2])
    c_col = cx.dram_in("c_col", [128, 8])
    ada_w = cx.dram_in("ada_w", [D, 3 * D])
    ada_bT = cx.dram_in("ada_bT", [128, 24])
    w_in = cx.dram_in("w_in", [D, 2 * DRNN])
    convT = cx.dram_in("convT", [96, 16 * 6])
    w_a = cx.dram_in("w_a", [16, 96, 96])
    w_x = cx.dram_in("w_x", [16, 96, 96])
    b_aT = cx.dram_in("b_aT", [96, 16])
    b_xT = cx.dram_in("b_xT", [96, 16])
    lamT = cx.dram_in("lamT", [96, 16])
    carry_only = bool(cx.over.get("carry_only"))
    if not carry_only:
        cT = cx.dram_out("cT", [DRNN, T])
        GT = cx.dram_out("GT", [DRNN, T])
        h1T = cx.dram_out("h1T", [DRNN, T])
        cv = cT.rearrange("(n p) t -> p n t", p=96)
        Gv = GT.rearrange("(n p) t -> p n t", p=96)
        hv = h1T.rearrange("(n p) t -> p n t", p=96)

    XB = RotBanks(cx, "pX", 3)
    GB = RotBanks(cx, "pG", 2)
    RI = RotBanks(cx, "pRI", 2)
    misc = cx.ps("pmisc")
    misc_b = cx.buf("pmisc")
    zt = cx.sb("zt", [128, 8, N], F32)
    bzt = cx.buf("zt")
    pieces = [zt[:, 0:4, :].rearrange("p a (b n) -> p (a b) n", n=128), zt[:, 4:8, :].rearrange("p a (b n) -> p (a b) n", n=128)]
    ada = emit_adaln(cx, c_col, ada_w, ada_bT, misc, misc_b, pieces, [cx.buf("pc0"), cx.buf("pc1")], [bzt])
    mod, mod1p, b_mod, b_mod1p = ada["mod"], ada["mod1p"], ada["b_mod"], ada["b_mod1p"]

    w_inb = cx.sb("w_inb", [128, 8, 2 * DRNN], BF16)
    b_win = cx.buf("win")
    winv = w_in.rearrange("(k p) n -> p k n", p=128)
    for k in range(8):
        s.dma("pool", w_inb[:, k, :], winv[:, k, :], writes=[b_win], anchor=b_win)
    g = load_gate_params(cx, w_a, w_x, b_aT, b_xT, lamT)
    cw = cx.sb("cw", [96, 16 * 6], F32)
    b_cw = cx.buf("cw")
    s.dma("sp", cw[:], convT[:, :], writes=[b_cw], anchor=b_cw)

    xv = xT.rearrange("(k p) t -> p k t", p=128)
    xt = Rot(cx, "xt", 2, [128, 8, NX], F32)
    htp = Rot(cx, "ht", 2, [128, 8, NX], BF16)
    tail = cx.sb("tail", [96, 16, 2], F32)
    b_tail = cx.buf("tail")
    s.op("pool", lambda e: e.memset(tail[:], 0.0), writes=[b_tail])
    carry = cx.sb("carry", [96, 16], F32)
    b_carry = cx.buf("carry")
    s.op("pool", lambda e: e.memset(carry[:], 0.0), writes=[b_carry])
    P = {"xr": Rot(cx, "xrb", 3, [96, NX + 2], F32), "G": Rot(cx, "G", 2, [96, N], F32),
         "c": Rot(cx, "c", 5, [96, N], F32), "ct": Rot(cx, "ct", 2, [96, N], F32), "cb": Rot(cx, "cb", 2, [96, N], BF16),
         "r": Rot(cx, "r", 2, [96, N], F32), "i": Rot(cx, "i", 3, [96, N], F32), "a": Rot(cx, "a", 11, [96, N], F32),
         "m": Rot(cx, "m", 10, [96, N], F32), "u": Rot(cx, "u", 10, [96, N], F32), "h": Rot(cx, "h", 2, [96, N], F32)}
    halo = cx.over.get("halo_sb")
    state = {"nxt": None}

    def load_x(it):
        t_, b_ = xt.get()
        if halo is not None and it == NT - 1:
            s.dma("sp", t_[:, :, 0:N], xv[:, :, it * N:it * N + N], writes=[b_], anchor=b_)
            s.op("pool", lambda e, t_=t_: e.tensor_copy(out=t_[:, :, N:NX], in_=halo[:]), writes=[b_])
        else:
            s.dma("sp", t_[:], xv[:, :, it * N:it * N + NX], writes=[b_], anchor=b_)
        return t_, b_

    state["nxt"] = load_x(0)

    def st_proj(itm):
        it, n = itm["it"], itm["n"]
        if n == 0:
            x_t, xb = state["nxt"]
            if it + 1 < NT:
                state["nxt"] = load_x(it + 1)
            h_t, h_b = htp.get()
            state["ht"] = (h_t, h_b)
            for k in range(8):
                s.op("act", lambda e, k=k: e.activation(out=h_t[:, k, :], in_=x_t[:, k, :], func=AF.Identity,
                                                        bias=mod[:, k:k + 1], scale=mod1p[:, 8 + k:9 + k]),
                     reads=[xb, b_mod, b_mod1p], writes=[h_b])
            if not carry_only:
                for n2 in range(16):
                    pg, pgb = GB.get()
                    for k in range(8):
                        s.op("pe", lambda e, k=k, pg=pg, n2=n2: e.matmul(pg[0:96, 0:N], lhsT=w_inb[:, k, DRNN + n2 * 96:DRNN + (n2 + 1) * 96],
                                                                       rhs=h_t[:, k, 0:N], start=(k == 0), stop=(k == 7)),
                             reads=[b_win, h_b], writes=[pgb], inc=(k == 7))
                    G_t, G_b = P["G"].get()
                    s.op("act", lambda e, pg=pg, G_t=G_t: e.activation(out=G_t[:], in_=pg[0:96, 0:N], func=AF.Gelu), reads=[pgb], writes=[G_b])
                    s.dma("sp", Gv[:, n2, it * N:it * N + N], G_t[:], reads=[G_b], anchor=G_b)
        h_t, h_b = state["ht"]
        pbk, pbb_ = XB.get()
        itm["X"], itm["X_b"] = pbk, pbb_
        for k in range(8):
            s.op("pe", lambda e, k=k: e.matmul(pbk[0:96, 0:NX], lhsT=w_inb[:, k, n * 96:(n + 1) * 96], rhs=h_t[:, k, :],
                                               start=(k == 0), stop=(k == 7)),
                 reads=[b_win, h_b], writes=[pbb_], inc=(k == 7))

    def st_evac(itm):
        it, n = itm["it"], itm["n"]
        t0 = it * N
        xr_t, xr_b = P["xr"].get()
        itm["xr_t"], itm["xr_b"] = xr_t, xr_b
        pbk = itm["X"]
        s.op("act", lambda e: e.activation(out=xr_t[:, 2:NX + 2], in_=pbk[0:96, 0:NX], func=AF.Identity),
             reads=[itm["X_b"]], writes=[xr_b])

    def st_conv(itm):
        it, n = itm["it"], itm["n"]
        t0 = it * N
        xr_t, xr_b = itm["xr_t"], itm["xr_b"]
        s.op("pool", lambda e: e.tensor_copy(out=xr_t[:, 0:2], in_=tail[:, n, :]), reads=[b_tail], writes=[xr_b])
        s.op("pool", lambda e: e.tensor_copy(out=tail[:, n, :], in_=xr_t[:, N:N + 2]), reads=[xr_b], writes=[b_tail])
        c_t, c_b = P["c"].get()
        itm["c_t"], itm["c_b"] = c_t, c_b
        s.op("pool", lambda e: e.tensor_scalar(out=c_t[:], in0=xr_t[:, 0:N], scalar1=cw[:, n * 6:n * 6 + 1],
                                               scalar2=cw[:, n * 6 + 5:n * 6 + 6], op0=ALU.mult, op1=ALU.add),
             reads=[xr_b, b_cw], writes=[c_b])
        for j in range(1, 5):
            s.op("dve", lambda e, j=j: e.scalar_tensor_tensor(out=c_t[:], in0=xr_t[:, j:j + N], scalar=cw[:, n * 6 + j:n * 6 + j + 1],
                                                              in1=c_t[:], op0=ALU.mult, op1=ALU.add),
                 reads=[xr_b, b_cw, c_b], writes=[c_b])
        if not carry_only:
            s.dma("sp", cv[:, n, t0:t0 + N], c_t[:], reads=[c_b], anchor=c_b)

    items = [{"it": it, "n": n, "idx": it * 16 + n} for it in range(NT) for n in range(16)]
    st_gmm, st_sig, st_m, st_sqrt = gate_stage_fns(cx, g, N, RI, P, items)

    def st_scan(itm):
        it, n = itm["it"], itm["n"]
        t0 = it * N
        u_t, u_b, m_t, m_b, a_t, a_b = itm["u_t"], itm["u_b"], itm["m_t"], itm["m_b"], itm["a_t"], itm["a_b"]
        s.op("dve", lambda e: e.scalar_tensor_tensor(out=u_t[:], in0=u_t[:], scalar=0.5, in1=m_t[:], op0=ALU.mult, op1=ALU.mult),
             reads=[u_b, m_b], writes=[u_b])
        h_t, h_b = P["h"].get()
        s.op("dve", lambda e: e.tensor_tensor_scan(out=h_t[:], data0=a_t[:], data1=u_t[:], initial=carry[:, n:n + 1],
                                                   op0=ALU.mult, op1=ALU.add),
             reads=[a_b, u_b, b_carry], writes=[h_b])
        s.op("pool", lambda e: e.tensor_copy(out=carry[:, n:n + 1], in_=h_t[:, N - 1:N]), reads=[h_b], writes=[b_carry])
        if not carry_only:
            s.dma("sp", hv[:, n, t0:t0 + N], h_t[:], reads=[h_b], anchor=h_b)

    nop = lambda itm: None
    run_pipeline(items, [st_proj, st_evac, st_conv, st_gmm, st_sig, st_m, nop, nop, nop, st_sqrt, nop, nop, nop, st_scan])
    if cx.over.get("carry_out") is not None:
        s.dma("sp", cx.over["carry_out"], carry[:], reads=[b_carry], anchor=b_carry)
    s.final_wait("sp", P["c"].b + P["G"].b + P["h"].b + [b_carry])
    return cx.end_phase()


def build_rnn2(cx=None, prefix="", over=None):
    cx = cx or Ctx()
    cx.begin_phase(prefix, over)
    s = cx.s
    N = 256
    NT = T // N
    xT = cx.dram_in("xT", [D, T])
    cT = cx.dram_in("cT", [DRNN, T])
    GT = cx.dram_in("GT", [DRNN, T])
    h1T = cx.dram_in("h1T", [DRNN, T])
    carry_in = None if (over and over.get("carry_sb") is not None) else cx.dram_in("carry_in", [96, 16])
    c_col = cx.dram_in("c_col", [128, 8])
    ada_w = cx.dram_in("ada_w", [D, 3 * D])
    ada_bT = cx.dram_in("ada_bT", [128, 24])
    ln_gT = cx.dram_in("ln_gT", [128, 8])
    ln_bT = cx.dram_in("ln_bT", [128, 8])
    w_a = cx.dram_in("w_a", [16, 96, 96])
    w_x = cx.dram_in("w_x", [16, 96, 96])
    b_aT = cx.dram_in("b_aT", [96, 16])
    b_xT = cx.dram_in("b_xT", [96, 16])
    lamT = cx.dram_in("lamT", [96, 16])
    w_out = cx.dram_in("w_out", [DRNN, D])
    yT = cx.dram_out("yT", [D, T])

    cst = emit_consts(cx)
    RI = RotBanks(cx, "pRI", 3)
    WB = RotBanks(cx, "pW", 4)
    stp = cx.ps("pst")
    st_b = cx.buf("pst")
    z = cx.sb("z", [128, 8, N], F32)
    bz = cx.buf("z")
    pieces = [z[:, 0:4, :].rearrange("p a (b n) -> p (a b) n", n=128), z[:, 4:8, :].rearrange("p a (b n) -> p (a b) n", n=128)]
    ada = emit_adaln(cx, c_col, ada_w, ada_bT, stp, st_b, pieces, [cx.buf("pc0"), cx.buf("pc1")], [bz])
    mod1p, b_mod1p = ada["mod1p"], ada["b_mod1p"]
    gT = cx.sb("gT", [128, 8], F32)
    bT = cx.sb("bT", [128, 8], F32)
    b_par = cx.buf("par")
    s.dma("sp", gT[:], ln_gT[:, :], writes=[b_par], anchor=b_par)
    s.dma("sp", bT[:], ln_bT[:, :], writes=[b_par], anchor=b_par)
    g = load_gate_params(cx, w_a, w_x, b_aT, b_xT, lamT)
    woutb = cx.sb("woutb", [96, 16, D], BF16)
    b_wout = cx.buf("wout")
    s_w = w_out.rearrange("(n p) d -> p n d", p=96)
    for n in range(16):
        s.dma("pool", woutb[:, n, :], s_w[:, n, :], writes=[b_wout], anchor=b_wout)
    carry = cx.sb("carry", [96, 16], F32)
    b_carry = cx.buf("carry")
    if cx.over.get("carry_sb") is not None:
        s.op("pool", lambda e: e.tensor_copy(out=carry[:], in_=cx.over["carry_sb"][:]), writes=[b_carry])
    else:
        s.dma("sp", carry[:], carry_in[:, :], writes=[b_carry], anchor=b_carry)

    xv = xT.rearrange("(k p) t -> p k t", p=128)
    yv = yT.rearrange("(k p) t -> p k t", p=128)
    cv = cT.rearrange("(n p) t -> p n t", p=96)
    Gv = GT.rearrange("(n p) t -> p n t", p=96)
    hv = h1T.rearrange("(n p) t -> p n t", p=96)
    xt = Rot(cx, "xt", 2, [128, 8, N], F32)
    P = {"c": Rot(cx, "c", 5, [96, N], F32), "G": Rot(cx, "G", 4, [96, N], F32), "h1": Rot(cx, "h1", 4, [96, N], F32),
         "cb": Rot(cx, "cb", 2, [96, N], BF16), "r": Rot(cx, "r", 2, [96, N], F32), "i": Rot(cx, "i", 3, [96, N], F32),
         "a": Rot(cx, "a", 11, [96, N], F32), "m": Rot(cx, "m", 10, [96, N], F32), "u": Rot(cx, "u", 10, [96, N], F32),
         "h2": Rot(cx, "h2", 2, [96, N], F32)}
    ytp = Rot(cx, "yt", 2, [96, 16, N], BF16)
    zb_t = cx.sb("zb", [128, 8, N], BF16)
    bzb = cx.buf("zb")
    zsq_t = cx.sb("zsq", [128, 8, N], BF16)
    bzsq = cx.buf("zsq")
    tmp = cx.sb("tmp", [128, 4, N], F32)
    btmp = cx.buf("tmp")
    state = {}

    def st_load(itm):
        it, n = itm["it"], itm["n"]
        t0 = it * N
        if n == 0:
            x_t, xb = xt.get()
            s.dma("sp", x_t[:], xv[:, :, t0:t0 + N], writes=[xb], anchor=xb)
            s.op("pool", lambda e: e.tensor_scalar(out=x_t[:], in0=x_t[:], scalar1=float(DN_ALPHA), scalar2=None, op0=ALU.mult),
                 reads=[xb], writes=[xb])
            state[("x", it)] = (x_t, xb)
            state[("y", it)] = ytp.get()
        c_t, c_b = P["c"].get()
        itm.update(c_t=c_t, c_b=c_b)
        s.dma("sp", c_t[:], cv[:, n, t0:t0 + N], writes=[c_b], anchor=c_b)

    def st_load2(itm):
        it, n = itm["it"], itm["n"]
        t0 = it * N
        G_t, G_b = P["G"].get()
        h1_t, h1_b = P["h1"].get()
        itm.update(G_t=G_t, G_b=G_b, h1_t=h1_t, h1_b=h1_b)
        s.dma("sp", G_t[:], Gv[:, n, t0:t0 + N], writes=[G_b], anchor=G_b)
        s.dma("sp", h1_t[:], hv[:, n, t0:t0 + N], writes=[h1_b], anchor=h1_b)

    items = [{"it": it, "n": n} for it in range(NT - 1, -1, -1) for n in range(16)]
    for j_, itm_ in enumerate(items):
        itm_["idx"] = j_
    st_gmm, st_sig, st_m, st_sqrt = gate_stage_fns(cx, g, N, RI, P, items)

    def st_scan(itm):
        it, n = itm["it"], itm["n"]
        t0 = it * N
        u_t, u_b, m_t, m_b, a_t, a_b = itm["u_t"], itm["u_b"], itm["m_t"], itm["m_b"], itm["a_t"], itm["a_b"]
        h1_t, h1_b, G_t, G_b = itm["h1_t"], itm["h1_b"], itm["G_t"], itm["G_b"]
        y_t, y_b = state[("y", it)]
        s.op("dve", lambda e: e.scalar_tensor_tensor(out=u_t[:], in0=u_t[:], scalar=0.5, in1=m_t[:], op0=ALU.mult, op1=ALU.mult),
             reads=[u_b, m_b], writes=[u_b])
        h2_t, h2_b = P["h2"].get()
        s.op("dve", lambda e: e.tensor_tensor_scan(out=h2_t[:, ::-1], data0=a_t[:, ::-1], data1=u_t[:, ::-1], initial=carry[:, n:n + 1],
                                                   op0=ALU.mult, op1=ALU.add),
             reads=[a_b, u_b, b_carry], writes=[h2_b])
        s.op("pool", lambda e: e.tensor_copy(out=carry[:, n:n + 1], in_=h2_t[:, 0:1]), reads=[h2_b], writes=[b_carry])
        s.op("dve", lambda e: e.tensor_tensor(out=h2_t[:], in0=h2_t[:], in1=h1_t[:], op=ALU.add), reads=[h2_b, h1_b], writes=[h2_b])
        s.op("dve", lambda e: e.tensor_tensor(out=y_t[:, n, :], in0=h2_t[:], in1=G_t[:], op=ALU.mult), reads=[h2_b, G_b], writes=[y_b])
        if n == 15:
            x_t, xb = state[("x", it)]
            for oc in range(8):
                pbk, pbb_ = WB.get()
                for nn in range(16):
                    s.op("pe", lambda e, pbk=pbk, nn=nn, oc=oc: e.matmul(pbk[:, 0:N], lhsT=woutb[:, nn, oc * 128:(oc + 1) * 128], rhs=y_t[:, nn, :],
                                                                         start=(nn == 0), stop=(nn == 15)),
                         reads=[b_wout, y_b], writes=[pbb_], inc=(nn == 15))
                s.op("dve", lambda e, pbk=pbk, oc=oc: e.scalar_tensor_tensor(out=z[:, oc, :], in0=pbk[:, 0:N], scalar=mod1p[:, 16 + oc:17 + oc],
                                                                             in1=x_t[:, oc, :], op0=ALU.mult, op1=ALU.add),
                     reads=[pbb_, xb, b_mod1p], writes=[bz])
            emit_ln_tile(cx, cst, z, zb_t, zsq_t, stp[:, 0:N], st_b, stp[:, N:2 * N], st_b, tmp, N, gT, bT, b_par, z, bz, bzb, bzsq, btmp, bz)
            s.dma("sp", yv[:, :, t0:t0 + N], z[:], reads=[bz], anchor=bz)

    nop = lambda itm: None
    run_pipeline(items, [st_load, st_gmm, st_sig, st_m, nop, nop, nop, st_sqrt, nop, st_load2, nop, st_scan])
    s.final_wait("sp", [bz])
    return cx.end_phase()


def build_xch(cx, prefix, bounce_in, bounce_out, P, F, result_sb, swap2):
    cx.begin_phase(prefix, None)
    s = cx.s
    sel_d = cx.dram_in("sel", [128, 8])
    sel = cx.sb("sel_sb", [128, 8], F32)
    b_sel = cx.buf("sel")
    s.dma("sp", sel[:], sel_d[:, :], writes=[b_sel], anchor=b_sel)
    b_in, b_out = cx.buf("bin"), cx.buf("bout")
    s.collective(bounce_in.ap(), bounce_out.ap(), reads=[b_in], writes=[b_out], anchor=b_out)
    g = cx.sb("xg", [P, 8, F], F32)
    b_g = cx.buf("xg")
    s.dma("sp", g[:], bounce_out.ap().rearrange("(r p) f -> p r f", p=P), reads=[b_out], writes=[b_g], anchor=b_g)
    acc = cx.sb("xacc", [P, F], F32)
    b_acc = cx.buf("xacc")
    s.op("dve", lambda e: e.tensor_scalar(out=acc[:], in0=g[:, 0, :], scalar1=sel[0:P, 0:1], scalar2=None, op0=ALU.mult),
         reads=[b_g, b_sel], writes=[b_acc])
    for r in range(1, 8):
        s.op("dve", lambda e, r=r: e.scalar_tensor_tensor(out=acc[:], in0=g[:, r, :], scalar=sel[0:P, r:r + 1], in1=acc[:],
                                                          op0=ALU.mult, op1=ALU.add),
             reads=[b_g, b_sel, b_acc], writes=[b_acc])
    b_res = cx.buf("res")
    if swap2:
        av = acc[:].rearrange("p (k t) -> p k t", t=2)
        s.op("dve", lambda e: e.tensor_copy(out=result_sb[:, :, 0:1], in_=av[:, :, 1:2]), reads=[b_acc], writes=[b_res])
        s.op("dve", lambda e: e.tensor_copy(out=result_sb[:, :, 1:2], in_=av[:, :, 0:1]), reads=[b_acc], writes=[b_res])
    else:
        s.op("dve", lambda e: e.tensor_copy(out=result_sb[:], in_=acc[:]), reads=[b_acc], writes=[b_res])
    return cx.end_phase()


def build_lxch(cx, prefix, bounce, P, F, result_sb, swap2):
    cx.begin_phase(prefix, None)
    s = cx.s
    acc = cx.sb("xacc", [P, F], F32)
    b_acc = cx.buf("xacc")
    s.dma("sp", acc[:], bounce.ap(), writes=[b_acc], anchor=b_acc)
    b_res = cx.buf("res")
    if swap2:
        av = acc[:].rearrange("p (k t) -> p k t", t=2)
        s.op("dve", lambda e: e.tensor_copy(out=result_sb[:, :, 0:1], in_=av[:, :, 1:2]), reads=[b_acc], writes=[b_res])
        s.op("dve", lambda e: e.tensor_copy(out=result_sb[:, :, 1:2], in_=av[:, :, 0:1]), reads=[b_acc], writes=[b_res])
    else:
        s.op("dve", lambda e: e.tensor_copy(out=result_sb[:], in_=acc[:]), reads=[b_acc], writes=[b_res])
    return cx.end_phase()


def build_fused2():
    cx = Ctx(fused=True)
    tmp = lambda n, sh: cx.dram_tmp(n, sh).ap()
    x0s, x1s, x0o, x1o = tmp("x0s", [D, T]), tmp("x1s", [D, T]), tmp("x0o", [D, T]), tmp("x1o", [D, T])
    cT, GT, h1T, x2 = tmp("cT_s", [DRNN, T]), tmp("GT_s", [DRNN, T]), tmp("h1T_s", [DRNN, T]), tmp("x2", [D, T])
    bh_s, bh_o = cx.dram_tmp("bh_s", [128, 16]), cx.dram_tmp("bh_o", [128, 16])
    bc_o = cx.dram_tmp("bc_o", [96, 16])
    bc_s = cx.dram_tmp("bc_s", [96, 16])
    halo_s = cx.gsb("halo_s", [128, 8, 2], F32)
    halo_o = cx.gsb("halo_o", [128, 8, 2], F32)
    carry_sb = cx.gsb("carry_sb", [96, 16], F32)
    cx.ada_tiles = {k: (cx.gsb("ada_mod_" + k, [128, 24], F32), cx.gsb("ada_mod1p_" + k, [128, 24], F32)) for k in ("00", "01", "10", "11")}
    out = cx.nc.dram_tensor("out", [D, T], F32, kind="ExternalOutput").ap()
    D_ = cx.decl

    def share(src, dst_names):
        return {n: D_[src + n] for n in dst_names}

    build_attn(cx, "a_", {"yT": x0s, "ada_key": "00"})
    ov = share("a_", ["c_col", "ada_w", "ada_bT", "ln_gT", "ln_bT", "w_in", "w_out", "perm", "mprev", "mnext", "sink_rep"])
    ov.update({"yT": x0o, "ada_key": "00"})
    build_attn(cx, "b_", ov)
    build_mlp(cx, "m0_", {"xT": x0s, "yT": x1s, "ada_key": "01",
                           "jobs": [(x0s, x1s, bh_s.ap()), (x0o, x1o, bh_o.ap())]})
    build_lxch(cx, "l1_", bh_o, 128, 16, halo_s, True)
    build_lxch(cx, "l2_", bh_s, 128, 16, halo_o, True)
    build_rnn1(cx, "r1_", {"xT": x1s, "halo_sb": halo_s, "cT": cT, "GT": GT, "h1T": h1T, "carry_out": bc_s.ap(), "ada_key": "10"})
    ov = share("r1_", ["c_col", "ada_w", "ada_bT", "w_in"])
    ov.update({"xT": x1o, "halo_sb": halo_o, "carry_only": True, "carry_out": bc_o.ap(), "ada_key": "10"})
    build_rnn1(cx, "q1_", ov)
    build_lxch(cx, "l3_", bc_o, 96, 16, carry_sb, False)
    ov = share("r1_", ["c_col", "ada_w", "ada_bT"])
    ov.update({"xT": x1s, "cT": cT, "GT": GT, "h1T": h1T, "carry_sb": carry_sb, "yT": x2, "ada_key": "10"})
    build_rnn2(cx, "r2_", ov)
    build_mlp(cx, "m1_", {"xT": x2, "yT": out, "ada_key": "11"})
    cx.gstack.close()
    return cx.nc


def build_fused():
    cx = Ctx(fused=True)
    x0a = cx.dram_tmp("x0a", [D, T]).ap()
    x1 = cx.dram_tmp("x1", [D, T]).ap()
    cT = cx.dram_tmp("cT_s", [DRNN, T]).ap()
    GT = cx.dram_tmp("GT_s", [DRNN, T]).ap()
    h1T = cx.dram_tmp("h1T_s", [DRNN, T]).ap()
    x2 = cx.dram_tmp("x2", [D, T]).ap()
    b1_in = cx.dram_tmp("b1_in", [128, 16])
    b1_out = cx.dram_tmp("b1_out", [8 * 128, 16])
    b2_in = cx.dram_tmp("b2_in", [96, 16])
    b2_out = cx.dram_tmp("b2_out", [8 * 96, 16])
    halo_sb = cx.gsb("halo_sb", [128, 8, 2], F32)
    carry_sb = cx.gsb("carry_sb", [96, 16], F32)
    out = cx.nc.dram_tensor("out", [D, T], F32, kind="ExternalOutput").ap()
    build_attn(cx, "a_", {"yT": x0a})
    build_mlp(cx, "m0_", {"xT": x0a, "yT": x1, "tail_out": b1_in.ap()})
    build_xch(cx, "x1_", b1_in, b1_out, 128, 16, halo_sb, True)
    build_rnn1(cx, "r1_", {"xT": x1, "halo_sb": halo_sb, "cT": cT, "GT": GT, "h1T": h1T, "carry_out": b2_in.ap()})
    build_xch(cx, "x2_", b2_in, b2_out, 96, 16, carry_sb, False)
    build_rnn2(cx, "r2_", {"xT": x1, "cT": cT, "GT": GT, "h1T": h1T, "carry_sb": carry_sb, "yT": x2})
    build_mlp(cx, "m1_", {"xT": x2, "yT": out})
    cx.gstack.close()
    return cx.nc


def colT(v, n):
    return np.ascontiguousarray(np.asarray(v, np.float32).reshape(n, 128).T)


_PROGS = {}


def get_prog(name):
    if name not in _PROGS:
        _PROGS[name] = {"mlp": build_mlp, "attn": build_attn, "rnn1": build_rnn1, "rnn2": build_rnn2, "fused": build_fused, "fused2": build_fused2}[name]()
    return _PROGS[name]


def run_mlp(xT_list, c, ada_w, ada_b, ln_g, ln_b, w1, w2):
    nc = get_prog("mlp")
    in_maps = []
    for core in range(NCORES):
        b = core // 2
        in_maps.append({
            "xT": xT_list[core], "c_col": colT(c[b], 8), "ada_w": np.ascontiguousarray(ada_w),
            "ada_bT": colT(ada_b, 24), "ln_gT": colT(ln_g, 8), "ln_bT": colT(ln_b, 8),
            "w1": np.ascontiguousarray(w1), "w2": np.ascontiguousarray(w2),
        })
    res = run_bass_kernel_spmd(nc, in_maps, core_ids=list(range(NCORES)))
    return [r["yT"] for r in res.results]


ROT = 32
ROPE_THETA = 500000.0


def rope_tables(pos):
    inv_freq = (np.float32(ROPE_THETA) ** (-np.arange(0, ROT, 2, dtype=np.float32) / np.float32(ROT))).astype(np.float32)
    ang = (pos.astype(np.float32)[None, :] * inv_freq[:, None]).astype(np.float32)
    c = np.cos(ang).astype(np.float32)
    sn = np.sin(ang).astype(np.float32)
    return np.ascontiguousarray(np.concatenate([c, c], 0)), np.ascontiguousarray(np.concatenate([-sn, sn], 0))


def local_positions(core):
    half = core % 2
    if half == 0:
        return np.arange(0, T + 128)
    return np.arange(2 * T - 1, T - 129, -1)


def attn_consts():
    perm = np.zeros((32, 32), np.float32)
    for i in range(32):
        perm[(i + 16) % 32, i] = 1.0
    j = np.arange(128)[:, None]
    q = np.arange(128)[None, :]
    mprev = np.tile((j >= q).astype(np.float32), (1, 4))
    mnext = np.tile((j <= q).astype(np.float32), (1, 4))
    return perm, np.ascontiguousarray(mprev), np.ascontiguousarray(mnext)


def run_attn(xTh_list, c, ada_w, ada_b, ln_g, ln_b, w_in, w_out, sinks):
    nc = get_prog("attn")
    perm, mprev, mnext = attn_consts()
    in_maps = []
    for core in range(NCORES):
        b = core // 2
        cosT, sinT = rope_tables(local_positions(core))
        in_maps.append({
            "xT": xTh_list[core], "c_col": colT(c[b], 8), "ada_w": np.ascontiguousarray(ada_w),
            "ada_bT": colT(ada_b, 24), "ln_gT": colT(ln_g, 8), "ln_bT": colT(ln_b, 8),
            "w_in": np.ascontiguousarray(w_in), "w_out": np.ascontiguousarray(w_out),
            "cosT": cosT, "sinT": sinT, "perm": perm, "mprev": mprev, "mnext": mnext,
            "sink_rep": np.ascontiguousarray(np.tile(np.asarray(sinks, np.float32)[None, :], (128, 1))),
        })
    res = run_bass_kernel_spmd(nc, in_maps, core_ids=list(range(NCORES)))
    return [r["yT"] for r in res.results]


def shard_x(x):
    out = []
    for core in range(NCORES):
        b = core // 2
        idx = local_positions(core)
        out.append(np.ascontiguousarray(x[b][idx, :].T))
    return out


def colT96(v):
    return np.ascontiguousarray(np.asarray(v, np.float32).reshape(16, 96).T)


def conv_table(conv_w, conv_b, half):
    taps = np.zeros((5, DRNN), np.float32)
    for j in range(4):
        if half == 0:
            taps[j] = conv_w[j]
        else:
            taps[4 - j] = conv_w[j]
    tab = np.zeros((96, 16, 6), np.float32)
    for j in range(5):
        tab[:, :, j] = taps[j].reshape(16, 96).T
    tab[:, :, 5] = np.asarray(conv_b, np.float32).reshape(16, 96).T
    return np.ascontiguousarray(tab.reshape(96, 96))


def _common(c, ada_w, ada_b, core):
    b = core // 2
    return {"c_col": colT(c[b], 8), "ada_w": np.ascontiguousarray(ada_w), "ada_bT": colT(ada_b, 24)}


def run_rnn1(x1_list, c, ada_w, ada_b, w_in, conv_w, conv_b, w_a, b_a, w_x, b_x, lam):
    nc = get_prog("rnn1")
    in_maps = []
    for core in range(NCORES):
        half = core % 2
        par = x1_list[core ^ 1]
        xh = np.ascontiguousarray(np.concatenate([x1_list[core], par[:, T - 1:T], par[:, T - 2:T - 1]], axis=1))
        d1 = half
        m = _common(c, ada_w, ada_b, core)
        m.update({"xT": xh, "w_in": np.ascontiguousarray(w_in), "convT": conv_table(conv_w, conv_b, half),
                  "w_a": np.ascontiguousarray(w_a[d1]), "w_x": np.ascontiguousarray(w_x[d1]),
                  "b_aT": colT96(b_a[d1]), "b_xT": colT96(b_x[d1]), "lamT": colT96(lam[d1])})
        in_maps.append(m)
    res = run_bass_kernel_spmd(nc, in_maps, core_ids=list(range(NCORES)))
    return [(r["cT"], r["GT"], r["h1T"]) for r in res.results]


def run_rnn2(x1_list, r1, c, ada_w, ada_b, ln_g, ln_b, w_a, b_a, w_x, b_x, lam, w_out):
    nc = get_prog("rnn2")
    in_maps = []
    for core in range(NCORES):
        half = core % 2
        d2 = 1 - half
        cT, GT, h1T = r1[core]
        carry = colT96(r1[core ^ 1][2][:, T - 1])
        m = _common(c, ada_w, ada_b, core)
        m.update({"xT": x1_list[core], "cT": cT, "GT": GT, "h1T": h1T, "carry_in": carry,
                  "ln_gT": colT(ln_g, 8), "ln_bT": colT(ln_b, 8),
                  "w_a": np.ascontiguousarray(w_a[d2]), "w_x": np.ascontiguousarray(w_x[d2]),
                  "b_aT": colT96(b_a[d2]), "b_xT": colT96(b_x[d2]), "lamT": colT96(lam[d2]),
                  "w_out": np.ascontiguousarray(w_out)})
        in_maps.append(m)
    res = run_bass_kernel_spmd(nc, in_maps, core_ids=list(range(NCORES)))
    return [r["yT"] for r in res.results]


def kernel_unfused(x, c, ada_w, ada_b, ln_g, ln_b, attn_w_in, attn_w_out, attn_sinks,
                   rnn_w_in, rnn_conv_w, rnn_conv_b, rnn_w_a, rnn_b_a, rnn_w_x, rnn_b_x, rnn_lam,
                   rnn_w_out, mlp_w1, mlp_w2):
    f = lambda a: np.asarray(a, np.float32)
    x, c, ada_w, ada_b, ln_g, ln_b = f(x), f(c), f(ada_w), f(ada_b), f(ln_g), f(ln_b)
    xs = shard_x(x)
    a0 = run_attn(xs, c, ada_w[0, 0], ada_b[0, 0], ln_g[0, 0], ln_b[0, 0], f(attn_w_in)[0], f(attn_w_out)[0], f(attn_sinks)[0])
    m0 = run_mlp(a0, c, ada_w[0, 1], ada_b[0, 1], ln_g[0, 1], ln_b[0, 1], f(mlp_w1)[0], f(mlp_w2)[0])
    r1 = run_rnn1(m0, c, ada_w[1, 0], ada_b[1, 0], f(rnn_w_in)[0], f(rnn_conv_w)[0], f(rnn_conv_b)[0],
                  f(rnn_w_a)[0], f(rnn_b_a)[0], f(rnn_w_x)[0], f(rnn_b_x)[0], f(rnn_lam)[0])
    r2 = run_rnn2(m0, r1, c, ada_w[1, 0], ada_b[1, 0], ln_g[1, 0], ln_b[1, 0],
                  f(rnn_w_a)[0], f(rnn_b_a)[0], f(rnn_w_x)[0], f(rnn_b_x)[0], f(rnn_lam)[0], f(rnn_w_out)[0])
    m1 = run_mlp(r2, c, ada_w[1, 1], ada_b[1, 1], ln_g[1, 1], ln_b[1, 1], f(mlp_w1)[1], f(mlp_w2)[1])
    out = np.empty((4, 2 * T, D), np.float32)
    for core in range(NCORES):
        idx = local_positions(core)[:T]
        out[core // 2][idx, :] = m1[core].T
    return out


def kernel(x, c, ada_w, ada_b, ln_g, ln_b, attn_w_in, attn_w_out, attn_sinks,
           rnn_w_in, rnn_conv_w, rnn_conv_b, rnn_w_a, rnn_b_a, rnn_w_x, rnn_b_x, rnn_lam,
           rnn_w_out, mlp_w1, mlp_w2):
    f = lambda a: np.ascontiguousarray(np.asarray(a, np.float32))
    x, c, ada_w, ada_b, ln_g, ln_b = f(x), f(c), f(ada_w), f(ada_b), f(ln_g), f(ln_b)
    attn_w_in, attn_w_out, attn_sinks = f(attn_w_in), f(attn_w_out), f(attn_sinks)
    rnn_w_in, rnn_conv_w, rnn_conv_b, rnn_w_out = f(rnn_w_in), f(rnn_conv_w), f(rnn_conv_b), f(rnn_w_out)
    rnn_w_a, rnn_b_a, rnn_w_x, rnn_b_x, rnn_lam = f(rnn_w_a), f(rnn_b_a), f(rnn_w_x), f(rnn_b_x), f(rnn_lam)
    mlp_w1, mlp_w2 = f(mlp_w1), f(mlp_w2)
    nc = get_prog("fused2")
    xs = shard_x(x)
    perm, mprev, mnext = attn_consts()
    in_maps = []
    for core in range(NCORES):
        b = core // 2
        half = core % 2
        d1, d2 = half, 1 - half
        cosT, sinT = rope_tables(local_positions(core))
        sel = np.zeros((128, 8), np.float32)
        sel[:, core ^ 1] = 1.0
        m = {}

        def ada(pfx, i, j, ln=True):
            m[pfx + "c_col"] = colT(c[b], 8)
            m[pfx + "ada_w"] = ada_w[i, j]
            m[pfx + "ada_bT"] = colT(ada_b[i, j], 24)
            if ln:
                m[pfx + "ln_gT"] = colT(ln_g[i, j], 8)
                m[pfx + "ln_bT"] = colT(ln_b[i, j], 8)

        ada("a_", 0, 0)
        m.update({"a_xT": xs[core], "a_w_in": attn_w_in[0], "a_w_out": attn_w_out[0], "a_cosT": cosT, "a_sinT": sinT,
                  "a_perm": perm, "a_mprev": mprev, "a_mnext": mnext,
                  "a_sink_rep": np.ascontiguousarray(np.tile(attn_sinks[0][None, :], (128, 1)))})
        ada("m0_", 0, 1)
        m.update({"m0_w1": mlp_w1[0], "m0_w2": mlp_w2[0]})
        oc_ = core ^ 1
        oh = oc_ % 2
        cosO, sinO = rope_tables(local_positions(oc_))
        m.update({"b_xT": xs[oc_], "b_cosT": cosO, "b_sinT": sinO})
        m.update({"q1_convT": conv_table(rnn_conv_w[0], rnn_conv_b[0], oh),
                  "q1_w_a": rnn_w_a[0, oh], "q1_w_x": rnn_w_x[0, oh], "q1_b_aT": colT96(rnn_b_a[0, oh]),
                  "q1_b_xT": colT96(rnn_b_x[0, oh]), "q1_lamT": colT96(rnn_lam[0, oh])})
        ada("r1_", 1, 0, ln=False)
        m.update({"r1_w_in": rnn_w_in[0], "r1_convT": conv_table(rnn_conv_w[0], rnn_conv_b[0], half),
                  "r1_w_a": rnn_w_a[0, d1], "r1_w_x": rnn_w_x[0, d1], "r1_b_aT": colT96(rnn_b_a[0, d1]),
                  "r1_b_xT": colT96(rnn_b_x[0, d1]), "r1_lamT": colT96(rnn_lam[0, d1])})
        m["r2_ln_gT"] = colT(ln_g[1, 0], 8)
        m["r2_ln_bT"] = colT(ln_b[1, 0], 8)
        m.update({"r2_w_a": rnn_w_a[0, d2], "r2_w_x": rnn_w_x[0, d2], "r2_b_aT": colT96(rnn_b_a[0, d2]),
                  "r2_b_xT": colT96(rnn_b_x[0, d2]), "r2_lamT": colT96(rnn_lam[0, d2]), "r2_w_out": rnn_w_out[0]})
        ada("m1_", 1, 1)
        m.update({"m1_w1": mlp_w1[1], "m1_w2": mlp_w2[1]})
        in_maps.append({k: np.ascontiguousarray(v) for k, v in m.items()})
    res = run_bass_kernel_spmd(nc, in_maps, core_ids=list(range(NCORES)))
    out = np.empty((4, 2 * T, D), np.float32)
    for core in range(NCORES):
        idx = local_positions(core)[:T]
        out[core // 2][idx, :] = res.results[core]["out"].T
    return out
```

```python
import numpy as np
from contextlib import ExitStack
import concourse.bass as bass
import concourse.mybir as mybir
from concourse.bass_utils import run_bass_kernel_spmd

AF = mybir.ActivationFunctionType
ALU = mybir.AluOpType
F32 = mybir.dt.float32
BF16 = mybir.dt.bfloat16

D = 1024
KC = 8
T = 4096
NCORES = 8
DFF = 4096
DEPTH = 2
DN_ALPHA = (2.0 * DEPTH) ** 0.25
LN_EPS = 1e-5
DRNN = 1536
NBLK = 16
BW = 96
SEM_LIMIT = 3000


class Buf:
    __slots__ = ("name", "w", "r", "dsem", "dcnt")

    def __init__(self, name):
        self.name = name
        self.w = None
        self.r = []
        self.dsem = None
        self.dcnt = 0


class Sched:
    ENG = ("pe", "act", "dve", "pool", "sp")

    def __init__(self, nc, stack):
        self.nc = nc
        self.stack = stack
        self.stream = {e: [] for e in self.ENG}
        self.sem = {e: None for e in self.ENG}
        self.cnt = {e: 0 for e in self.ENG}
        self.seen = {e: {} for e in self.ENG}
        self.nsem = 0
        self.handles = []
        self.dma_bufs = []

    def _newsem(self, tag):
        self.nsem += 1
        h = self.nc.alloc_semaphore(name="%ss%d_%s" % (getattr(self, "prefix", ""), self.nsem, tag))
        self.handles.append(h)
        return h

    def _peek(self, e):
        if self.sem[e] is None or self.cnt[e] >= SEM_LIMIT:
            self.sem[e] = self._newsem(e)
            self.cnt[e] = 0
        return (self.sem[e], self.cnt[e] + 1)

    def _waits(self, e, reads, writes, skip_sem=None):
        need = {}

        def add(t):
            if t is None:
                return
            s, v = t
            k = id(s)
            if k not in need or need[k][1] < v:
                need[k] = (s, v)

        for b in reads:
            add(b.w)
        for b in writes:
            add(b.w)
            for t in b.r:
                add(t)
        out = []
        seen = self.seen[e]
        for k, (s, v) in need.items():
            if e == "pe" and s is self.sem["pe"]:
                continue
            if skip_sem is not None and s is skip_sem:
                continue
            if seen.get(k, 0) >= v:
                continue
            seen[k] = v
            out.append((s, v))
        return out

    def op(self, e, fn, reads=(), writes=(), inc=True):
        waits = self._waits(e, reads, writes)
        tk = self._peek(e)
        if inc:
            self.cnt[e] += 1
        self.stream[e].append((waits, fn, tk if inc else None))
        for b in reads:
            b.r.append(tk)
        for b in writes:
            b.w = tk
            b.r = []
        return tk

    def dma(self, e, out_ap, in_ap, reads=(), writes=(), anchor=None):
        a = anchor
        if a.dsem is None or a.dcnt >= SEM_LIMIT:
            a.dsem = self._newsem("d")
            a.dcnt = 0
            self.dma_bufs.append(a)
        waits = self._waits(e, reads, writes, skip_sem=a.dsem)
        a.dcnt += 16
        tk = (a.dsem, a.dcnt)

        def fn(eng, out_ap=out_ap, in_ap=in_ap):
            return eng.dma_start(out=out_ap, in_=in_ap)

        self.stream[e].append((waits, fn, ("dma", a.dsem)))
        for b in reads:
            b.r.append(tk)
        for b in writes:
            b.w = tk
            b.r = []
        return tk

    def collective(self, in_ap, out_ap, reads=(), writes=(), anchor=None):
        a = anchor
        if a.dsem is None:
            a.dsem = self._newsem("cc")
            a.dcnt = 0
            self.dma_bufs.append(a)
        waits = self._waits("pool", reads, writes, skip_sem=a.dsem)
        a.dcnt += 1
        tk = (a.dsem, a.dcnt)

        def fn(eng, in_ap=in_ap, out_ap=out_ap):
            return eng.collective_compute("AllGather", ALU.bypass, replica_groups=[list(range(NCORES))], ins=[in_ap], outs=[out_ap])

        self.stream["pool"].append((waits, fn, ("cc", a.dsem)))
        for b in reads:
            b.r.append(tk)
        for b in writes:
            b.w = tk
            b.r = []
        return tk

    def barrier(self):
        targets = []
        for e in self.ENG:
            if self.sem[e] is not None and self.cnt[e] > 0:
                targets.append((self.sem[e], self.cnt[e]))
        for a in self.dma_bufs:
            targets.append((a.dsem, a.dcnt))
        for e in self.ENG:
            w = []
            for (s, v) in targets:
                if e == "pe" and s is self.sem["pe"]:
                    continue
                if self.seen[e].get(id(s), 0) >= v:
                    continue
                self.seen[e][id(s)] = v
                w.append((s, v))
            if w:
                self.stream[e].append((w, None, None))

    def final_wait(self, e, bufs):
        w = []
        for b in bufs:
            for t in [b.w] + list(b.r):
                if t is not None:
                    w.append(t)
        self.stream[e].append((w, None, None))

    def emit(self):
        nc = self.nc
        block = self.stack.enter_context(nc.Block())

        def replay(eng, items):
            for waits, fn, tk in items:
                for (s, v) in waits:
                    eng.wait_ge(s, v)
                if fn is None:
                    continue
                ins = fn(eng)
                if tk is None:
                    continue
                if tk[0] == "dma":
                    ins.then_inc(tk[1], 16)
                elif tk[0] == "cc":
                    ins.then_inc(tk[1])
                else:
                    ins.then_inc(tk[0], 1)

        st = self.stream

        @block.sync
        def _(eng):
            replay(eng, st["sp"])

        @block.tensor
        def _(eng):
            replay(eng, st["pe"])

        @block.scalar
        def _(eng):
            replay(eng, st["act"])

        @block.vector
        def _(eng):
            replay(eng, st["dve"])

        @block.gpsimd
        def _(eng):
            replay(eng, st["pool"])


class Ctx:
    def __init__(self, fused=False):
        self.nc = bass.Bass("TRN2", target_bir_lowering=False)
        self.fused = fused
        self.gstack = ExitStack()
        self.nbuf = 0
        self.decl = {}
        self.ada_cache = {}
        self.prefix = ""
        self.over = {}
        self.stack = None
        self.s = None

    def begin_phase(self, prefix="", over=None):
        self.prefix = prefix
        self.over = over or {}
        self.stack = ExitStack()
        self.s = Sched(self.nc, self.stack)
        self.s.prefix = prefix

    def end_phase(self):
        if self.fused:
            self.s.barrier()
        self.s.emit()
        self.stack.close()
        self.nc.all_engine_barrier()
        self.nc.clear_and_free_semaphores(self.s.handles)
        self.nc.all_engine_barrier()
        return self.nc

    def dram_in(self, name, shape, dt=F32):
        if name in self.over:
            return self.over[name]
        ap = self.nc.dram_tensor(self.prefix + name, list(shape), dt, kind="ExternalInput").ap()
        self.decl[self.prefix + name] = ap
        return ap

    def dram_out(self, name, shape, dt=F32):
        if name in self.over:
            return self.over[name]
        return self.nc.dram_tensor(self.prefix + name, list(shape), dt, kind="ExternalOutput").ap()

    def dram_tmp(self, name, shape, dt=F32):
        return self.nc.dram_tensor(name, list(shape), dt)

    def gsb(self, name, shape, dt):
        return self.gstack.enter_context(self.nc.sbuf_tensor(name, list(shape), dt))

    def sb(self, name, shape, dt):
        return self.stack.enter_context(self.nc.sbuf_tensor(self.prefix + name, list(shape), dt))

    def ps(self, name, shape=(128, 512), dt=F32):
        return self.stack.enter_context(self.nc.psum_tensor(self.prefix + name, list(shape), dt))

    def buf(self, name="b"):
        self.nbuf += 1
        return Buf("%s%d" % (name, self.nbuf))


def emit_consts(cx):
    s = cx.s
    c = {}
    c["ones_t"] = cx.sb("ones_t", [128, 128], BF16)
    c["ones_b"] = cx.buf("ones")
    c["eps_t"] = cx.sb("eps_t", [128, 1], F32)
    c["eps_b"] = cx.buf("eps")
    s.op("pool", lambda e: e.memset(c["ones_t"][:], 1.0 / 1024.0), writes=[c["ones_b"]])
    s.op("pool", lambda e: e.memset(c["eps_t"][:], LN_EPS), writes=[c["eps_b"]])
    return c


def emit_adaln(cx, c_col, ada_w, ada_bT, scratch_ps, scratch_ps_b, pieces, piece_bufs, owner_bufs):
    s = cx.s
    key = cx.over.get("ada_key")
    if key is not None and key in cx.ada_cache:
        mod, mod1p = cx.ada_tiles[key]
        return dict(mod=mod, mod1p=mod1p, b_mod=cx.buf("adamod"), b_mod1p=cx.buf("adamod1p"))
    ccol = cx.sb("ada_c", [128, 8], F32)
    csil = cx.sb("ada_cs", [128, 8], F32)
    bT = cx.sb("ada_b", [128, 24], F32)
    if key is not None:
        mod, mod1p = cx.ada_tiles[key]
        cx.ada_cache[key] = True
    else:
        mod = cx.sb("ada_mod", [128, 24], F32)
        mod1p = cx.sb("ada_mod1p", [128, 24], F32)
    b_c, b_cs, b_mod, b_mod1p = cx.buf("adac"), cx.buf("adacs"), cx.buf("adamod"), cx.buf("adamod1p")
    b_bT = b_c
    s.dma("sp", ccol[:], c_col[:, :], writes=[b_c], anchor=b_c)
    s.dma("sp", bT[:], ada_bT[:, :], writes=[b_bT], anchor=b_bT)
    s.op("act", lambda e: e.activation(out=csil[:], in_=ccol[:], func=AF.Silu), reads=[b_c], writes=[b_cs])
    wv = ada_w.rearrange("(k p) n -> p k n", p=128)
    for col in range(24):
        t = pieces[col % 2]
        tb = piece_bufs[col % 2]
        s.dma("sp", t, wv[:, :, col * 128:(col + 1) * 128], writes=[tb], anchor=tb)
        for k in range(8):
            s.op("pe", lambda e, t=t, k=k, col=col: e.matmul(
                scratch_ps[:, col:col + 1], lhsT=t[:, k, :], rhs=csil[:, k:k + 1],
                start=(k == 0), stop=(k == 7)),
                reads=[tb, b_cs], writes=[scratch_ps_b], inc=(k == 7))
    s.op("dve", lambda e: e.tensor_tensor(out=mod[:], in0=scratch_ps[:, 0:24], in1=bT[:], op=ALU.add),
         reads=[scratch_ps_b, b_bT] + list(piece_bufs), writes=[b_mod] + list(owner_bufs))
    s.op("dve", lambda e: e.tensor_scalar_add(out=mod1p[:], in0=mod[:], scalar1=1.0), reads=[b_mod], writes=[b_mod1p])
    return dict(mod=mod, mod1p=mod1p, b_mod=b_mod, b_mod1p=b_mod1p)


def emit_ln_tile(cx, cst, z, zb_t, zsq_t, ps_sum, b_sum, ps_sq, b_sq, tmp, N, gT, bT, b_par, out_t, bz, bzb, bzsq, btmp, bout):
    s = cx.s
    for k in range(8):
        s.op("act", lambda e, k=k: e.activation(out=zb_t[:, k, :], in_=z[:, k, :], func=AF.Identity), reads=[bz], writes=[bzb])
        s.op("act", lambda e, k=k: e.activation(out=zsq_t[:, k, :], in_=z[:, k, :], func=AF.Square), reads=[bz], writes=[bzsq])
    for k in range(8):
        s.op("pe", lambda e, k=k: e.matmul(ps_sum, lhsT=cst["ones_t"][:], rhs=zb_t[:, k, :], start=(k == 0), stop=(k == 7)),
             reads=[cst["ones_b"], bzb], writes=[b_sum], inc=(k == 7))
    for k in range(8):
        s.op("pe", lambda e, k=k: e.matmul(ps_sq, lhsT=cst["ones_t"][:], rhs=zsq_t[:, k, :], start=(k == 0), stop=(k == 7)),
             reads=[cst["ones_b"], bzsq], writes=[b_sq], inc=(k == 7))
    mean = tmp[:, 0, :]
    msq = tmp[:, 1, :]
    var = tmp[:, 2, :]
    rstd = tmp[:, 3, :]
    s.op("dve", lambda e: e.tensor_copy(out=mean, in_=ps_sum), reads=[b_sum], writes=[btmp])
    s.op("dve", lambda e: e.tensor_tensor(out=msq, in0=mean, in1=mean, op=ALU.mult), reads=[btmp], writes=[btmp])
    s.op("dve", lambda e: e.tensor_tensor(out=var, in0=ps_sq, in1=msq, op=ALU.subtract), reads=[b_sq, btmp], writes=[btmp])
    s.op("act", lambda e: e.activation(out=var, in_=var, func=AF.Sqrt, bias=cst["eps_t"][:, 0:1]), reads=[btmp, cst["eps_b"]], writes=[btmp])
    s.op("dve", lambda e: e.reciprocal(out=rstd, in_=var), reads=[btmp], writes=[btmp])
    for k in range(8):
        s.op("dve", lambda e, k=k: e.tensor_tensor(out=z[:, k, :], in0=z[:, k, :], in1=mean, op=ALU.subtract), reads=[bz, btmp], writes=[bz])
        s.op("dve", lambda e, k=k: e.tensor_tensor(out=z[:, k, :], in0=z[:, k, :], in1=rstd, op=ALU.mult), reads=[bz, btmp], writes=[bz])
        s.op("act", lambda e, k=k: e.activation(out=out_t[:, k, :], in_=z[:, k, :], func=AF.Identity,
                                                 bias=bT[:, k:k + 1], scale=gT[:, k:k + 1]),
             reads=[bz, b_par], writes=[bout])


def build_mlp(cx=None, prefix="", over=None):
    cx = cx or Ctx()
    cx.begin_phase(prefix, over)
    s = cx.s
    N = 256
    NT = T // N
    xT = cx.dram_in("xT", [D, T])
    c_col = cx.dram_in("c_col", [128, 8])
    ada_w = cx.dram_in("ada_w", [D, 3 * D])
    ada_bT = cx.dram_in("ada_bT", [128, 24])
    ln_gT = cx.dram_in("ln_gT", [128, 8])
    ln_bT = cx.dram_in("ln_bT", [128, 8])
    w1 = cx.dram_in("w1", [D, DFF])
    w2 = cx.dram_in("w2", [DFF, D])
    yT = cx.dram_out("yT", [D, T])

    cst = emit_consts(cx)
    acc = [cx.ps("acc%d" % i) for i in range(4)]
    acc_b = [cx.buf("acc") for i in range(8)]
    ph = [cx.ps("ph%d" % i) for i in range(3)]
    ph_b = [cx.buf("ph") for i in range(3)]
    stp = cx.ps("stp")
    st_b = cx.buf("st")

    z = cx.sb("z", [128, 8, N], F32)
    bz = cx.buf("z")
    pieces = [z[:, 0:4, :].rearrange("p a (b n) -> p (a b) n", n=128), z[:, 4:8, :].rearrange("p a (b n) -> p (a b) n", n=128)]
    ada = emit_adaln(cx, c_col, ada_w, ada_bT, stp, st_b, pieces, [cx.buf("pc0"), cx.buf("pc1")], [bz])
    mod, mod1p = ada["mod"], ada["mod1p"]
    b_mod, b_mod1p = ada["b_mod"], ada["b_mod1p"]

    gT = cx.sb("gT", [128, 8], F32)
    bT = cx.sb("bT", [128, 8], F32)
    b_par = cx.buf("par")
    s.dma("sp", gT[:], ln_gT[:, :], writes=[b_par], anchor=b_par)
    s.dma("sp", bT[:], ln_bT[:, :], writes=[b_par], anchor=b_par)

    w1b = cx.sb("w1b", [128, 8, DFF], BF16)
    w2b = cx.sb("w2b", [128, 32, D], BF16)
    b_w1x = cx.buf("w1")
    b_w2x = [cx.buf("w2") for k in range(4)]
    b_w1 = [b_w1x for k in range(8)]
    b_w2 = [b_w2x[k // 8] for k in range(32)]
    w1v = w1.rearrange("(k p) n -> p k n", p=128)
    w2v = w2.rearrange("(k p) n -> p k n", p=128)
    for k in range(8):
        s.dma("pool", w1b[:, k, :], w1v[:, k, :], writes=[b_w1[k]], anchor=b_w1[k])
    for k in range(32):
        s.dma("pool", w2b[:, k, :], w2v[:, k, :], writes=[b_w2[k]], anchor=b_w2[k])

    jobs = cx.over.get("jobs") or [(xT, yT, cx.over.get("tail_out"))]
    jobv = [(xj.rearrange("(k p) t -> p k t", p=128), yj.rearrange("(k p) t -> p k t", p=128), tj) for (xj, yj, tj) in jobs]
    NXB = 2
    xt = [cx.sb("xt%d" % i, [128, 8, N], F32) for i in range(NXB)]
    xt_b = [cx.buf("xt") for i in range(NXB)]
    ht = cx.sb("ht", [128, 8, N], BF16)
    ht_b = cx.buf("ht")
    rb = [cx.sb("rb%d" % i, [128, N], F32) for i in range(3)]
    rb_b = [cx.buf("rb") for i in range(3)]
    hid = [cx.sb("hid%d" % i, [128, N], BF16) for i in range(4)]
    hid_b = [cx.buf("hid") for i in range(4)]
    zb_t = cx.sb("zb", [128, 8, N], BF16)
    bzb = cx.buf("zb")
    zsq_t = cx.sb("zsq", [128, 8, N], BF16)
    bzsq = cx.buf("zsq")
    tmp = cx.sb("tmp", [128, 4, N], F32)
    btmp = cx.buf("tmp")

    def load_x(gi):
        xv_ = jobv[gi // NT][0]
        it_ = gi % NT
        s.dma("sp", xt[gi % NXB][:], xv_[:, :, it_ * N:(it_ + 1) * N], writes=[xt_b[gi % NXB]], anchor=xt_b[gi % NXB])

    load_x(0)
    for gi in range(NT * len(jobv)):
        it = gi % NT
        yv, tail_out = jobv[gi // NT][1], jobv[gi // NT][2]
        if gi + 1 < NT * len(jobv):
            load_x(gi + 1)
        x_t = xt[gi % NXB]
        xb = xt_b[gi % NXB]
        for k in range(8):
            s.op("act", lambda e, k=k, x_t=x_t: e.activation(out=ht[:, k, :], in_=x_t[:, k, :], func=AF.Identity,
                                                             bias=mod[:, k:k + 1], scale=mod1p[:, 8 + k:9 + k]),
                 reads=[xb, b_mod, b_mod1p], writes=[ht_b])
        s.op("act", lambda e, x_t=x_t: e.activation(out=x_t[:], in_=x_t[:], func=AF.Identity, scale=float(DN_ALPHA)),
             reads=[xb], writes=[xb])

        def g1(hc):
            p = ph[hc % 3]
            for k in range(8):
                s.op("pe", lambda e, p=p, k=k, hc=hc: e.matmul(p[:, 0:N], lhsT=w1b[:, k, hc * 128:(hc + 1) * 128], rhs=ht[:, k, :],
                                                              start=(k == 0), stop=(k == 7)),
                     reads=[b_w1[k], ht_b], writes=[ph_b[hc % 3]], inc=(k == 7))

        def ew(hc):
            p = ph[hc % 3]
            r = rb[hc % 3]
            h_ = hid[hc % 4]
            s.op("act", lambda e, p=p, r=r: e.activation(out=r[:], in_=p[:, 0:N], func=AF.Relu), reads=[ph_b[hc % 3]], writes=[rb_b[hc % 3]])
            s.op("dve", lambda e, r=r, h_=h_: e.tensor_tensor(out=h_[:], in0=r[:], in1=r[:], op=ALU.mult), reads=[rb_b[hc % 3]], writes=[hid_b[hc % 4]])

        def g2(hc):
            h_ = hid[hc % 4]
            for oc in range(8):
                a = acc[oc // 2]
                s.op("pe", lambda e, a=a, oc=oc, hc=hc, h_=h_: e.matmul(a[:, (oc % 2) * N:(oc % 2 + 1) * N],
                                                                        lhsT=w2b[:, hc, oc * 128:(oc + 1) * 128], rhs=h_[:],
                                                                        start=(hc == 0 and oc % 2 == 0), stop=(hc == 31),
                                                                        skip_group_check=True),
                     reads=[b_w2[hc], hid_b[hc % 4]], writes=[acc_b[oc]], inc=(oc == 7 or hc == 31))

        g1(0)
        g1(1)
        ew(0)
        for hc in range(32):
            if hc + 2 < 32:
                g1(hc + 2)
            if hc + 1 < 32:
                ew(hc + 1)
            g2(hc)
        for oc in range(8):
            a = acc[oc // 2]
            s.op("dve", lambda e, a=a, oc=oc, x_t=x_t: e.scalar_tensor_tensor(
                out=z[:, oc, :], in0=a[:, (oc % 2) * N:(oc % 2 + 1) * N], scalar=mod1p[:, 16 + oc:17 + oc], in1=x_t[:, oc, :],
                op0=ALU.mult, op1=ALU.add), reads=[acc_b[oc], xb, b_mod1p], writes=[bz])
        emit_ln_tile(cx, cst, z, zb_t, zsq_t, stp[:, 0:N], st_b, stp[:, N:2 * N], st_b, tmp, N, gT, bT, b_par, z, bz, bzb, bzsq, btmp, bz)
        s.dma("sp", yv[:, :, it * N:(it + 1) * N], z[:], reads=[bz], anchor=bz)
        if it == NT - 1 and tail_out is not None:
            s.dma("sp", tail_out.rearrange("p (k t) -> p k t", t=2), z[:, :, N - 2:N], reads=[bz], anchor=bz)
    s.final_wait("sp", [bz])
    return cx.end_phase()


TH = T + 128


def build_attn(cx=None, prefix="", over=None):
    cx = cx or Ctx()
    cx.begin_phase(prefix, over)
    s = cx.s
    N = 512
    NT = T // N
    SCALE = 128.0 ** -0.5
    xT = cx.dram_in("xT", [D, TH])
    c_col = cx.dram_in("c_col", [128, 8])
    ada_w = cx.dram_in("ada_w", [D, 3 * D])
    ada_bT = cx.dram_in("ada_bT", [128, 24])
    ln_gT = cx.dram_in("ln_gT", [128, 8])
    ln_bT = cx.dram_in("ln_bT", [128, 8])
    w_in = cx.dram_in("w_in", [D, 1536])
    w_out = cx.dram_in("w_out", [D, D])
    cosT = cx.dram_in("cosT", [32, TH])
    sinT = cx.dram_in("sinT", [32, TH])
    perm = cx.dram_in("perm", [32, 32])
    mprev = cx.dram_in("mprev", [128, 512])
    mnext = cx.dram_in("mnext", [128, 512])
    sink_rep = cx.dram_in("sink_rep", [128, 8])
    yT = cx.dram_out("yT", [D, T])

    cst = emit_consts(cx)
    one_t = cx.sb("one_t", [128, 128], BF16)
    one_b = cx.buf("one")
    s.op("pool", lambda e: e.memset(one_t[:], 1.0), writes=[one_b])

    banks = [cx.ps("bank%d" % i) for i in range(8)]
    bank_b = [cx.buf("bank") for i in range(8)]
    bctr = [0]

    def bank():
        i = bctr[0] % 8
        bctr[0] += 1
        return banks[i], bank_b[i]

    z = cx.sb("z", [128, 8, N], F32)
    bz = cx.buf("z")
    pieces = [z[:, 0:2, :].rearrange("p a (b n) -> p (a b) n", n=128), z[:, 2:4, :].rearrange("p a (b n) -> p (a b) n", n=128)]
    pb, pbb = bank()
    ada = emit_adaln(cx, c_col, ada_w, ada_bT, pb, pbb, pieces, [cx.buf("pc0"), cx.buf("pc1")], [bz])
    mod, mod1p = ada["mod"], ada["mod1p"]
    b_mod, b_mod1p = ada["b_mod"], ada["b_mod1p"]

    gT = cx.sb("gT", [128, 8], F32)
    bT = cx.sb("bT", [128, 8], F32)
    b_par = cx.buf("par")
    s.dma("sp", gT[:], ln_gT[:, :], writes=[b_par], anchor=b_par)
    s.dma("sp", bT[:], ln_bT[:, :], writes=[b_par], anchor=b_par)

    perm_t = cx.sb("perm_t", [32, 32], BF16)
    mprev_t = cx.sb("mprev_t", [128, 512], BF16)
    mnext_t = cx.sb("mnext_t", [128, 512], BF16)
    b_cm = cx.buf("cm")
    s.dma("pool", perm_t[:], perm[:, :], writes=[b_cm], anchor=b_cm)
    s.dma("pool", mprev_t[:], mprev[:, :], writes=[b_cm], anchor=b_cm)
    s.dma("pool", mnext_t[:], mnext[:, :], writes=[b_cm], anchor=b_cm)
    sk = cx.sb("sk", [128, 8], F32)
    b_sk = cx.buf("sk")
    s.dma("sp", sk[:], sink_rep[:, :], writes=[b_sk], anchor=b_sk)
    s.op("act", lambda e: e.activation(out=sk[:], in_=sk[:], func=AF.Exp), reads=[b_sk], writes=[b_sk])
    es_full = cx.sb("es_full", [128, 2, 512], F32)
    b_es = cx.buf("es")
    s.op("pool", lambda e: e.memset(es_full[:], 0.0), writes=[b_es])
    for hh in range(8):
        s.op("pool", lambda e, hh=hh: e.tensor_scalar(out=es_full[:, hh // 4, (hh % 4) * 128:(hh % 4 + 1) * 128],
                                                       in0=es_full[:, hh // 4, (hh % 4) * 128:(hh % 4 + 1) * 128],
                                                       scalar1=sk[:, hh:hh + 1], scalar2=None, op0=ALU.add),
             reads=[b_sk, b_es], writes=[b_es])

    w_inb = cx.sb("w_inb", [128, 8, 1536], BF16)
    w_outb = cx.sb("w_outb", [128, 8, D], BF16)
    b_win = cx.buf("win")
    b_wout = cx.buf("wout")
    winv = w_in.rearrange("(k p) n -> p k n", p=128)
    woutv = w_out.rearrange("(k p) n -> p k n", p=128)
    for k in range(8):
        s.dma("pool", w_inb[:, k, :], winv[:, k, :], writes=[b_win], anchor=b_win)
    for k in range(8):
        s.dma("pool", w_outb[:, k, :], woutv[:, k, :], writes=[b_wout], anchor=b_wout)

    kT_all = cx.sb("kT_all", [128, 2, TH], BF16)
    v_all = cx.sb("v_all", [128, TH // 128, 256], BF16)
    b_k = [cx.buf("k") for i in range(NT + 1)]
    b_v = [cx.buf("v") for i in range(NT + 1)]

    xv = xT.rearrange("(k p) t -> p k t", p=128)
    yv = yT.rearrange("(k p) t -> p k t", p=128)
    xt = cx.sb("xt", [128, 8, N], F32)
    xb = cx.buf("xt")
    ht = cx.sb("ht", [128, 8, N], BF16)
    ht_b = cx.buf("ht")
    cs_t = cx.sb("cs_t", [32, 2, N], F32)
    b_cs = cx.buf("cs")
    rtmp = cx.sb("rtmp", [32, 2, N], F32)
    b_rtmp = cx.buf("rtmp")
    qtmp = cx.sb("qtmp", [128, N], BF16)
    b_qtmp = cx.buf("qtmp")
    qT = cx.sb("qT", [128, 4, 8, 128], BF16)
    b_q = cx.buf("q")
    pT = [cx.sb("pT%d" % i, [128, 512], BF16) for i in range(6)]
    pT_b = [cx.buf("pT") for i in range(6)]
    pctr = [0]
    oT = cx.sb("oT", [128, 8, N], BF16)
    b_o = cx.buf("o")
    zb_t = cx.sb("zb", [128, 8, N], BF16)
    bzb = cx.buf("zb")
    zsq_t = cx.sb("zsq", [128, 8, N], BF16)
    bzsq = cx.buf("zsq")
    tmp = cx.sb("tmp", [128, 4, N], F32)
    btmp = cx.buf("tmp")

    def load_tile(t0, n):
        s.dma("sp", xt[:, :, 0:n], xv[:, :, t0:t0 + n], writes=[xb], anchor=xb)
        s.dma("sp", cs_t[:, 0, 0:n], cosT[:, t0:t0 + n], writes=[b_cs], anchor=b_cs)
        s.dma("sp", cs_t[:, 1, 0:n], sinT[:, t0:t0 + n], writes=[b_cs], anchor=b_cs)
        for k in range(8):
            s.op("act", lambda e, k=k, n=n: e.activation(out=ht[:, k, 0:n], in_=xt[:, k, 0:n], func=AF.Identity,
                                                         bias=mod[:, k:k + 1], scale=mod1p[:, 8 + k:9 + k]),
                 reads=[xb, b_mod, b_mod1p], writes=[ht_b])

    def rope(view32, vb, n):
        pbk, pbb_ = bank()
        s.op("pe", lambda e: e.matmul(pbk[0:32, 0:n], lhsT=perm_t[:, :], rhs=view32, start=True, stop=True),
             reads=[b_cm, vb], writes=[pbb_])
        s.op("dve", lambda e: e.tensor_tensor(out=rtmp[:, 0, 0:n], in0=pbk[0:32, 0:n], in1=cs_t[:, 1, 0:n], op=ALU.mult),
             reads=[pbb_, b_cs], writes=[b_rtmp])
        s.op("dve", lambda e: e.tensor_tensor(out=rtmp[:, 1, 0:n], in0=view32, in1=cs_t[:, 0, 0:n], op=ALU.mult),
             reads=[vb, b_cs], writes=[b_rtmp])
        s.op("dve", lambda e: e.tensor_tensor(out=view32, in0=rtmp[:, 0, 0:n], in1=rtmp[:, 1, 0:n], op=ALU.add),
             reads=[b_rtmp], writes=[vb])

    for it in range(NT + 1):
        t0 = it * N
        n = N if it < NT else 128
        load_tile(t0, n)
        for kvh in range(2):
            pbk, pbb_ = bank()
            for k in range(8):
                s.op("pe", lambda e, pbk=pbk, k=k, kvh=kvh, n=n: e.matmul(pbk[:, 0:n], lhsT=w_inb[:, k, 1024 + kvh * 128:1152 + kvh * 128],
                                                                          rhs=ht[:, k, 0:n], start=(k == 0), stop=(k == 7)),
                     reads=[b_win, ht_b], writes=[pbb_], inc=(k == 7))
            s.op("act", lambda e, pbk=pbk, kvh=kvh, t0=t0, n=n: e.activation(out=kT_all[:, kvh, t0:t0 + n], in_=pbk[:, 0:n], func=AF.Identity),
                 reads=[pbb_], writes=[b_k[it]])
            rope(kT_all[0:32, kvh, t0:t0 + n], b_k[it], n)
        for blk in range(n // 128):
            pbk, pbb_ = bank()
            for k in range(8):
                s.op("pe", lambda e, pbk=pbk, k=k, blk=blk: e.matmul(pbk[:, 0:256], lhsT=ht[:, k, blk * 128:(blk + 1) * 128],
                                                                     rhs=w_inb[:, k, 1280:1536], start=(k == 0), stop=(k == 7)),
                     reads=[b_win, ht_b], writes=[pbb_], inc=(k == 7))
            s.op("act", lambda e, pbk=pbk, blk=blk, t0=t0: e.activation(out=v_all[:, t0 // 128 + blk, :], in_=pbk[:, 0:256], func=AF.Identity),
                 reads=[pbb_], writes=[b_v[it]])

    class _RV:
        def __init__(self, ts, bs):
            self.t, self.b, self.i = ts, bs, 0

        def get(self):
            j = self.i % len(self.t)
            self.i += 1
            return self.t[j], self.b[j]

    SCB = _RV(banks[0:4], bank_b[0:4])
    PVB = _RV(banks[4:8], bank_b[4:8])
    PTP = _RV(pT, pT_b)
    DENP = Rot(cx, "den", 2, [128, 512], F32)
    for it in range(NT):
        t0 = it * N
        load_tile(t0, N)
        for hd in range(8):
            pbk, pbb_ = bank()
            for k in range(8):
                s.op("pe", lambda e, pbk=pbk, k=k, hd=hd: e.matmul(pbk[:, 0:N], lhsT=w_inb[:, k, hd * 128:(hd + 1) * 128],
                                                                   rhs=ht[:, k, :], start=(k == 0), stop=(k == 7)),
                     reads=[b_win, ht_b], writes=[pbb_], inc=(k == 7))
            s.op("act", lambda e, pbk=pbk: e.activation(out=qtmp[:], in_=pbk[:, 0:N], func=AF.Identity),
                 reads=[pbb_], writes=[b_qtmp])
            rope(qtmp[0:32, :], b_qtmp, N)
            s.op("pool", lambda e, hd=hd: e.tensor_copy(out=qT[:, :, hd, :], in_=qtmp[:].rearrange("p (a n) -> p a n", n=128)),
                 reads=[b_qtmp], writes=[b_q])
        s.op("act", lambda e: e.activation(out=xt[:], in_=xt[:], func=AF.Identity, scale=float(DN_ALPHA)),
             reads=[xb], writes=[xb])
        def a_s0(itm, it=it):
            qb, kvh = itm["qb"], itm["kvh"]
            B = it * 4 + qb
            qrhs = qT[:, qb, kvh * 4:(kvh + 1) * 4, :].rearrange("p h n -> p (h n)")
            itm["sc"] = []
            for jb in (B - 1, B, B + 1):
                if jb < 0:
                    continue
                pbk, pbb_ = SCB.get()
                kb = b_k[min(jb // 4, NT)]
                s.op("pe", lambda e, pbk=pbk, jb=jb: e.matmul(pbk[:, :], lhsT=kT_all[:, kvh, jb * 128:(jb + 1) * 128], rhs=qrhs,
                                                              start=True, stop=True),
                     reads=[kb, b_q], writes=[pbb_])
                itm["sc"].append((jb, B, pbk, pbb_))

        def a_s1(itm):
            itm["pts"] = []
            for (jb, B, pbk, pbb_) in itm["sc"]:
                p_t, p_b = PTP.get()
                s.op("act", lambda e, pbk=pbk, p_t=p_t: e.activation(out=p_t[:], in_=pbk[:, :], func=AF.Exp, scale=SCALE),
                     reads=[pbb_], writes=[p_b])
                if jb == B - 1:
                    s.op("pool", lambda e, p_t=p_t: e.tensor_tensor(out=p_t[:], in0=p_t[:], in1=mprev_t[:], op=ALU.mult),
                         reads=[p_b, b_cm], writes=[p_b])
                elif jb == B + 1:
                    s.op("pool", lambda e, p_t=p_t: e.tensor_tensor(out=p_t[:], in0=p_t[:], in1=mnext_t[:], op=ALU.mult),
                         reads=[p_b, b_cm], writes=[p_b])
                itm["pts"].append((jb, p_t, p_b))

        def a_s2(itm):
            kvh = itm["kvh"]
            pts = itm["pts"]
            po, pob = PVB.get()
            pd, pdb = PVB.get()
            itm.update(po=po, pob=pob, pd=pd, pdb=pdb)
            for i, (jb, p_t, p_b) in enumerate(pts):
                vb = b_v[min(jb // 4, NT)]
                s.op("pe", lambda e, jb=jb, p_t=p_t, i=i: e.matmul(po[:, :], lhsT=v_all[:, jb, kvh * 128:(kvh + 1) * 128],
                                                                   rhs=p_t[:], start=(i == 0), stop=(i == len(pts) - 1)),
                     reads=[vb, p_b], writes=[pob], inc=(i == len(pts) - 1))
            for i, (jb, p_t, p_b) in enumerate(pts):
                s.op("pe", lambda e, p_t=p_t, i=i: e.matmul(pd[:, :], lhsT=one_t[:], rhs=p_t[:], start=(i == 0), stop=(i == len(pts) - 1)),
                     reads=[one_b, p_b], writes=[pdb], inc=(i == len(pts) - 1))

        def a_s3(itm):
            qb, kvh = itm["qb"], itm["kvh"]
            po, pob, pd, pdb = itm["po"], itm["pob"], itm["pd"], itm["pdb"]
            den, den_b = DENP.get()
            s.op("dve", lambda e: e.tensor_tensor(out=den[:], in0=pd[:, :], in1=es_full[:, kvh, :], op=ALU.add),
                 reads=[pdb, b_es], writes=[den_b])
            s.op("dve", lambda e: e.reciprocal(out=den[:], in_=den[:]), reads=[den_b], writes=[den_b])
            s.op("dve", lambda e: e.tensor_tensor(
                out=oT[:, kvh * 4:(kvh + 1) * 4, qb * 128:(qb + 1) * 128], in0=po[:, :].rearrange("p (h n) -> p h n", n=128),
                in1=den[:].rearrange("p (h n) -> p h n", n=128), op=ALU.mult),
                reads=[pob, den_b], writes=[b_o])

        run_pipeline([{"qb": qb, "kvh": kvh} for qb in range(4) for kvh in range(2)], [a_s0, a_s1, a_s2, a_s3])
        for oc in range(8):
            pbk, pbb_ = bank()
            for k in range(8):
                s.op("pe", lambda e, pbk=pbk, k=k, oc=oc: e.matmul(pbk[:, :], lhsT=w_outb[:, k, oc * 128:(oc + 1) * 128], rhs=oT[:, k, :],
                                                                   start=(k == 0), stop=(k == 7)),
                     reads=[b_wout, b_o], writes=[pbb_], inc=(k == 7))
            s.op("dve", lambda e, pbk=pbk, oc=oc: e.scalar_tensor_tensor(out=z[:, oc, :], in0=pbk[:, :], scalar=mod1p[:, 16 + oc:17 + oc],
                                                                         in1=xt[:, oc, :], op0=ALU.mult, op1=ALU.add),
                 reads=[pbb_, xb, b_mod1p], writes=[bz])
        ps1, ps1b = bank()
        ps2, ps2b = bank()
        emit_ln_tile(cx, cst, z, zb_t, zsq_t, ps1[:, :], ps1b, ps2[:, :], ps2b, tmp, N, gT, bT, b_par, z, bz, bzb, bzsq, btmp, bz)
        s.dma("sp", yv[:, :, t0:t0 + N], z[:], reads=[bz], anchor=bz)
    s.final_wait("sp", [bz])
    return cx.end_phase()


class Rot:
    def __init__(self, cx, name, n, shape, dt):
        self.t = [cx.sb("%s%d" % (name, i), shape, dt) for i in range(n)]
        self.b = [cx.buf(name) for i in range(n)]
        self.i = 0

    def get(self):
        j = self.i % len(self.t)
        self.i += 1
        return self.t[j], self.b[j]


def load_gate_params(cx, w_a, w_x, b_aT, b_xT, lamT):
    s = cx.s
    g = {}
    g["wa"] = cx.sb("g_wa", [96, 16, 96], BF16)
    g["wx"] = cx.sb("g_wx", [96, 16, 96], BF16)
    g["b_w"] = cx.buf("gw")
    s.dma("pool", g["wa"][:], w_a.rearrange("n i j -> i n j"), writes=[g["b_w"]], anchor=g["b_w"])
    s.dma("pool", g["wx"][:], w_x.rearrange("n i j -> i n j"), writes=[g["b_w"]], anchor=g["b_w"])
    g["ba"] = cx.sb("g_ba", [96, 16], F32)
    g["bx"] = cx.sb("g_bx", [96, 16], F32)
    g["lamc"] = cx.sb("g_lamc", [96, 16], F32)
    g["one1"] = cx.sb("g_one1", [96, 1], F32)
    g["b_p"] = cx.buf("gp")
    s.dma("sp", g["ba"][:], b_aT[:, :], writes=[g["b_p"]], anchor=g["b_p"])
    s.dma("sp", g["bx"][:], b_xT[:, :], writes=[g["b_p"]], anchor=g["b_p"])
    s.dma("sp", g["lamc"][:], lamT[:, :], writes=[g["b_p"]], anchor=g["b_p"])
    s.op("pool", lambda e: e.memset(g["one1"][:], 1.0), writes=[g["b_p"]])
    s.op("act", lambda e: e.activation(out=g["lamc"][:], in_=g["lamc"][:], func=AF.Exp, scale=-1.0), reads=[g["b_p"]], writes=[g["b_p"]])
    s.op("act", lambda e: e.activation(out=g["lamc"][:], in_=g["lamc"][:], func=AF.Ln, bias=g["one1"][:, 0:1]), reads=[g["b_p"]], writes=[g["b_p"]])
    s.op("pool", lambda e: e.tensor_scalar(out=g["lamc"][:], in0=g["lamc"][:], scalar1=-4.0, scalar2=None, op0=ALU.mult),
         reads=[g["b_p"]], writes=[g["b_p"]])
    g["lam2"] = cx.sb("g_lam2", [96, 16], F32)
    s.op("pool", lambda e: e.tensor_scalar(out=g["lam2"][:], in0=g["lamc"][:], scalar1=2.0, scalar2=None, op0=ALU.mult),
         reads=[g["b_p"]], writes=[g["b_p"]])
    s.op("pool", lambda e: e.tensor_scalar(out=g["ba"][:], in0=g["ba"][:], scalar1=0.5, scalar2=None, op0=ALU.mult),
         reads=[g["b_p"]], writes=[g["b_p"]])
    s.op("pool", lambda e: e.tensor_scalar(out=g["bx"][:], in0=g["bx"][:], scalar1=0.5, scalar2=None, op0=ALU.mult),
         reads=[g["b_p"]], writes=[g["b_p"]])
    return g


def emit_gates(cx, g, n, c_t, c_b, cb_t, cb_b, bank_fn, pools, N):
    s = cx.s
    pbk, pbb_ = bank_fn()
    s.op("pe", lambda e: e.matmul(pbk[0:96, 0:N], lhsT=g["wa"][:, n, :], rhs=cb_t[:], start=True, stop=True),
         reads=[g["b_w"], cb_b], writes=[pbb_])
    s.op("pe", lambda e: e.matmul(pbk[0:96, N:2 * N], lhsT=g["wx"][:, n, :], rhs=cb_t[:], start=True, stop=True),
         reads=[g["b_w"], cb_b], writes=[pbb_])
    r_t, r_b = pools["r"].get()
    i_t, i_b = pools["i"].get()
    a_t, a_b = pools["a"].get()
    m_t, m_b = pools["m"].get()
    u_t, u_b = pools["u"].get()
    s.op("act", lambda e: e.activation(out=r_t[:], in_=pbk[0:96, 0:N], func=AF.Sigmoid, bias=g["ba"][:, n:n + 1]),
         reads=[pbb_, g["b_p"]], writes=[r_b])
    s.op("act", lambda e: e.activation(out=i_t[:], in_=pbk[0:96, N:2 * N], func=AF.Sigmoid, bias=g["bx"][:, n:n + 1]),
         reads=[pbb_, g["b_p"]], writes=[i_b])
    s.op("act", lambda e: e.activation(out=a_t[:], in_=r_t[:], func=AF.Exp, scale=g["lamc"][:, n:n + 1]),
         reads=[r_b, g["b_p"]], writes=[a_b])
    s.op("dve", lambda e: e.tensor_tensor(out=m_t[:], in0=a_t[:], in1=a_t[:], op=ALU.mult), reads=[a_b], writes=[m_b])
    s.op("dve", lambda e: e.tensor_scalar(out=m_t[:], in0=m_t[:], scalar1=-1.0, scalar2=1.0, op0=ALU.mult, op1=ALU.add),
         reads=[m_b], writes=[m_b])
    s.op("act", lambda e: e.activation(out=m_t[:], in_=m_t[:], func=AF.Sqrt), reads=[m_b], writes=[m_b])
    s.op("dve", lambda e: e.tensor_tensor(out=u_t[:], in0=i_t[:], in1=c_t, op=ALU.mult), reads=[i_b, c_b], writes=[u_b])
    s.op("dve", lambda e: e.tensor_tensor(out=u_t[:], in0=u_t[:], in1=m_t[:], op=ALU.mult), reads=[u_b, m_b], writes=[u_b])
    return a_t, a_b, u_t, u_b


def gate_pools(cx, N):
    return {k: Rot(cx, "gp_" + k, 2, [96, N], F32) for k in ("r", "i", "a", "m", "u")}


def make_banks(cx, nb):
    banks = [cx.ps("bank%d" % i) for i in range(nb)]
    bank_b = [cx.buf("bank") for i in range(nb)]
    ctr = [0]

    def bank():
        i = ctr[0] % nb
        ctr[0] += 1
        return banks[i], bank_b[i]
    return bank


def run_pipeline(items, stages):
    S = len(stages)
    for step in range(len(items) + S - 1):
        for k in range(S - 1, -1, -1):
            i = step - k
            if 0 <= i < len(items):
                stages[k](items[i])


class RotBanks:
    def __init__(self, cx, name, n):
        self.t = [cx.ps("%s%d" % (name, i)) for i in range(n)]
        self.b = [cx.buf(name) for i in range(n)]
        self.i = 0

    def get(self):
        j = self.i % len(self.t)
        self.i += 1
        return self.t[j], self.b[j]


SQB = 4


def gate_stage_fns(cx, g, N, RI, P, items_ref):
    s = cx.s

    def st_gmm(it):
        n = it["n"]
        cb_t, cb_b = P["cb"].get()
        c_t = it["c_t"]
        s.op("pool", lambda e: e.tensor_copy(out=cb_t[:], in_=c_t[:]), reads=[it["c_b"]], writes=[cb_b])
        pbk, pbb_ = RI.get()
        it["ri"], it["ri_b"] = pbk, pbb_
        s.op("pe", lambda e: e.matmul(pbk[0:96, 0:N], lhsT=g["wa"][:, n, :], rhs=cb_t[:], start=True, stop=True),
             reads=[g["b_w"], cb_b], writes=[pbb_])
        s.op("pe", lambda e: e.matmul(pbk[0:96, N:2 * N], lhsT=g["wx"][:, n, :], rhs=cb_t[:], start=True, stop=True),
             reads=[g["b_w"], cb_b], writes=[pbb_])

    def st_sig(it):
        n = it["n"]
        pbk, pbb_ = it["ri"], it["ri_b"]
        r_t, r_b = P["r"].get()
        i_t, i_b = P["i"].get()
        a_t, a_b = P["a"].get()
        m_t, m_b = P["m"].get()
        it.update(i_t=i_t, i_b=i_b, a_t=a_t, a_b=a_b, m_t=m_t, m_b=m_b)
        s.op("act", lambda e: e.activation(out=r_t[:], in_=pbk[0:96, 0:N], func=AF.Tanh, bias=g["ba"][:, n:n + 1], scale=0.5),
             reads=[pbb_, g["b_p"]], writes=[r_b])
        s.op("act", lambda e: e.activation(out=i_t[:], in_=pbk[0:96, N:2 * N], func=AF.Tanh, bias=g["bx"][:, n:n + 1], scale=0.5),
             reads=[pbb_, g["b_p"]], writes=[i_b])
        s.op("act", lambda e: e.activation(out=a_t[:], in_=r_t[:], func=AF.Exp, bias=g["lamc"][:, n:n + 1], scale=g["lamc"][:, n:n + 1]),
             reads=[r_b, g["b_p"]], writes=[a_b])
        s.op("act", lambda e: e.activation(out=m_t[:], in_=r_t[:], func=AF.Exp, bias=g["lam2"][:, n:n + 1], scale=g["lam2"][:, n:n + 1]),
             reads=[r_b, g["b_p"]], writes=[m_b])

    def st_m(it):
        i_t, i_b = it["i_t"], it["i_b"]
        m_t, m_b = it["m_t"], it["m_b"]
        u_t, u_b = P["u"].get()
        it.update(u_t=u_t, u_b=u_b)
        c_t = it["c_t"]
        s.op("pool", lambda e: e.tensor_scalar(out=m_t[:], in0=m_t[:], scalar1=-1.0, scalar2=1.0, op0=ALU.mult, op1=ALU.add),
             reads=[m_b], writes=[m_b])
        s.op("dve", lambda e: e.scalar_tensor_tensor(out=u_t[:], in0=i_t[:], scalar=1.0, in1=c_t[:], op0=ALU.add, op1=ALU.mult),
             reads=[i_b, it["c_b"]], writes=[u_b])

    def st_sqrt(it):
        if it["idx"] % SQB != SQB - 1:
            return
        for j in range(it["idx"] - SQB + 1, it["idx"] + 1):
            m_t, m_b = items_ref[j]["m_t"], items_ref[j]["m_b"]
            s.op("act", lambda e, m_t=m_t: e.activation(out=m_t[:], in_=m_t[:], func=AF.Sqrt), reads=[m_b], writes=[m_b])

    return st_gmm, st_sig, st_m, st_sqrt


def build_rnn1(cx=None, prefix="", over=None):
    cx = cx or Ctx()
    cx.begin_phase(prefix, over)
    s = cx.s
    N = 256
    NT = T // N
    NX = N + 2
    xT = cx.dram_in("xT", [D, T + 2])
    c_col = cx.dram_in("c_col", [128, 8])
    ada_w = cx.dram_in("ada_w", [D, 3 * D])
    ada_bT = cx.dram_in("ada_bT", [128, 24])
    w_in = cx.dram_in("w_in", [D, 2 * DRNN])
    convT = cx.dram_in("convT", [96, 16 * 6])
    w_a = cx.dram_in("w_a", [16, 96, 96])
    w_x = cx.dram_in("w_x", [16, 96, 96])
    b_aT = cx.dram_in("b_aT", [96, 16])
    b_xT = cx.dram_in("b_xT", [96, 16])
    lamT = cx.dram_in("lamT", [96, 16])
    carry_only = bool(cx.over.get("carry_only"))
    if not carry_only:
        cT = cx.dram_out("cT", [DRNN, T])
        GT = cx.dram_out("GT", [DRNN, T])
        h1T = cx.dram_out("h1T", [DRNN, T])
        cv = cT.rearrange("(n p) t -> p n t", p=96)
        Gv = GT.rearrange("(n p) t -> p n t", p=96)
        hv = h1T.rearrange("(n p) t -> p n t", p=96)

    XB = RotBanks(cx, "pX", 3)
    GB = RotBanks(cx, "pG", 2)
    RI = RotBanks(cx, "pRI", 2)
    misc = cx.ps("pmisc")
    misc_b = cx.buf("pmisc")
    zt = cx.sb("zt", [128, 8, N], F32)
    bzt = cx.buf("zt")
    pieces = [zt[:, 0:4, :].rearrange("p a (b n) -> p (a b) n", n=128), zt[:, 4:8, :].rearrange("p a (b n) -> p (a b) n", n=128)]
    ada = emit_adaln(cx, c_col, ada_w, ada_bT, misc, misc_b, pieces, [cx.buf("pc0"), cx.buf("pc1")], [bzt])
    mod, mod1p, b_mod, b_mod1p = ada["mod"], ada["mod1p"], ada["b_mod"], ada["b_mod1p"]

    w_inb = cx.sb("w_inb", [128, 8, 2 * DRNN], BF16)
    b_win = cx.buf("win")
    winv = w_in.rearrange("(k p) n -> p k n", p=128)
    for k in range(8):
        s.dma("pool", w_inb[:, k, :], winv[:, k, :], writes=[b_win], anchor=b_win)
    g = load_gate_params(cx, w_a, w_x, b_aT, b_xT, lamT)
    cw = cx.sb("cw", [96, 16 * 6], F32)
    b_cw = cx.buf("cw")
    s.dma("sp", cw[:], convT[:, :], writes=[b_cw], anchor=b_cw)

    xv = xT.rearrange("(k p) t -> p k t", p=128)
    xt = Rot(cx, "xt", 2, [128, 8, NX], F32)
    htp = Rot(cx, "ht", 2, [128, 8, NX], BF16)
    tail = cx.sb("tail", [96, 16, 2], F32)
    b_tail = cx.buf("tail")
    s.op("pool", lambda e: e.memset(tail[:], 0.0), writes=[b_tail])
    carry = cx.sb("carry", [96, 16], F32)
    b_carry = cx.buf("carry")
    s.op("pool", lambda e: e.memset(carry[:], 0.0), writes=[b_carry])
    P = {"xr": Rot(cx, "xrb", 3, [96, NX + 2], F32), "G": Rot(cx, "G", 2, [96, N], F32),
         "c": Rot(cx, "c", 5, [96, N], F32), "ct": Rot(cx, "ct", 2, [96, N], F32), "cb": Rot(cx, "cb", 2, [96, N], BF16),
         "r": Rot(cx, "r", 2, [96, N], F32), "i": Rot(cx, "i", 3, [96, N], F32), "a": Rot(cx, "a", 11, [96, N], F32),
         "m": Rot(cx, "m", 10, [96, N], F32), "u": Rot(cx, "u", 10, [96, N], F32), "h": Rot(cx, "h", 2, [96, N], F32)}
    halo = cx.over.get("halo_sb")
    state = {"nxt": None}

    def load_x(it):
        t_, b_ = xt.get()
        if halo is not None and it == NT - 1:
            s.dma("sp", t_[:, :, 0:N], xv[:, :, it * N:it * N + N], writes=[b_], anchor=b_)
            s.op("pool", lambda e, t_=t_: e.tensor_copy(out=t_[:, :, N:NX], in_=halo[:]), writes=[b_])
        else:
            s.dma("sp", t_[:], xv[:, :, it * N:it * N + NX], writes=[b_], anchor=b_)
        return t_, b_

    state["nxt"] = load_x(0)

    def st_proj(itm):
        it, n = itm["it"], itm["n"]
        if n == 0:
            x_t, xb = state["nxt"]
            if it + 1 < NT:
                state["nxt"] = load_x(it + 1)
            h_t, h_b = htp.get()
            state["ht"] = (h_t, h_b)
            for k in range(8):
                s.op("act", lambda e, k=k: e.activation(out=h_t[:, k, :], in_=x_t[:, k, :], func=AF.Identity,
                                                        bias=mod[:, k:k + 1], scale=mod1p[:, 8 + k:9 + k]),
                     reads=[xb, b_mod, b_mod1p], writes=[h_b])
            if not carry_only:
                for n2 in range(16):
                    pg, pgb = GB.get()
                    for k in range(8):
                        s.op("pe", lambda e, k=k, pg=pg, n2=n2: e.matmul(pg[0:96, 0:N], lhsT=w_inb[:, k, DRNN + n2 * 96:DRNN + (n2 + 1) * 96],
                                                                       rhs=h_t[:, k, 0:N], start=(k == 0), stop=(k == 7)),
                             reads=[b_win, h_b], writes=[pgb], inc=(k == 7))
                    G_t, G_b = P["G"].get()
                    s.op("act", lambda e, pg=pg, G_t=G_t: e.activation(out=G_t[:], in_=pg[0:96, 0:N], func=AF.Gelu), reads=[pgb], writes=[G_b])
                    s.dma("sp", Gv[:, n2, it * N:it * N + N], G_t[:], reads=[G_b], anchor=G_b)
        h_t, h_b = state["ht"]
        pbk, pbb_ = XB.get()
        itm["X"], itm["X_b"] = pbk, pbb_
        for k in range(8):
            s.op("pe", lambda e, k=k: e.matmul(pbk[0:96, 0:NX], lhsT=w_inb[:, k, n * 96:(n + 1) * 96], rhs=h_t[:, k, :],
                                               start=(k == 0), stop=(k == 7)),
                 reads=[b_win, h_b], writes=[pbb_], inc=(k == 7))

    def st_evac(itm):
        it, n = itm["it"], itm["n"]
        t0 = it * N
        xr_t, xr_b = P["xr"].get()
        itm["xr_t"], itm["xr_b"] = xr_t, xr_b
        pbk = itm["X"]
        s.op("act", lambda e: e.activation(out=xr_t[:, 2:NX + 2], in_=pbk[0:96, 0:NX], func=AF.Identity),
             reads=[itm["X_b"]], writes=[xr_b])

    def st_conv(itm):
        it, n = itm["it"], itm["n"]
        t0 = it * N
        xr_t, xr_b = itm["xr_t"], itm["xr_b"]
        s.op("pool", lambda e: e.tensor_copy(out=xr_t[:, 0:2], in_=tail[:, n, :]), reads=[b_tail], writes=[xr_b])
        s.op("pool", lambda e: e.tensor_copy(out=tail[:, n, :], in_=xr_t[:, N:N + 2]), reads=[xr_b], writes=[b_tail])
        c_t, c_b = P["c"].get()
        itm["c_t"], itm["c_b"] = c_t, c_b
        s.op("pool", lambda e: e.tensor_scalar(out=c_t[:], in0=xr_t[:, 0:N], scalar1=cw[:, n * 6:n * 6 + 1],
                                               scalar2=cw[:, n * 6 + 5:n * 6 + 6], op0=ALU.mult, op1=ALU.add),
             reads=[xr_b, b_cw], writes=[c_b])
        for j in range(1, 5):
            s.op("dve", lambda e, j=j: e.scalar_tensor_tensor(out=c_t[:], in0=xr_t[:, j:j + N], scalar=cw[:, n * 6 + j:n * 6 + j + 1],
                                                              in1=c_t[:], op0=ALU.mult, op1=ALU.add),
                 reads=[xr_b, b_cw, c_b], writes=[c_b])
        if not carry_only:
            s.dma("sp", cv[:, n, t0:t0 + N], c_t[:], reads=[c_b], anchor=c_b)

    items = [{"it": it, "n": n, "idx": it * 16 + n} for it in range(NT) for n in range(16)]
    st_gmm, st_sig, st_m, st_sqrt = gate_stage_fns(cx, g, N, RI, P, items)

    def st_scan(itm):
        it, n = itm["it"], itm["n"]
        t0 = it * N
        u_t, u_b, m_t, m_b, a_t, a_b = itm["u_t"], itm["u_b"], itm["m_t"], itm["m_b"], itm["a_t"], itm["a_b"]
        s.op("dve", lambda e: e.scalar_tensor_tensor(out=u_t[:], in0=u_t[:], scalar=0.5, in1=m_t[:], op0=ALU.mult, op1=ALU.mult),
             reads=[u_b, m_b], writes=[u_b])
        h_t, h_b = P["h"].get()
        s.op("dve", lambda e: e.tensor_tensor_scan(out=h_t[:], data0=a_t[:], data1=u_t[:], initial=carry[:, n:n + 1],
                                                   op0=ALU.mult, op1=ALU.add),
             reads=[a_b, u_b, b_carry], writes=[h_b])
        s.op("pool", lambda e: e.tensor_copy(out=carry[:, n:n + 1], in_=h_t[:, N - 1:N]), reads=[h_b], writes=[b_carry])
        if not carry_only:
            s.dma("sp", hv[:, n, t0:t0 + N], h_t[:], reads=[h_b], anchor=h_b)

    nop = lambda itm: None
    run_pipeline(items, [st_proj, st_evac, st_conv, st_gmm, st_sig, st_m, nop, nop, nop, st_sqrt, nop, nop, nop, st_scan])
    if cx.over.get("carry_out") is not None:
        s.dma("sp", cx.over["carry_out"], carry[:], reads=[b_carry], anchor=b_carry)
    s.final_wait("sp", P["c"].b + P["G"].b + P["h"].b + [b_carry])
    return cx.end_phase()


def build_rnn2(cx=None, prefix="", over=None):
    cx = cx or Ctx()
    cx.begin_phase(prefix, over)
    s = cx.s
    N = 256
    NT = T // N
    xT = cx.dram_in("xT", [D, T])
    cT = cx.dram_in("cT", [DRNN, T])
    GT = cx.dram_in("GT", [DRNN, T])
    h1T = cx.dram_in("h1T", [DRNN, T])
    carry_in = None if (over and over.get("carry_sb") is not None) else cx.dram_in("carry_in", [96, 16])
    c_col = cx.dram_in("c_col", [128, 8])
    ada_w = cx.dram_in("ada_w", [D, 3 * D])
    ada_bT = cx.dram_in("ada_bT", [128, 24])
    ln_gT = cx.dram_in("ln_gT", [128, 8])
    ln_bT = cx.dram_in("ln_bT", [128, 8])
    w_a = cx.dram_in("w_a", [16, 96, 96])
    w_x = cx.dram_in("w_x", [16, 96, 96])
    b_aT = cx.dram_in("b_aT", [96, 16])
    b_xT = cx.dram_in("b_xT", [96, 16])
    lamT = cx.dram_in("lamT", [96, 16])
    w_out = cx.dram_in("w_out", [DRNN, D])
    yT = cx.dram_out("yT", [D, T])

    cst = emit_consts(cx)
    RI = RotBanks(cx, "pRI", 3)
    WB = RotBanks(cx, "pW", 4)
    stp = cx.ps("pst")
    st_b = cx.buf("pst")
    z = cx.sb("z", [128, 8, N], F32)
    bz = cx.buf("z")
    pieces = [z[:, 0:4, :].rearrange("p a (b n) -> p (a b) n", n=128), z[:, 4:8, :].rearrange("p a (b n) -> p (a b) n", n=128)]
    ada = emit_adaln(cx, c_col, ada_w, ada_bT, stp, st_b, pieces, [cx.buf("pc0"), cx.buf("pc1")], [bz])
    mod1p, b_mod1p = ada["mod1p"], ada["b_mod1p"]
    gT = cx.sb("gT", [128, 8], F32)
    bT = cx.sb("bT", [128, 8], F32)
    b_par = cx.buf("par")
    s.dma("sp", gT[:], ln_gT[:, :], writes=[b_par], anchor=b_par)
    s.dma("sp", bT[:], ln_bT[:, :], writes=[b_par], anchor=b_par)
    g = load_gate_params(cx, w_a, w_x, b_aT, b_xT, lamT)
    woutb = cx.sb("woutb", [96, 16, D], BF16)
    b_wout = cx.buf("wout")
    s_w = w_out.rearrange("(n p) d -> p n d", p=96)
    for n in range(16):
        s.dma("pool", woutb[:, n, :], s_w[:, n, :], writes=[b_wout], anchor=b_wout)
    carry = cx.sb("carry", [96, 16], F32)
    b_carry = cx.buf("carry")
    if cx.over.get("carry_sb") is not None:
        s.op("pool", lambda e: e.tensor_copy(out=carry[:], in_=cx.over["carry_sb"][:]), writes=[b_carry])
    else:
        s.dma("sp", carry[:], carry_in[:, :], writes=[b_carry], anchor=b_carry)

    xv = xT.rearrange("(k p) t -> p k t", p=128)
    yv = yT.rearrange("(k p) t -> p k t", p=128)
    cv = cT.rearrange("(n p) t -> p n t", p=96)
    Gv = GT.rearrange("(n p) t -> p n t", p=96)
    hv = h1T.rearrange("(n p) t -> p n t", p=96)
    xt = Rot(cx, "xt", 2, [128, 8, N], F32)
    P = {"c": Rot(cx, "c", 5, [96, N], F32), "G": Rot(cx, "G", 4, [96, N], F32), "h1": Rot(cx, "h1", 4, [96, N], F32),
         "cb": Rot(cx, "cb", 2, [96, N], BF16), "r": Rot(cx, "r", 2, [96, N], F32), "i": Rot(cx, "i", 3, [96, N], F32),
         "a": Rot(cx, "a", 11, [96, N], F32), "m": Rot(cx, "m", 10, [96, N], F32), "u": Rot(cx, "u", 10, [96, N], F32),
         "h2": Rot(cx, "h2", 2, [96, N], F32)}
    ytp = Rot(cx, "yt", 2, [96, 16, N], BF16)
    zb_t = cx.sb("zb", [128, 8, N], BF16)
    bzb = cx.buf("zb")
    zsq_t = cx.sb("zsq", [128, 8, N], BF16)
    bzsq = cx.buf("zsq")
    tmp = cx.sb("tmp", [128, 4, N], F32)
    btmp = cx.buf("tmp")
    state = {}

    def st_load(itm):
        it, n = itm["it"], itm["n"]
        t0 = it * N
        if n == 0:
            x_t, xb = xt.get()
            s.dma("sp", x_t[:], xv[:, :, t0:t0 + N], writes=[xb], anchor=xb)
            s.op("act", lambda e: e.activation(out=x_t[:], in_=x_t[:], func=AF.Identity, scale=float(DN_ALPHA)),
                 reads=[xb], writes=[xb])
            state[("x", it)] = (x_t, xb)
            state[("y", it)] = ytp.get()
        c_t, c_b = P["c"].get()
        itm.update(c_t=c_t, c_b=c_b)
        s.dma("sp", c_t[:], cv[:, n, t0:t0 + N], writes=[c_b], anchor=c_b)

    def st_load2(itm):
        it, n = itm["it"], itm["n"]
        t0 = it * N
        G_t, G_b = P["G"].get()
        h1_t, h1_b = P["h1"].get()
        itm.update(G_t=G_t, G_b=G_b, h1_t=h1_t, h1_b=h1_b)
        s.dma("sp", G_t[:], Gv[:, n, t0:t0 + N], writes=[G_b], anchor=G_b)
        s.dma("sp", h1_t[:], hv[:, n, t0:t0 + N], writes=[h1_b], anchor=h1_b)

    items = [{"it": it, "n": n} for it in range(NT - 1, -1, -1) for n in range(16)]
    for j_, itm_ in enumerate(items):
        itm_["idx"] = j_
    st_gmm, st_sig, st_m, st_sqrt = gate_stage_fns(cx, g, N, RI, P, items)

    def st_scan(itm):
        it, n = itm["it"], itm["n"]
        t0 = it * N
        u_t, u_b, m_t, m_b, a_t, a_b = itm["u_t"], itm["u_b"], itm["m_t"], itm["m_b"], itm["a_t"], itm["a_b"]
        h1_t, h1_b, G_t, G_b = itm["h1_t"], itm["h1_b"], itm["G_t"], itm["G_b"]
        y_t, y_b = state[("y", it)]
        s.op("dve", lambda e: e.scalar_tensor_tensor(out=u_t[:], in0=u_t[:], scalar=0.5, in1=m_t[:], op0=ALU.mult, op1=ALU.mult),
             reads=[u_b, m_b], writes=[u_b])
        h2_t, h2_b = P["h2"].get()
        s.op("dve", lambda e: e.tensor_tensor_scan(out=h2_t[:, ::-1], data0=a_t[:, ::-1], data1=u_t[:, ::-1], initial=carry[:, n:n + 1],
                                                   op0=ALU.mult, op1=ALU.add),
             reads=[a_b, u_b, b_carry], writes=[h2_b])
        s.op("pool", lambda e: e.tensor_copy(out=carry[:, n:n + 1], in_=h2_t[:, 0:1]), reads=[h2_b], writes=[b_carry])
        s.op("dve", lambda e: e.tensor_tensor(out=h2_t[:], in0=h2_t[:], in1=h1_t[:], op=ALU.add), reads=[h2_b, h1_b], writes=[h2_b])
        s.op("dve", lambda e: e.tensor_tensor(out=y_t[:, n, :], in0=h2_t[:], in1=G_t[:], op=ALU.mult), reads=[h2_b, G_b], writes=[y_b])
        if n == 15:
            x_t, xb = state[("x", it)]
            for oc in range(8):
                pbk, pbb_ = WB.get()
                for nn in range(16):
                    s.op("pe", lambda e, pbk=pbk, nn=nn, oc=oc: e.matmul(pbk[:, 0:N], lhsT=woutb[:, nn, oc * 128:(oc + 1) * 128], rhs=y_t[:, nn, :],
                                                                         start=(nn == 0), stop=(nn == 15)),
                         reads=[b_wout, y_b], writes=[pbb_], inc=(nn == 15))
                s.op("dve", lambda e, pbk=pbk, oc=oc: e.scalar_tensor_tensor(out=z[:, oc, :], in0=pbk[:, 0:N], scalar=mod1p[:, 16 + oc:17 + oc],
                                                                             in1=x_t[:, oc, :], op0=ALU.mult, op1=ALU.add),
                     reads=[pbb_, xb, b_mod1p], writes=[bz])
            emit_ln_tile(cx, cst, z, zb_t, zsq_t, stp[:, 0:N], st_b, stp[:, N:2 * N], st_b, tmp, N, gT, bT, b_par, z, bz, bzb, bzsq, btmp, bz)
            s.dma("sp", yv[:, :, t0:t0 + N], z[:], reads=[bz], anchor=bz)

    nop = lambda itm: None
    run_pipeline(items, [st_load, st_gmm, st_sig, st_m, nop, nop, nop, st_sqrt, nop, st_load2, nop, st_scan])
    s.final_wait("sp", [bz])
    return cx.end_phase()


def build_xch(cx, prefix, bounce_in, bounce_out, P, F, result_sb, swap2):
    cx.begin_phase(prefix, None)
    s = cx.s
    sel_d = cx.dram_in("sel", [128, 8])
    sel = cx.sb("sel_sb", [128, 8], F32)
    b_sel = cx.buf("sel")
    s.dma("sp", sel[:], sel_d[:, :], writes=[b_sel], anchor=b_sel)
    b_in, b_out = cx.buf("bin"), cx.buf("bout")
    s.collective(bounce_in.ap(), bounce_out.ap(), reads=[b_in], writes=[b_out], anchor=b_out)
    g = cx.sb("xg", [P, 8, F], F32)
    b_g = cx.buf("xg")
    s.dma("sp", g[:], bounce_out.ap().rearrange("(r p) f -> p r f", p=P), reads=[b_out], writes=[b_g], anchor=b_g)
    acc = cx.sb("xacc", [P, F], F32)
    b_acc = cx.buf("xacc")
    s.op("dve", lambda e: e.tensor_scalar(out=acc[:], in0=g[:, 0, :], scalar1=sel[0:P, 0:1], scalar2=None, op0=ALU.mult),
         reads=[b_g, b_sel], writes=[b_acc])
    for r in range(1, 8):
        s.op("dve", lambda e, r=r: e.scalar_tensor_tensor(out=acc[:], in0=g[:, r, :], scalar=sel[0:P, r:r + 1], in1=acc[:],
                                                          op0=ALU.mult, op1=ALU.add),
             reads=[b_g, b_sel, b_acc], writes=[b_acc])
    b_res = cx.buf("res")
    if swap2:
        av = acc[:].rearrange("p (k t) -> p k t", t=2)
        s.op("dve", lambda e: e.tensor_copy(out=result_sb[:, :, 0:1], in_=av[:, :, 1:2]), reads=[b_acc], writes=[b_res])
        s.op("dve", lambda e: e.tensor_copy(out=result_sb[:, :, 1:2], in_=av[:, :, 0:1]), reads=[b_acc], writes=[b_res])
    else:
        s.op("dve", lambda e: e.tensor_copy(out=result_sb[:], in_=acc[:]), reads=[b_acc], writes=[b_res])
    return cx.end_phase()


def build_lxch(cx, prefix, bounce, P, F, result_sb, swap2):
    cx.begin_phase(prefix, None)
    s = cx.s
    acc = cx.sb("xacc", [P, F], F32)
    b_acc = cx.buf("xacc")
    s.dma("sp", acc[:], bounce.ap(), writes=[b_acc], anchor=b_acc)
    b_res = cx.buf("res")
    if swap2:
        av = acc[:].rearrange("p (k t) -> p k t", t=2)
        s.op("dve", lambda e: e.tensor_copy(out=result_sb[:, :, 0:1], in_=av[:, :, 1:2]), reads=[b_acc], writes=[b_res])
        s.op("dve", lambda e: e.tensor_copy(out=result_sb[:, :, 1:2], in_=av[:, :, 0:1]), reads=[b_acc], writes=[b_res])
    else:
        s.op("dve", lambda e: e.tensor_copy(out=result_sb[:], in_=acc[:]), reads=[b_acc], writes=[b_res])
    return cx.end_phase()


def build_fused2():
    cx = Ctx(fused=True)
    tmp = lambda n, sh: cx.dram_tmp(n, sh).ap()
    x0s, x1s, x0o, x1o = tmp("x0s", [D, T]), tmp("x1s", [D, T]), tmp("x0o", [D, T]), tmp("x1o", [D, T])
    cT, GT, h1T, x2 = tmp("cT_s", [DRNN, T]), tmp("GT_s", [DRNN, T]), tmp("h1T_s", [DRNN, T]), tmp("x2", [D, T])
    bh_s, bh_o = cx.dram_tmp("bh_s", [128, 16]), cx.dram_tmp("bh_o", [128, 16])
    bc_o = cx.dram_tmp("bc_o", [96, 16])
    bc_s = cx.dram_tmp("bc_s", [96, 16])
    halo_s = cx.gsb("halo_s", [128, 8, 2], F32)
    halo_o = cx.gsb("halo_o", [128, 8, 2], F32)
    carry_sb = cx.gsb("carry_sb", [96, 16], F32)
    cx.ada_tiles = {k: (cx.gsb("ada_mod_" + k, [128, 24], F32), cx.gsb("ada_mod1p_" + k, [128, 24], F32)) for k in ("00", "01", "10", "11")}
    out = cx.nc.dram_tensor("out", [D, T], F32, kind="ExternalOutput").ap()
    D_ = cx.decl

    def share(src, dst_names):
        return {n: D_[src + n] for n in dst_names}

    build_attn(cx, "a_", {"yT": x0s, "ada_key": "00"})
    ov = share("a_", ["c_col", "ada_w", "ada_bT", "ln_gT", "ln_bT", "w_in", "w_out", "perm", "mprev", "mnext", "sink_rep"])
    ov.update({"yT": x0o, "ada_key": "00"})
    build_attn(cx, "b_", ov)
    build_mlp(cx, "m0_", {"xT": x0s, "yT": x1s, "ada_key": "01",
                           "jobs": [(x0s, x1s, bh_s.ap()), (x0o, x1o, bh_o.ap())]})
    build_lxch(cx, "l1_", bh_o, 128, 16, halo_s, True)
    build_lxch(cx, "l2_", bh_s, 128, 16, halo_o, True)
    build_rnn1(cx, "r1_", {"xT": x1s, "halo_sb": halo_s, "cT": cT, "GT": GT, "h1T": h1T, "carry_out": bc_s.ap(), "ada_key": "10"})
    ov = share("r1_", ["c_col", "ada_w", "ada_bT", "w_in"])
    ov.update({"xT": x1o, "halo_sb": halo_o, "carry_only": True, "carry_out": bc_o.ap(), "ada_key": "10"})
    build_rnn1(cx, "q1_", ov)
    build_lxch(cx, "l3_", bc_o, 96, 16, carry_sb, False)
    ov = share("r1_", ["c_col", "ada_w", "ada_bT"])
    ov.update({"xT": x1s, "cT": cT, "GT": GT, "h1T": h1T, "carry_sb": carry_sb, "yT": x2, "ada_key": "10"})
    build_rnn2(cx, "r2_", ov)
    build_mlp(cx, "m1_", {"xT": x2, "yT": out, "ada_key": "11"})
    cx.gstack.close()
    return cx.nc


def build_fused():
    cx = Ctx(fused=True)
    x0a = cx.dram_tmp("x0a", [D, T]).ap()
    x1 = cx.dram_tmp("x1", [D, T]).ap()
    cT = cx.dram_tmp("cT_s", [DRNN, T]).ap()
    GT = cx.dram_tmp("GT_s", [DRNN, T]).ap()
    h1T = cx.dram_tmp("h1T_s", [DRNN, T]).ap()
    x2 = cx.dram_tmp("x2", [D, T]).ap()
    b1_in = cx.dram_tmp("b1_in", [128, 16])
    b1_out = cx.dram_tmp("b1_out", [8 * 128, 16])
    b2_in = cx.dram_tmp("b2_in", [96, 16])
    b2_out = cx.dram_tmp("b2_out", [8 * 96, 16])
    halo_sb = cx.gsb("halo_sb", [128, 8, 2], F32)
    carry_sb = cx.gsb("carry_sb", [96, 16], F32)
    out = cx.nc.dram_tensor("out", [D, T], F32, kind="ExternalOutput").ap()
    build_attn(cx, "a_", {"yT": x0a})
    build_mlp(cx, "m0_", {"xT": x0a, "yT": x1, "tail_out": b1_in.ap()})
    build_xch(cx, "x1_", b1_in, b1_out, 128, 16, halo_sb, True)
    build_rnn1(cx, "r1_", {"xT": x1, "halo_sb": halo_sb, "cT": cT, "GT": GT, "h1T": h1T, "carry_out": b2_in.ap()})
    build_xch(cx, "x2_", b2_in, b2_out, 96, 16, carry_sb, False)
    build_rnn2(cx, "r2_", {"xT": x1, "cT": cT, "GT": GT, "h1T": h1T, "carry_sb": carry_sb, "yT": x2})
    build_mlp(cx, "m1_", {"xT": x2, "yT": out})
    cx.gstack.close()
    return cx.nc


def colT(v, n):
    return np.ascontiguousarray(np.asarray(v, np.float32).reshape(n, 128).T)


_PROGS = {}


def get_prog(name):
    if name not in _PROGS:
        _PROGS[name] = {"mlp": build_mlp, "attn": build_attn, "rnn1": build_rnn1, "rnn2": build_rnn2, "fused": build_fused, "fused2": build_fused2}[name]()
    return _PROGS[name]


def run_mlp(xT_list, c, ada_w, ada_b, ln_g, ln_b, w1, w2):
    nc = get_prog("mlp")
    in_maps = []
    for core in range(NCORES):
        b = core // 2
        in_maps.append({
            "xT": xT_list[core], "c_col": colT(c[b], 8), "ada_w": np.ascontiguousarray(ada_w),
            "ada_bT": colT(ada_b, 24), "ln_gT": colT(ln_g, 8), "ln_bT": colT(ln_b, 8),
            "w1": np.ascontiguousarray(w1), "w2": np.ascontiguousarray(w2),
        })
    res = run_bass_kernel_spmd(nc, in_maps, core_ids=list(range(NCORES)))
    return [r["yT"] for r in res.results]


ROT = 32
ROPE_THETA = 500000.0


def rope_tables(pos):
    inv_freq = (np.float32(ROPE_THETA) ** (-np.arange(0, ROT, 2, dtype=np.float32) / np.float32(ROT))).astype(np.float32)
    ang = (pos.astype(np.float32)[None, :] * inv_freq[:, None]).astype(np.float32)
    c = np.cos(ang).astype(np.float32)
    sn = np.sin(ang).astype(np.float32)
    return np.ascontiguousarray(np.concatenate([c, c], 0)), np.ascontiguousarray(np.concatenate([-sn, sn], 0))


def local_positions(core):
    half = core % 2
    if half == 0:
        return np.arange(0, T + 128)
    return np.arange(2 * T - 1, T - 129, -1)


def attn_consts():
    perm = np.zeros((32, 32), np.float32)
    for i in range(32):
        perm[(i + 16) % 32, i] = 1.0
    j = np.arange(128)[:, None]
    q = np.arange(128)[None, :]
    mprev = np.tile((j >= q).astype(np.float32), (1, 4))
    mnext = np.tile((j <= q).astype(np.float32), (1, 4))
    return perm, np.ascontiguousarray(mprev), np.ascontiguousarray(mnext)


def run_attn(xTh_list, c, ada_w, ada_b, ln_g, ln_b, w_in, w_out, sinks):
    nc = get_prog("attn")
    perm, mprev, mnext = attn_consts()
    in_maps = []
    for core in range(NCORES):
        b = core // 2
        cosT, sinT = rope_tables(local_positions(core))
        in_maps.append({
            "xT": xTh_list[core], "c_col": colT(c[b], 8), "ada_w": np.ascontiguousarray(ada_w),
            "ada_bT": colT(ada_b, 24), "ln_gT": colT(ln_g, 8), "ln_bT": colT(ln_b, 8),
            "w_in": np.ascontiguousarray(w_in), "w_out": np.ascontiguousarray(w_out),
            "cosT": cosT, "sinT": sinT, "perm": perm, "mprev": mprev, "mnext": mnext,
            "sink_rep": np.ascontiguousarray(np.tile(np.asarray(sinks, np.float32)[None, :], (128, 1))),
        })
    res = run_bass_kernel_spmd(nc, in_maps, core_ids=list(range(NCORES)))
    return [r["yT"] for r in res.results]


def shard_x(x):
    out = []
    for core in range(NCORES):
        b = core // 2
        idx = local_positions(core)
        out.append(np.ascontiguousarray(x[b][idx, :].T))
    return out


def colT96(v):
    return np.ascontiguousarray(np.asarray(v, np.float32).reshape(16, 96).T)


def conv_table(conv_w, conv_b, half):
    taps = np.zeros((5, DRNN), np.float32)
    for j in range(4):
        if half == 0:
            taps[j] = conv_w[j]
        else:
            taps[4 - j] = conv_w[j]
    tab = np.zeros((96, 16, 6), np.float32)
    for j in range(5):
        tab[:, :, j] = taps[j].reshape(16, 96).T
    tab[:, :, 5] = np.asarray(conv_b, np.float32).reshape(16, 96).T
    return np.ascontiguousarray(tab.reshape(96, 96))


def _common(c, ada_w, ada_b, core):
    b = core // 2
    return {"c_col": colT(c[b], 8), "ada_w": np.ascontiguousarray(ada_w), "ada_bT": colT(ada_b, 24)}


def run_rnn1(x1_list, c, ada_w, ada_b, w_in, conv_w, conv_b, w_a, b_a, w_x, b_x, lam):
    nc = get_prog("rnn1")
    in_maps = []
    for core in range(NCORES):
        half = core % 2
        par = x1_list[core ^ 1]
        xh = np.ascontiguousarray(np.concatenate([x1_list[core], par[:, T - 1:T], par[:, T - 2:T - 1]], axis=1))
        d1 = half
        m = _common(c, ada_w, ada_b, core)
        m.update({"xT": xh, "w_in": np.ascontiguousarray(w_in), "convT": conv_table(conv_w, conv_b, half),
                  "w_a": np.ascontiguousarray(w_a[d1]), "w_x": np.ascontiguousarray(w_x[d1]),
                  "b_aT": colT96(b_a[d1]), "b_xT": colT96(b_x[d1]), "lamT": colT96(lam[d1])})
        in_maps.append(m)
    res = run_bass_kernel_spmd(nc, in_maps, core_ids=list(range(NCORES)))
    return [(r["cT"], r["GT"], r["h1T"]) for r in res.results]


def run_rnn2(x1_list, r1, c, ada_w, ada_b, ln_g, ln_b, w_a, b_a, w_x, b_x, lam, w_out):
    nc = get_prog("rnn2")
    in_maps = []
    for core in range(NCORES):
        half = core % 2
        d2 = 1 - half
        cT, GT, h1T = r1[core]
        carry = colT96(r1[core ^ 1][2][:, T - 1])
        m = _common(c, ada_w, ada_b, core)
        m.update({"xT": x1_list[core], "cT": cT, "GT": GT, "h1T": h1T, "carry_in": carry,
                  "ln_gT": colT(ln_g, 8), "ln_bT": colT(ln_b, 8),
                  "w_a": np.ascontiguousarray(w_a[d2]), "w_x": np.ascontiguousarray(w_x[d2]),
                  "b_aT": colT96(b_a[d2]), "b_xT": colT96(b_x[d2]), "lamT": colT96(lam[d2]),
                  "w_out": np.ascontiguousarray(w_out)})
        in_maps.append(m)
    res = run_bass_kernel_spmd(nc, in_maps, core_ids=list(range(NCORES)))
    return [r["yT"] for r in res.results]


def kernel_unfused(x, c, ada_w, ada_b, ln_g, ln_b, attn_w_in, attn_w_out, attn_sinks,
                   rnn_w_in, rnn_conv_w, rnn_conv_b, rnn_w_a, rnn_b_a, rnn_w_x, rnn_b_x, rnn_lam,
                   rnn_w_out, mlp_w1, mlp_w2):
    f = lambda a: np.asarray(a, np.float32)
    x, c, ada_w, ada_b, ln_g, ln_b = f(x), f(c), f(ada_w), f(ada_b), f(ln_g), f(ln_b)
    xs = shard_x(x)
    a0 = run_attn(xs, c, ada_w[0, 0], ada_b[0, 0], ln_g[0, 0], ln_b[0, 0], f(attn_w_in)[0], f(attn_w_out)[0], f(attn_sinks)[0])
    m0 = run_mlp(a0, c, ada_w[0, 1], ada_b[0, 1], ln_g[0, 1], ln_b[0, 1], f(mlp_w1)[0], f(mlp_w2)[0])
    r1 = run_rnn1(m0, c, ada_w[1, 0], ada_b[1, 0], f(rnn_w_in)[0], f(rnn_conv_w)[0], f(rnn_conv_b)[0],
                  f(rnn_w_a)[0], f(rnn_b_a)[0], f(rnn_w_x)[0], f(rnn_b_x)[0], f(rnn_lam)[0])
    r2 = run_rnn2(m0, r1, c, ada_w[1, 0], ada_b[1, 0], ln_g[1, 0], ln_b[1, 0],
                  f(rnn_w_a)[0], f(rnn_b_a)[0], f(rnn_w_x)[0], f(rnn_b_x)[0], f(rnn_lam)[0], f(rnn_w_out)[0])
    m1 = run_mlp(r2, c, ada_w[1, 1], ada_b[1, 1], ln_g[1, 1], ln_b[1, 1], f(mlp_w1)[1], f(mlp_w2)[1])
    out = np.empty((4, 2 * T, D), np.float32)
    for core in range(NCORES):
        idx = local_positions(core)[:T]
        out[core // 2][idx, :] = m1[core].T
    return out


def kernel(x, c, ada_w, ada_b, ln_g, ln_b, attn_w_in, attn_w_out, attn_sinks,
           rnn_w_in, rnn_conv_w, rnn_conv_b, rnn_w_a, rnn_b_a, rnn_w_x, rnn_b_x, rnn_lam,
           rnn_w_out, mlp_w1, mlp_w2):
    f = lambda a: np.ascontiguousarray(np.asarray(a, np.float32))
    x, c, ada_w, ada_b, ln_g, ln_b = f(x), f(c), f(ada_w), f(ada_b), f(ln_g), f(ln_b)
    attn_w_in, attn_w_out, attn_sinks = f(attn_w_in), f(attn_w_out), f(attn_sinks)
    rnn_w_in, rnn_conv_w, rnn_conv_b, rnn_w_out = f(rnn_w_in), f(rnn_conv_w), f(rnn_conv_b), f(rnn_w_out)
    rnn_w_a, rnn_b_a, rnn_w_x, rnn_b_x, rnn_lam = f(rnn_w_a), f(rnn_b_a), f(rnn_w_x), f(rnn_b_x), f(rnn_lam)
    mlp_w1, mlp_w2 = f(mlp_w1), f(mlp_w2)
    nc = get_prog("fused2")
    xs = shard_x(x)
    perm, mprev, mnext = attn_consts()
    in_maps = []
    for core in range(NCORES):
        b = core // 2
        half = core % 2
        d1, d2 = half, 1 - half
        cosT, sinT = rope_tables(local_positions(core))
        sel = np.zeros((128, 8), np.float32)
        sel[:, core ^ 1] = 1.0
        m = {}

        def ada(pfx, i, j, ln=True):
            m[pfx + "c_col"] = colT(c[b], 8)
            m[pfx + "ada_w"] = ada_w[i, j]
            m[pfx + "ada_bT"] = colT(ada_b[i, j], 24)
            if ln:
                m[pfx + "ln_gT"] = colT(ln_g[i, j], 8)
                m[pfx + "ln_bT"] = colT(ln_b[i, j], 8)

        ada("a_", 0, 0)
        m.update({"a_xT": xs[core], "a_w_in": attn_w_in[0], "a_w_out": attn_w_out[0], "a_cosT": cosT, "a_sinT": sinT,
                  "a_perm": perm, "a_mprev": mprev, "a_mnext": mnext,
                  "a_sink_rep": np.ascontiguousarray(np.tile(attn_sinks[0][None, :], (128, 1)))})
        ada("m0_", 0, 1)
        m.update({"m0_w1": mlp_w1[0], "m0_w2": mlp_w2[0]})
        oc_ = core ^ 1
        oh = oc_ % 2
        cosO, sinO = rope_tables(local_positions(oc_))
        m.update({"b_xT": xs[oc_], "b_cosT": cosO, "b_sinT": sinO})
        m.update({"q1_convT": conv_table(rnn_conv_w[0], rnn_conv_b[0], oh),
                  "q1_w_a": rnn_w_a[0, oh], "q1_w_x": rnn_w_x[0, oh], "q1_b_aT": colT96(rnn_b_a[0, oh]),
                  "q1_b_xT": colT96(rnn_b_x[0, oh]), "q1_lamT": colT96(rnn_lam[0, oh])})
        ada("r1_", 1, 0, ln=False)
        m.update({"r1_w_in": rnn_w_in[0], "r1_convT": conv_table(rnn_conv_w[0], rnn_conv_b[0], half),
                  "r1_w_a": rnn_w_a[0, d1], "r1_w_x": rnn_w_x[0, d1], "r1_b_aT": colT96(rnn_b_a[0, d1]),
                  "r1_b_xT": colT96(rnn_b_x[0, d1]), "r1_lamT": colT96(rnn_lam[0, d1])})
        m["r2_ln_gT"] = colT(ln_g[1, 0], 8)
        m["r2_ln_bT"] = colT(ln_b[1, 0], 8)
        m.update({"r2_w_a": rnn_w_a[0, d2], "r2_w_x": rnn_w_x[0, d2], "r2_b_aT": colT96(rnn_b_a[0, d2]),
                  "r2_b_xT": colT96(rnn_b_x[0, d2]), "r2_lamT": colT96(rnn_lam[0, d2]), "r2_w_out": rnn_w_out[0]})
        ada("m1_", 1, 1)
        m.update({"m1_w1": mlp_w1[1], "m1_w2": mlp_w2[1]})
        in_maps.append({k: np.ascontiguousarray(v) for k, v in m.items()})
    res = run_bass_kernel_spmd(nc, in_maps, core_ids=list(range(NCORES)))
    out = np.empty((4, 2 * T, D), np.float32)
    for core in range(NCORES):
        idx = local_positions(core)[:T]
        out[core // 2][idx, :] = res.results[core]["out"].T
    return out
```

```python
import numpy as np
from contextlib import ExitStack
import concourse.bass as bass
import concourse.mybir as mybir
from concourse.bass_utils import run_bass_kernel_spmd

AF = mybir.ActivationFunctionType
ALU = mybir.AluOpType
F32 = mybir.dt.float32
BF16 = mybir.dt.bfloat16

D = 1024
KC = 8
T = 4096
NCORES = 8
DFF = 4096
DEPTH = 2
DN_ALPHA = (2.0 * DEPTH) ** 0.25
LN_EPS = 1e-5
DRNN = 1536
NBLK = 16
BW = 96
SEM_LIMIT = 3000


class Buf:
    __slots__ = ("name", "w", "r", "dsem", "dcnt")

    def __init__(self, name):
        self.name = name
        self.w = None
        self.r = []
        self.dsem = None
        self.dcnt = 0


class Sched:
    ENG = ("pe", "act", "dve", "pool", "sp")

    def __init__(self, nc, stack):
        self.nc = nc
        self.stack = stack
        self.stream = {e: [] for e in self.ENG}
        self.sem = {e: None for e in self.ENG}
        self.cnt = {e: 0 for e in self.ENG}
        self.seen = {e: {} for e in self.ENG}
        self.nsem = 0
        self.handles = []
        self.dma_bufs = []

    def _newsem(self, tag):
        self.nsem += 1
        h = self.nc.alloc_semaphore(name="%ss%d_%s" % (getattr(self, "prefix", ""), self.nsem, tag))
        self.handles.append(h)
        return h

    def _peek(self, e):
        if self.sem[e] is None or self.cnt[e] >= SEM_LIMIT:
            self.sem[e] = self._newsem(e)
            self.cnt[e] = 0
        return (self.sem[e], self.cnt[e] + 1)

    def _waits(self, e, reads, writes, skip_sem=None):
        need = {}

        def add(t):
            if t is None:
                return
            s, v = t
            k = id(s)
            if k not in need or need[k][1] < v:
                need[k] = (s, v)

        for b in reads:
            add(b.w)
        for b in writes:
            add(b.w)
            for t in b.r:
                add(t)
        out = []
        seen = self.seen[e]
        for k, (s, v) in need.items():
            if e == "pe" and s is self.sem["pe"]:
                continue
            if skip_sem is not None and s is skip_sem:
                continue
            if seen.get(k, 0) >= v:
                continue
            seen[k] = v
            out.append((s, v))
        return out

    def op(self, e, fn, reads=(), writes=(), inc=True):
        waits = self._waits(e, reads, writes)
        tk = self._peek(e)
        if inc:
            self.cnt[e] += 1
        self.stream[e].append((waits, fn, tk if inc else None))
        for b in reads:
            b.r.append(tk)
        for b in writes:
            b.w = tk
            b.r = []
        return tk

    def dma(self, e, out_ap, in_ap, reads=(), writes=(), anchor=None):
        a = anchor
        if a.dsem is None or a.dcnt >= SEM_LIMIT:
            a.dsem = self._newsem("d")
            a.dcnt = 0
            self.dma_bufs.append(a)
        waits = self._waits(e, reads, writes, skip_sem=a.dsem)
        a.dcnt += 16
        tk = (a.dsem, a.dcnt)

        def fn(eng, out_ap=out_ap, in_ap=in_ap):
            return eng.dma_start(out=out_ap, in_=in_ap)

        self.stream[e].append((waits, fn, ("dma", a.dsem)))
        for b in reads:
            b.r.append(tk)
        for b in writes:
            b.w = tk
            b.r = []
        return tk

    def collective(self, in_ap, out_ap, reads=(), writes=(), anchor=None):
        a = anchor
        if a.dsem is None:
            a.dsem = self._newsem("cc")
            a.dcnt = 0
            self.dma_bufs.append(a)
        waits = self._waits("pool", reads, writes, skip_sem=a.dsem)
        a.dcnt += 1
        tk = (a.dsem, a.dcnt)

        def fn(eng, in_ap=in_ap, out_ap=out_ap):
            return eng.collective_compute("AllGather", ALU.bypass, replica_groups=[list(range(NCORES))], ins=[in_ap], outs=[out_ap])

        self.stream["pool"].append((waits, fn, ("cc", a.dsem)))
        for b in reads:
            b.r.append(tk)
        for b in writes:
            b.w = tk
            b.r = []
        return tk

    def barrier(self):
        targets = []
        for e in self.ENG:
            if self.sem[e] is not None and self.cnt[e] > 0:
                targets.append((self.sem[e], self.cnt[e]))
        for a in self.dma_bufs:
            targets.append((a.dsem, a.dcnt))
        for e in self.ENG:
            w = []
            for (s, v) in targets:
                if e == "pe" and s is self.sem["pe"]:
                    continue
                if self.seen[e].get(id(s), 0) >= v:
                    continue
                self.seen[e][id(s)] = v
                w.append((s, v))
            if w:
                self.stream[e].append((w, None, None))

    def final_wait(self, e, bufs):
        w = []
        for b in bufs:
            for t in [b.w] + list(b.r):
                if t is not None:
                    w.append(t)
        self.stream[e].append((w, None, None))

    def emit(self):
        nc = self.nc
        block = self.stack.enter_context(nc.Block())

        def replay(eng, items):
            for waits, fn, tk in items:
                for (s, v) in waits:
                    eng.wait_ge(s, v)
                if fn is None:
                    continue
                ins = fn(eng)
                if tk is None:
                    continue
                if tk[0] == "dma":
                    ins.then_inc(tk[1], 16)
                elif tk[0] == "cc":
                    ins.then_inc(tk[1])
                else:
                    ins.then_inc(tk[0], 1)

        st = self.stream

        @block.sync
        def _(eng):
            replay(eng, st["sp"])

        @block.tensor
        def _(eng):
            replay(eng, st["pe"])

        @block.scalar
        def _(eng):
            replay(eng, st["act"])

        @block.vector
        def _(eng):
            replay(eng, st["dve"])

        @block.gpsimd
        def _(eng):
            replay(eng, st["pool"])


class Ctx:
    def __init__(self, fused=False):
        self.nc = bass.Bass("TRN2", target_bir_lowering=False)
        self.fused = fused
        self.gstack = ExitStack()
        self.nbuf = 0
        self.decl = {}
        self.ada_cache = {}
        self.prefix = ""
        self.over = {}
        self.stack = None
        self.s = None

    def begin_phase(self, prefix="", over=None):
        self.prefix = prefix
        self.over = over or {}
        self.stack = ExitStack()
        self.s = Sched(self.nc, self.stack)
        self.s.prefix = prefix

    def end_phase(self):
        if self.fused:
            self.s.barrier()
        self.s.emit()
        self.stack.close()
        self.nc.all_engine_barrier()
        self.nc.clear_and_free_semaphores(self.s.handles)
        self.nc.all_engine_barrier()
        return self.nc

    def dram_in(self, name, shape, dt=F32):
        if name in self.over:
            return self.over[name]
        ap = self.nc.dram_tensor(self.prefix + name, list(shape), dt, kind="ExternalInput").ap()
        self.decl[self.prefix + name] = ap
        return ap

    def dram_out(self, name, shape, dt=F32):
        if name in self.over:
            return self.over[name]
        return self.nc.dram_tensor(self.prefix + name, list(shape), dt, kind="ExternalOutput").ap()

    def dram_tmp(self, name, shape, dt=F32):
        return self.nc.dram_tensor(name, list(shape), dt)

    def gsb(self, name, shape, dt):
        return self.gstack.enter_context(self.nc.sbuf_tensor(name, list(shape), dt))

    def sb(self, name, shape, dt):
        return self.stack.enter_context(self.nc.sbuf_tensor(self.prefix + name, list(shape), dt))

    def ps(self, name, shape=(128, 512), dt=F32):
        return self.stack.enter_context(self.nc.psum_tensor(self.prefix + name, list(shape), dt))

    def buf(self, name="b"):
        self.nbuf += 1
        return Buf("%s%d" % (name, self.nbuf))


def emit_consts(cx):
    s = cx.s
    c = {}
    c["ones_t"] = cx.sb("ones_t", [128, 128], BF16)
    c["ones_b"] = cx.buf("ones")
    c["eps_t"] = cx.sb("eps_t", [128, 1], F32)
    c["eps_b"] = cx.buf("eps")
    s.op("pool", lambda e: e.memset(c["ones_t"][:], 1.0 / 1024.0), writes=[c["ones_b"]])
    s.op("pool", lambda e: e.memset(c["eps_t"][:], LN_EPS), writes=[c["eps_b"]])
    return c


def emit_adaln(cx, c_col, ada_w, ada_bT, scratch_ps, scratch_ps_b, pieces, piece_bufs, owner_bufs):
    s = cx.s
    key = cx.over.get("ada_key")
    if key is not None and key in cx.ada_cache:
        mod, mod1p = cx.ada_tiles[key]
        return dict(mod=mod, mod1p=mod1p, b_mod=cx.buf("adamod"), b_mod1p=cx.buf("adamod1p"))
    ccol = cx.sb("ada_c", [128, 8], F32)
    csil = cx.sb("ada_cs", [128, 8], F32)
    bT = cx.sb("ada_b", [128, 24], F32)
    if key is not None:
        mod, mod1p = cx.ada_tiles[key]
        cx.ada_cache[key] = True
    else:
        mod = cx.sb("ada_mod", [128, 24], F32)
        mod1p = cx.sb("ada_mod1p", [128, 24], F32)
    b_c, b_cs, b_mod, b_mod1p = cx.buf("adac"), cx.buf("adacs"), cx.buf("adamod"), cx.buf("adamod1p")
    b_bT = b_c
    s.dma("sp", ccol[:], c_col[:, :], writes=[b_c], anchor=b_c)
    s.dma("sp", bT[:], ada_bT[:, :], writes=[b_bT], anchor=b_bT)
    s.op("act", lambda e: e.activation(out=csil[:], in_=ccol[:], func=AF.Silu), reads=[b_c], writes=[b_cs])
    wv = ada_w.rearrange("(k p) n -> p k n", p=128)
    for col in range(24):
        t = pieces[col % 2]
        tb = piece_bufs[col % 2]
        s.dma("sp", t, wv[:, :, col * 128:(col + 1) * 128], writes=[tb], anchor=tb)
        for k in range(8):
            s.op("pe", lambda e, t=t, k=k, col=col: e.matmul(
                scratch_ps[:, col:col + 1], lhsT=t[:, k, :], rhs=csil[:, k:k + 1],
                start=(k == 0), stop=(k == 7)),
                reads=[tb, b_cs], writes=[scratch_ps_b], inc=(k == 7))
    s.op("dve", lambda e: e.tensor_tensor(out=mod[:], in0=scratch_ps[:, 0:24], in1=bT[:], op=ALU.add),
         reads=[scratch_ps_b, b_bT] + list(piece_bufs), writes=[b_mod] + list(owner_bufs))
    s.op("dve", lambda e: e.tensor_scalar_add(out=mod1p[:], in0=mod[:], scalar1=1.0), reads=[b_mod], writes=[b_mod1p])
    return dict(mod=mod, mod1p=mod1p, b_mod=b_mod, b_mod1p=b_mod1p)


def emit_ln_tile(cx, cst, z, zb_t, zsq_t, ps_sum, b_sum, ps_sq, b_sq, tmp, N, gT, bT, b_par, out_t, bz, bzb, bzsq, btmp, bout):
    s = cx.s
    for k in range(8):
        s.op("act", lambda e, k=k: e.activation(out=zb_t[:, k, :], in_=z[:, k, :], func=AF.Identity), reads=[bz], writes=[bzb])
        s.op("act", lambda e, k=k: e.activation(out=zsq_t[:, k, :], in_=z[:, k, :], func=AF.Square), reads=[bz], writes=[bzsq])
    for k in range(8):
        s.op("pe", lambda e, k=k: e.matmul(ps_sum, lhsT=cst["ones_t"][:], rhs=zb_t[:, k, :], start=(k == 0), stop=(k == 7)),
             reads=[cst["ones_b"], bzb], writes=[b_sum], inc=(k == 7))
    for k in range(8):
        s.op("pe", lambda e, k=k: e.matmul(ps_sq, lhsT=cst["ones_t"][:], rhs=zsq_t[:, k, :], start=(k == 0), stop=(k == 7)),
             reads=[cst["ones_b"], bzsq], writes=[b_sq], inc=(k == 7))
    mean = tmp[:, 0, :]
    msq = tmp[:, 1, :]
    var = tmp[:, 2, :]
    rstd = tmp[:, 3, :]
    s.op("dve", lambda e: e.tensor_copy(out=mean, in_=ps_sum), reads=[b_sum], writes=[btmp])
    s.op("dve", lambda e: e.tensor_tensor(out=msq, in0=mean, in1=mean, op=ALU.mult), reads=[btmp], writes=[btmp])
    s.op("dve", lambda e: e.tensor_tensor(out=var, in0=ps_sq, in1=msq, op=ALU.subtract), reads=[b_sq, btmp], writes=[btmp])
    s.op("act", lambda e: e.activation(out=var, in_=var, func=AF.Sqrt, bias=cst["eps_t"][:, 0:1]), reads=[btmp, cst["eps_b"]], writes=[btmp])
    s.op("dve", lambda e: e.reciprocal(out=rstd, in_=var), reads=[btmp], writes=[btmp])
    for k in range(8):
        s.op("dve", lambda e, k=k: e.tensor_tensor(out=z[:, k, :], in0=z[:, k, :], in1=mean, op=ALU.subtract), reads=[bz, btmp], writes=[bz])
        s.op("dve", lambda e, k=k: e.tensor_tensor(out=z[:, k, :], in0=z[:, k, :], in1=rstd, op=ALU.mult), reads=[bz, btmp], writes=[bz])
        s.op("act", lambda e, k=k: e.activation(out=out_t[:, k, :], in_=z[:, k, :], func=AF.Identity,
                                                 bias=bT[:, k:k + 1], scale=gT[:, k:k + 1]),
             reads=[bz, b_par], writes=[bout])


def build_mlp(cx=None, prefix="", over=None):
    cx = cx or Ctx()
    cx.begin_phase(prefix, over)
    s = cx.s
    N = 256
    NT = T // N
    xT = cx.dram_in("xT", [D, T])
    c_col = cx.dram_in("c_col", [128, 8])
    ada_w = cx.dram_in("ada_w", [D, 3 * D])
    ada_bT = cx.dram_in("ada_bT", [128, 24])
    ln_gT = cx.dram_in("ln_gT", [128, 8])
    ln_bT = cx.dram_in("ln_bT", [128, 8])
    w1 = cx.dram_in("w1", [D, DFF])
    w2 = cx.dram_in("w2", [DFF, D])
    yT = cx.dram_out("yT", [D, T])

    cst = emit_consts(cx)
    acc = [cx.ps("acc%d" % i) for i in range(4)]
    acc_b = [cx.buf("acc") for i in range(8)]
    ph = [cx.ps("ph%d" % i) for i in range(3)]
    ph_b = [cx.buf("ph") for i in range(3)]
    stp = cx.ps("stp")
    st_b = cx.buf("st")

    z = cx.sb("z", [128, 8, N], F32)
    bz = cx.buf("z")
    pieces = [z[:, 0:4, :].rearrange("p a (b n) -> p (a b) n", n=128), z[:, 4:8, :].rearrange("p a (b n) -> p (a b) n", n=128)]
    ada = emit_adaln(cx, c_col, ada_w, ada_bT, stp, st_b, pieces, [cx.buf("pc0"), cx.buf("pc1")], [bz])
    mod, mod1p = ada["mod"], ada["mod1p"]
    b_mod, b_mod1p = ada["b_mod"], ada["b_mod1p"]

    gT = cx.sb("gT", [128, 8], F32)
    bT = cx.sb("bT", [128, 8], F32)
    b_par = cx.buf("par")
    s.dma("sp", gT[:], ln_gT[:, :], writes=[b_par], anchor=b_par)
    s.dma("sp", bT[:], ln_bT[:, :], writes=[b_par], anchor=b_par)

    w1b = cx.sb("w1b", [128, 8, DFF], BF16)
    w2b = cx.sb("w2b", [128, 32, D], BF16)
    b_w1x = cx.buf("w1")
    b_w2x = [cx.buf("w2") for k in range(4)]
    b_w1 = [b_w1x for k in range(8)]
    b_w2 = [b_w2x[k // 8] for k in range(32)]
    w1v = w1.rearrange("(k p) n -> p k n", p=128)
    w2v = w2.rearrange("(k p) n -> p k n", p=128)
    for k in range(8):
        s.dma("pool", w1b[:, k, :], w1v[:, k, :], writes=[b_w1[k]], anchor=b_w1[k])
    for k in range(32):
        s.dma("pool", w2b[:, k, :], w2v[:, k, :], writes=[b_w2[k]], anchor=b_w2[k])

    jobs = cx.over.get("jobs") or [(xT, yT, cx.over.get("tail_out"))]
    jobv = [(xj.rearrange("(k p) t -> p k t", p=128), yj.rearrange("(k p) t -> p k t", p=128), tj) for (xj, yj, tj) in jobs]
    NXB = 2
    xt = [cx.sb("xt%d" % i, [128, 8, N], F32) for i in range(NXB)]
    xt_b = [cx.buf("xt") for i in range(NXB)]
    ht = cx.sb("ht", [128, 8, N], BF16)
    ht_b = cx.buf("ht")
    rb = [cx.sb("rb%d" % i, [128, N], F32) for i in range(3)]
    rb_b = [cx.buf("rb") for i in range(3)]
    hid = [cx.sb("hid%d" % i, [128, N], BF16) for i in range(4)]
    hid_b = [cx.buf("hid") for i in range(4)]
    zb_t = cx.sb("zb", [128, 8, N], BF16)
    bzb = cx.buf("zb")
    zsq_t = cx.sb("zsq", [128, 8, N], BF16)
    bzsq = cx.buf("zsq")
    tmp = cx.sb("tmp", [128, 4, N], F32)
    btmp = cx.buf("tmp")

    def load_x(gi):
        xv_ = jobv[gi // NT][0]
        it_ = gi % NT
        s.dma("sp", xt[gi % NXB][:], xv_[:, :, it_ * N:(it_ + 1) * N], writes=[xt_b[gi % NXB]], anchor=xt_b[gi % NXB])

    hts = [ht, cx.sb("ht2", [128, 8, N], BF16)]
    hts_b = [ht_b, cx.buf("ht")]
    NG = NT * len(jobv)

    def modulate(gi):
        x_t, xb = xt[gi % NXB], xt_b[gi % NXB]
        h_t, h_b = hts[gi % 2], hts_b[gi % 2]
        for k in range(8):
            s.op("act", lambda e, k=k: e.activation(out=h_t[:, k, :], in_=x_t[:, k, :], func=AF.Identity,
                                                    bias=mod[:, k:k + 1], scale=mod1p[:, 8 + k:9 + k]),
                 reads=[xb, b_mod, b_mod1p], writes=[h_b])
        s.op("act", lambda e: e.activation(out=x_t[:], in_=x_t[:], func=AF.Identity, scale=float(DN_ALPHA)),
             reads=[xb], writes=[xb])

    def g1(gi, hc):
        p = ph[hc % 3]
        h_t, h_b = hts[gi % 2], hts_b[gi % 2]
        for k in range(8):
            s.op("pe", lambda e, p=p, k=k, hc=hc: e.matmul(p[:, 0:N], lhsT=w1b[:, k, hc * 128:(hc + 1) * 128], rhs=h_t[:, k, :],
                                                          start=(k == 0), stop=(k == 7)),
                 reads=[b_w1[k], h_b], writes=[ph_b[hc % 3]], inc=(k == 7))

    def ew(hc):
        p = ph[hc % 3]
        r = rb[hc % 3]
        h_ = hid[hc % 4]
        s.op("act", lambda e, p=p, r=r: e.activation(out=r[:], in_=p[:, 0:N], func=AF.Relu), reads=[ph_b[hc % 3]], writes=[rb_b[hc % 3]])
        s.op("dve", lambda e, r=r, h_=h_: e.tensor_tensor(out=h_[:], in0=r[:], in1=r[:], op=ALU.mult), reads=[rb_b[hc % 3]], writes=[hid_b[hc % 4]])

    def g2(hc):
        h_ = hid[hc % 4]
        for oc in range(8):
            a = acc[oc // 2]
            s.op("pe", lambda e, a=a, oc=oc, hc=hc, h_=h_: e.matmul(a[:, (oc % 2) * N:(oc % 2 + 1) * N],
                                                                    lhsT=w2b[:, hc, oc * 128:(oc + 1) * 128], rhs=h_[:],
                                                                    start=(hc == 0 and oc % 2 == 0), stop=(hc == 31),
                                                                    skip_group_check=True),
                 reads=[b_w2[hc], hid_b[hc % 4]], writes=[acc_b[oc]], inc=(oc == 7 or hc == 31))

    def head(gi):
        g1(gi, 0)
        g1(gi, 1)
        ew(0)

    def epilogue(gi):
        it = gi % NT
        yv, tail_out = jobv[gi // NT][1], jobv[gi // NT][2]
        x_t, xb = xt[gi % NXB], xt_b[gi % NXB]
        for oc in range(8):
            a = acc[oc // 2]
            s.op("dve", lambda e, a=a, oc=oc: e.scalar_tensor_tensor(
                out=z[:, oc, :], in0=a[:, (oc % 2) * N:(oc % 2 + 1) * N], scalar=mod1p[:, 16 + oc:17 + oc], in1=x_t[:, oc, :],
                op0=ALU.mult, op1=ALU.add), reads=[acc_b[oc], xb, b_mod1p], writes=[bz])
        emit_ln_tile(cx, cst, z, zb_t, zsq_t, stp[:, 0:N], st_b, stp[:, N:2 * N], st_b, tmp, N, gT, bT, b_par, z, bz, bzb, bzsq, btmp, bz)
        s.dma("sp", yv[:, :, it * N:(it + 1) * N], z[:], reads=[bz], anchor=bz)
        if it == NT - 1 and tail_out is not None:
            s.dma("sp", tail_out.rearrange("p (k t) -> p k t", t=2), z[:, :, N - 2:N], reads=[bz], anchor=bz)

    load_x(0)
    if NG > 1:
        load_x(1)
    modulate(0)
    head(0)
    for gi in range(NG):
        for hc in range(32):
            if hc + 2 < 32:
                g1(gi, hc + 2)
            if hc + 1 < 32:
                ew(hc + 1)
            g2(hc)
            if hc == 27 and gi + 1 < NG:
                modulate(gi + 1)
        if gi + 1 < NG:
            head(gi + 1)
        epilogue(gi)
        if gi + 2 < NG:
            load_x(gi + 2)
    s.final_wait("sp", [bz])
    return cx.end_phase()


TH = T + 128


def build_attn(cx=None, prefix="", over=None):
    cx = cx or Ctx()
    cx.begin_phase(prefix, over)
    s = cx.s
    N = 512
    NT = T // N
    SCALE = 128.0 ** -0.5
    xT = cx.dram_in("xT", [D, TH])
    c_col = cx.dram_in("c_col", [128, 8])
    ada_w = cx.dram_in("ada_w", [D, 3 * D])
    ada_bT = cx.dram_in("ada_bT", [128, 24])
    ln_gT = cx.dram_in("ln_gT", [128, 8])
    ln_bT = cx.dram_in("ln_bT", [128, 8])
    w_in = cx.dram_in("w_in", [D, 1536])
    w_out = cx.dram_in("w_out", [D, D])
    cosT = cx.dram_in("cosT", [32, TH])
    sinT = cx.dram_in("sinT", [32, TH])
    perm = cx.dram_in("perm", [32, 32])
    mprev = cx.dram_in("mprev", [128, 512])
    mnext = cx.dram_in("mnext", [128, 512])
    sink_rep = cx.dram_in("sink_rep", [128, 8])
    yT = cx.dram_out("yT", [D, T])

    cst = emit_consts(cx)
    one_t = cx.sb("one_t", [128, 128], BF16)
    one_b = cx.buf("one")
    s.op("pool", lambda e: e.memset(one_t[:], 1.0), writes=[one_b])

    banks = [cx.ps("bank%d" % i) for i in range(8)]
    bank_b = [cx.buf("bank") for i in range(8)]
    bctr = [0]

    def bank():
        i = bctr[0] % 8
        bctr[0] += 1
        return banks[i], bank_b[i]

    z = cx.sb("z", [128, 8, N], F32)
    bz = cx.buf("z")
    pieces = [z[:, 0:2, :].rearrange("p a (b n) -> p (a b) n", n=128), z[:, 2:4, :].rearrange("p a (b n) -> p (a b) n", n=128)]
    pb, pbb = bank()
    ada = emit_adaln(cx, c_col, ada_w, ada_bT, pb, pbb, pieces, [cx.buf("pc0"), cx.buf("pc1")], [bz])
    mod, mod1p = ada["mod"], ada["mod1p"]
    b_mod, b_mod1p = ada["b_mod"], ada["b_mod1p"]

    gT = cx.sb("gT", [128, 8], F32)
    bT = cx.sb("bT", [128, 8], F32)
    b_par = cx.buf("par")
    s.dma("sp", gT[:], ln_gT[:, :], writes=[b_par], anchor=b_par)
    s.dma("sp", bT[:], ln_bT[:, :], writes=[b_par], anchor=b_par)

    perm_t = cx.sb("perm_t", [32, 32], BF16)
    mprev_t = cx.sb("mprev_t", [128, 512], BF16)
    mnext_t = cx.sb("mnext_t", [128, 512], BF16)
    b_cm = cx.buf("cm")
    s.dma("pool", perm_t[:], perm[:, :], writes=[b_cm], anchor=b_cm)
    s.dma("pool", mprev_t[:], mprev[:, :], writes=[b_cm], anchor=b_cm)
    s.dma("pool", mnext_t[:], mnext[:, :], writes=[b_cm], anchor=b_cm)
    sk = cx.sb("sk", [128, 8], F32)
    b_sk = cx.buf("sk")
    s.dma("sp", sk[:], sink_rep[:, :], writes=[b_sk], anchor=b_sk)
    s.op("act", lambda e: e.activation(out=sk[:], in_=sk[:], func=AF.Exp), reads=[b_sk], writes=[b_sk])
    es_full = cx.sb("es_full", [128, 2, 512], F32)
    b_es = cx.buf("es")
    s.op("pool", lambda e: e.memset(es_full[:], 0.0), writes=[b_es])
    for hh in range(8):
        s.op("pool", lambda e, hh=hh: e.tensor_scalar(out=es_full[:, hh // 4, (hh % 4) * 128:(hh % 4 + 1) * 128],
                                                       in0=es_full[:, hh // 4, (hh % 4) * 128:(hh % 4 + 1) * 128],
                                                       scalar1=sk[:, hh:hh + 1], scalar2=None, op0=ALU.add),
             reads=[b_sk, b_es], writes=[b_es])

    w_inb = cx.sb("w_inb", [128, 8, 1536], BF16)
    w_outb = cx.sb("w_outb", [128, 8, D], BF16)
    b_win = cx.buf("win")
    b_wout = cx.buf("wout")
    winv = w_in.rearrange("(k p) n -> p k n", p=128)
    woutv = w_out.rearrange("(k p) n -> p k n", p=128)
    for k in range(8):
        s.dma("pool", w_inb[:, k, :], winv[:, k, :], writes=[b_win], anchor=b_win)
    for k in range(8):
        s.dma("pool", w_outb[:, k, :], woutv[:, k, :], writes=[b_wout], anchor=b_wout)

    kT_all = cx.sb("kT_all", [128, 2, TH], BF16)
    v_all = cx.sb("v_all", [128, TH // 128, 256], BF16)
    b_k = [cx.buf("k") for i in range(NT + 1)]
    b_v = [cx.buf("v") for i in range(NT + 1)]

    xv = xT.rearrange("(k p) t -> p k t", p=128)
    yv = yT.rearrange("(k p) t -> p k t", p=128)
    xt = cx.sb("xt", [128, 8, N], F32)
    xb = cx.buf("xt")
    ht = cx.sb("ht", [128, 8, N], BF16)
    ht_b = cx.buf("ht")
    cs_t = cx.sb("cs_t", [32, 2, N], F32)
    b_cs = cx.buf("cs")
    rtmp = cx.sb("rtmp", [32, 2, N], F32)
    b_rtmp = cx.buf("rtmp")
    qtmp = cx.sb("qtmp", [128, N], BF16)
    b_qtmp = cx.buf("qtmp")
    qT = cx.sb("qT", [128, 4, 8, 128], BF16)
    b_q = cx.buf("q")
    pT = [cx.sb("pT%d" % i, [128, 512], BF16) for i in range(6)]
    pT_b = [cx.buf("pT") for i in range(6)]
    pctr = [0]
    oT = cx.sb("oT", [128, 8, N], BF16)
    b_o = cx.buf("o")
    zb_t = cx.sb("zb", [128, 8, N], BF16)
    bzb = cx.buf("zb")
    zsq_t = cx.sb("zsq", [128, 8, N], BF16)
    bzsq = cx.buf("zsq")
    tmp = cx.sb("tmp", [128, 4, N], F32)
    btmp = cx.buf("tmp")

    def load_tile(t0, n):
        s.dma("sp", xt[:, :, 0:n], xv[:, :, t0:t0 + n], writes=[xb], anchor=xb)
        s.dma("sp", cs_t[:, 0, 0:n], cosT[:, t0:t0 + n], writes=[b_cs], anchor=b_cs)
        s.dma("sp", cs_t[:, 1, 0:n], sinT[:, t0:t0 + n], writes=[b_cs], anchor=b_cs)
        for k in range(8):
            s.op("act", lambda e, k=k, n=n: e.activation(out=ht[:, k, 0:n], in_=xt[:, k, 0:n], func=AF.Identity,
                                                         bias=mod[:, k:k + 1], scale=mod1p[:, 8 + k:9 + k]),
                 reads=[xb, b_mod, b_mod1p], writes=[ht_b])

    def rope(view32, vb, n):
        pbk, pbb_ = bank()
        s.op("pe", lambda e: e.matmul(pbk[0:32, 0:n], lhsT=perm_t[:, :], rhs=view32, start=True, stop=True),
             reads=[b_cm, vb], writes=[pbb_])
        s.op("dve", lambda e: e.tensor_tensor(out=rtmp[:, 0, 0:n], in0=pbk[0:32, 0:n], in1=cs_t[:, 1, 0:n], op=ALU.mult),
             reads=[pbb_, b_cs], writes=[b_rtmp])
        s.op("dve", lambda e: e.tensor_tensor(out=rtmp[:, 1, 0:n], in0=view32, in1=cs_t[:, 0, 0:n], op=ALU.mult),
             reads=[vb, b_cs], writes=[b_rtmp])
        s.op("dve", lambda e: e.tensor_tensor(out=view32, in0=rtmp[:, 0, 0:n], in1=rtmp[:, 1, 0:n], op=ALU.add),
             reads=[b_rtmp], writes=[vb])

    for it in range(NT + 1):
        t0 = it * N
        n = N if it < NT else 128
        load_tile(t0, n)
        for kvh in range(2):
            pbk, pbb_ = bank()
            for k in range(8):
                s.op("pe", lambda e, pbk=pbk, k=k, kvh=kvh, n=n: e.matmul(pbk[:, 0:n], lhsT=w_inb[:, k, 1024 + kvh * 128:1152 + kvh * 128],
                                                                          rhs=ht[:, k, 0:n], start=(k == 0), stop=(k == 7)),
                     reads=[b_win, ht_b], writes=[pbb_], inc=(k == 7))
            s.op("act", lambda e, pbk=pbk, kvh=kvh, t0=t0, n=n: e.activation(out=kT_all[:, kvh, t0:t0 + n], in_=pbk[:, 0:n], func=AF.Identity),
                 reads=[pbb_], writes=[b_k[it]])
            rope(kT_all[0:32, kvh, t0:t0 + n], b_k[it], n)
        for blk in range(n // 128):
            pbk, pbb_ = bank()
            for k in range(8):
                s.op("pe", lambda e, pbk=pbk, k=k, blk=blk: e.matmul(pbk[:, 0:256], lhsT=ht[:, k, blk * 128:(blk + 1) * 128],
                                                                     rhs=w_inb[:, k, 1280:1536], start=(k == 0), stop=(k == 7)),
                     reads=[b_win, ht_b], writes=[pbb_], inc=(k == 7))
            s.op("act", lambda e, pbk=pbk, blk=blk, t0=t0: e.activation(out=v_all[:, t0 // 128 + blk, :], in_=pbk[:, 0:256], func=AF.Identity),
                 reads=[pbb_], writes=[b_v[it]])

    class _RV:
        def __init__(self, ts, bs):
            self.t, self.b, self.i = ts, bs, 0

        def get(self):
            j = self.i % len(self.t)
            self.i += 1
            return self.t[j], self.b[j]

    SCB = _RV(banks[0:4], bank_b[0:4])
    PVB = _RV(banks[4:8], bank_b[4:8])
    PTP = _RV(pT, pT_b)
    DENP = Rot(cx, "den", 2, [128, 512], F32)
    for it in range(NT):
        t0 = it * N
        load_tile(t0, N)
        for hd in range(8):
            pbk, pbb_ = bank()
            for k in range(8):
                s.op("pe", lambda e, pbk=pbk, k=k, hd=hd: e.matmul(pbk[:, 0:N], lhsT=w_inb[:, k, hd * 128:(hd + 1) * 128],
                                                                   rhs=ht[:, k, :], start=(k == 0), stop=(k == 7)),
                     reads=[b_win, ht_b], writes=[pbb_], inc=(k == 7))
            s.op("act", lambda e, pbk=pbk: e.activation(out=qtmp[:], in_=pbk[:, 0:N], func=AF.Identity),
                 reads=[pbb_], writes=[b_qtmp])
            rope(qtmp[0:32, :], b_qtmp, N)
            s.op("pool", lambda e, hd=hd: e.tensor_copy(out=qT[:, :, hd, :], in_=qtmp[:].rearrange("p (a n) -> p a n", n=128)),
                 reads=[b_qtmp], writes=[b_q])
        s.op("act", lambda e: e.activation(out=xt[:], in_=xt[:], func=AF.Identity, scale=float(DN_ALPHA)),
             reads=[xb], writes=[xb])
        def a_s0(itm, it=it):
            qb, kvh = itm["qb"], itm["kvh"]
            B = it * 4 + qb
            qrhs = qT[:, qb, kvh * 4:(kvh + 1) * 4, :].rearrange("p h n -> p (h n)")
            itm["sc"] = []
            for jb in (B - 1, B, B + 1):
                if jb < 0:
                    continue
                pbk, pbb_ = SCB.get()
                kb = b_k[min(jb // 4, NT)]
                s.op("pe", lambda e, pbk=pbk, jb=jb: e.matmul(pbk[:, :], lhsT=kT_all[:, kvh, jb * 128:(jb + 1) * 128], rhs=qrhs,
                                                              start=True, stop=True),
                     reads=[kb, b_q], writes=[pbb_])
                itm["sc"].append((jb, B, pbk, pbb_))

        def a_s1(itm):
            itm["pts"] = []
            for (jb, B, pbk, pbb_) in itm["sc"]:
                p_t, p_b = PTP.get()
                s.op("act", lambda e, pbk=pbk, p_t=p_t: e.activation(out=p_t[:], in_=pbk[:, :], func=AF.Exp, scale=SCALE),
                     reads=[pbb_], writes=[p_b])
                if jb == B - 1:
                    s.op("pool", lambda e, p_t=p_t: e.tensor_tensor(out=p_t[:], in0=p_t[:], in1=mprev_t[:], op=ALU.mult),
                         reads=[p_b, b_cm], writes=[p_b])
                elif jb == B + 1:
                    s.op("pool", lambda e, p_t=p_t: e.tensor_tensor(out=p_t[:], in0=p_t[:], in1=mnext_t[:], op=ALU.mult),
                         reads=[p_b, b_cm], writes=[p_b])
                itm["pts"].append((jb, p_t, p_b))

        def a_s2(itm):
            kvh = itm["kvh"]
            pts = itm["pts"]
            po, pob = PVB.get()
            pd, pdb = PVB.get()
            itm.update(po=po, pob=pob, pd=pd, pdb=pdb)
            for i, (jb, p_t, p_b) in enumerate(pts):
                vb = b_v[min(jb // 4, NT)]
                s.op("pe", lambda e, jb=jb, p_t=p_t, i=i: e.matmul(po[:, :], lhsT=v_all[:, jb, kvh * 128:(kvh + 1) * 128],
                                                                   rhs=p_t[:], start=(i == 0), stop=(i == len(pts) - 1)),
                     reads=[vb, p_b], writes=[pob], inc=(i == len(pts) - 1))
            for i, (jb, p_t, p_b) in enumerate(pts):
                s.op("pe", lambda e, p_t=p_t, i=i: e.matmul(pd[:, :], lhsT=one_t[:], rhs=p_t[:], start=(i == 0), stop=(i == len(pts) - 1)),
                     reads=[one_b, p_b], writes=[pdb], inc=(i == len(pts) - 1))

        def a_s3(itm):
            qb, kvh = itm["qb"], itm["kvh"]
            po, pob, pd, pdb = itm["po"], itm["pob"], itm["pd"], itm["pdb"]
            den, den_b = DENP.get()
            s.op("dve", lambda e: e.tensor_tensor(out=den[:], in0=pd[:, :], in1=es_full[:, kvh, :], op=ALU.add),
                 reads=[pdb, b_es], writes=[den_b])
            s.op("dve", lambda e: e.reciprocal(out=den[:], in_=den[:]), reads=[den_b], writes=[den_b])
            s.op("dve", lambda e: e.tensor_tensor(
                out=oT[:, kvh * 4:(kvh + 1) * 4, qb * 128:(qb + 1) * 128], in0=po[:, :].rearrange("p (h n) -> p h n", n=128),
                in1=den[:].rearrange("p (h n) -> p h n", n=128), op=ALU.mult),
                reads=[pob, den_b], writes=[b_o])

        run_pipeline([{"qb": qb, "kvh": kvh} for qb in range(4) for kvh in range(2)], [a_s0, a_s1, a_s2, a_s3])
        for oc in range(8):
            pbk, pbb_ = bank()
            for k in range(8):
                s.op("pe", lambda e, pbk=pbk, k=k, oc=oc: e.matmul(pbk[:, :], lhsT=w_outb[:, k, oc * 128:(oc + 1) * 128], rhs=oT[:, k, :],
                                                                   start=(k == 0), stop=(k == 7)),
                     reads=[b_wout, b_o], writes=[pbb_], inc=(k == 7))
            s.op("dve", lambda e, pbk=pbk, oc=oc: e.scalar_tensor_tensor(out=z[:, oc, :], in0=pbk[:, :], scalar=mod1p[:, 16 + oc:17 + oc],
                                                                         in1=xt[:, oc, :], op0=ALU.mult, op1=ALU.add),
                 reads=[pbb_, xb, b_mod1p], writes=[bz])
        ps1, ps1b = bank()
        ps2, ps2b = bank()
        emit_ln_tile(cx, cst, z, zb_t, zsq_t, ps1[:, :], ps1b, ps2[:, :], ps2b, tmp, N, gT, bT, b_par, z, bz, bzb, bzsq, btmp, bz)
        s.dma("sp", yv[:, :, t0:t0 + N], z[:], reads=[bz], anchor=bz)
    s.final_wait("sp", [bz])
    return cx.end_phase()


class Rot:
    def __init__(self, cx, name, n, shape, dt):
        self.t = [cx.sb("%s%d" % (name, i), shape, dt) for i in range(n)]
        self.b = [cx.buf(name) for i in range(n)]
        self.i = 0

    def get(self):
        j = self.i % len(self.t)
        self.i += 1
        return self.t[j], self.b[j]


def load_gate_params(cx, w_a, w_x, b_aT, b_xT, lamT):
    s = cx.s
    g = {}
    g["wa"] = cx.sb("g_wa", [96, 16, 96], BF16)
    g["wx"] = cx.sb("g_wx", [96, 16, 96], BF16)
    g["b_w"] = cx.buf("gw")
    s.dma("pool", g["wa"][:], w_a.rearrange("n i j -> i n j"), writes=[g["b_w"]], anchor=g["b_w"])
    s.dma("pool", g["wx"][:], w_x.rearrange("n i j -> i n j"), writes=[g["b_w"]], anchor=g["b_w"])
    g["ba"] = cx.sb("g_ba", [96, 16], F32)
    g["bx"] = cx.sb("g_bx", [96, 16], F32)
    g["lamc"] = cx.sb("g_lamc", [96, 16], F32)
    g["one1"] = cx.sb("g_one1", [96, 1], F32)
    g["b_p"] = cx.buf("gp")
    s.dma("sp", g["ba"][:], b_aT[:, :], writes=[g["b_p"]], anchor=g["b_p"])
    s.dma("sp", g["bx"][:], b_xT[:, :], writes=[g["b_p"]], anchor=g["b_p"])
    s.dma("sp", g["lamc"][:], lamT[:, :], writes=[g["b_p"]], anchor=g["b_p"])
    s.op("pool", lambda e: e.memset(g["one1"][:], 1.0), writes=[g["b_p"]])
    s.op("act", lambda e: e.activation(out=g["lamc"][:], in_=g["lamc"][:], func=AF.Exp, scale=-1.0), reads=[g["b_p"]], writes=[g["b_p"]])
    s.op("act", lambda e: e.activation(out=g["lamc"][:], in_=g["lamc"][:], func=AF.Ln, bias=g["one1"][:, 0:1]), reads=[g["b_p"]], writes=[g["b_p"]])
    s.op("pool", lambda e: e.tensor_scalar(out=g["lamc"][:], in0=g["lamc"][:], scalar1=-4.0, scalar2=None, op0=ALU.mult),
         reads=[g["b_p"]], writes=[g["b_p"]])
    g["lam2"] = cx.sb("g_lam2", [96, 16], F32)
    s.op("pool", lambda e: e.tensor_scalar(out=g["lam2"][:], in0=g["lamc"][:], scalar1=2.0, scalar2=None, op0=ALU.mult),
         reads=[g["b_p"]], writes=[g["b_p"]])
    s.op("pool", lambda e: e.tensor_scalar(out=g["ba"][:], in0=g["ba"][:], scalar1=0.5, scalar2=None, op0=ALU.mult),
         reads=[g["b_p"]], writes=[g["b_p"]])
    s.op("pool", lambda e: e.tensor_scalar(out=g["bx"][:], in0=g["bx"][:], scalar1=0.5, scalar2=None, op0=ALU.mult),
         reads=[g["b_p"]], writes=[g["b_p"]])
    return g


def emit_gates(cx, g, n, c_t, c_b, cb_t, cb_b, bank_fn, pools, N):
    s = cx.s
    pbk, pbb_ = bank_fn()
    s.op("pe", lambda e: e.matmul(pbk[0:96, 0:N], lhsT=g["wa"][:, n, :], rhs=cb_t[:], start=True, stop=True),
         reads=[g["b_w"], cb_b], writes=[pbb_])
    s.op("pe", lambda e: e.matmul(pbk[0:96, N:2 * N], lhsT=g["wx"][:, n, :], rhs=cb_t[:], start=True, stop=True),
         reads=[g["b_w"], cb_b], writes=[pbb_])
    r_t, r_b = pools["r"].get()
    i_t, i_b = pools["i"].get()
    a_t, a_b = pools["a"].get()
    m_t, m_b = pools["m"].get()
    u_t, u_b = pools["u"].get()
    s.op("act", lambda e: e.activation(out=r_t[:], in_=pbk[0:96, 0:N], func=AF.Sigmoid, bias=g["ba"][:, n:n + 1]),
         reads=[pbb_, g["b_p"]], writes=[r_b])
    s.op("act", lambda e: e.activation(out=i_t[:], in_=pbk[0:96, N:2 * N], func=AF.Sigmoid, bias=g["bx"][:, n:n + 1]),
         reads=[pbb_, g["b_p"]], writes=[i_b])
    s.op("act", lambda e: e.activation(out=a_t[:], in_=r_t[:], func=AF.Exp, scale=g["lamc"][:, n:n + 1]),
         reads=[r_b, g["b_p"]], writes=[a_b])
    s.op("dve", lambda e: e.tensor_tensor(out=m_t[:], in0=a_t[:], in1=a_t[:], op=ALU.mult), reads=[a_b], writes=[m_b])
    s.op("dve", lambda e: e.tensor_scalar(out=m_t[:], in0=m_t[:], scalar1=-1.0, scalar2=1.0, op0=ALU.mult, op1=ALU.add),
         reads=[m_b], writes=[m_b])
    s.op("act", lambda e: e.activation(out=m_t[:], in_=m_t[:], func=AF.Sqrt), reads=[m_b], writes=[m_b])
    s.op("dve", lambda e: e.tensor_tensor(out=u_t[:], in0=i_t[:], in1=c_t, op=ALU.mult), reads=[i_b, c_b], writes=[u_b])
    s.op("dve", lambda e: e.tensor_tensor(out=u_t[:], in0=u_t[:], in1=m_t[:], op=ALU.mult), reads=[u_b, m_b], writes=[u_b])
    return a_t, a_b, u_t, u_b


def gate_pools(cx, N):
    return {k: Rot(cx, "gp_" + k, 2, [96, N], F32) for k in ("r", "i", "a", "m", "u")}


def make_banks(cx, nb):
    banks = [cx.ps("bank%d" % i) for i in range(nb)]
    bank_b = [cx.buf("bank") for i in range(nb)]
    ctr = [0]

    def bank():
        i = ctr[0] % nb
        ctr[0] += 1
        return banks[i], bank_b[i]
    return bank


def run_pipeline(items, stages):
    S = len(stages)
    for step in range(len(items) + S - 1):
        for k in range(S - 1, -1, -1):
            i = step - k
            if 0 <= i < len(items):
                stages[k](items[i])


class RotBanks:
    def __init__(self, cx, name, n):
        self.t = [cx.ps("%s%d" % (name, i)) for i in range(n)]
        self.b = [cx.buf(name) for i in range(n)]
        self.i = 0

    def get(self):
        j = self.i % len(self.t)
        self.i += 1
        return self.t[j], self.b[j]


SQB = 4


def gate_stage_fns(cx, g, N, RI, P, items_ref):
    s = cx.s

    def st_gmm(it):
        n = it["n"]
        cb_t, cb_b = P["cb"].get()
        c_t = it["c_t"]
        s.op("pool", lambda e: e.tensor_copy(out=cb_t[:], in_=c_t[:]), reads=[it["c_b"]], writes=[cb_b])
        pbk, pbb_ = RI.get()
        it["ri"], it["ri_b"] = pbk, pbb_
        s.op("pe", lambda e: e.matmul(pbk[0:96, 0:N], lhsT=g["wa"][:, n, :], rhs=cb_t[:], start=True, stop=True),
             reads=[g["b_w"], cb_b], writes=[pbb_])
        s.op("pe", lambda e: e.matmul(pbk[0:96, N:2 * N], lhsT=g["wx"][:, n, :], rhs=cb_t[:], start=True, stop=True),
             reads=[g["b_w"], cb_b], writes=[pbb_])

    def st_sig(it):
        n = it["n"]
        pbk, pbb_ = it["ri"], it["ri_b"]
        r_t, r_b = P["r"].get()
        i_t, i_b = P["i"].get()
        a_t, a_b = P["a"].get()
        m_t, m_b = P["m"].get()
        it.update(i_t=i_t, i_b=i_b, a_t=a_t, a_b=a_b, m_t=m_t, m_b=m_b)
        s.op("act", lambda e: e.activation(out=r_t[:], in_=pbk[0:96, 0:N], func=AF.Tanh, bias=g["ba"][:, n:n + 1], scale=0.5),
             reads=[pbb_, g["b_p"]], writes=[r_b])
        s.op("act", lambda e: e.activation(out=i_t[:], in_=pbk[0:96, N:2 * N], func=AF.Tanh, bias=g["bx"][:, n:n + 1], scale=0.5),
             reads=[pbb_, g["b_p"]], writes=[i_b])
        s.op("act", lambda e: e.activation(out=a_t[:], in_=r_t[:], func=AF.Exp, bias=g["lamc"][:, n:n + 1], scale=g["lamc"][:, n:n + 1]),
             reads=[r_b, g["b_p"]], writes=[a_b])
        s.op("act", lambda e: e.activation(out=m_t[:], in_=r_t[:], func=AF.Exp, bias=g["lam2"][:, n:n + 1], scale=g["lam2"][:, n:n + 1]),
             reads=[r_b, g["b_p"]], writes=[m_b])

    def st_m(it):
        i_t, i_b = it["i_t"], it["i_b"]
        m_t, m_b = it["m_t"], it["m_b"]
        u_t, u_b = P["u"].get()
        it.update(u_t=u_t, u_b=u_b)
        c_t = it["c_t"]
        s.op("pool", lambda e: e.tensor_scalar(out=m_t[:], in0=m_t[:], scalar1=-1.0, scalar2=1.0, op0=ALU.mult, op1=ALU.add),
             reads=[m_b], writes=[m_b])
        s.op("dve", lambda e: e.scalar_tensor_tensor(out=u_t[:], in0=i_t[:], scalar=1.0, in1=c_t[:], op0=ALU.add, op1=ALU.mult),
             reads=[i_b, it["c_b"]], writes=[u_b])

    def st_sqrt(it):
        if it["idx"] % SQB != SQB - 1:
            return
        for j in range(it["idx"] - SQB + 1, it["idx"] + 1):
            m_t, m_b = items_ref[j]["m_t"], items_ref[j]["m_b"]
            s.op("act", lambda e, m_t=m_t: e.activation(out=m_t[:], in_=m_t[:], func=AF.Sqrt), reads=[m_b], writes=[m_b])

    return st_gmm, st_sig, st_m, st_sqrt


def build_rnn1(cx=None, prefix="", over=None):
    cx = cx or Ctx()
    cx.begin_phase(prefix, over)
    s = cx.s
    N = 256
    NT = T // N
    NX = N + 2
    xT = cx.dram_in("xT", [D, T + 2])
    c_col = cx.dram_in("c_col", [128, 8])
    ada_w = cx.dram_in("ada_w", [D, 3 * D])
    ada_bT = cx.dram_in("ada_bT", [128, 24])
    w_in = cx.dram_in("w_in", [D, 2 * DRNN])
    convT = cx.dram_in("convT", [96, 16 * 6])
    w_a = cx.dram_in("w_a", [16, 96, 96])
    w_x = cx.dram_in("w_x", [16, 96, 96])
    b_aT = cx.dram_in("b_aT", [96, 16])
    b_xT = cx.dram_in("b_xT", [96, 16])
    lamT = cx.dram_in("lamT", [96, 16])
    carry_only = bool(cx.over.get("carry_only"))
    if not carry_only:
        cT = cx.dram_out("cT", [DRNN, T])
        GT = cx.dram_out("GT", [DRNN, T])
        h1T = cx.dram_out("h1T", [DRNN, T])
        cv = cT.rearrange("(n p) t -> p n t", p=96)
        Gv = GT.rearrange("(n p) t -> p n t", p=96)
        hv = h1T.rearrange("(n p) t -> p n t", p=96)

    XB = RotBanks(cx, "pX", 3)
    GB = RotBanks(cx, "pG", 2)
    RI = RotBanks(cx, "pRI", 2)
    misc = cx.ps("pmisc")
    misc_b = cx.buf("pmisc")
    zt = cx.sb("zt", [128, 8, N], F32)
    bzt = cx.buf("zt")
    pieces = [zt[:, 0:4, :].rearrange("p a (b n) -> p (a b) n", n=128), zt[:, 4:8, :].rearrange("p a (b n) -> p (a b) n", n=128)]
    ada = emit_adaln(cx, c_col, ada_w, ada_bT, misc, misc_b, pieces, [cx.buf("pc0"), cx.buf("pc1")], [bzt])
    mod, mod1p, b_mod, b_mod1p = ada["mod"], ada["mod1p"], ada["b_mod"], ada["b_mod1p"]

    w_inb = cx.sb("w_inb", [128, 8, 2 * DRNN], BF16)
    b_win = cx.buf("win")
    winv = w_in.rearrange("(k p) n -> p k n", p=128)
    for k in range(8):
        s.dma("pool", w_inb[:, k, :], winv[:, k, :], writes=[b_win], anchor=b_win)
    g = load_gate_params(cx, w_a, w_x, b_aT, b_xT, lamT)
    cw = cx.sb("cw", [96, 16 * 6], F32)
    b_cw = cx.buf("cw")
    s.dma("sp", cw[:], convT[:, :], writes=[b_cw], anchor=b_cw)

    xv = xT.rearrange("(k p) t -> p k t", p=128)
    xt = Rot(cx, "xt", 2, [128, 8, NX], F32)
    htp = Rot(cx, "ht", 2, [128, 8, NX], BF16)
    tail = cx.sb("tail", [96, 16, 2], F32)
    b_tail = cx.buf("tail")
    s.op("pool", lambda e: e.memset(tail[:], 0.0), writes=[b_tail])
    carry = cx.sb("carry", [96, 16], F32)
    b_carry = cx.buf("carry")
    s.op("pool", lambda e: e.memset(carry[:], 0.0), writes=[b_carry])
    P = {"xr": Rot(cx, "xrb", 3, [96, NX + 2], F32), "G": Rot(cx, "G", 2, [96, N], F32),
         "c": Rot(cx, "c", 5, [96, N], F32), "ct": Rot(cx, "ct", 2, [96, N], F32), "cb": Rot(cx, "cb", 2, [96, N], BF16),
         "r": Rot(cx, "r", 2, [96, N], F32), "i": Rot(cx, "i", 3, [96, N], F32), "a": Rot(cx, "a", 11, [96, N], F32),
         "m": Rot(cx, "m", 10, [96, N], F32), "u": Rot(cx, "u", 10, [96, N], F32), "h": Rot(cx, "h", 2, [96, N], F32)}
    halo = cx.over.get("halo_sb")
    state = {"nxt": None}

    def load_x(it):
        t_, b_ = xt.get()
        if halo is not None and it == NT - 1:
            s.dma("sp", t_[:, :, 0:N], xv[:, :, it * N:it * N + N], writes=[b_], anchor=b_)
            s.op("pool", lambda e, t_=t_: e.tensor_copy(out=t_[:, :, N:NX], in_=halo[:]), writes=[b_])
        else:
            s.dma("sp", t_[:], xv[:, :, it * N:it * N + NX], writes=[b_], anchor=b_)
        return t_, b_

    state["nxt"] = load_x(0)

    def st_proj(itm):
        it, n = itm["it"], itm["n"]
        if n == 0:
            x_t, xb = state["nxt"]
            if it + 1 < NT:
                state["nxt"] = load_x(it + 1)
            h_t, h_b = htp.get()
            state["ht"] = (h_t, h_b)
            for k in range(8):
                s.op("act", lambda e, k=k: e.activation(out=h_t[:, k, :], in_=x_t[:, k, :], func=AF.Identity,
                                                        bias=mod[:, k:k + 1], scale=mod1p[:, 8 + k:9 + k]),
                     reads=[xb, b_mod, b_mod1p], writes=[h_b])
            if not carry_only:
                for n2 in range(16):
                    pg, pgb = GB.get()
                    for k in range(8):
                        s.op("pe", lambda e, k=k, pg=pg, n2=n2: e.matmul(pg[0:96, 0:N], lhsT=w_inb[:, k, DRNN + n2 * 96:DRNN + (n2 + 1) * 96],
                                                                       rhs=h_t[:, k, 0:N], start=(k == 0), stop=(k == 7)),
                             reads=[b_win, h_b], writes=[pgb], inc=(k == 7))
                    G_t, G_b = P["G"].get()
                    s.op("act", lambda e, pg=pg, G_t=G_t: e.activation(out=G_t[:], in_=pg[0:96, 0:N], func=AF.Gelu), reads=[pgb], writes=[G_b])
                    s.dma("sp", Gv[:, n2, it * N:it * N + N], G_t[:], reads=[G_b], anchor=G_b)
        h_t, h_b = state["ht"]
        pbk, pbb_ = XB.get()
        itm["X"], itm["X_b"] = pbk, pbb_
        for k in range(8):
            s.op("pe", lambda e, k=k: e.matmul(pbk[0:96, 0:NX], lhsT=w_inb[:, k, n * 96:(n + 1) * 96], rhs=h_t[:, k, :],
                                               start=(k == 0), stop=(k == 7)),
                 reads=[b_win, h_b], writes=[pbb_], inc=(k == 7))

    def st_evac(itm):
        it, n = itm["it"], itm["n"]
        t0 = it * N
        xr_t, xr_b = P["xr"].get()
        itm["xr_t"], itm["xr_b"] = xr_t, xr_b
        pbk = itm["X"]
        s.op("act", lambda e: e.activation(out=xr_t[:, 2:NX + 2], in_=pbk[0:96, 0:NX], func=AF.Identity),
             reads=[itm["X_b"]], writes=[xr_b])

    def st_conv(itm):
        it, n = itm["it"], itm["n"]
        t0 = it * N
        xr_t, xr_b = itm["xr_t"], itm["xr_b"]
        s.op("pool", lambda e: e.tensor_copy(out=xr_t[:, 0:2], in_=tail[:, n, :]), reads=[b_tail], writes=[xr_b])
        s.op("pool", lambda e: e.tensor_copy(out=tail[:, n, :], in_=xr_t[:, N:N + 2]), reads=[xr_b], writes=[b_tail])
        c_t, c_b = P["c"].get()
        itm["c_t"], itm["c_b"] = c_t, c_b
        s.op("pool", lambda e: e.tensor_scalar(out=c_t[:], in0=xr_t[:, 0:N], scalar1=cw[:, n * 6:n * 6 + 1],
                                               scalar2=cw[:, n * 6 + 5:n * 6 + 6], op0=ALU.mult, op1=ALU.add),
             reads=[xr_b, b_cw], writes=[c_b])
        for j in range(1, 5):
            s.op("dve", lambda e, j=j: e.scalar_tensor_tensor(out=c_t[:], in0=xr_t[:, j:j + N], scalar=cw[:, n * 6 + j:n * 6 + j + 1],
                                                              in1=c_t[:], op0=ALU.mult, op1=ALU.add),
                 reads=[xr_b, b_cw, c_b], writes=[c_b])
        if not carry_only:
            s.dma("sp", cv[:, n, t0:t0 + N], c_t[:], reads=[c_b], anchor=c_b)

    items = [{"it": it, "n": n, "idx": it * 16 + n} for it in range(NT) for n in range(16)]
    st_gmm, st_sig, st_m, st_sqrt = gate_stage_fns(cx, g, N, RI, P, items)

    def st_scan(itm):
        it, n = itm["it"], itm["n"]
        t0 = it * N
        u_t, u_b, m_t, m_b, a_t, a_b = itm["u_t"], itm["u_b"], itm["m_t"], itm["m_b"], itm["a_t"], itm["a_b"]
        s.op("dve", lambda e: e.scalar_tensor_tensor(out=u_t[:], in0=u_t[:], scalar=0.5, in1=m_t[:], op0=ALU.mult, op1=ALU.mult),
             reads=[u_b, m_b], writes=[u_b])
        h_t, h_b = P["h"].get()
        s.op("dve", lambda e: e.tensor_tensor_scan(out=h_t[:], data0=a_t[:], data1=u_t[:], initial=carry[:, n:n + 1],
                                                   op0=ALU.mult, op1=ALU.add),
             reads=[a_b, u_b, b_carry], writes=[h_b])
        s.op("pool", lambda e: e.tensor_copy(out=carry[:, n:n + 1], in_=h_t[:, N - 1:N]), reads=[h_b], writes=[b_carry])
        if not carry_only:
            s.dma("sp", hv[:, n, t0:t0 + N], h_t[:], reads=[h_b], anchor=h_b)

    nop = lambda itm: None
    run_pipeline(items, [st_proj, st_evac, st_conv, st_gmm, st_sig, st_m, nop, nop, nop, st_sqrt, nop, nop, nop, st_scan])
    if cx.over.get("carry_out") is not None:
        s.dma("sp", cx.over["carry_out"], carry[:], reads=[b_carry], anchor=b_carry)
    s.final_wait("sp", P["c"].b + P["G"].b + P["h"].b + [b_carry])
    return cx.end_phase()


def build_rnn2(cx=None, prefix="", over=None):
    cx = cx or Ctx()
    cx.begin_phase(prefix, over)
    s = cx.s
    N = 256
    NT = T // N
    xT = cx.dram_in("xT", [D, T])
    cT = cx.dram_in("cT", [DRNN, T])
    GT = cx.dram_in("GT", [DRNN, T])
    h1T = cx.dram_in("h1T", [DRNN, T])
    carry_in = None if (over and over.get("carry_sb") is not None) else cx.dram_in("carry_in", [96, 16])
    c_col = cx.dram_in("c_col", [128, 8])
    ada_w = cx.dram_in("ada_w", [D, 3 * D])
    ada_bT = cx.dram_in("ada_bT", [128, 24])
    ln_gT = cx.dram_in("ln_gT", [128, 8])
    ln_bT = cx.dram_in("ln_bT", [128, 8])
    w_a = cx.dram_in("w_a", [16, 96, 96])
    w_x = cx.dram_in("w_x", [16, 96, 96])
    b_aT = cx.dram_in("b_aT", [96, 16])
    b_xT = cx.dram_in("b_xT", [96, 16])
    lamT = cx.dram_in("lamT", [96, 16])
    w_out = cx.dram_in("w_out", [DRNN, D])
    yT = cx.dram_out("yT", [D, T])

    cst = emit_consts(cx)
    RI = RotBanks(cx, "pRI", 3)
    WB = RotBanks(cx, "pW", 4)
    stp = cx.ps("pst")
    st_b = cx.buf("pst")
    z = cx.sb("z", [128, 8, N], F32)
    bz = cx.buf("z")
    pieces = [z[:, 0:4, :].rearrange("p a (b n) -> p (a b) n", n=128), z[:, 4:8, :].rearrange("p a (b n) -> p (a b) n", n=128)]
    ada = emit_adaln(cx, c_col, ada_w, ada_bT, stp, st_b, pieces, [cx.buf("pc0"), cx.buf("pc1")], [bz])
    mod1p, b_mod1p = ada["mod1p"], ada["b_mod1p"]
    gT = cx.sb("gT", [128, 8], F32)
    bT = cx.sb("bT", [128, 8], F32)
    b_par = cx.buf("par")
    s.dma("sp", gT[:], ln_gT[:, :], writes=[b_par], anchor=b_par)
    s.dma("sp", bT[:], ln_bT[:, :], writes=[b_par], anchor=b_par)
    g = load_gate_params(cx, w_a, w_x, b_aT, b_xT, lamT)
    woutb = cx.sb("woutb", [96, 16, D], BF16)
    b_wout = cx.buf("wout")
    s_w = w_out.rearrange("(n p) d -> p n d", p=96)
    for n in range(16):
        s.dma("pool", woutb[:, n, :], s_w[:, n, :], writes=[b_wout], anchor=b_wout)
    carry = cx.sb("carry", [96, 16], F32)
    b_carry = cx.buf("carry")
    if cx.over.get("carry_sb") is not None:
        s.op("pool", lambda e: e.tensor_copy(out=carry[:], in_=cx.over["carry_sb"][:]), writes=[b_carry])
    else:
        s.dma("sp", carry[:], carry_in[:, :], writes=[b_carry], anchor=b_carry)

    xv = xT.rearrange("(k p) t -> p k t", p=128)
    yv = yT.rearrange("(k p) t -> p k t", p=128)
    cv = cT.rearrange("(n p) t -> p n t", p=96)
    Gv = GT.rearrange("(n p) t -> p n t", p=96)
    hv = h1T.rearrange("(n p) t -> p n t", p=96)
    xt = Rot(cx, "xt", 2, [128, 8, N], F32)
    P = {"c": Rot(cx, "c", 5, [96, N], F32), "G": Rot(cx, "G", 4, [96, N], F32), "h1": Rot(cx, "h1", 4, [96, N], F32),
         "cb": Rot(cx, "cb", 2, [96, N], BF16), "r": Rot(cx, "r", 2, [96, N], F32), "i": Rot(cx, "i", 3, [96, N], F32),
         "a": Rot(cx, "a", 11, [96, N], F32), "m": Rot(cx, "m", 10, [96, N], F32), "u": Rot(cx, "u", 10, [96, N], F32),
         "h2": Rot(cx, "h2", 2, [96, N], F32)}
    ytp = Rot(cx, "yt", 2, [96, 16, N], BF16)
    zb_t = cx.sb("zb", [128, 8, N], BF16)
    bzb = cx.buf("zb")
    zsq_t = cx.sb("zsq", [128, 8, N], BF16)
    bzsq = cx.buf("zsq")
    tmp = cx.sb("tmp", [128, 4, N], F32)
    btmp = cx.buf("tmp")
    state = {}

    def st_load(itm):
        it, n = itm["it"], itm["n"]
        t0 = it * N
        if n == 0:
            x_t, xb = xt.get()
            s.dma("sp", x_t[:], xv[:, :, t0:t0 + N], writes=[xb], anchor=xb)
            s.op("act", lambda e: e.activation(out=x_t[:], in_=x_t[:], func=AF.Identity, scale=float(DN_ALPHA)),
                 reads=[xb], writes=[xb])
            state[("x", it)] = (x_t, xb)
            state[("y", it)] = ytp.get()
        c_t, c_b = P["c"].get()
        itm.update(c_t=c_t, c_b=c_b)
        s.dma("sp", c_t[:], cv[:, n, t0:t0 + N], writes=[c_b], anchor=c_b)

    def st_load2(itm):
        it, n = itm["it"], itm["n"]
        t0 = it * N
        G_t, G_b = P["G"].get()
        h1_t, h1_b = P["h1"].get()
        itm.update(G_t=G_t, G_b=G_b, h1_t=h1_t, h1_b=h1_b)
        s.dma("sp", G_t[:], Gv[:, n, t0:t0 + N], writes=[G_b], anchor=G_b)
        s.dma("sp", h1_t[:], hv[:, n, t0:t0 + N], writes=[h1_b], anchor=h1_b)

    items = [{"it": it, "n": n} for it in range(NT - 1, -1, -1) for n in range(16)]
    for j_, itm_ in enumerate(items):
        itm_["idx"] = j_
    st_gmm, st_sig, st_m, st_sqrt = gate_stage_fns(cx, g, N, RI, P, items)

    def st_scan(itm):
        it, n = itm["it"], itm["n"]
        t0 = it * N
        u_t, u_b, m_t, m_b, a_t, a_b = itm["u_t"], itm["u_b"], itm["m_t"], itm["m_b"], itm["a_t"], itm["a_b"]
        h1_t, h1_b, G_t, G_b = itm["h1_t"], itm["h1_b"], itm["G_t"], itm["G_b"]
        y_t, y_b = state[("y", it)]
        s.op("dve", lambda e: e.scalar_tensor_tensor(out=u_t[:], in0=u_t[:], scalar=0.5, in1=m_t[:], op0=ALU.mult, op1=ALU.mult),
             reads=[u_b, m_b], writes=[u_b])
        h2_t, h2_b = P["h2"].get()
        s.op("dve", lambda e: e.tensor_tensor_scan(out=h2_t[:, ::-1], data0=a_t[:, ::-1], data1=u_t[:, ::-1], initial=carry[:, n:n + 1],
                                                   op0=ALU.mult, op1=ALU.add),
             reads=[a_b, u_b, b_carry], writes=[h2_b])
        s.op("pool", lambda e: e.tensor_copy(out=carry[:, n:n + 1], in_=h2_t[:, 0:1]), reads=[h2_b], writes=[b_carry])
        s.op("dve", lambda e: e.tensor_tensor(out=h2_t[:], in0=h2_t[:], in1=h1_t[:], op=ALU.add), reads=[h2_b, h1_b], writes=[h2_b])
        s.op("dve", lambda e: e.tensor_tensor(out=y_t[:, n, :], in0=h2_t[:], in1=G_t[:], op=ALU.mult), reads=[h2_b, G_b], writes=[y_b])
        if n == 15:
            x_t, xb = state[("x", it)]
            for oc in range(8):
                pbk, pbb_ = WB.get()
                for nn in range(16):
                    s.op("pe", lambda e, pbk=pbk, nn=nn, oc=oc: e.matmul(pbk[:, 0:N], lhsT=woutb[:, nn, oc * 128:(oc + 1) * 128], rhs=y_t[:, nn, :],
                                                                         start=(nn == 0), stop=(nn == 15)),
                         reads=[b_wout, y_b], writes=[pbb_], inc=(nn == 15))
                s.op("dve", lambda e, pbk=pbk, oc=oc: e.scalar_tensor_tensor(out=z[:, oc, :], in0=pbk[:, 0:N], scalar=mod1p[:, 16 + oc:17 + oc],
                                                                             in1=x_t[:, oc, :], op0=ALU.mult, op1=ALU.add),
                     reads=[pbb_, xb, b_mod1p], writes=[bz])
            emit_ln_tile(cx, cst, z, zb_t, zsq_t, stp[:, 0:N], st_b, stp[:, N:2 * N], st_b, tmp, N, gT, bT, b_par, z, bz, bzb, bzsq, btmp, bz)
            s.dma("sp", yv[:, :, t0:t0 + N], z[:], reads=[bz], anchor=bz)

    nop = lambda itm: None
    run_pipeline(items, [st_load, st_gmm, st_sig, st_m, nop, nop, nop, st_sqrt, nop, st_load2, nop, st_scan])
    s.final_wait("sp", [bz])
    return cx.end_phase()


def build_xch(cx, prefix, bounce_in, bounce_out, P, F, result_sb, swap2):
    cx.begin_phase(prefix, None)
    s = cx.s
    sel_d = cx.dram_in("sel", [128, 8])
    sel = cx.sb("sel_sb", [128, 8], F32)
    b_sel = cx.buf("sel")
    s.dma("sp", sel[:], sel_d[:, :], writes=[b_sel], anchor=b_sel)
    b_in, b_out = cx.buf("bin"), cx.buf("bout")
    s.collective(bounce_in.ap(), bounce_out.ap(), reads=[b_in], writes=[b_out], anchor=b_out)
    g = cx.sb("xg", [P, 8, F], F32)
    b_g = cx.buf("xg")
    s.dma("sp", g[:], bounce_out.ap().rearrange("(r p) f -> p r f", p=P), reads=[b_out], writes=[b_g], anchor=b_g)
    acc = cx.sb("xacc", [P, F], F32)
    b_acc = cx.buf("xacc")
    s.op("dve", lambda e: e.tensor_scalar(out=acc[:], in0=g[:, 0, :], scalar1=sel[0:P, 0:1], scalar2=None, op0=ALU.mult),
         reads=[b_g, b_sel], writes=[b_acc])
    for r in range(1, 8):
        s.op("dve", lambda e, r=r: e.scalar_tensor_tensor(out=acc[:], in0=g[:, r, :], scalar=sel[0:P, r:r + 1], in1=acc[:],
                                                          op0=ALU.mult, op1=ALU.add),
             reads=[b_g, b_sel, b_acc], writes=[b_acc])
    b_res = cx.buf("res")
    if swap2:
        av = acc[:].rearrange("p (k t) -> p k t", t=2)
        s.op("dve", lambda e: e.tensor_copy(out=result_sb[:, :, 0:1], in_=av[:, :, 1:2]), reads=[b_acc], writes=[b_res])
        s.op("dve", lambda e: e.tensor_copy(out=result_sb[:, :, 1:2], in_=av[:, :, 0:1]), reads=[b_acc], writes=[b_res])
    else:
        s.op("dve", lambda e: e.tensor_copy(out=result_sb[:], in_=acc[:]), reads=[b_acc], writes=[b_res])
    return cx.end_phase()


def build_lxch(cx, prefix, bounce, P, F, result_sb, swap2):
    cx.begin_phase(prefix, None)
    s = cx.s
    acc = cx.sb("xacc", [P, F], F32)
    b_acc = cx.buf("xacc")
    s.dma("sp", acc[:], bounce.ap(), writes=[b_acc], anchor=b_acc)
    b_res = cx.buf("res")
    if swap2:
        av = acc[:].rearrange("p (k t) -> p k t", t=2)
        s.op("dve", lambda e: e.tensor_copy(out=result_sb[:, :, 0:1], in_=av[:, :, 1:2]), reads=[b_acc], writes=[b_res])
        s.op("dve", lambda e: e.tensor_copy(out=result_sb[:, :, 1:2], in_=av[:, :, 0:1]), reads=[b_acc], writes=[b_res])
    else:
        s.op("dve", lambda e: e.tensor_copy(out=result_sb[:], in_=acc[:]), reads=[b_acc], writes=[b_res])
    return cx.end_phase()


def build_fused2():
    cx = Ctx(fused=True)
    tmp = lambda n, sh: cx.dram_tmp(n, sh).ap()
    x0s, x1s, x0o, x1o = tmp("x0s", [D, T]), tmp("x1s", [D, T]), tmp("x0o", [D, T]), tmp("x1o", [D, T])
    cT, GT, h1T, x2 = tmp("cT_s", [DRNN, T]), tmp("GT_s", [DRNN, T]), tmp("h1T_s", [DRNN, T]), tmp("x2", [D, T])
    bh_s, bh_o = cx.dram_tmp("bh_s", [128, 16]), cx.dram_tmp("bh_o", [128, 16])
    bc_o = cx.dram_tmp("bc_o", [96, 16])
    bc_s = cx.dram_tmp("bc_s", [96, 16])
    halo_s = cx.gsb("halo_s", [128, 8, 2], F32)
    halo_o = cx.gsb("halo_o", [128, 8, 2], F32)
    carry_sb = cx.gsb("carry_sb", [96, 16], F32)
    cx.ada_tiles = {k: (cx.gsb("ada_mod_" + k, [128, 24], F32), cx.gsb("ada_mod1p_" + k, [128, 24], F32)) for k in ("00", "01", "10", "11")}
    out = cx.nc.dram_tensor("out", [D, T], F32, kind="ExternalOutput").ap()
    D_ = cx.decl

    def share(src, dst_names):
        return {n: D_[src + n] for n in dst_names}

    build_attn(cx, "a_", {"yT": x0s, "ada_key": "00"})
    ov = share("a_", ["c_col", "ada_w", "ada_bT", "ln_gT", "ln_bT", "w_in", "w_out", "perm", "mprev", "mnext", "sink_rep"])
    ov.update({"yT": x0o, "ada_key": "00"})
    build_attn(cx, "b_", ov)
    build_mlp(cx, "m0_", {"xT": x0s, "yT": x1s, "ada_key": "01",
                           "jobs": [(x0s, x1s, bh_s.ap()), (x0o, x1o, bh_o.ap())]})
    build_lxch(cx, "l1_", bh_o, 128, 16, halo_s, True)
    build_lxch(cx, "l2_", bh_s, 128, 16, halo_o, True)
    build_rnn1(cx, "r1_", {"xT": x1s, "halo_sb": halo_s, "cT": cT, "GT": GT, "h1T": h1T, "carry_out": bc_s.ap(), "ada_key": "10"})
    ov = share("r1_", ["c_col", "ada_w", "ada_bT", "w_in"])
    ov.update({"xT": x1o, "halo_sb": halo_o, "carry_only": True, "carry_out": bc_o.ap(), "ada_key": "10"})
    build_rnn1(cx, "q1_", ov)
    build_lxch(cx, "l3_", bc_o, 96, 16, carry_sb, False)
    ov = share("r1_", ["c_col", "ada_w", "ada_bT"])
    ov.update({"xT": x1s, "cT": cT, "GT": GT, "h1T": h1T, "carry_sb": carry_sb, "yT": x2, "ada_key": "10"})
    build_rnn2(cx, "r2_", ov)
    build_mlp(cx, "m1_", {"xT": x2, "yT": out, "ada_key": "11"})
    cx.gstack.close()
    return cx.nc


def build_fused():
    cx = Ctx(fused=True)
    x0a = cx.dram_tmp("x0a", [D, T]).ap()
    x1 = cx.dram_tmp("x1", [D, T]).ap()
    cT = cx.dram_tmp("cT_s", [DRNN, T]).ap()
    GT = cx.dram_tmp("GT_s", [DRNN, T]).ap()
    h1T = cx.dram_tmp("h1T_s", [DRNN, T]).ap()
    x2 = cx.dram_tmp("x2", [D, T]).ap()
    b1_in = cx.dram_tmp("b1_in", [128, 16])
    b1_out = cx.dram_tmp("b1_out", [8 * 128, 16])
    b2_in = cx.dram_tmp("b2_in", [96, 16])
    b2_out = cx.dram_tmp("b2_out", [8 * 96, 16])
    halo_sb = cx.gsb("halo_sb", [128, 8, 2], F32)
    carry_sb = cx.gsb("carry_sb", [96, 16], F32)
    out = cx.nc.dram_tensor("out", [D, T], F32, kind="ExternalOutput").ap()
    build_attn(cx, "a_", {"yT": x0a})
    build_mlp(cx, "m0_", {"xT": x0a, "yT": x1, "tail_out": b1_in.ap()})
    build_xch(cx, "x1_", b1_in, b1_out, 128, 16, halo_sb, True)
    build_rnn1(cx, "r1_", {"xT": x1, "halo_sb": halo_sb, "cT": cT, "GT": GT, "h1T": h1T, "carry_out": b2_in.ap()})
    build_xch(cx, "x2_", b2_in, b2_out, 96, 16, carry_sb, False)
    build_rnn2(cx, "r2_", {"xT": x1, "cT": cT, "GT": GT, "h1T": h1T, "carry_sb": carry_sb, "yT": x2})
    build_mlp(cx, "m1_", {"xT": x2, "yT": out})
    cx.gstack.close()
    return cx.nc


def colT(v, n):
    return np.ascontiguousarray(np.asarray(v, np.float32).reshape(n, 128).T)


_PROGS = {}


def get_prog(name):
    if name not in _PROGS:
        _PROGS[name] = {"mlp": build_mlp, "attn": build_attn, "rnn1": build_rnn1, "rnn2": build_rnn2, "fused": build_fused, "fused2": build_fused2}[name]()
    return _PROGS[name]


def run_mlp(xT_list, c, ada_w, ada_b, ln_g, ln_b, w1, w2):
    nc = get_prog("mlp")
    in_maps = []
    for core in range(NCORES):
        b = core // 2
        in_maps.append({
            "xT": xT_list[core], "c_col": colT(c[b], 8), "ada_w": np.ascontiguousarray(ada_w),
            "ada_bT": colT(ada_b, 24), "ln_gT": colT(ln_g, 8), "ln_bT": colT(ln_b, 8),
            "w1": np.ascontiguousarray(w1), "w2": np.ascontiguousarray(w2),
        })
    res = run_bass_kernel_spmd(nc, in_maps, core_ids=list(range(NCORES)))
    return [r["yT"] for r in res.results]


ROT = 32
ROPE_THETA = 500000.0


def rope_tables(pos):
    inv_freq = (np.float32(ROPE_THETA) ** (-np.arange(0, ROT, 2, dtype=np.float32) / np.float32(ROT))).astype(np.float32)
    ang = (pos.astype(np.float32)[None, :] * inv_freq[:, None]).astype(np.float32)
    c = np.cos(ang).astype(np.float32)
    sn = np.sin(ang).astype(np.float32)
    return np.ascontiguousarray(np.concatenate([c, c], 0)), np.ascontiguousarray(np.concatenate([-sn, sn], 0))


def local_positions(core):
    half = core % 2
    if half == 0:
        return np.arange(0, T + 128)
    return np.arange(2 * T - 1, T - 129, -1)


def attn_consts():
    perm = np.zeros((32, 32), np.float32)
    for i in range(32):
        perm[(i + 16) % 32, i] = 1.0
    j = np.arange(128)[:, None]
    q = np.arange(128)[None, :]
    mprev = np.tile((j >= q).astype(np.float32), (1, 4))
    mnext = np.tile((j <= q).astype(np.float32), (1, 4))
    return perm, np.ascontiguousarray(mprev), np.ascontiguousarray(mnext)


def run_attn(xTh_list, c, ada_w, ada_b, ln_g, ln_b, w_in, w_out, sinks):
    nc = get_prog("attn")
    perm, mprev, mnext = attn_consts()
    in_maps = []
    for core in range(NCORES):
        b = core // 2
        cosT, sinT = rope_tables(local_positions(core))
        in_maps.append({
            "xT": xTh_list[core], "c_col": colT(c[b], 8), "ada_w": np.ascontiguousarray(ada_w),
            "ada_bT": colT(ada_b, 24), "ln_gT": colT(ln_g, 8), "ln_bT": colT(ln_b, 8),
            "w_in": np.ascontiguousarray(w_in), "w_out": np.ascontiguousarray(w_out),
            "cosT": cosT, "sinT": sinT, "perm": perm, "mprev": mprev, "mnext": mnext,
            "sink_rep": np.ascontiguousarray(np.tile(np.asarray(sinks, np.float32)[None, :], (128, 1))),
        })
    res = run_bass_kernel_spmd(nc, in_maps, core_ids=list(range(NCORES)))
    return [r["yT"] for r in res.results]


def shard_x(x):
    out = []
    for core in range(NCORES):
        b = core // 2
        idx = local_positions(core)
        out.append(np.ascontiguousarray(x[b][idx, :].T))
    return out


def colT96(v):
    return np.ascontiguousarray(np.asarray(v, np.float32).reshape(16, 96).T)


def conv_table(conv_w, conv_b, half):
    taps = np.zeros((5, DRNN), np.float32)
    for j in range(4):
        if half == 0:
            taps[j] = conv_w[j]
        else:
            taps[4 - j] = conv_w[j]
    tab = np.zeros((96, 16, 6), np.float32)
    for j in range(5):
        tab[:, :, j] = taps[j].reshape(16, 96).T
    tab[:, :, 5] = np.asarray(conv_b, np.float32).reshape(16, 96).T
    return np.ascontiguousarray(tab.reshape(96, 96))


def _common(c, ada_w, ada_b, core):
    b = core // 2
    return {"c_col": colT(c[b], 8), "ada_w": np.ascontiguousarray(ada_w), "ada_bT": colT(ada_b, 24)}


def run_rnn1(x1_list, c, ada_w, ada_b, w_in, conv_w, conv_b, w_a, b_a, w_x, b_x, lam):
    nc = get_prog("rnn1")
    in_maps = []
    for core in range(NCORES):
        half = core % 2
        par = x1_list[core ^ 1]
        xh = np.ascontiguousarray(np.concatenate([x1_list[core], par[:, T - 1:T], par[:, T - 2:T - 1]], axis=1))
        d1 = half
        m = _common(c, ada_w, ada_b, core)
        m.update({"xT": xh, "w_in": np.ascontiguousarray(w_in), "convT": conv_table(conv_w, conv_b, half),
                  "w_a": np.ascontiguousarray(w_a[d1]), "w_x": np.ascontiguousarray(w_x[d1]),
                  "b_aT": colT96(b_a[d1]), "b_xT": colT96(b_x[d1]), "lamT": colT96(lam[d1])})
        in_maps.append(m)
    res = run_bass_kernel_spmd(nc, in_maps, core_ids=list(range(NCORES)))
    return [(r["cT"], r["GT"], r["h1T"]) for r in res.results]


def run_rnn2(x1_list, r1, c, ada_w, ada_b, ln_g, ln_b, w_a, b_a, w_x, b_x, lam, w_out):
    nc = get_prog("rnn2")
    in_maps = []
    for core in range(NCORES):
        half = core % 2
        d2 = 1 - half
        cT, GT, h1T = r1[core]
        carry = colT96(r1[core ^ 1][2][:, T - 1])
        m = _common(c, ada_w, ada_b, core)
        m.update({"xT": x1_list[core], "cT": cT, "GT": GT, "h1T": h1T, "carry_in": carry,
                  "ln_gT": colT(ln_g, 8), "ln_bT": colT(ln_b, 8),
                  "w_a": np.ascontiguousarray(w_a[d2]), "w_x": np.ascontiguousarray(w_x[d2]),
                  "b_aT": colT96(b_a[d2]), "b_xT": colT96(b_x[d2]), "lamT": colT96(lam[d2]),
                  "w_out": np.ascontiguousarray(w_out)})
        in_maps.append(m)
    res = run_bass_kernel_spmd(nc, in_maps, core_ids=list(range(NCORES)))
    return [r["yT"] for r in res.results]


def kernel_unfused(x, c, ada_w, ada_b, ln_g, ln_b, attn_w_in, attn_w_out, attn_sinks,
                   rnn_w_in, rnn_conv_w, rnn_conv_b, rnn_w_a, rnn_b_a, rnn_w_x, rnn_b_x, rnn_lam,
                   rnn_w_out, mlp_w1, mlp_w2):
    f = lambda a: np.asarray(a, np.float32)
    x, c, ada_w, ada_b, ln_g, ln_b = f(x), f(c), f(ada_w), f(ada_b), f(ln_g), f(ln_b)
    xs = shard_x(x)
    a0 = run_attn(xs, c, ada_w[0, 0], ada_b[0, 0], ln_g[0, 0], ln_b[0, 0], f(attn_w_in)[0], f(attn_w_out)[0], f(attn_sinks)[0])
    m0 = run_mlp(a0, c, ada_w[0, 1], ada_b[0, 1], ln_g[0, 1], ln_b[0, 1], f(mlp_w1)[0], f(mlp_w2)[0])
    r1 = run_rnn1(m0, c, ada_w[1, 0], ada_b[1, 0], f(rnn_w_in)[0], f(rnn_conv_w)[0], f(rnn_conv_b)[0],
                  f(rnn_w_a)[0], f(rnn_b_a)[0], f(rnn_w_x)[0], f(rnn_b_x)[0], f(rnn_lam)[0])
    r2 = run_rnn2(m0, r1, c, ada_w[1, 0], ada_b[1, 0], ln_g[1, 0], ln_b[1, 0],
                  f(rnn_w_a)[0], f(rnn_b_a)[0], f(rnn_w_x)[0], f(rnn_b_x)[0], f(rnn_lam)[0], f(rnn_w_out)[0])
    m1 = run_mlp(r2, c, ada_w[1, 1], ada_b[1, 1], ln_g[1, 1], ln_b[1, 1], f(mlp_w1)[1], f(mlp_w2)[1])
    out = np.empty((4, 2 * T, D), np.float32)
    for core in range(NCORES):
        idx = local_positions(core)[:T]
        out[core // 2][idx, :] = m1[core].T
    return out


def kernel(x, c, ada_w, ada_b, ln_g, ln_b, attn_w_in, attn_w_out, attn_sinks,
           rnn_w_in, rnn_conv_w, rnn_conv_b, rnn_w_a, rnn_b_a, rnn_w_x, rnn_b_x, rnn_lam,
           rnn_w_out, mlp_w1, mlp_w2):
    f = lambda a: np.ascontiguousarray(np.asarray(a, np.float32))
    x, c, ada_w, ada_b, ln_g, ln_b = f(x), f(c), f(ada_w), f(ada_b), f(ln_g), f(ln_b)
    attn_w_in, attn_w_out, attn_sinks = f(attn_w_in), f(attn_w_out), f(attn_sinks)
    rnn_w_in, rnn_conv_w, rnn_conv_b, rnn_w_out = f(rnn_w_in), f(rnn_conv_w), f(rnn_conv_b), f(rnn_w_out)
    rnn_w_a, rnn_b_a, rnn_w_x, rnn_b_x, rnn_lam = f(rnn_w_a), f(rnn_b_a), f(rnn_w_x), f(rnn_b_x), f(rnn_lam)
    mlp_w1, mlp_w2 = f(mlp_w1), f(mlp_w2)
    nc = get_prog("fused2")
    xs = shard_x(x)
    perm, mprev, mnext = attn_consts()
    in_maps = []
    for core in range(NCORES):
        b = core // 2
        half = core % 2
        d1, d2 = half, 1 - half
        cosT, sinT = rope_tables(local_positions(core))
        sel = np.zeros((128, 8), np.float32)
        sel[:, core ^ 1] = 1.0
        m = {}

        def ada(pfx, i, j, ln=True):
            m[pfx + "c_col"] = colT(c[b], 8)
            m[pfx + "ada_w"] = ada_w[i, j]
            m[pfx + "ada_bT"] = colT(ada_b[i, j], 24)
            if ln:
                m[pfx + "ln_gT"] = colT(ln_g[i, j], 8)
                m[pfx + "ln_bT"] = colT(ln_b[i, j], 8)

        ada("a_", 0, 0)
        m.update({"a_xT": xs[core], "a_w_in": attn_w_in[0], "a_w_out": attn_w_out[0], "a_cosT": cosT, "a_sinT": sinT,
                  "a_perm": perm, "a_mprev": mprev, "a_mnext": mnext,
                  "a_sink_rep": np.ascontiguousarray(np.tile(attn_sinks[0][None, :], (128, 1)))})
        ada("m0_", 0, 1)
        m.update({"m0_w1": mlp_w1[0], "m0_w2": mlp_w2[0]})
        oc_ = core ^ 1
        oh = oc_ % 2
        cosO, sinO = rope_tables(local_positions(oc_))
        m.update({"b_xT": xs[oc_], "b_cosT": cosO, "b_sinT": sinO})
        m.update({"q1_convT": conv_table(rnn_conv_w[0], rnn_conv_b[0], oh),
                  "q1_w_a": rnn_w_a[0, oh], "q1_w_x": rnn_w_x[0, oh], "q1_b_aT": colT96(rnn_b_a[0, oh]),
                  "q1_b_xT": colT96(rnn_b_x[0, oh]), "q1_lamT": colT96(rnn_lam[0, oh])})
        ada("r1_", 1, 0, ln=False)
        m.update({"r1_w_in": rnn_w_in[0], "r1_convT": conv_table(rnn_conv_w[0], rnn_conv_b[0], half),
                  "r1_w_a": rnn_w_a[0, d1], "r1_w_x": rnn_w_x[0, d1], "r1_b_aT": colT96(rnn_b_a[0, d1]),
                  "r1_b_xT": colT96(rnn_b_x[0, d1]), "r1_lamT": colT96(rnn_lam[0, d1])})
        m["r2_ln_gT"] = colT(ln_g[1, 0], 8)
        m["r2_ln_bT"] = colT(ln_b[1, 0], 8)
        m.update({"r2_w_a": rnn_w_a[0, d2], "r2_w_x": rnn_w_x[0, d2], "r2_b_aT": colT96(rnn_b_a[0, d2]),
                  "r2_b_xT": colT96(rnn_b_x[0, d2]), "r2_lamT": colT96(rnn_lam[0, d2]), "r2_w_out": rnn_w_out[0]})
        ada("m1_", 1, 1)
        m.update({"m1_w1": mlp_w1[1], "m1_w2": mlp_w2[1]})
        in_maps.append({k: np.ascontiguousarray(v) for k, v in m.items()})
    res = run_bass_kernel_spmd(nc, in_maps, core_ids=list(range(NCORES)))
    out = np.empty((4, 2 * T, D), np.float32)
    for core in range(NCORES):
        idx = local_positions(core)[:T]
        out[core // 2][idx, :] = res.results[core]["out"].T
    return out
```

```python
import numpy as np
from contextlib import ExitStack
import concourse.bass as bass
import concourse.mybir as mybir
from concourse.bass_utils import run_bass_kernel_spmd

AF = mybir.ActivationFunctionType
ALU = mybir.AluOpType
F32 = mybir.dt.float32
BF16 = mybir.dt.bfloat16

D = 1024
KC = 8
T = 4096
NCORES = 8
DFF = 4096
DEPTH = 2
DN_ALPHA = (2.0 * DEPTH) ** 0.25
LN_EPS = 1e-5
DRNN = 1536
NBLK = 16
BW = 96
SEM_LIMIT = 3000


class Buf:
    __slots__ = ("name", "w", "r", "dsem", "dcnt")

    def __init__(self, name):
        self.name = name
        self.w = None
        self.r = []
        self.dsem = None
        self.dcnt = 0


class Sched:
    ENG = ("pe", "act", "dve", "pool", "sp")

    def __init__(self, nc, stack):
        self.nc = nc
        self.stack = stack
        self.stream = {e: [] for e in self.ENG}
        self.sem = {e: None for e in self.ENG}
        self.cnt = {e: 0 for e in self.ENG}
        self.seen = {e: {} for e in self.ENG}
        self.nsem = 0
        self.handles = []
        self.dma_bufs = []

    def _newsem(self, tag):
        self.nsem += 1
        h = self.nc.alloc_semaphore(name="%ss%d_%s" % (getattr(self, "prefix", ""), self.nsem, tag))
        self.handles.append(h)
        return h

    def _peek(self, e):
        if self.sem[e] is None or self.cnt[e] >= SEM_LIMIT:
            self.sem[e] = self._newsem(e)
            self.cnt[e] = 0
        return (self.sem[e], self.cnt[e] + 1)

    def _waits(self, e, reads, writes, skip_sem=None):
        need = {}

        def add(t):
            if t is None:
                return
            s, v = t
            k = id(s)
            if k not in need or need[k][1] < v:
                need[k] = (s, v)

        for b in reads:
            add(b.w)
        for b in writes:
            add(b.w)
            for t in b.r:
                add(t)
        out = []
        seen = self.seen[e]
        for k, (s, v) in need.items():
            if e == "pe" and s is self.sem["pe"]:
                continue
            if skip_sem is not None and s is skip_sem:
                continue
            if seen.get(k, 0) >= v:
                continue
            seen[k] = v
            out.append((s, v))
        return out

    def op(self, e, fn, reads=(), writes=(), inc=True):
        waits = self._waits(e, reads, writes)
        tk = self._peek(e)
        if inc:
            self.cnt[e] += 1
        self.stream[e].append((waits, fn, tk if inc else None))
        for b in reads:
            b.r.append(tk)
        for b in writes:
            b.w = tk
            b.r = []
        return tk

    def dma(self, e, out_ap, in_ap, reads=(), writes=(), anchor=None):
        a = anchor
        if a.dsem is None or a.dcnt >= SEM_LIMIT:
            a.dsem = self._newsem("d")
            a.dcnt = 0
            self.dma_bufs.append(a)
        waits = self._waits(e, reads, writes, skip_sem=a.dsem)
        a.dcnt += 16
        tk = (a.dsem, a.dcnt)

        def fn(eng, out_ap=out_ap, in_ap=in_ap):
            return eng.dma_start(out=out_ap, in_=in_ap)

        self.stream[e].append((waits, fn, ("dma", a.dsem)))
        for b in reads:
            b.r.append(tk)
        for b in writes:
            b.w = tk
            b.r = []
        return tk

    def collective(self, in_ap, out_ap, reads=(), writes=(), anchor=None):
        a = anchor
        if a.dsem is None:
            a.dsem = self._newsem("cc")
            a.dcnt = 0
            self.dma_bufs.append(a)
        waits = self._waits("pool", reads, writes, skip_sem=a.dsem)
        a.dcnt += 1
        tk = (a.dsem, a.dcnt)

        def fn(eng, in_ap=in_ap, out_ap=out_ap):
            return eng.collective_compute("AllGather", ALU.bypass, replica_groups=[list(range(NCORES))], ins=[in_ap], outs=[out_ap])

        self.stream["pool"].append((waits, fn, ("cc", a.dsem)))
        for b in reads:
            b.r.append(tk)
        for b in writes:
            b.w = tk
            b.r = []
        return tk

    def barrier(self):
        targets = []
        for e in self.ENG:
            if self.sem[e] is not None and self.cnt[e] > 0:
                targets.append((self.sem[e], self.cnt[e]))
        for a in self.dma_bufs:
            targets.append((a.dsem, a.dcnt))
        for e in self.ENG:
            w = []
            for (s, v) in targets:
                if e == "pe" and s is self.sem["pe"]:
                    continue
                if self.seen[e].get(id(s), 0) >= v:
                    continue
                self.seen[e][id(s)] = v
                w.append((s, v))
            if w:
                self.stream[e].append((w, None, None))

    def final_wait(self, e, bufs):
        w = []
        for b in bufs:
            for t in [b.w] + list(b.r):
                if t is not None:
                    w.append(t)
        self.stream[e].append((w, None, None))

    def emit(self):
        nc = self.nc
        block = self.stack.enter_context(nc.Block())

        def replay(eng, items):
            for waits, fn, tk in items:
                for (s, v) in waits:
                    eng.wait_ge(s, v)
                if fn is None:
                    continue
                ins = fn(eng)
                if tk is None:
                    continue
                if tk[0] == "dma":
                    ins.then_inc(tk[1], 16)
                elif tk[0] == "cc":
                    ins.then_inc(tk[1])
                else:
                    ins.then_inc(tk[0], 1)

        st = self.stream

        @block.sync
        def _(eng):
            replay(eng, st["sp"])

        @block.tensor
        def _(eng):
            replay(eng, st["pe"])

        @block.scalar
        def _(eng):
            replay(eng, st["act"])

        @block.vector
        def _(eng):
            replay(eng, st["dve"])

        @block.gpsimd
        def _(eng):
            replay(eng, st["pool"])


class Ctx:
    def __init__(self, fused=False):
        self.nc = bass.Bass("TRN2", target_bir_lowering=False)
        self.fused = fused
        self.gstack = ExitStack()
        self.nbuf = 0
        self.decl = {}
        self.ada_cache = {}
        self.prefix = ""
        self.over = {}
        self.stack = None
        self.s = None

    def begin_phase(self, prefix="", over=None):
        self.prefix = prefix
        self.over = over or {}
        self.stack = ExitStack()
        self.s = Sched(self.nc, self.stack)
        self.s.prefix = prefix

    def end_phase(self):
        if self.fused:
            self.s.barrier()
        self.s.emit()
        self.stack.close()
        self.nc.all_engine_barrier()
        self.nc.clear_and_free_semaphores(self.s.handles)
        self.nc.all_engine_barrier()
        return self.nc

    def dram_in(self, name, shape, dt=F32):
        if name in self.over:
            return self.over[name]
        ap = self.nc.dram_tensor(self.prefix + name, list(shape), dt, kind="ExternalInput").ap()
        self.decl[self.prefix + name] = ap
        return ap

    def dram_out(self, name, shape, dt=F32):
        if name in self.over:
            return self.over[name]
        return self.nc.dram_tensor(self.prefix + name, list(shape), dt, kind="ExternalOutput").ap()

    def dram_tmp(self, name, shape, dt=F32):
        return self.nc.dram_tensor(name, list(shape), dt)

    def gsb(self, name, shape, dt):
        return self.gstack.enter_context(self.nc.sbuf_tensor(name, list(shape), dt))

    def sb(self, name, shape, dt):
        return self.stack.enter_context(self.nc.sbuf_tensor(self.prefix + name, list(shape), dt))

    def ps(self, name, shape=(128, 512), dt=F32):
        return self.stack.enter_context(self.nc.psum_tensor(self.prefix + name, list(shape), dt))

    def buf(self, name="b"):
        self.nbuf += 1
        return Buf("%s%d" % (name, self.nbuf))


def emit_consts(cx):
    s = cx.s
    c = {}
    c["ones_t"] = cx.sb("ones_t", [128, 128], BF16)
    c["ones_b"] = cx.buf("ones")
    c["eps_t"] = cx.sb("eps_t", [128, 1], F32)
    c["eps_b"] = cx.buf("eps")
    s.op("pool", lambda e: e.memset(c["ones_t"][:], 1.0 / 1024.0), writes=[c["ones_b"]])
    s.op("pool", lambda e: e.memset(c["eps_t"][:], LN_EPS), writes=[c["eps_b"]])
    return c


def emit_adaln(cx, c_col, ada_w, ada_bT, scratch_ps, scratch_ps_b, pieces, piece_bufs, owner_bufs):
    s = cx.s
    key = cx.over.get("ada_key")
    if key is not None and key in cx.ada_cache:
        mod, mod1p = cx.ada_tiles[key]
        return dict(mod=mod, mod1p=mod1p, b_mod=cx.buf("adamod"), b_mod1p=cx.buf("adamod1p"))
    ccol = cx.sb("ada_c", [128, 8], F32)
    csil = cx.sb("ada_cs", [128, 8], F32)
    bT = cx.sb("ada_b", [128, 24], F32)
    if key is not None:
        mod, mod1p = cx.ada_tiles[key]
        cx.ada_cache[key] = True
    else:
        mod = cx.sb("ada_mod", [128, 24], F32)
        mod1p = cx.sb("ada_mod1p", [128, 24], F32)
    b_c, b_cs, b_mod, b_mod1p = cx.buf("adac"), cx.buf("adacs"), cx.buf("adamod"), cx.buf("adamod1p")
    b_bT = b_c
    s.dma("sp", ccol[:], c_col[:, :], writes=[b_c], anchor=b_c)
    s.dma("sp", bT[:], ada_bT[:, :], writes=[b_bT], anchor=b_bT)
    s.op("act", lambda e: e.activation(out=csil[:], in_=ccol[:], func=AF.Silu), reads=[b_c], writes=[b_cs])
    wv = ada_w.rearrange("(k p) n -> p k n", p=128)
    for col in range(24):
        t = pieces[col % 2]
        tb = piece_bufs[col % 2]
        s.dma("sp", t, wv[:, :, col * 128:(col + 1) * 128], writes=[tb], anchor=tb)
        for k in range(8):
            s.op("pe", lambda e, t=t, k=k, col=col: e.matmul(
                scratch_ps[:, col:col + 1], lhsT=t[:, k, :], rhs=csil[:, k:k + 1],
                start=(k == 0), stop=(k == 7)),
                reads=[tb, b_cs], writes=[scratch_ps_b], inc=(k == 7))
    s.op("dve", lambda e: e.tensor_tensor(out=mod[:], in0=scratch_ps[:, 0:24], in1=bT[:], op=ALU.add),
         reads=[scratch_ps_b, b_bT] + list(piece_bufs), writes=[b_mod] + list(owner_bufs))
    s.op("dve", lambda e: e.tensor_scalar_add(out=mod1p[:], in0=mod[:], scalar1=1.0), reads=[b_mod], writes=[b_mod1p])
    return dict(mod=mod, mod1p=mod1p, b_mod=b_mod, b_mod1p=b_mod1p)


def emit_ln_tile(cx, cst, z, zb_t, zsq_t, ps_sum, b_sum, ps_sq, b_sq, tmp, N, gT, bT, b_par, out_t, bz, bzb, bzsq, btmp, bout):
    s = cx.s
    for k in range(8):
        s.op("act", lambda e, k=k: e.activation(out=zb_t[:, k, :], in_=z[:, k, :], func=AF.Identity), reads=[bz], writes=[bzb])
        s.op("act", lambda e, k=k: e.activation(out=zsq_t[:, k, :], in_=z[:, k, :], func=AF.Square), reads=[bz], writes=[bzsq])
    for k in range(8):
        s.op("pe", lambda e, k=k: e.matmul(ps_sum, lhsT=cst["ones_t"][:], rhs=zb_t[:, k, :], start=(k == 0), stop=(k == 7)),
             reads=[cst["ones_b"], bzb], writes=[b_sum], inc=(k == 7))
    for k in range(8):
        s.op("pe", lambda e, k=k: e.matmul(ps_sq, lhsT=cst["ones_t"][:], rhs=zsq_t[:, k, :], start=(k == 0), stop=(k == 7)),
             reads=[cst["ones_b"], bzsq], writes=[b_sq], inc=(k == 7))
    mean = tmp[:, 0, :]
    msq = tmp[:, 1, :]
    var = tmp[:, 2, :]
    rstd = tmp[:, 3, :]
    s.op("dve", lambda e: e.tensor_copy(out=mean, in_=ps_sum), reads=[b_sum], writes=[btmp])
    s.op("dve", lambda e: e.tensor_tensor(out=msq, in0=mean, in1=mean, op=ALU.mult), reads=[btmp], writes=[btmp])
    s.op("dve", lambda e: e.tensor_tensor(out=var, in0=ps_sq, in1=msq, op=ALU.subtract), reads=[b_sq, btmp], writes=[btmp])
    s.op("act", lambda e: e.activation(out=var, in_=var, func=AF.Sqrt, bias=cst["eps_t"][:, 0:1]), reads=[btmp, cst["eps_b"]], writes=[btmp])
    s.op("dve", lambda e: e.reciprocal(out=rstd, in_=var), reads=[btmp], writes=[btmp])
    for k in range(8):
        s.op("dve", lambda e, k=k: e.tensor_tensor(out=z[:, k, :], in0=z[:, k, :], in1=mean, op=ALU.subtract), reads=[bz, btmp], writes=[bz])
        s.op("dve", lambda e, k=k: e.tensor_tensor(out=z[:, k, :], in0=z[:, k, :], in1=rstd, op=ALU.mult), reads=[bz, btmp], writes=[bz])
        s.op("act", lambda e, k=k: e.activation(out=out_t[:, k, :], in_=z[:, k, :], func=AF.Identity,
                                                 bias=bT[:, k:k + 1], scale=gT[:, k:k + 1]),
             reads=[bz, b_par], writes=[bout])


def build_mlp(cx=None, prefix="", over=None):
    cx = cx or Ctx()
    cx.begin_phase(prefix, over)
    s = cx.s
    N = 256
    NT = T // N
    xT = cx.dram_in("xT", [D, T])
    c_col = cx.dram_in("c_col", [128, 8])
    ada_w = cx.dram_in("ada_w", [D, 3 * D])
    ada_bT = cx.dram_in("ada_bT", [128, 24])
    ln_gT = cx.dram_in("ln_gT", [128, 8])
    ln_bT = cx.dram_in("ln_bT", [128, 8])
    w1 = cx.dram_in("w1", [D, DFF])
    w2 = cx.dram_in("w2", [DFF, D])
    yT = cx.dram_out("yT", [D, T])

    cst = emit_consts(cx)
    acc = [cx.ps("acc%d" % i) for i in range(4)]
    acc_b = [cx.buf("acc") for i in range(8)]
    ph = [cx.ps("ph%d" % i) for i in range(3)]
    ph_b = [cx.buf("ph") for i in range(3)]
    stp = cx.ps("stp")
    st_b = cx.buf("st")

    z = cx.sb("z", [128, 8, N], F32)
    bz = cx.buf("z")
    pieces = [z[:, 0:4, :].rearrange("p a (b n) -> p (a b) n", n=128), z[:, 4:8, :].rearrange("p a (b n) -> p (a b) n", n=128)]
    ada = emit_adaln(cx, c_col, ada_w, ada_bT, stp, st_b, pieces, [cx.buf("pc0"), cx.buf("pc1")], [bz])
    mod, mod1p = ada["mod"], ada["mod1p"]
    b_mod, b_mod1p = ada["b_mod"], ada["b_mod1p"]

    gT = cx.sb("gT", [128, 8], F32)
    bT = cx.sb("bT", [128, 8], F32)
    b_par = cx.buf("par")
    s.dma("sp", gT[:], ln_gT[:, :], writes=[b_par], anchor=b_par)
    s.dma("sp", bT[:], ln_bT[:, :], writes=[b_par], anchor=b_par)

    w1b = cx.sb("w1b", [128, 8, DFF], BF16)
    w2b = cx.sb("w2b", [128, 32, D], BF16)
    b_w1x = cx.buf("w1")
    b_w2x = [cx.buf("w2") for k in range(4)]
    b_w1 = [b_w1x for k in range(8)]
    b_w2 = [b_w2x[k // 8] for k in range(32)]
    w1v = w1.rearrange("(k p) n -> p k n", p=128)
    w2v = w2.rearrange("(k p) n -> p k n", p=128)
    for k in range(8):
        s.dma("pool", w1b[:, k, :], w1v[:, k, :], writes=[b_w1[k]], anchor=b_w1[k])
    for k in range(32):
        s.dma("pool", w2b[:, k, :], w2v[:, k, :], writes=[b_w2[k]], anchor=b_w2[k])

    jobs = cx.over.get("jobs") or [(xT, yT, cx.over.get("tail_out"))]
    jobv = [(xj.rearrange("(k p) t -> p k t", p=128), yj.rearrange("(k p) t -> p k t", p=128), tj) for (xj, yj, tj) in jobs]
    NXB = 2
    xt = [cx.sb("xt%d" % i, [128, 8, N], F32) for i in range(NXB)]
    xt_b = [cx.buf("xt") for i in range(NXB)]
    ht = cx.sb("ht", [128, 8, N], BF16)
    ht_b = cx.buf("ht")
    rb = [cx.sb("rb%d" % i, [128, N], F32) for i in range(3)]
    rb_b = [cx.buf("rb") for i in range(3)]
    hid = [cx.sb("hid%d" % i, [128, N], BF16) for i in range(4)]
    hid_b = [cx.buf("hid") for i in range(4)]
    zb_t = cx.sb("zb", [128, 8, N], BF16)
    bzb = cx.buf("zb")
    zsq_t = cx.sb("zsq", [128, 8, N], BF16)
    bzsq = cx.buf("zsq")
    tmp = cx.sb("tmp", [128, 4, N], F32)
    btmp = cx.buf("tmp")

    def load_x(gi):
        xv_ = jobv[gi // NT][0]
        it_ = gi % NT
        s.dma("sp", xt[gi % NXB][:], xv_[:, :, it_ * N:(it_ + 1) * N], writes=[xt_b[gi % NXB]], anchor=xt_b[gi % NXB])

    hts = [ht, cx.sb("ht2", [128, 8, N], BF16)]
    hts_b = [ht_b, cx.buf("ht")]
    NG = NT * len(jobv)

    def modulate(gi):
        x_t, xb = xt[gi % NXB], xt_b[gi % NXB]
        h_t, h_b = hts[gi % 2], hts_b[gi % 2]
        for k in range(8):
            s.op("act", lambda e, k=k: e.activation(out=h_t[:, k, :], in_=x_t[:, k, :], func=AF.Identity,
                                                    bias=mod[:, k:k + 1], scale=mod1p[:, 8 + k:9 + k]),
                 reads=[xb, b_mod, b_mod1p], writes=[h_b])
        s.op("act", lambda e: e.activation(out=x_t[:], in_=x_t[:], func=AF.Identity, scale=float(DN_ALPHA)),
             reads=[xb], writes=[xb])

    def g1(gi, hc):
        p = ph[hc % 3]
        h_t, h_b = hts[gi % 2], hts_b[gi % 2]
        for k in range(8):
            s.op("pe", lambda e, p=p, k=k, hc=hc: e.matmul(p[:, 0:N], lhsT=w1b[:, k, hc * 128:(hc + 1) * 128], rhs=h_t[:, k, :],
                                                          start=(k == 0), stop=(k == 7)),
                 reads=[b_w1[k], h_b], writes=[ph_b[hc % 3]], inc=(k == 7))

    def ew(hc):
        p = ph[hc % 3]
        r = rb[hc % 3]
        h_ = hid[hc % 4]
        s.op("act", lambda e, p=p, r=r: e.activation(out=r[:], in_=p[:, 0:N], func=AF.Relu), reads=[ph_b[hc % 3]], writes=[rb_b[hc % 3]])
        s.op("dve", lambda e, r=r, h_=h_: e.tensor_tensor(out=h_[:], in0=r[:], in1=r[:], op=ALU.mult), reads=[rb_b[hc % 3]], writes=[hid_b[hc % 4]])

    def g2(hc):
        h_ = hid[hc % 4]
        for oc in range(8):
            a = acc[oc // 2]
            s.op("pe", lambda e, a=a, oc=oc, hc=hc, h_=h_: e.matmul(a[:, (oc % 2) * N:(oc % 2 + 1) * N],
                                                                    lhsT=w2b[:, hc, oc * 128:(oc + 1) * 128], rhs=h_[:],
                                                                    start=(hc == 0 and oc % 2 == 0), stop=(hc == 31),
                                                                    skip_group_check=True),
                 reads=[b_w2[hc], hid_b[hc % 4]], writes=[acc_b[oc]], inc=(oc == 7 or hc == 31))

    def head(gi):
        g1(gi, 0)
        g1(gi, 1)
        ew(0)

    def epilogue(gi):
        it = gi % NT
        yv, tail_out = jobv[gi // NT][1], jobv[gi // NT][2]
        x_t, xb = xt[gi % NXB], xt_b[gi % NXB]
        for oc in range(8):
            a = acc[oc // 2]
            s.op("dve", lambda e, a=a, oc=oc: e.scalar_tensor_tensor(
                out=z[:, oc, :], in0=a[:, (oc % 2) * N:(oc % 2 + 1) * N], scalar=mod1p[:, 16 + oc:17 + oc], in1=x_t[:, oc, :],
                op0=ALU.mult, op1=ALU.add), reads=[acc_b[oc], xb, b_mod1p], writes=[bz])
        emit_ln_tile(cx, cst, z, zb_t, zsq_t, stp[:, 0:N], st_b, stp[:, N:2 * N], st_b, tmp, N, gT, bT, b_par, z, bz, bzb, bzsq, btmp, bz)
        s.dma("sp", yv[:, :, it * N:(it + 1) * N], z[:], reads=[bz], anchor=bz)
        if it == NT - 1 and tail_out is not None:
            s.dma("sp", tail_out.rearrange("p (k t) -> p k t", t=2), z[:, :, N - 2:N], reads=[bz], anchor=bz)

    load_x(0)
    if NG > 1:
        load_x(1)
    modulate(0)
    head(0)
    for gi in range(NG):
        for hc in range(32):
            if hc + 2 < 32:
                g1(gi, hc + 2)
            if hc + 1 < 32:
                ew(hc + 1)
            g2(hc)
            if hc == 27 and gi + 1 < NG:
                modulate(gi + 1)
        if gi + 1 < NG:
            head(gi + 1)
        epilogue(gi)
        if gi + 2 < NG:
            load_x(gi + 2)
    s.final_wait("sp", [bz])
    return cx.end_phase()


TH = T + 128


def build_attn(cx=None, prefix="", over=None):
    cx = cx or Ctx()
    cx.begin_phase(prefix, over)
    s = cx.s
    N = 512
    NT = T // N
    SCALE = 128.0 ** -0.5
    xT = cx.dram_in("xT", [D, TH])
    c_col = cx.dram_in("c_col", [128, 8])
    ada_w = cx.dram_in("ada_w", [D, 3 * D])
    ada_bT = cx.dram_in("ada_bT", [128, 24])
    ln_gT = cx.dram_in("ln_gT", [128, 8])
    ln_bT = cx.dram_in("ln_bT", [128, 8])
    w_in = cx.dram_in("w_in", [D, 1536])
    w_out = cx.dram_in("w_out", [D, D])
    cosT = cx.dram_in("cosT", [32, TH])
    sinT = cx.dram_in("sinT", [32, TH])
    perm = cx.dram_in("perm", [32, 32])
    mprev = cx.dram_in("mprev", [128, 512])
    mnext = cx.dram_in("mnext", [128, 512])
    sink_rep = cx.dram_in("sink_rep", [128, 8])
    yT = cx.dram_out("yT", [D, T])

    cst = emit_consts(cx)
    one_t = cx.sb("one_t", [128, 128], BF16)
    one_b = cx.buf("one")
    s.op("pool", lambda e: e.memset(one_t[:], 1.0), writes=[one_b])

    banks = [cx.ps("bank%d" % i) for i in range(8)]
    bank_b = [cx.buf("bank") for i in range(8)]
    bctr = [0]

    def bank():
        i = bctr[0] % 8
        bctr[0] += 1
        return banks[i], bank_b[i]

    z = cx.sb("z", [128, 8, N], F32)
    bz = cx.buf("z")
    pieces = [z[:, 0:2, :].rearrange("p a (b n) -> p (a b) n", n=128), z[:, 2:4, :].rearrange("p a (b n) -> p (a b) n", n=128)]
    pb, pbb = bank()
    ada = emit_adaln(cx, c_col, ada_w, ada_bT, pb, pbb, pieces, [cx.buf("pc0"), cx.buf("pc1")], [bz])
    mod, mod1p = ada["mod"], ada["mod1p"]
    b_mod, b_mod1p = ada["b_mod"], ada["b_mod1p"]

    gT = cx.sb("gT", [128, 8], F32)
    bT = cx.sb("bT", [128, 8], F32)
    b_par = cx.buf("par")
    s.dma("sp", gT[:], ln_gT[:, :], writes=[b_par], anchor=b_par)
    s.dma("sp", bT[:], ln_bT[:, :], writes=[b_par], anchor=b_par)

    perm_t = cx.sb("perm_t", [32, 32], BF16)
    mprev_t = cx.sb("mprev_t", [128, 512], BF16)
    mnext_t = cx.sb("mnext_t", [128, 512], BF16)
    b_cm = cx.buf("cm")
    s.dma("pool", perm_t[:], perm[:, :], writes=[b_cm], anchor=b_cm)
    s.dma("pool", mprev_t[:], mprev[:, :], writes=[b_cm], anchor=b_cm)
    s.dma("pool", mnext_t[:], mnext[:, :], writes=[b_cm], anchor=b_cm)
    sk = cx.sb("sk", [128, 8], F32)
    b_sk = cx.buf("sk")
    s.dma("sp", sk[:], sink_rep[:, :], writes=[b_sk], anchor=b_sk)
    s.op("act", lambda e: e.activation(out=sk[:], in_=sk[:], func=AF.Exp), reads=[b_sk], writes=[b_sk])
    es_full = cx.sb("es_full", [128, 2, 512], F32)
    b_es = cx.buf("es")
    s.op("pool", lambda e: e.memset(es_full[:], 0.0), writes=[b_es])
    for hh in range(8):
        s.op("pool", lambda e, hh=hh: e.tensor_scalar(out=es_full[:, hh // 4, (hh % 4) * 128:(hh % 4 + 1) * 128],
                                                       in0=es_full[:, hh // 4, (hh % 4) * 128:(hh % 4 + 1) * 128],
                                                       scalar1=sk[:, hh:hh + 1], scalar2=None, op0=ALU.add),
             reads=[b_sk, b_es], writes=[b_es])

    w_inb = cx.sb("w_inb", [128, 8, 1536], BF16)
    w_outb = cx.sb("w_outb", [128, 8, D], BF16)
    b_win = cx.buf("win")
    b_wout = cx.buf("wout")
    winv = w_in.rearrange("(k p) n -> p k n", p=128)
    woutv = w_out.rearrange("(k p) n -> p k n", p=128)
    for k in range(8):
        s.dma("pool", w_inb[:, k, :], winv[:, k, :], writes=[b_win], anchor=b_win)
    for k in range(8):
        s.dma("pool", w_outb[:, k, :], woutv[:, k, :], writes=[b_wout], anchor=b_wout)

    kT_all = cx.sb("kT_all", [128, 2, TH], BF16)
    v_all = cx.sb("v_all", [128, TH // 128, 256], BF16)
    b_k = [cx.buf("k") for i in range(NT + 1)]
    b_v = [cx.buf("v") for i in range(NT + 1)]

    xv = xT.rearrange("(k p) t -> p k t", p=128)
    yv = yT.rearrange("(k p) t -> p k t", p=128)
    xt = cx.sb("xt", [128, 8, N], F32)
    xb = cx.buf("xt")
    ht = cx.sb("ht", [128, 8, N], BF16)
    ht_b = cx.buf("ht")
    cs_t = cx.sb("cs_t", [32, 2, N], F32)
    b_cs = cx.buf("cs")
    rtmp = cx.sb("rtmp", [32, 2, N], F32)
    b_rtmp = cx.buf("rtmp")
    qtmp = cx.sb("qtmp", [128, N], BF16)
    b_qtmp = cx.buf("qtmp")
    qT = cx.sb("qT", [128, 4, 8, 128], BF16)
    b_q = cx.buf("q")
    pT = [cx.sb("pT%d" % i, [128, 512], BF16) for i in range(6)]
    pT_b = [cx.buf("pT") for i in range(6)]
    pctr = [0]
    oT = cx.sb("oT", [128, 8, N], BF16)
    b_o = cx.buf("o")
    zb_t = cx.sb("zb", [128, 8, N], BF16)
    bzb = cx.buf("zb")
    zsq_t = cx.sb("zsq", [128, 8, N], BF16)
    bzsq = cx.buf("zsq")
    tmp = cx.sb("tmp", [128, 4, N], F32)
    btmp = cx.buf("tmp")

    def load_tile(t0, n):
        s.dma("sp", xt[:, :, 0:n], xv[:, :, t0:t0 + n], writes=[xb], anchor=xb)
        s.dma("sp", cs_t[:, 0, 0:n], cosT[:, t0:t0 + n], writes=[b_cs], anchor=b_cs)
        s.dma("sp", cs_t[:, 1, 0:n], sinT[:, t0:t0 + n], writes=[b_cs], anchor=b_cs)
        for k in range(8):
            s.op("act", lambda e, k=k, n=n: e.activation(out=ht[:, k, 0:n], in_=xt[:, k, 0:n], func=AF.Identity,
                                                         bias=mod[:, k:k + 1], scale=mod1p[:, 8 + k:9 + k]),
                 reads=[xb, b_mod, b_mod1p], writes=[ht_b])

    def rope(view32, vb, n):
        pbk, pbb_ = bank()
        s.op("pe", lambda e: e.matmul(pbk[0:32, 0:n], lhsT=perm_t[:, :], rhs=view32, start=True, stop=True),
             reads=[b_cm, vb], writes=[pbb_])
        s.op("dve", lambda e: e.tensor_tensor(out=rtmp[:, 0, 0:n], in0=pbk[0:32, 0:n], in1=cs_t[:, 1, 0:n], op=ALU.mult),
             reads=[pbb_, b_cs], writes=[b_rtmp])
        s.op("dve", lambda e: e.tensor_tensor(out=rtmp[:, 1, 0:n], in0=view32, in1=cs_t[:, 0, 0:n], op=ALU.mult),
             reads=[vb, b_cs], writes=[b_rtmp])
        s.op("dve", lambda e: e.tensor_tensor(out=view32, in0=rtmp[:, 0, 0:n], in1=rtmp[:, 1, 0:n], op=ALU.add),
             reads=[b_rtmp], writes=[vb])

    for it in range(NT + 1):
        t0 = it * N
        n = N if it < NT else 128
        load_tile(t0, n)
        for kvh in range(2):
            pbk, pbb_ = bank()
            for k in range(8):
                s.op("pe", lambda e, pbk=pbk, k=k, kvh=kvh, n=n: e.matmul(pbk[:, 0:n], lhsT=w_inb[:, k, 1024 + kvh * 128:1152 + kvh * 128],
                                                                          rhs=ht[:, k, 0:n], start=(k == 0), stop=(k == 7)),
                     reads=[b_win, ht_b], writes=[pbb_], inc=(k == 7))
            s.op("act", lambda e, pbk=pbk, kvh=kvh, t0=t0, n=n: e.activation(out=kT_all[:, kvh, t0:t0 + n], in_=pbk[:, 0:n], func=AF.Identity),
                 reads=[pbb_], writes=[b_k[it]])
            rope(kT_all[0:32, kvh, t0:t0 + n], b_k[it], n)
        for blk in range(n // 128):
            pbk, pbb_ = bank()
            for k in range(8):
                s.op("pe", lambda e, pbk=pbk, k=k, blk=blk: e.matmul(pbk[:, 0:256], lhsT=ht[:, k, blk * 128:(blk + 1) * 128],
                                                                     rhs=w_inb[:, k, 1280:1536], start=(k == 0), stop=(k == 7)),
                     reads=[b_win, ht_b], writes=[pbb_], inc=(k == 7))
            s.op("act", lambda e, pbk=pbk, blk=blk, t0=t0: e.activation(out=v_all[:, t0 // 128 + blk, :], in_=pbk[:, 0:256], func=AF.Identity),
                 reads=[pbb_], writes=[b_v[it]])

    class _RV:
        def __init__(self, ts, bs):
            self.t, self.b, self.i = ts, bs, 0

        def get(self):
            j = self.i % len(self.t)
            self.i += 1
            return self.t[j], self.b[j]

    SCB = _RV(banks[0:4], bank_b[0:4])
    PVB = _RV(banks[4:8], bank_b[4:8])
    PTP = _RV(pT, pT_b)
    DENP = Rot(cx, "den", 2, [128, 512], F32)
    for it in range(NT):
        t0 = it * N
        load_tile(t0, N)
        for hd in range(8):
            pbk, pbb_ = bank()
            for k in range(8):
                s.op("pe", lambda e, pbk=pbk, k=k, hd=hd: e.matmul(pbk[:, 0:N], lhsT=w_inb[:, k, hd * 128:(hd + 1) * 128],
                                                                   rhs=ht[:, k, :], start=(k == 0), stop=(k == 7)),
                     reads=[b_win, ht_b], writes=[pbb_], inc=(k == 7))
            s.op("act", lambda e, pbk=pbk: e.activation(out=qtmp[:], in_=pbk[:, 0:N], func=AF.Identity),
                 reads=[pbb_], writes=[b_qtmp])
            rope(qtmp[0:32, :], b_qtmp, N)
            s.op("pool", lambda e, hd=hd: e.tensor_copy(out=qT[:, :, hd, :], in_=qtmp[:].rearrange("p (a n) -> p a n", n=128)),
                 reads=[b_qtmp], writes=[b_q])
        s.op("act", lambda e: e.activation(out=xt[:], in_=xt[:], func=AF.Identity, scale=float(DN_ALPHA)),
             reads=[xb], writes=[xb])
        def a_s0(itm, it=it):
            qb, kvh = itm["qb"], itm["kvh"]
            B = it * 4 + qb
            qrhs = qT[:, qb, kvh * 4:(kvh + 1) * 4, :].rearrange("p h n -> p (h n)")
            itm["sc"] = []
            for jb in (B - 1, B, B + 1):
                if jb < 0:
                    continue
                pbk, pbb_ = SCB.get()
                kb = b_k[min(jb // 4, NT)]
                s.op("pe", lambda e, pbk=pbk, jb=jb: e.matmul(pbk[:, :], lhsT=kT_all[:, kvh, jb * 128:(jb + 1) * 128], rhs=qrhs,
                                                              start=True, stop=True),
                     reads=[kb, b_q], writes=[pbb_])
                itm["sc"].append((jb, B, pbk, pbb_))

        def a_s1(itm):
            itm["pts"] = []
            for (jb, B, pbk, pbb_) in itm["sc"]:
                p_t, p_b = PTP.get()
                s.op("act", lambda e, pbk=pbk, p_t=p_t: e.activation(out=p_t[:], in_=pbk[:, :], func=AF.Exp, scale=SCALE),
                     reads=[pbb_], writes=[p_b])
                if jb == B - 1:
                    s.op("pool", lambda e, p_t=p_t: e.tensor_tensor(out=p_t[:], in0=p_t[:], in1=mprev_t[:], op=ALU.mult),
                         reads=[p_b, b_cm], writes=[p_b])
                elif jb == B + 1:
                    s.op("pool", lambda e, p_t=p_t: e.tensor_tensor(out=p_t[:], in0=p_t[:], in1=mnext_t[:], op=ALU.mult),
                         reads=[p_b, b_cm], writes=[p_b])
                itm["pts"].append((jb, p_t, p_b))

        def a_s2(itm):
            kvh = itm["kvh"]
            pts = itm["pts"]
            po, pob = PVB.get()
            pd, pdb = PVB.get()
            itm.update(po=po, pob=pob, pd=pd, pdb=pdb)
            for i, (jb, p_t, p_b) in enumerate(pts):
                vb = b_v[min(jb // 4, NT)]
                s.op("pe", lambda e, jb=jb, p_t=p_t, i=i: e.matmul(po[:, :], lhsT=v_all[:, jb, kvh * 128:(kvh + 1) * 128],
                                                                   rhs=p_t[:], start=(i == 0), stop=(i == len(pts) - 1)),
                     reads=[vb, p_b], writes=[pob], inc=(i == len(pts) - 1))
            for i, (jb, p_t, p_b) in enumerate(pts):
                s.op("pe", lambda e, p_t=p_t, i=i: e.matmul(pd[:, :], lhsT=one_t[:], rhs=p_t[:], start=(i == 0), stop=(i == len(pts) - 1)),
                     reads=[one_b, p_b], writes=[pdb], inc=(i == len(pts) - 1))

        def a_s3(itm):
            qb, kvh = itm["qb"], itm["kvh"]
            po, pob, pd, pdb = itm["po"], itm["pob"], itm["pd"], itm["pdb"]
            den, den_b = DENP.get()
            s.op("dve", lambda e: e.tensor_tensor(out=den[:], in0=pd[:, :], in1=es_full[:, kvh, :], op=ALU.add),
                 reads=[pdb, b_es], writes=[den_b])
            s.op("dve", lambda e: e.reciprocal(out=den[:], in_=den[:]), reads=[den_b], writes=[den_b])
            s.op("dve", lambda e: e.tensor_tensor(
                out=oT[:, kvh * 4:(kvh + 1) * 4, qb * 128:(qb + 1) * 128], in0=po[:, :].rearrange("p (h n) -> p h n", n=128),
                in1=den[:].rearrange("p (h n) -> p h n", n=128), op=ALU.mult),
                reads=[pob, den_b], writes=[b_o])

        run_pipeline([{"qb": qb, "kvh": kvh} for qb in range(4) for kvh in range(2)], [a_s0, a_s1, a_s2, a_s3])
        for oc in range(8):
            pbk, pbb_ = bank()
            for k in range(8):
                s.op("pe", lambda e, pbk=pbk, k=k, oc=oc: e.matmul(pbk[:, :], lhsT=w_outb[:, k, oc * 128:(oc + 1) * 128], rhs=oT[:, k, :],
                                                                   start=(k == 0), stop=(k == 7)),
                     reads=[b_wout, b_o], writes=[pbb_], inc=(k == 7))
            s.op("dve", lambda e, pbk=pbk, oc=oc: e.scalar_tensor_tensor(out=z[:, oc, :], in0=pbk[:, :], scalar=mod1p[:, 16 + oc:17 + oc],
                                                                         in1=xt[:, oc, :], op0=ALU.mult, op1=ALU.add),
                 reads=[pbb_, xb, b_mod1p], writes=[bz])
        ps1, ps1b = bank()
        ps2, ps2b = bank()
        emit_ln_tile(cx, cst, z, zb_t, zsq_t, ps1[:, :], ps1b, ps2[:, :], ps2b, tmp, N, gT, bT, b_par, z, bz, bzb, bzsq, btmp, bz)
        s.dma("sp", yv[:, :, t0:t0 + N], z[:], reads=[bz], anchor=bz)
    s.final_wait("sp", [bz])
    return cx.end_phase()


class Rot:
    def __init__(self, cx, name, n, shape, dt):
        self.t = [cx.sb("%s%d" % (name, i), shape, dt) for i in range(n)]
        self.b = [cx.buf(name) for i in range(n)]
        self.i = 0

    def get(self):
        j = self.i % len(self.t)
        self.i += 1
        return self.t[j], self.b[j]


def load_gate_params(cx, w_a, w_x, b_aT, b_xT, lamT):
    s = cx.s
    g = {}
    g["wa"] = cx.sb("g_wa", [96, 16, 96], BF16)
    g["wx"] = cx.sb("g_wx", [96, 16, 96], BF16)
    g["b_w"] = cx.buf("gw")
    s.dma("pool", g["wa"][:], w_a.rearrange("n i j -> i n j"), writes=[g["b_w"]], anchor=g["b_w"])
    s.dma("pool", g["wx"][:], w_x.rearrange("n i j -> i n j"), writes=[g["b_w"]], anchor=g["b_w"])
    g["ba"] = cx.sb("g_ba", [96, 16], F32)
    g["bx"] = cx.sb("g_bx", [96, 16], F32)
    g["lamc"] = cx.sb("g_lamc", [96, 16], F32)
    g["one1"] = cx.sb("g_one1", [96, 1], F32)
    g["b_p"] = cx.buf("gp")
    s.dma("sp", g["ba"][:], b_aT[:, :], writes=[g["b_p"]], anchor=g["b_p"])
    s.dma("sp", g["bx"][:], b_xT[:, :], writes=[g["b_p"]], anchor=g["b_p"])
    s.dma("sp", g["lamc"][:], lamT[:, :], writes=[g["b_p"]], anchor=g["b_p"])
    s.op("pool", lambda e: e.memset(g["one1"][:], 1.0), writes=[g["b_p"]])
    s.op("act", lambda e: e.activation(out=g["lamc"][:], in_=g["lamc"][:], func=AF.Exp, scale=-1.0), reads=[g["b_p"]], writes=[g["b_p"]])
    s.op("act", lambda e: e.activation(out=g["lamc"][:], in_=g["lamc"][:], func=AF.Ln, bias=g["one1"][:, 0:1]), reads=[g["b_p"]], writes=[g["b_p"]])
    s.op("pool", lambda e: e.tensor_scalar(out=g["lamc"][:], in0=g["lamc"][:], scalar1=-4.0, scalar2=None, op0=ALU.mult),
         reads=[g["b_p"]], writes=[g["b_p"]])
    g["lam2"] = cx.sb("g_lam2", [96, 16], F32)
    s.op("pool", lambda e: e.tensor_scalar(out=g["lam2"][:], in0=g["lamc"][:], scalar1=2.0, scalar2=None, op0=ALU.mult),
         reads=[g["b_p"]], writes=[g["b_p"]])
    s.op("pool", lambda e: e.tensor_scalar(out=g["ba"][:], in0=g["ba"][:], scalar1=0.5, scalar2=None, op0=ALU.mult),
         reads=[g["b_p"]], writes=[g["b_p"]])
    s.op("pool", lambda e: e.tensor_scalar(out=g["bx"][:], in0=g["bx"][:], scalar1=0.5, scalar2=None, op0=ALU.mult),
         reads=[g["b_p"]], writes=[g["b_p"]])
    return g


def emit_gates(cx, g, n, c_t, c_b, cb_t, cb_b, bank_fn, pools, N):
    s = cx.s
    pbk, pbb_ = bank_fn()
    s.op("pe", lambda e: e.matmul(pbk[0:96, 0:N], lhsT=g["wa"][:, n, :], rhs=cb_t[:], start=True, stop=True),
         reads=[g["b_w"], cb_b], writes=[pbb_])
    s.op("pe", lambda e: e.matmul(pbk[0:96, N:2 * N], lhsT=g["wx"][:, n, :], rhs=cb_t[:], start=True, stop=True),
         reads=[g["b_w"], cb_b], writes=[pbb_])
    r_t, r_b = pools["r"].get()
    i_t, i_b = pools["i"].get()
    a_t, a_b = pools["a"].get()
    m_t, m_b = pools["m"].get()
    u_t, u_b = pools["u"].get()
    s.op("act", lambda e: e.activation(out=r_t[:], in_=pbk[0:96, 0:N], func=AF.Sigmoid, bias=g["ba"][:, n:n + 1]),
         reads=[pbb_, g["b_p"]], writes=[r_b])
    s.op("act", lambda e: e.activation(out=i_t[:], in_=pbk[0:96, N:2 * N], func=AF.Sigmoid, bias=g["bx"][:, n:n + 1]),
         reads=[pbb_, g["b_p"]], writes=[i_b])
    s.op("act", lambda e: e.activation(out=a_t[:], in_=r_t[:], func=AF.Exp, scale=g["lamc"][:, n:n + 1]),
         reads=[r_b, g["b_p"]], writes=[a_b])
    s.op("dve", lambda e: e.tensor_tensor(out=m_t[:], in0=a_t[:], in1=a_t[:], op=ALU.mult), reads=[a_b], writes=[m_b])
    s.op("dve", lambda e: e.tensor_scalar(out=m_t[:], in0=m_t[:], scalar1=-1.0, scalar2=1.0, op0=ALU.mult, op1=ALU.add),
         reads=[m_b], writes=[m_b])
    s.op("act", lambda e: e.activation(out=m_t[:], in_=m_t[:], func=AF.Sqrt), reads=[m_b], writes=[m_b])
    s.op("dve", lambda e: e.tensor_tensor(out=u_t[:], in0=i_t[:], in1=c_t, op=ALU.mult), reads=[i_b, c_b], writes=[u_b])
    s.op("dve", lambda e: e.tensor_tensor(out=u_t[:], in0=u_t[:], in1=m_t[:], op=ALU.mult), reads=[u_b, m_b], writes=[u_b])
    return a_t, a_b, u_t, u_b


def gate_pools(cx, N):
    return {k: Rot(cx, "gp_" + k, 2, [96, N], F32) for k in ("r", "i", "a", "m", "u")}


def make_banks(cx, nb):
    banks = [cx.ps("bank%d" % i) for i in range(nb)]
    bank_b = [cx.buf("bank") for i in range(nb)]
    ctr = [0]

    def bank():
        i = ctr[0] % nb
        ctr[0] += 1
        return banks[i], bank_b[i]
    return bank


def run_pipeline(items, stages):
    S = len(stages)
    for step in range(len(items) + S - 1):
        gens = []
        for k in range(S - 1, -1, -1):
            i = step - k
            if 0 <= i < len(items):
                r = stages[k](items[i])
                if r is not None and hasattr(r, "__next__"):
                    gens.append(r)
        while gens:
            alive = []
            for g_ in gens:
                try:
                    next(g_)
                    alive.append(g_)
                except StopIteration:
                    pass
            gens = alive


class RotBanks:
    def __init__(self, cx, name, n):
        self.t = [cx.ps("%s%d" % (name, i)) for i in range(n)]
        self.b = [cx.buf(name) for i in range(n)]
        self.i = 0

    def get(self):
        j = self.i % len(self.t)
        self.i += 1
        return self.t[j], self.b[j]


SQB = 4


def gate_stage_fns(cx, g, N, RI, P, items_ref):
    s = cx.s

    def st_gmm(it):
        n = it["n"]
        cb_t, cb_b = P["cb"].get()
        c_t = it["c_t"]
        s.op("pool", lambda e: e.tensor_copy(out=cb_t[:], in_=c_t[:]), reads=[it["c_b"]], writes=[cb_b])
        pbk, pbb_ = RI.get()
        it["ri"], it["ri_b"] = pbk, pbb_
        s.op("pe", lambda e: e.matmul(pbk[0:96, 0:N], lhsT=g["wa"][:, n, :], rhs=cb_t[:], start=True, stop=True),
             reads=[g["b_w"], cb_b], writes=[pbb_])
        s.op("pe", lambda e: e.matmul(pbk[0:96, N:2 * N], lhsT=g["wx"][:, n, :], rhs=cb_t[:], start=True, stop=True),
             reads=[g["b_w"], cb_b], writes=[pbb_])

    def st_sig(it):
        n = it["n"]
        pbk, pbb_ = it["ri"], it["ri_b"]
        r_t, r_b = P["r"].get()
        i_t, i_b = P["i"].get()
        a_t, a_b = P["a"].get()
        m_t, m_b = P["m"].get()
        it.update(i_t=i_t, i_b=i_b, a_t=a_t, a_b=a_b, m_t=m_t, m_b=m_b)
        s.op("act", lambda e: e.activation(out=r_t[:], in_=pbk[0:96, 0:N], func=AF.Tanh, bias=g["ba"][:, n:n + 1], scale=0.5),
             reads=[pbb_, g["b_p"]], writes=[r_b])
        s.op("act", lambda e: e.activation(out=i_t[:], in_=pbk[0:96, N:2 * N], func=AF.Tanh, bias=g["bx"][:, n:n + 1], scale=0.5),
             reads=[pbb_, g["b_p"]], writes=[i_b])
        yield
        s.op("act", lambda e: e.activation(out=a_t[:], in_=r_t[:], func=AF.Exp, bias=g["lamc"][:, n:n + 1], scale=g["lamc"][:, n:n + 1]),
             reads=[r_b, g["b_p"]], writes=[a_b])
        s.op("act", lambda e: e.activation(out=m_t[:], in_=r_t[:], func=AF.Exp, bias=g["lam2"][:, n:n + 1], scale=g["lam2"][:, n:n + 1]),
             reads=[r_b, g["b_p"]], writes=[m_b])

    def st_m(it):
        i_t, i_b = it["i_t"], it["i_b"]
        m_t, m_b = it["m_t"], it["m_b"]
        u_t, u_b = P["u"].get()
        it.update(u_t=u_t, u_b=u_b)
        c_t = it["c_t"]
        s.op("pool", lambda e: e.tensor_scalar(out=m_t[:], in0=m_t[:], scalar1=-1.0, scalar2=1.0, op0=ALU.mult, op1=ALU.add),
             reads=[m_b], writes=[m_b])
        s.op("dve", lambda e: e.scalar_tensor_tensor(out=u_t[:], in0=i_t[:], scalar=1.0, in1=c_t[:], op0=ALU.add, op1=ALU.mult),
             reads=[i_b, it["c_b"]], writes=[u_b])

    def st_sqrt(it):
        if it["idx"] % SQB != SQB - 1:
            return
        for j in range(it["idx"] - SQB + 1, it["idx"] + 1):
            m_t, m_b = items_ref[j]["m_t"], items_ref[j]["m_b"]
            s.op("act", lambda e, m_t=m_t: e.activation(out=m_t[:], in_=m_t[:], func=AF.Sqrt), reads=[m_b], writes=[m_b])

    return st_gmm, st_sig, st_m, st_sqrt


def build_rnn1(cx=None, prefix="", over=None):
    cx = cx or Ctx()
    cx.begin_phase(prefix, over)
    s = cx.s
    N = 256
    NT = T // N
    NX = N + 2
    xT = cx.dram_in("xT", [D, T + 2])
    c_col = cx.dram_in("c_col", [128, 8])
    ada_w = cx.dram_in("ada_w", [D, 3 * D])
    ada_bT = cx.dram_in("ada_bT", [128, 24])
    w_in = cx.dram_in("w_in", [D, 2 * DRNN])
    convT = cx.dram_in("convT", [96, 16 * 6])
    w_a = cx.dram_in("w_a", [16, 96, 96])
    w_x = cx.dram_in("w_x", [16, 96, 96])
    b_aT = cx.dram_in("b_aT", [96, 16])
    b_xT = cx.dram_in("b_xT", [96, 16])
    lamT = cx.dram_in("lamT", [96, 16])
    carry_only = bool(cx.over.get("carry_only"))
    if not carry_only:
        cT = cx.dram_out("cT", [DRNN, T])
        GT = cx.dram_out("GT", [DRNN, T])
        h1T = cx.dram_out("h1T", [DRNN, T])
        cv = cT.rearrange("(n p) t -> p n t", p=96)
        Gv = GT.rearrange("(n p) t -> p n t", p=96)
        hv = h1T.rearrange("(n p) t -> p n t", p=96)

    XB = RotBanks(cx, "pX", 3)
    GB = RotBanks(cx, "pG", 2)
    RI = RotBanks(cx, "pRI", 2)
    misc = cx.ps("pmisc")
    misc_b = cx.buf("pmisc")
    zt = cx.sb("zt", [128, 8, N], F32)
    bzt = cx.buf("zt")
    pieces = [zt[:, 0:4, :].rearrange("p a (b n) -> p (a b) n", n=128), zt[:, 4:8, :].rearrange("p a (b n) -> p (a b) n", n=128)]
    ada = emit_adaln(cx, c_col, ada_w, ada_bT, misc, misc_b, pieces, [cx.buf("pc0"), cx.buf("pc1")], [bzt])
    mod, mod1p, b_mod, b_mod1p = ada["mod"], ada["mod1p"], ada["b_mod"], ada["b_mod1p"]

    w_inb = cx.sb("w_inb", [128, 8, 2 * DRNN], BF16)
    b_win = cx.buf("win")
    winv = w_in.rearrange("(k p) n -> p k n", p=128)
    for k in range(8):
        s.dma("pool", w_inb[:, k, :], winv[:, k, :], writes=[b_win], anchor=b_win)
    g = load_gate_params(cx, w_a, w_x, b_aT, b_xT, lamT)
    cw = cx.sb("cw", [96, 16 * 6], F32)
    b_cw = cx.buf("cw")
    s.dma("sp", cw[:], convT[:, :], writes=[b_cw], anchor=b_cw)

    xv = xT.rearrange("(k p) t -> p k t", p=128)
    xt = Rot(cx, "xt", 2, [128, 8, NX], F32)
    htp = Rot(cx, "ht", 2, [128, 8, NX], BF16)
    tail = cx.sb("tail", [96, 16, 2], F32)
    b_tail = cx.buf("tail")
    s.op("pool", lambda e: e.memset(tail[:], 0.0), writes=[b_tail])
    carry = cx.sb("carry", [96, 16], F32)
    b_carry = cx.buf("carry")
    s.op("pool", lambda e: e.memset(carry[:], 0.0), writes=[b_carry])
    P = {"xr": Rot(cx, "xrb", 3, [96, NX + 2], F32), "G": Rot(cx, "G", 2, [96, N], F32),
         "c": Rot(cx, "c", 5, [96, N], F32), "ct": Rot(cx, "ct", 2, [96, N], F32), "cb": Rot(cx, "cb", 2, [96, N], BF16),
         "r": Rot(cx, "r", 2, [96, N], F32), "i": Rot(cx, "i", 3, [96, N], F32), "a": Rot(cx, "a", 11, [96, N], F32),
         "m": Rot(cx, "m", 10, [96, N], F32), "u": Rot(cx, "u", 10, [96, N], F32), "h": Rot(cx, "h", 2, [96, N], F32)}
    halo = cx.over.get("halo_sb")
    state = {"nxt": None}

    def load_x(it):
        t_, b_ = xt.get()
        if halo is not None and it == NT - 1:
            s.dma("sp", t_[:, :, 0:N], xv[:, :, it * N:it * N + N], writes=[b_], anchor=b_)
            s.op("pool", lambda e, t_=t_: e.tensor_copy(out=t_[:, :, N:NX], in_=halo[:]), writes=[b_])
        else:
            s.dma("sp", t_[:], xv[:, :, it * N:it * N + NX], writes=[b_], anchor=b_)
        return t_, b_

    state["nxt"] = load_x(0)

    def st_proj(itm):
        it, n = itm["it"], itm["n"]
        if n == 0:
            x_t, xb = state["nxt"]
            if it + 1 < NT:
                state["nxt"] = load_x(it + 1)
            h_t, h_b = htp.get()
            state["ht"] = (h_t, h_b)
            for k in range(8):
                s.op("act", lambda e, k=k: e.activation(out=h_t[:, k, :], in_=x_t[:, k, :], func=AF.Identity,
                                                        bias=mod[:, k:k + 1], scale=mod1p[:, 8 + k:9 + k]),
                     reads=[xb, b_mod, b_mod1p], writes=[h_b])
            if not carry_only:
                for n2 in range(16):
                    pg, pgb = GB.get()
                    for k in range(8):
                        s.op("pe", lambda e, k=k, pg=pg, n2=n2: e.matmul(pg[0:96, 0:N], lhsT=w_inb[:, k, DRNN + n2 * 96:DRNN + (n2 + 1) * 96],
                                                                       rhs=h_t[:, k, 0:N], start=(k == 0), stop=(k == 7)),
                             reads=[b_win, h_b], writes=[pgb], inc=(k == 7))
                    G_t, G_b = P["G"].get()
                    s.op("act", lambda e, pg=pg, G_t=G_t: e.activation(out=G_t[:], in_=pg[0:96, 0:N], func=AF.Gelu), reads=[pgb], writes=[G_b])
                    s.dma("sp", Gv[:, n2, it * N:it * N + N], G_t[:], reads=[G_b], anchor=G_b)
        h_t, h_b = state["ht"]
        pbk, pbb_ = XB.get()
        itm["X"], itm["X_b"] = pbk, pbb_
        for k in range(8):
            s.op("pe", lambda e, k=k: e.matmul(pbk[0:96, 0:NX], lhsT=w_inb[:, k, n * 96:(n + 1) * 96], rhs=h_t[:, k, :],
                                               start=(k == 0), stop=(k == 7)),
                 reads=[b_win, h_b], writes=[pbb_], inc=(k == 7))

    def st_evac(itm):
        it, n = itm["it"], itm["n"]
        t0 = it * N
        xr_t, xr_b = P["xr"].get()
        itm["xr_t"], itm["xr_b"] = xr_t, xr_b
        pbk = itm["X"]
        s.op("act", lambda e: e.activation(out=xr_t[:, 2:NX + 2], in_=pbk[0:96, 0:NX], func=AF.Identity),
             reads=[itm["X_b"]], writes=[xr_b])

    def st_conv(itm):
        it, n = itm["it"], itm["n"]
        t0 = it * N
        xr_t, xr_b = itm["xr_t"], itm["xr_b"]
        s.op("pool", lambda e: e.tensor_copy(out=xr_t[:, 0:2], in_=tail[:, n, :]), reads=[b_tail], writes=[xr_b])
        s.op("pool", lambda e: e.tensor_copy(out=tail[:, n, :], in_=xr_t[:, N:N + 2]), reads=[xr_b], writes=[b_tail])
        c_t, c_b = P["c"].get()
        itm["c_t"], itm["c_b"] = c_t, c_b
        s.op("pool", lambda e: e.tensor_scalar(out=c_t[:], in0=xr_t[:, 0:N], scalar1=cw[:, n * 6:n * 6 + 1],
                                               scalar2=cw[:, n * 6 + 5:n * 6 + 6], op0=ALU.mult, op1=ALU.add),
             reads=[xr_b, b_cw], writes=[c_b])
        for j in range(1, 5):
            yield
            s.op("dve", lambda e, j=j: e.scalar_tensor_tensor(out=c_t[:], in0=xr_t[:, j:j + N], scalar=cw[:, n * 6 + j:n * 6 + j + 1],
                                                              in1=c_t[:], op0=ALU.mult, op1=ALU.add),
                 reads=[xr_b, b_cw, c_b], writes=[c_b])
        if not carry_only:
            s.dma("sp", cv[:, n, t0:t0 + N], c_t[:], reads=[c_b], anchor=c_b)

    items = [{"it": it, "n": n, "idx": it * 16 + n} for it in range(NT) for n in range(16)]
    st_gmm, st_sig, st_m, st_sqrt = gate_stage_fns(cx, g, N, RI, P, items)

    def st_scan(itm):
        it, n = itm["it"], itm["n"]
        t0 = it * N
        u_t, u_b, m_t, m_b, a_t, a_b = itm["u_t"], itm["u_b"], itm["m_t"], itm["m_b"], itm["a_t"], itm["a_b"]
        s.op("dve", lambda e: e.scalar_tensor_tensor(out=u_t[:], in0=u_t[:], scalar=0.5, in1=m_t[:], op0=ALU.mult, op1=ALU.mult),
             reads=[u_b, m_b], writes=[u_b])
        h_t, h_b = P["h"].get()
        yield
        s.op("dve", lambda e: e.tensor_tensor_scan(out=h_t[:], data0=a_t[:], data1=u_t[:], initial=carry[:, n:n + 1],
                                                   op0=ALU.mult, op1=ALU.add),
             reads=[a_b, u_b, b_carry], writes=[h_b])
        s.op("pool", lambda e: e.tensor_copy(out=carry[:, n:n + 1], in_=h_t[:, N - 1:N]), reads=[h_b], writes=[b_carry])
        if not carry_only:
            s.dma("sp", hv[:, n, t0:t0 + N], h_t[:], reads=[h_b], anchor=h_b)

    nop = lambda itm: None
    run_pipeline(items, [st_proj, st_evac, st_conv, st_gmm, st_sig, st_m, nop, nop, nop, st_sqrt, nop, nop, nop, st_scan])
    if cx.over.get("carry_out") is not None:
        s.dma("sp", cx.over["carry_out"], carry[:], reads=[b_carry], anchor=b_carry)
    s.final_wait("sp", P["c"].b + P["G"].b + P["h"].b + [b_carry])
    return cx.end_phase()


def build_rnn2(cx=None, prefix="", over=None):
    cx = cx or Ctx()
    cx.begin_phase(prefix, over)
    s = cx.s
    N = 256
    NT = T // N
    xT = cx.dram_in("xT", [D, T])
    cT = cx.dram_in("cT", [DRNN, T])
    GT = cx.dram_in("GT", [DRNN, T])
    h1T = cx.dram_in("h1T", [DRNN, T])
    carry_in = None if (over and over.get("carry_sb") is not None) else cx.dram_in("carry_in", [96, 16])
    c_col = cx.dram_in("c_col", [128, 8])
    ada_w = cx.dram_in("ada_w", [D, 3 * D])
    ada_bT = cx.dram_in("ada_bT", [128, 24])
    ln_gT = cx.dram_in("ln_gT", [128, 8])
    ln_bT = cx.dram_in("ln_bT", [128, 8])
    w_a = cx.dram_in("w_a", [16, 96, 96])
    w_x = cx.dram_in("w_x", [16, 96, 96])
    b_aT = cx.dram_in("b_aT", [96, 16])
    b_xT = cx.dram_in("b_xT", [96, 16])
    lamT = cx.dram_in("lamT", [96, 16])
    w_out = cx.dram_in("w_out", [DRNN, D])
    yT = cx.dram_out("yT", [D, T])

    cst = emit_consts(cx)
    RI = RotBanks(cx, "pRI", 3)
    WB = RotBanks(cx, "pW", 4)
    stp = cx.ps("pst")
    st_b = cx.buf("pst")
    z = cx.sb("z", [128, 8, N], F32)
    bz = cx.buf("z")
    pieces = [z[:, 0:4, :].rearrange("p a (b n) -> p (a b) n", n=128), z[:, 4:8, :].rearrange("p a (b n) -> p (a b) n", n=128)]
    ada = emit_adaln(cx, c_col, ada_w, ada_bT, stp, st_b, pieces, [cx.buf("pc0"), cx.buf("pc1")], [bz])
    mod1p, b_mod1p = ada["mod1p"], ada["b_mod1p"]
    gT = cx.sb("gT", [128, 8], F32)
    bT = cx.sb("bT", [128, 8], F32)
    b_par = cx.buf("par")
    s.dma("sp", gT[:], ln_gT[:, :], writes=[b_par], anchor=b_par)
    s.dma("sp", bT[:], ln_bT[:, :], writes=[b_par], anchor=b_par)
    g = load_gate_params(cx, w_a, w_x, b_aT, b_xT, lamT)
    woutb = cx.sb("woutb", [96, 16, D], BF16)
    b_wout = cx.buf("wout")
    s_w = w_out.rearrange("(n p) d -> p n d", p=96)
    for n in range(16):
        s.dma("pool", woutb[:, n, :], s_w[:, n, :], writes=[b_wout], anchor=b_wout)
    carry = cx.sb("carry", [96, 16], F32)
    b_carry = cx.buf("carry")
    if cx.over.get("carry_sb") is not None:
        s.op("pool", lambda e: e.tensor_copy(out=carry[:], in_=cx.over["carry_sb"][:]), writes=[b_carry])
    else:
        s.dma("sp", carry[:], carry_in[:, :], writes=[b_carry], anchor=b_carry)

    xv = xT.rearrange("(k p) t -> p k t", p=128)
    yv = yT.rearrange("(k p) t -> p k t", p=128)
    cv = cT.rearrange("(n p) t -> p n t", p=96)
    Gv = GT.rearrange("(n p) t -> p n t", p=96)
    hv = h1T.rearrange("(n p) t -> p n t", p=96)
    xt = Rot(cx, "xt", 2, [128, 8, N], F32)
    P = {"c": Rot(cx, "c", 5, [96, N], F32), "G": Rot(cx, "G", 4, [96, N], F32), "h1": Rot(cx, "h1", 4, [96, N], F32),
         "cb": Rot(cx, "cb", 2, [96, N], BF16), "r": Rot(cx, "r", 2, [96, N], F32), "i": Rot(cx, "i", 3, [96, N], F32),
         "a": Rot(cx, "a", 11, [96, N], F32), "m": Rot(cx, "m", 10, [96, N], F32), "u": Rot(cx, "u", 10, [96, N], F32),
         "h2": Rot(cx, "h2", 2, [96, N], F32)}
    ytp = Rot(cx, "yt", 2, [96, 16, N], BF16)
    zb_t = cx.sb("zb", [128, 8, N], BF16)
    bzb = cx.buf("zb")
    zsq_t = cx.sb("zsq", [128, 8, N], BF16)
    bzsq = cx.buf("zsq")
    tmp = cx.sb("tmp", [128, 4, N], F32)
    btmp = cx.buf("tmp")
    state = {}

    def st_load(itm):
        it, n = itm["it"], itm["n"]
        t0 = it * N
        if n == 0:
            x_t, xb = xt.get()
            s.dma("sp", x_t[:], xv[:, :, t0:t0 + N], writes=[xb], anchor=xb)
            s.op("act", lambda e: e.activation(out=x_t[:], in_=x_t[:], func=AF.Identity, scale=float(DN_ALPHA)),
                 reads=[xb], writes=[xb])
            state[("x", it)] = (x_t, xb)
            state[("y", it)] = ytp.get()
        c_t, c_b = P["c"].get()
        itm.update(c_t=c_t, c_b=c_b)
        s.dma("sp", c_t[:], cv[:, n, t0:t0 + N], writes=[c_b], anchor=c_b)

    def st_load2(itm):
        it, n = itm["it"], itm["n"]
        t0 = it * N
        G_t, G_b = P["G"].get()
        h1_t, h1_b = P["h1"].get()
        itm.update(G_t=G_t, G_b=G_b, h1_t=h1_t, h1_b=h1_b)
        s.dma("sp", G_t[:], Gv[:, n, t0:t0 + N], writes=[G_b], anchor=G_b)
        s.dma("sp", h1_t[:], hv[:, n, t0:t0 + N], writes=[h1_b], anchor=h1_b)

    items = [{"it": it, "n": n} for it in range(NT - 1, -1, -1) for n in range(16)]
    for j_, itm_ in enumerate(items):
        itm_["idx"] = j_
    st_gmm, st_sig, st_m, st_sqrt = gate_stage_fns(cx, g, N, RI, P, items)

    def st_scan(itm):
        it, n = itm["it"], itm["n"]
        t0 = it * N
        u_t, u_b, m_t, m_b, a_t, a_b = itm["u_t"], itm["u_b"], itm["m_t"], itm["m_b"], itm["a_t"], itm["a_b"]
        h1_t, h1_b, G_t, G_b = itm["h1_t"], itm["h1_b"], itm["G_t"], itm["G_b"]
        y_t, y_b = state[("y", it)]
        s.op("dve", lambda e: e.scalar_tensor_tensor(out=u_t[:], in0=u_t[:], scalar=0.5, in1=m_t[:], op0=ALU.mult, op1=ALU.mult),
             reads=[u_b, m_b], writes=[u_b])
        h2_t, h2_b = P["h2"].get()
        yield
        s.op("dve", lambda e: e.tensor_tensor_scan(out=h2_t[:, ::-1], data0=a_t[:, ::-1], data1=u_t[:, ::-1], initial=carry[:, n:n + 1],
                                                   op0=ALU.mult, op1=ALU.add),
             reads=[a_b, u_b, b_carry], writes=[h2_b])
        s.op("pool", lambda e: e.tensor_copy(out=carry[:, n:n + 1], in_=h2_t[:, 0:1]), reads=[h2_b], writes=[b_carry])
        yield
        s.op("dve", lambda e: e.tensor_tensor(out=h2_t[:], in0=h2_t[:], in1=h1_t[:], op=ALU.add), reads=[h2_b, h1_b], writes=[h2_b])
        yield
        s.op("dve", lambda e: e.tensor_tensor(out=y_t[:, n, :], in0=h2_t[:], in1=G_t[:], op=ALU.mult), reads=[h2_b, G_b], writes=[y_b])
        if n == 15:
            x_t, xb = state[("x", it)]
            for oc in range(8):
                pbk, pbb_ = WB.get()
                for nn in range(16):
                    s.op("pe", lambda e, pbk=pbk, nn=nn, oc=oc: e.matmul(pbk[:, 0:N], lhsT=woutb[:, nn, oc * 128:(oc + 1) * 128], rhs=y_t[:, nn, :],
                                                                         start=(nn == 0), stop=(nn == 15)),
                         reads=[b_wout, y_b], writes=[pbb_], inc=(nn == 15))
                s.op("dve", lambda e, pbk=pbk, oc=oc: e.scalar_tensor_tensor(out=z[:, oc, :], in0=pbk[:, 0:N], scalar=mod1p[:, 16 + oc:17 + oc],
                                                                             in1=x_t[:, oc, :], op0=ALU.mult, op1=ALU.add),
                     reads=[pbb_, xb, b_mod1p], writes=[bz])
            emit_ln_tile(cx, cst, z, zb_t, zsq_t, stp[:, 0:N], st_b, stp[:, N:2 * N], st_b, tmp, N, gT, bT, b_par, z, bz, bzb, bzsq, btmp, bz)
            s.dma("sp", yv[:, :, t0:t0 + N], z[:], reads=[bz], anchor=bz)

    nop = lambda itm: None
    run_pipeline(items, [st_load, st_gmm, st_sig, st_m, nop, nop, nop, st_sqrt, nop, st_load2, nop, st_scan])
    s.final_wait("sp", [bz])
    return cx.end_phase()


def build_xch(cx, prefix, bounce_in, bounce_out, P, F, result_sb, swap2):
    cx.begin_phase(prefix, None)
    s = cx.s
    sel_d = cx.dram_in("sel", [128, 8])
    sel = cx.sb("sel_sb", [128, 8], F32)
    b_sel = cx.buf("sel")
    s.dma("sp", sel[:], sel_d[:, :], writes=[b_sel], anchor=b_sel)
    b_in, b_out = cx.buf("bin"), cx.buf("bout")
    s.collective(bounce_in.ap(), bounce_out.ap(), reads=[b_in], writes=[b_out], anchor=b_out)
    g = cx.sb("xg", [P, 8, F], F32)
    b_g = cx.buf("xg")
    s.dma("sp", g[:], bounce_out.ap().rearrange("(r p) f -> p r f", p=P), reads=[b_out], writes=[b_g], anchor=b_g)
    acc = cx.sb("xacc", [P, F], F32)
    b_acc = cx.buf("xacc")
    s.op("dve", lambda e: e.tensor_scalar(out=acc[:], in0=g[:, 0, :], scalar1=sel[0:P, 0:1], scalar2=None, op0=ALU.mult),
         reads=[b_g, b_sel], writes=[b_acc])
    for r in range(1, 8):
        s.op("dve", lambda e, r=r: e.scalar_tensor_tensor(out=acc[:], in0=g[:, r, :], scalar=sel[0:P, r:r + 1], in1=acc[:],
                                                          op0=ALU.mult, op1=ALU.add),
             reads=[b_g, b_sel, b_acc], writes=[b_acc])
    b_res = cx.buf("res")
    if swap2:
        av = acc[:].rearrange("p (k t) -> p k t", t=2)
        s.op("dve", lambda e: e.tensor_copy(out=result_sb[:, :, 0:1], in_=av[:, :, 1:2]), reads=[b_acc], writes=[b_res])
        s.op("dve", lambda e: e.tensor_copy(out=result_sb[:, :, 1:2], in_=av[:, :, 0:1]), reads=[b_acc], writes=[b_res])
    else:
        s.op("dve", lambda e: e.tensor_copy(out=result_sb[:], in_=acc[:]), reads=[b_acc], writes=[b_res])
    return cx.end_phase()


def build_lxch(cx, prefix, bounce, P, F, result_sb, swap2):
    cx.begin_phase(prefix, None)
    s = cx.s
    acc = cx.sb("xacc", [P, F], F32)
    b_acc = cx.buf("xacc")
    s.dma("sp", acc[:], bounce.ap(), writes=[b_acc], anchor=b_acc)
    b_res = cx.buf("res")
    if swap2:
        av = acc[:].rearrange("p (k t) -> p k t", t=2)
        s.op("dve", lambda e: e.tensor_copy(out=result_sb[:, :, 0:1], in_=av[:, :, 1:2]), reads=[b_acc], writes=[b_res])
        s.op("dve", lambda e: e.tensor_copy(out=result_sb[:, :, 1:2], in_=av[:, :, 0:1]), reads=[b_acc], writes=[b_res])
    else:
        s.op("dve", lambda e: e.tensor_copy(out=result_sb[:], in_=acc[:]), reads=[b_acc], writes=[b_res])
    return cx.end_phase()


def build_fused2():
    cx = Ctx(fused=True)
    tmp = lambda n, sh: cx.dram_tmp(n, sh).ap()
    x0s, x1s, x0o, x1o = tmp("x0s", [D, T]), tmp("x1s", [D, T]), tmp("x0o", [D, T]), tmp("x1o", [D, T])
    cT, GT, h1T, x2 = tmp("cT_s", [DRNN, T]), tmp("GT_s", [DRNN, T]), tmp("h1T_s", [DRNN, T]), tmp("x2", [D, T])
    bh_s, bh_o = cx.dram_tmp("bh_s", [128, 16]), cx.dram_tmp("bh_o", [128, 16])
    bc_o = cx.dram_tmp("bc_o", [96, 16])
    bc_s = cx.dram_tmp("bc_s", [96, 16])
    halo_s = cx.gsb("halo_s", [128, 8, 2], F32)
    halo_o = cx.gsb("halo_o", [128, 8, 2], F32)
    carry_sb = cx.gsb("carry_sb", [96, 16], F32)
    cx.ada_tiles = {k: (cx.gsb("ada_mod_" + k, [128, 24], F32), cx.gsb("ada_mod1p_" + k, [128, 24], F32)) for k in ("00", "01", "10", "11")}
    out = cx.nc.dram_tensor("out", [D, T], F32, kind="ExternalOutput").ap()
    D_ = cx.decl

    def share(src, dst_names):
        return {n: D_[src + n] for n in dst_names}

    build_attn(cx, "a_", {"yT": x0s, "ada_key": "00"})
    ov = share("a_", ["c_col", "ada_w", "ada_bT", "ln_gT", "ln_bT", "w_in", "w_out", "perm", "mprev", "mnext", "sink_rep"])
    ov.update({"yT": x0o, "ada_key": "00"})
    build_attn(cx, "b_", ov)
    build_mlp(cx, "m0_", {"xT": x0s, "yT": x1s, "ada_key": "01",
                           "jobs": [(x0s, x1s, bh_s.ap()), (x0o, x1o, bh_o.ap())]})
    build_lxch(cx, "l1_", bh_o, 128, 16, halo_s, True)
    build_lxch(cx, "l2_", bh_s, 128, 16, halo_o, True)
    build_rnn1(cx, "r1_", {"xT": x1s, "halo_sb": halo_s, "cT": cT, "GT": GT, "h1T": h1T, "carry_out": bc_s.ap(), "ada_key": "10"})
    ov = share("r1_", ["c_col", "ada_w", "ada_bT", "w_in"])
    ov.update({"xT": x1o, "halo_sb": halo_o, "carry_only": True, "carry_out": bc_o.ap(), "ada_key": "10"})
    build_rnn1(cx, "q1_", ov)
    build_lxch(cx, "l3_", bc_o, 96, 16, carry_sb, False)
    ov = share("r1_", ["c_col", "ada_w", "ada_bT"])
    ov.update({"xT": x1s, "cT": cT, "GT": GT, "h1T": h1T, "carry_sb": carry_sb, "yT": x2, "ada_key": "10"})
    build_rnn2(cx, "r2_", ov)
    build_mlp(cx, "m1_", {"xT": x2, "yT": out, "ada_key": "11"})
    cx.gstack.close()
    return cx.nc


def build_fused():
    cx = Ctx(fused=True)
    x0a = cx.dram_tmp("x0a", [D, T]).ap()
    x1 = cx.dram_tmp("x1", [D, T]).ap()
    cT = cx.dram_tmp("cT_s", [DRNN, T]).ap()
    GT = cx.dram_tmp("GT_s", [DRNN, T]).ap()
    h1T = cx.dram_tmp("h1T_s", [DRNN, T]).ap()
    x2 = cx.dram_tmp("x2", [D, T]).ap()
    b1_in = cx.dram_tmp("b1_in", [128, 16])
    b1_out = cx.dram_tmp("b1_out", [8 * 128, 16])
    b2_in = cx.dram_tmp("b2_in", [96, 16])
    b2_out = cx.dram_tmp("b2_out", [8 * 96, 16])
    halo_sb = cx.gsb("halo_sb", [128, 8, 2], F32)
    carry_sb = cx.gsb("carry_sb", [96, 16], F32)
    out = cx.nc.dram_tensor("out", [D, T], F32, kind="ExternalOutput").ap()
    build_attn(cx, "a_", {"yT": x0a})
    build_mlp(cx, "m0_", {"xT": x0a, "yT": x1, "tail_out": b1_in.ap()})
    build_xch(cx, "x1_", b1_in, b1_out, 128, 16, halo_sb, True)
    build_rnn1(cx, "r1_", {"xT": x1, "halo_sb": halo_sb, "cT": cT, "GT": GT, "h1T": h1T, "carry_out": b2_in.ap()})
    build_xch(cx, "x2_", b2_in, b2_out, 96, 16, carry_sb, False)
    build_rnn2(cx, "r2_", {"xT": x1, "cT": cT, "GT": GT, "h1T": h1T, "carry_sb": carry_sb, "yT": x2})
    build_mlp(cx, "m1_", {"xT": x2, "yT": out})
    cx.gstack.close()
    return cx.nc


def colT(v, n):
    return np.ascontiguousarray(np.asarray(v, np.float32).reshape(n, 128).T)


_PROGS = {}


def get_prog(name):
    if name not in _PROGS:
        _PROGS[name] = {"mlp": build_mlp, "attn": build_attn, "rnn1": build_rnn1, "rnn2": build_rnn2, "fused": build_fused, "fused2": build_fused2}[name]()
    return _PROGS[name]


def run_mlp(xT_list, c, ada_w, ada_b, ln_g, ln_b, w1, w2):
    nc = get_prog("mlp")
    in_maps = []
    for core in range(NCORES):
        b = core // 2
        in_maps.append({
            "xT": xT_list[core], "c_col": colT(c[b], 8), "ada_w": np.ascontiguousarray(ada_w),
            "ada_bT": colT(ada_b, 24), "ln_gT": colT(ln_g, 8), "ln_bT": colT(ln_b, 8),
            "w1": np.ascontiguousarray(w1), "w2": np.ascontiguousarray(w2),
        })
    res = run_bass_kernel_spmd(nc, in_maps, core_ids=list(range(NCORES)))
    return [r["yT"] for r in res.results]


ROT = 32
ROPE_THETA = 500000.0


def rope_tables(pos):
    inv_freq = (np.float32(ROPE_THETA) ** (-np.arange(0, ROT, 2, dtype=np.float32) / np.float32(ROT))).astype(np.float32)
    ang = (pos.astype(np.float32)[None, :] * inv_freq[:, None]).astype(np.float32)
    c = np.cos(ang).astype(np.float32)
    sn = np.sin(ang).astype(np.float32)
    return np.ascontiguousarray(np.concatenate([c, c], 0)), np.ascontiguousarray(np.concatenate([-sn, sn], 0))


def local_positions(core):
    half = core % 2
    if half == 0:
        return np.arange(0, T + 128)
    return np.arange(2 * T - 1, T - 129, -1)


def attn_consts():
    perm = np.zeros((32, 32), np.float32)
    for i in range(32):
        perm[(i + 16) % 32, i] = 1.0
    j = np.arange(128)[:, None]
    q = np.arange(128)[None, :]
    mprev = np.tile((j >= q).astype(np.float32), (1, 4))
    mnext = np.tile((j <= q).astype(np.float32), (1, 4))
    return perm, np.ascontiguousarray(mprev), np.ascontiguousarray(mnext)


def run_attn(xTh_list, c, ada_w, ada_b, ln_g, ln_b, w_in, w_out, sinks):
    nc = get_prog("attn")
    perm, mprev, mnext = attn_consts()
    in_maps = []
    for core in range(NCORES):
        b = core // 2
        cosT, sinT = rope_tables(local_positions(core))
        in_maps.append({
            "xT": xTh_list[core], "c_col": colT(c[b], 8), "ada_w": np.ascontiguousarray(ada_w),
            "ada_bT": colT(ada_b, 24), "ln_gT": colT(ln_g, 8), "ln_bT": colT(ln_b, 8),
            "w_in": np.ascontiguousarray(w_in), "w_out": np.ascontiguousarray(w_out),
            "cosT": cosT, "sinT": sinT, "perm": perm, "mprev": mprev, "mnext": mnext,
            "sink_rep": np.ascontiguousarray(np.tile(np.asarray(sinks, np.float32)[None, :], (128, 1))),
        })
    res = run_bass_kernel_spmd(nc, in_maps, core_ids=list(range(NCORES)))
    return [r["yT"] for r in res.results]


def shard_x(x):
    out = []
    for core in range(NCORES):
        b = core // 2
        idx = local_positions(core)
        out.append(np.ascontiguousarray(x[b][idx, :].T))
    return out


def colT96(v):
    return np.ascontiguousarray(np.asarray(v, np.float32).reshape(16, 96).T)


def conv_table(conv_w, conv_b, half):
    taps = np.zeros((5, DRNN), np.float32)
    for j in range(4):
        if half == 0:
            taps[j] = conv_w[j]
        else:
            taps[4 - j] = conv_w[j]
    tab = np.zeros((96, 16, 6), np.float32)
    for j in range(5):
        tab[:, :, j] = taps[j].reshape(16, 96).T
    tab[:, :, 5] = np.asarray(conv_b, np.float32).reshape(16, 96).T
    return np.ascontiguousarray(tab.reshape(96, 96))


def _common(c, ada_w, ada_b, core):
    b = core // 2
    return {"c_col": colT(c[b], 8), "ada_w": np.ascontiguousarray(ada_w), "ada_bT": colT(ada_b, 24)}


def run_rnn1(x1_list, c, ada_w, ada_b, w_in, conv_w, conv_b, w_a, b_a, w_x, b_x, lam):
    nc = get_prog("rnn1")
    in_maps = []
    for core in range(NCORES):
        half = core % 2
        par = x1_list[core ^ 1]
        xh = np.ascontiguousarray(np.concatenate([x1_list[core], par[:, T - 1:T], par[:, T - 2:T - 1]], axis=1))
        d1 = half
        m = _common(c, ada_w, ada_b, core)
        m.update({"xT": xh, "w_in": np.ascontiguousarray(w_in), "convT": conv_table(conv_w, conv_b, half),
                  "w_a": np.ascontiguousarray(w_a[d1]), "w_x": np.ascontiguousarray(w_x[d1]),
                  "b_aT": colT96(b_a[d1]), "b_xT": colT96(b_x[d1]), "lamT": colT96(lam[d1])})
        in_maps.append(m)
    res = run_bass_kernel_spmd(nc, in_maps, core_ids=list(range(NCORES)))
    return [(r["cT"], r["GT"], r["h1T"]) for r in res.results]


def run_rnn2(x1_list, r1, c, ada_w, ada_b, ln_g, ln_b, w_a, b_a, w_x, b_x, lam, w_out):
    nc = get_prog("rnn2")
    in_maps = []
    for core in range(NCORES):
        half = core % 2
        d2 = 1 - half
        cT, GT, h1T = r1[core]
        carry = colT96(r1[core ^ 1][2][:, T - 1])
        m = _common(c, ada_w, ada_b, core)
        m.update({"xT": x1_list[core], "cT": cT, "GT": GT, "h1T": h1T, "carry_in": carry,
                  "ln_gT": colT(ln_g, 8), "ln_bT": colT(ln_b, 8),
                  "w_a": np.ascontiguousarray(w_a[d2]), "w_x": np.ascontiguousarray(w_x[d2]),
                  "b_aT": colT96(b_a[d2]), "b_xT": colT96(b_x[d2]), "lamT": colT96(lam[d2]),
                  "w_out": np.ascontiguousarray(w_out)})
        in_maps.append(m)
    res = run_bass_kernel_spmd(nc, in_maps, core_ids=list(range(NCORES)))
    return [r["yT"] for r in res.results]


def kernel_unfused(x, c, ada_w, ada_b, ln_g, ln_b, attn_w_in, attn_w_out, attn_sinks,
                   rnn_w_in, rnn_conv_w, rnn_conv_b, rnn_w_a, rnn_b_a, rnn_w_x, rnn_b_x, rnn_lam,
                   rnn_w_out, mlp_w1, mlp_w2):
    f = lambda a: np.asarray(a, np.float32)
    x, c, ada_w, ada_b, ln_g, ln_b = f(x), f(c), f(ada_w), f(ada_b), f(ln_g), f(ln_b)
    xs = shard_x(x)
    a0 = run_attn(xs, c, ada_w[0, 0], ada_b[0, 0], ln_g[0, 0], ln_b[0, 0], f(attn_w_in)[0], f(attn_w_out)[0], f(attn_sinks)[0])
    m0 = run_mlp(a0, c, ada_w[0, 1], ada_b[0, 1], ln_g[0, 1], ln_b[0, 1], f(mlp_w1)[0], f(mlp_w2)[0])
    r1 = run_rnn1(m0, c, ada_w[1, 0], ada_b[1, 0], f(rnn_w_in)[0], f(rnn_conv_w)[0], f(rnn_conv_b)[0],
                  f(rnn_w_a)[0], f(rnn_b_a)[0], f(rnn_w_x)[0], f(rnn_b_x)[0], f(rnn_lam)[0])
    r2 = run_rnn2(m0, r1, c, ada_w[1, 0], ada_b[1, 0], ln_g[1, 0], ln_b[1, 0],
                  f(rnn_w_a)[0], f(rnn_b_a)[0], f(rnn_w_x)[0], f(rnn_b_x)[0], f(rnn_lam)[0], f(rnn_w_out)[0])
    m1 = run_mlp(r2, c, ada_w[1, 1], ada_b[1, 1], ln_g[1, 1], ln_b[1, 1], f(mlp_w1)[1], f(mlp_w2)[1])
    out = np.empty((4, 2 * T, D), np.float32)
    for core in range(NCORES):
        idx = local_positions(core)[:T]
        out[core // 2][idx, :] = m1[core].T
    return out


def kernel(x, c, ada_w, ada_b, ln_g, ln_b, attn_w_in, attn_w_out, attn_sinks,
           rnn_w_in, rnn_conv_w, rnn_conv_b, rnn_w_a, rnn_b_a, rnn_w_x, rnn_b_x, rnn_lam,
           rnn_w_out, mlp_w1, mlp_w2):
    f = lambda a: np.ascontiguousarray(np.asarray(a, np.float32))
    x, c, ada_w, ada_b, ln_g, ln_b = f(x), f(c), f(ada_w), f(ada_b), f(ln_g), f(ln_b)
    attn_w_in, attn_w_out, attn_sinks = f(attn_w_in), f(attn_w_out), f(attn_sinks)
    rnn_w_in, rnn_conv_w, rnn_conv_b, rnn_w_out = f(rnn_w_in), f(rnn_conv_w), f(rnn_conv_b), f(rnn_w_out)
    rnn_w_a, rnn_b_a, rnn_w_x, rnn_b_x, rnn_lam = f(rnn_w_a), f(rnn_b_a), f(rnn_w_x), f(rnn_b_x), f(rnn_lam)
    mlp_w1, mlp_w2 = f(mlp_w1), f(mlp_w2)
    nc = get_prog("fused2")
    xs = shard_x(x)
    perm, mprev, mnext = attn_consts()
    in_maps = []
    for core in range(NCORES):
        b = core // 2
        half = core % 2
        d1, d2 = half, 1 - half
        cosT, sinT = rope_tables(local_positions(core))
        sel = np.zeros((128, 8), np.float32)
        sel[:, core ^ 1] = 1.0
        m = {}

        def ada(pfx, i, j, ln=True):
            m[pfx + "c_col"] = colT(c[b], 8)
            m[pfx + "ada_w"] = ada_w[i, j]
            m[pfx + "ada_bT"] = colT(ada_b[i, j], 24)
            if ln:
                m[pfx + "ln_gT"] = colT(ln_g[i, j], 8)
                m[pfx + "ln_bT"] = colT(ln_b[i, j], 8)

        ada("a_", 0, 0)
        m.update({"a_xT": xs[core], "a_w_in": attn_w_in[0], "a_w_out": attn_w_out[0], "a_cosT": cosT, "a_sinT": sinT,
                  "a_perm": perm, "a_mprev": mprev, "a_mnext": mnext,
                  "a_sink_rep": np.ascontiguousarray(np.tile(attn_sinks[0][None, :], (128, 1)))})
        ada("m0_", 0, 1)
        m.update({"m0_w1": mlp_w1[0], "m0_w2": mlp_w2[0]})
        oc_ = core ^ 1
        oh = oc_ % 2
        cosO, sinO = rope_tables(local_positions(oc_))
        m.update({"b_xT": xs[oc_], "b_cosT": cosO, "b_sinT": sinO})
        m.update({"q1_convT": conv_table(rnn_conv_w[0], rnn_conv_b[0], oh),
                  "q1_w_a": rnn_w_a[0, oh], "q1_w_x": rnn_w_x[0, oh], "q1_b_aT": colT96(rnn_b_a[0, oh]),
                  "q1_b_xT": colT96(rnn_b_x[0, oh]), "q1_lamT": colT96(rnn_lam[0, oh])})
        ada("r1_", 1, 0, ln=False)
        m.update({"r1_w_in": rnn_w_in[0], "r1_convT": conv_table(rnn_conv_w[0], rnn_conv_b[0], half),
                  "r1_w_a": rnn_w_a[0, d1], "r1_w_x": rnn_w_x[0, d1], "r1_b_aT": colT96(rnn_b_a[0, d1]),
                  "r1_b_xT": colT96(rnn_b_x[0, d1]), "r1_lamT": colT96(rnn_lam[0, d1])})
        m["r2_ln_gT"] = colT(ln_g[1, 0], 8)
        m["r2_ln_bT"] = colT(ln_b[1, 0], 8)
        m.update({"r2_w_a": rnn_w_a[0, d2], "r2_w_x": rnn_w_x[0, d2], "r2_b_aT": colT96(rnn_b_a[0, d2]),
                  "r2_b_xT": colT96(rnn_b_x[0, d2]), "r2_lamT": colT96(rnn_lam[0, d2]), "r2_w_out": rnn_w_out[0]})
        ada("m1_", 1, 1)
        m.update({"m1_w1": mlp_w1[1], "m1_w2": mlp_w2[1]})
        in_maps.append({k: np.ascontiguousarray(v) for k, v in m.items()})
    res = run_bass_kernel_spmd(nc, in_maps, core_ids=list(range(NCORES)))
    out = np.empty((4, 2 * T, D), np.float32)
    for core in range(NCORES):
        idx = local_positions(core)[:T]
        out[core // 2][idx, :] = res.results[core]["out"].T
    return out
```
